# Optimizing a Trainium2 kernel written in Bass

```python
import jax, jax.numpy as jnp
from jax import lax
import numpy as np

D_MODEL = 1024
BATCH = 8
SEQ = 4096
DEPTH = 4

N_EVEN = (DEPTH + 1) // 2
N_ODD = DEPTH // 2
D_PLE = 256
D_FF = 2816
EPS = 1e-6
N_NORMS = 5
NEG_BIG = -1e30
F_MIN = 1e-6
LB_MAX = 0.999

A_HEADS = 8
A_HEAD_DIM = 64
A_WIDTH = A_HEADS * A_HEAD_DIM
A_CHUNK = 128
B_HEADS = 4
B_KEY_DIM = 128
B_VAL_DIM = 128
B_KW = B_HEADS * B_KEY_DIM
B_VW = B_HEADS * B_VAL_DIM
B_CHUNK = 64
EVEN_IN = 2 * A_WIDTH + 2 * B_KW + 2 * B_VW
EVEN_OUT = A_WIDTH + B_VW
C_HEADS = 8
C_NOPE = 128
C_ROPE = 64
C_V = 128
C_QK = C_NOPE + C_ROPE
Q_LORA = 384
KV_LORA = 256
ODD_IN = Q_LORA + KV_LORA + C_ROPE
ATTN_BLOCK = 128
ROPE_THETA = 10000.0
ATTN_SCALE = C_QK ** -0.5

kernel_name = "hybrid_gmlp_hgrn2_mla_macaron_trunk"


def rms_norm(x, g):
    x32 = x.astype(jnp.float32)
    y = x32 * lax.rsqrt(jnp.mean(x32 * x32, axis=-1, keepdims=True) + EPS)
    return (y * g.astype(jnp.float32)).astype(x.dtype)


def swiglu(h, w_gate, w_up, w_down):
    return (jax.nn.silu(h @ w_gate) * (h @ w_up)) @ w_down


def gmlp_spatial(u, v, v_norm, w_s, b_s):
    bsz, seq = u.shape[:2]
    n_chunks = seq // A_CHUNK
    v = rms_norm(v, v_norm)
    vc = v.reshape(bsz, n_chunks, A_CHUNK, A_HEADS, A_HEAD_DIM)
    causal = jnp.tril(jnp.ones((A_CHUNK, A_CHUNK), dtype=bool))
    w = jnp.where(causal[None], w_s, 0.0).astype(vc.dtype)
    mixed = jnp.einsum('hts,bcshd->bcthd', w, vc) + b_s.T[None, None, :, :, None].astype(vc.dtype)
    return u * mixed.reshape(bsz, seq, A_HEADS, A_HEAD_DIM)


def hgrn2_chunkwise(q, log_f, k, v):
    bsz, seq = q.shape[:2]
    n = seq // B_CHUNK

    def to_chunks(t):
        t = t.astype(jnp.float32).reshape(bsz, n, B_CHUNK, B_HEADS, t.shape[-1])
        return jnp.transpose(t, (1, 0, 3, 2, 4))

    qc, kc, vc = to_chunks(q), to_chunks(k), to_chunks(v)
    bc = jnp.cumsum(to_chunks(log_f), axis=3)
    causal = jnp.tril(jnp.ones((B_CHUNK, B_CHUNK), dtype=bool))[:, :, None]

    def step(state, xs):
        q_, k_, v_, b_ = xs
        diff = b_[:, :, :, None, :] - b_[:, :, None, :, :]
        decay = jnp.exp(jnp.where(causal, diff, NEG_BIG))
        scores = jnp.einsum('bhtd,bhtsd->bhts', q_, decay * k_[:, :, None, :, :])
        o = jnp.einsum('bhts,bhse->bhte', scores, v_)
        o = o + jnp.einsum('bhtd,bhde->bhte', q_ * jnp.exp(b_), state)
        b_last = b_[:, :, -1:, :]
        state = jnp.exp(b_last)[:, :, 0, :, None] * state + jnp.einsum(
            'bhsd,bhse->bhde', k_ * jnp.exp(b_last - b_), v_)
        return state, o

    state0 = jnp.zeros((bsz, B_HEADS, B_KEY_DIM, B_VAL_DIM), jnp.float32)
    _, oc = lax.scan(step, state0, (qc, kc, vc, bc))
    return jnp.transpose(oc, (1, 0, 3, 2, 4)).reshape(bsz, seq, B_HEADS, B_VAL_DIM)


def even_mixer(h, w_in, v_norm, w_s, b_s, lb, out_norm, w_out):
    bsz, seq, _ = h.shape
    z = h @ w_in
    a_u, a_v, b_q, b_f, b_i, b_g = jnp.split(
        z, [A_WIDTH, 2 * A_WIDTH, 2 * A_WIDTH + B_KW, 2 * A_WIDTH + 2 * B_KW,
            2 * A_WIDTH + 2 * B_KW + B_VW], axis=-1)
    a_u = jax.nn.gelu(a_u).reshape(bsz, seq, A_HEADS, A_HEAD_DIM)
    a_v = jax.nn.gelu(a_v).reshape(bsz, seq, A_HEADS, A_HEAD_DIM)
    a_out = gmlp_spatial(a_u, a_v, v_norm, w_s, b_s).reshape(bsz, seq, A_WIDTH)
    zf = b_f.astype(jnp.float32)
    f = lb + (1.0 - lb) * jax.nn.sigmoid(zf)
    log_f = jnp.log(jnp.maximum(f, F_MIN))
    k = 1.0 - f
    o = hgrn2_chunkwise(b_q.reshape(bsz, seq, B_HEADS, B_KEY_DIM),
                        log_f.reshape(bsz, seq, B_HEADS, B_KEY_DIM),
                        k.reshape(bsz, seq, B_HEADS, B_KEY_DIM),
                        b_i.reshape(bsz, seq, B_HEADS, B_VAL_DIM))
    o = rms_norm(o, out_norm).astype(h.dtype) * jax.nn.silu(b_g.reshape(bsz, seq, B_HEADS, B_VAL_DIM))
    mixed = jnp.concatenate([a_out, o.reshape(bsz, seq, B_VW)], axis=-1)
    return mixed @ w_out


def rotate_half(x, cos, sin):
    x1, x2 = jnp.split(x, 2, axis=-1)
    return jnp.concatenate([x1 * cos - x2 * sin, x2 * cos + x1 * sin], axis=-1).astype(x.dtype)


def mla_mixer(h, positions, w_in, q_a_norm, kv_a_norm, w_q_b, w_kv_b, q_norm, k_norm, w_out):
    bsz, seq, _ = h.shape
    z = h @ w_in
    c_q, c_kv, k_rope = jnp.split(z, [Q_LORA, Q_LORA + KV_LORA], axis=-1)
    q = (rms_norm(c_q, q_a_norm) @ w_q_b).reshape(bsz, seq, C_HEADS, C_QK)
    kv = (rms_norm(c_kv, kv_a_norm) @ w_kv_b).reshape(bsz, seq, C_HEADS, C_NOPE + C_V)
    k_nope, v = jnp.split(kv, [C_NOPE], axis=-1)
    k_rope = jnp.broadcast_to(k_rope[:, :, None, :], (bsz, seq, C_HEADS, C_ROPE))
    k = jnp.concatenate([k_nope, k_rope], axis=-1)
    q = rms_norm(q, q_norm)
    k = rms_norm(k, k_norm)
    inv_freq = ROPE_THETA ** (-jnp.arange(0, C_ROPE, 2, dtype=jnp.float32) / C_ROPE)
    ang = positions.astype(jnp.float32)[..., None] * inv_freq
    cos = jnp.cos(ang)[:, :, None, :]
    sin = jnp.sin(ang)[:, :, None, :]
    q = jnp.concatenate([q[..., :C_NOPE], rotate_half(q[..., C_NOPE:], cos, sin)], axis=-1)
    k = jnp.concatenate([k[..., :C_NOPE], rotate_half(k[..., C_NOPE:], cos, sin)], axis=-1)
    q = jnp.transpose(q, (0, 2, 1, 3))
    k = jnp.transpose(k, (0, 2, 1, 3))
    v = jnp.transpose(v, (0, 2, 1, 3))
    n_blocks = seq // ATTN_BLOCK
    q_blocks = jnp.transpose(q.reshape(bsz, C_HEADS, n_blocks, ATTN_BLOCK, C_QK), (2, 0, 1, 3, 4))
    key_pos = jnp.arange(seq)

    def attend(args):
        qb, start = args
        s = jnp.einsum('bhqd,bhkd->bhqk', qb, k).astype(jnp.float32) * ATTN_SCALE
        q_pos = start + jnp.arange(ATTN_BLOCK)
        s = jnp.where(key_pos[None, :] <= q_pos[:, None], s, NEG_BIG)
        pr = jax.nn.softmax(s, axis=-1).astype(v.dtype)
        return jnp.einsum('bhqk,bhkd->bhqd', pr, v)

    out = lax.map(attend, (q_blocks, jnp.arange(n_blocks) * ATTN_BLOCK))
    out = jnp.transpose(out, (1, 0, 3, 2, 4)).reshape(bsz, seq, C_HEADS * C_V)
    return out @ w_out


def per_layer_embedding(h, p_i, w_gate, w_proj, post_norm):
    return rms_norm(jax.nn.sigmoid(h @ w_gate) * (p_i @ w_proj), post_norm)


def setup_inputs(seed: int = 0) -> dict:
    key = jax.random.key(seed)
    ks = jax.random.split(key, 32)
    f32 = jnp.float32

    def nrm(k, shape, fan_in):
        return jax.random.normal(k, shape, f32) * (fan_in ** -0.5)

    def gain(k, shape):
        return 1.0 + 0.02 * jax.random.normal(k, shape, f32)

    x = jax.random.normal(ks[0], (BATCH, SEQ, D_MODEL), f32)
    p = jax.random.normal(ks[1], (DEPTH, BATCH, SEQ, D_PLE), f32)
    offset = jax.random.randint(ks[2], (BATCH, 1), 0, 1024, dtype=jnp.int32)
    positions = offset + jnp.arange(SEQ, dtype=jnp.int32)[None, :]
    return {
        'x': x,
        'p': p,
        'positions': positions,
        'norm_gains': gain(ks[3], (DEPTH, N_NORMS, D_MODEL)),
        'ffn_w_gate': nrm(ks[4], (DEPTH, 2, D_MODEL, D_FF), D_MODEL),
        'ffn_w_up': nrm(ks[5], (DEPTH, 2, D_MODEL, D_FF), D_MODEL),
        'ffn_w_down': nrm(ks[6], (DEPTH, 2, D_FF, D_MODEL), D_FF),
        'ple_w_gate': nrm(ks[7], (DEPTH, D_MODEL, D_MODEL), D_MODEL),
        'ple_w_proj': nrm(ks[8], (DEPTH, D_PLE, D_MODEL), D_PLE),
        'even_w_in': nrm(ks[9], (N_EVEN, D_MODEL, EVEN_IN), D_MODEL),
        'gmlp_v_norm': gain(ks[10], (N_EVEN, A_HEADS, A_HEAD_DIM)),
        'gmlp_w_s': nrm(ks[11], (N_EVEN, A_HEADS, A_CHUNK, A_CHUNK), A_CHUNK),
        'gmlp_b_s': 1.0 + 0.01 * jax.random.normal(ks[12], (N_EVEN, A_HEADS, A_CHUNK), f32),
        'hgrn_lb_raw': 0.5 * jax.random.normal(ks[13], (N_EVEN, B_KW), f32),
        'hgrn_out_norm': gain(ks[14], (N_EVEN, B_VAL_DIM)),
        'even_w_out': nrm(ks[15], (N_EVEN, EVEN_OUT, D_MODEL), EVEN_OUT),
        'mla_w_in': nrm(ks[16], (N_ODD, D_MODEL, ODD_IN), D_MODEL),
        'mla_q_a_norm': gain(ks[17], (N_ODD, Q_LORA)),
        'mla_kv_a_norm': gain(ks[18], (N_ODD, KV_LORA)),
        'mla_w_q_b': nrm(ks[19], (N_ODD, Q_LORA, C_HEADS * C_QK), Q_LORA),
        'mla_w_kv_b': nrm(ks[20], (N_ODD, KV_LORA, C_HEADS * (C_NOPE + C_V)), KV_LORA),
        'mla_q_norm': gain(ks[21], (N_ODD, C_QK)),
        'mla_k_norm': gain(ks[22], (N_ODD, C_QK)),
        'mla_w_out': nrm(ks[23], (N_ODD, C_HEADS * C_V, D_MODEL), C_HEADS * C_V),
    }


def reference(x, p, positions, norm_gains, ffn_w_gate, ffn_w_up, ffn_w_down,
              ple_w_gate, ple_w_proj, even_w_in, gmlp_v_norm, gmlp_w_s, gmlp_b_s,
              hgrn_lb_raw, hgrn_out_norm, even_w_out, mla_w_in, mla_q_a_norm,
              mla_kv_a_norm, mla_w_q_b, mla_w_kv_b, mla_q_norm, mla_k_norm, mla_w_out):
    lb_sm = jax.nn.softmax(hgrn_lb_raw.astype(jnp.float32), axis=0)
    lower_bounds = jnp.clip(jnp.cumsum(lb_sm, axis=0) - lb_sm[0], 0.0, LB_MAX)
    for i in range(DEPTH):
        g = norm_gains[i]
        x = x + 0.5 * swiglu(rms_norm(x, g[0]), ffn_w_gate[i, 0], ffn_w_up[i, 0], ffn_w_down[i, 0])
        h = rms_norm(x, g[1])
        j = i // 2
        if i % 2 == 0:
            m = even_mixer(h, even_w_in[j], gmlp_v_norm[j], gmlp_w_s[j], gmlp_b_s[j],
                           lower_bounds[j], hgrn_out_norm[j], even_w_out[j])
        else:
            m = mla_mixer(h, positions, mla_w_in[j], mla_q_a_norm[j], mla_kv_a_norm[j],
                          mla_w_q_b[j], mla_w_kv_b[j], mla_q_norm[j], mla_k_norm[j], mla_w_out[j])
        x = x + m
        x = x + 0.5 * swiglu(rms_norm(x, g[2]), ffn_w_gate[i, 1], ffn_w_up[i, 1], ffn_w_down[i, 1])
        x = x + per_layer_embedding(rms_norm(x, g[3]), p[i], ple_w_gate[i], ple_w_proj[i], g[4])
    return x
```

```python
import numpy as np
from contextlib import ExitStack
import concourse.bass as bass
import concourse.mybir as mybir
from concourse.bass_utils import run_bass_kernel_spmd

F32 = mybir.dt.float32
BF16 = mybir.dt.bfloat16
I32 = mybir.dt.int32
AF = mybir.ActivationFunctionType
ALU = mybir.AluOpType
AX = mybir.AxisListType

D = 1024
DFF = 2816
NFC = DFF // 128
DEPTH = 4
EPS = 1e-6
TT = 256
DEBUG = False
EV_DBG = ''
ENG_ATTR = {'sp': 'sync', 'act': 'scalar', 'pe': 'tensor', 'dve': 'vector', 'pool': 'gpsimd'}


class Buf:
    __slots__ = ('name', 'w', 'r', 'pw')

    def __init__(self, name):
        self.name = name
        self.w = None
        self.r = []
        self.pw = []


class Op:
    __slots__ = ('eng', 'fn', 'deps', 'needed', 'sem', 'val', 'is_dma')


class Prog:
    def __init__(self, nc):
        self.nc = nc
        self.ops = {e: [] for e in ENG_ATTR}
        self.eng_sem = {e: nc.alloc_semaphore(name=f"es_{e}") for e in ENG_ATTR}
        self.eng_cnt = {e: 0 for e in ENG_ATTR}
        self.dma_sem = {}
        self.order = []
        self.bufs = []
        self.dmas = []

    def buf(self, name):
        b = Buf(name)
        self.bufs.append(b)
        return b

    def _deps(self, eng, is_dma, r, w, par=False):
        deps = []

        def add(o, war=False):
            if o is None:
                return
            if not is_dma and not o.is_dma and o.eng == eng:
                if eng in ('pe', 'act'):
                    return
            if o not in deps:
                deps.append(o)
        for b in r:
            add(b.w)
            for o in b.pw:
                add(o)
        for b in w:
            if not par:
                add(b.w)
                for o in b.pw:
                    add(o)
            for o in b.r:
                add(o, war=True)
        return deps

    def _commit(self, op, r, w, par=False):
        for o in op.deps:
            o.needed = True
        for b in w:
            if par:
                b.pw.append(op)
                continue
            b.w = op
            b.pw = []
            b.r = []
        for b in r:
            if b.w is not op:
                b.r.append(op)
        self.ops[op.eng].append(op)
        self.order.append(op)

    def op(self, eng, fn, r=(), w=()):
        o = Op()
        o.eng = eng
        o.fn = fn
        o.is_dma = False
        o.needed = False
        o.sem = None
        o.val = None
        o.deps = self._deps(eng, False, r, w)
        self._commit(o, r, w)
        return o

    def dma(self, out, in_, r=(), w=(), sig=None, eng='sp', slow=False, par=False):
        if sig not in self.dma_sem:
            self.dma_sem[sig] = [self.nc.alloc_semaphore(name=f"ds_{sig}"), 0]
        o = Op()
        o.eng = eng
        if slow:
            o.fn = lambda e, out=out, in_=in_: e.dma_start(out=out, in_=in_, allow_slow_non_contiguous=True)
        else:
            o.fn = lambda e, out=out, in_=in_: e.dma_start(out=out, in_=in_)
        o.is_dma = True
        o.needed = True
        ent = self.dma_sem[sig]
        ent[1] += 16
        o.sem = ent[0]
        o.val = ent[1]
        o.deps = self._deps(eng, True, r, w, par)
        self._commit(o, r, w, par)
        self.dmas.append(o)
        return o

    def capture(self, fn):
        lst = []
        self.op = lambda *a, **k: lst.append((0, a, k))
        self.dma = lambda *a, **k: lst.append((1, a, k))
        try:
            fn()
        finally:
            del self.op
            del self.dma
        return lst

    def replay(self, lists):
        n = max(len(l) for l in lists)
        for i in range(n):
            for l in lists:
                if i < len(l):
                    kind, a, k = l[i]
                    (self.dma if kind else self.op)(*a, **k)

    def flush(self):
        nc = self.nc
        fin = Op()
        fin.eng = 'sp'
        fin.fn = None
        fin.is_dma = False
        fin.needed = False
        fin.sem = None
        fin.val = None
        fin.deps = list(self.dmas)
        self.ops['sp'].append(fin)
        self.order.append(fin)
        for o in self.order:
            if not o.is_dma and o.needed:
                self.eng_cnt[o.eng] += 1
                o.sem = self.eng_sem[o.eng]
                o.val = self.eng_cnt[o.eng]
        with nc.Block() as block:
            for en, attr in ENG_ATTR.items():
                ops = self.ops[en]

                def body(e, ops=ops):
                    seen = {}
                    for o in ops:
                        for d in o.deps:
                            k = id(d.sem)
                            if seen.get(k, 0) >= d.val:
                                continue
                            seen[k] = d.val
                            e.wait_ge(d.sem, d.val)
                        if o.fn is None:
                            continue
                        ins = o.fn(e)
                        if o.is_dma:
                            ins.then_inc(o.sem, 16)
                        elif o.needed:
                            ins.then_inc(o.sem, 1)
                getattr(block, attr)(body)
        self.ops = {e: [] for e in ENG_ATTR}
        self.order = []
        self.dmas = []
        for b in self.bufs:
            b.w = None
            b.r = []
            b.pw = []
        self.bufs = []


class Ctx:
    pass


_UID = [0]


def uname(name):
    _UID[0] += 1
    return f"{name}_{_UID[0]}"


def load_cast(P, C, dst, src, dstbuf, n):
    i = C.stage_i
    C.stage_i += 1
    slot = i % len(C.stage)
    st, sb = C.stage[slot], C.stage_b[slot]
    P.dma(st[:, 0:n], src, w=[sb], sig=f"stage{slot}")
    eng = 'dve' if (i % 2 == 0) else 'act'
    if eng == 'dve':
        P.op('dve', lambda e: e.tensor_copy(out=dst, in_=st[:, 0:n]), r=[sb], w=[dstbuf])
    else:
        P.op('act', lambda e: e.activation(out=dst, in_=st[:, 0:n], func=AF.Copy), r=[sb], w=[dstbuf])


def emit_norm_T(P, C, xt, xb, gbc, hT, hTb, tp, tpb, nsub=2, N=None):
    if N is None:
        N = C
    ss, ssb = N.ss, N.ssb
    for s in range(nsub):
        P.op('act', lambda e, s=s: e.activation(out=C.junk[:, :], in_=xt[:, s, :], func=AF.Square,
                                                 accum_out=ss[:, s:s + 1]), r=[xb], w=[ssb[s], C.junkb])
    P.op('dve', lambda e: e.tensor_scalar(out=N.vv[:, 0:nsub], in0=ss[:, 0:nsub], scalar1=1.0 / D, scalar2=EPS,
                                          op0=ALU.mult, op1=ALU.add), r=[ssb[s] for s in range(nsub)], w=[N.vvb])
    P.op('pool', lambda e: e.tensor_tensor(out=N.rstd[:, 0:nsub], in0=N.vv[:, 0:nsub], in1=C.neghalf[:, 0:nsub],
                                           op=ALU.pow), r=[N.vvb, C.constb], w=[N.rstdb])
    for s in range(nsub):
        P.op('dve', lambda e, s=s: e.scalar_tensor_tensor(out=N.xs[:, s, :], in0=xt[:, s, :], scalar=N.rstd[:, s:s + 1],
                                                          in1=gbc, op0=ALU.mult, op1=ALU.mult),
             r=[xb, N.rstdb, C.gbcb], w=[N.xsb[s]])
    for s in range(nsub):
        for k in range(8):
            P.op('pe', lambda e, s=s, k=k: e.transpose(out=tp[:, k, s * 128:(s + 1) * 128],
                                                       in_=N.xs[:, s, k * 128:(k + 1) * 128], identity=C.ident[:, :]),
                 r=[N.xsb[s], C.constb], w=[tpb])
    W = nsub * 128
    for hh in range(2):
        P.op('act', lambda e, hh=hh: e.activation(out=hT[:, hh * 4:(hh + 1) * 4, 0:W], in_=tp[:, hh * 4:(hh + 1) * 4, 0:W],
                                                  func=AF.Copy), r=[tpb], w=[hTb])


def norm_scratch(P, sbf, i, nsub):
    N = Ctx()
    N.ss = sbf(f"nss{i}", [128, 8], F32)
    N.vv = sbf(f"nvv{i}", [128, 8], F32)
    N.rstd = sbf(f"nrstd{i}", [128, 8], F32)
    N.xs = sbf(f"nxs{i}", [128, nsub, D], BF16)
    N.ssb = [P.buf(f"nss{i}_{s}") for s in range(8)]
    N.vvb = P.buf(f"nvv{i}")
    N.rstdb = P.buf(f"nrstd{i}")
    N.xsb = [P.buf(f"nxs{i}_{s}") for s in range(nsub)]
    return N

def phase_ffn(P, C, x_in, x_out, wg, wu, wd, gain, S):
    nc = P.nc
    ntile = S // TT
    with ExitStack() as es:
        def sb(name, shape, dt):
            return es.enter_context(nc.sbuf_tensor(uname(name), shape, dt))

        def ps(name, shape, dt=F32):
            return es.enter_context(nc.psum_tensor(uname(name), shape, dt))
        Wg = sb("Wg", [128, 8, DFF], BF16)
        Wu = sb("Wu", [128, 8, DFF], BF16)
        Wd = sb("Wd", [128, NFC, D], BF16)
        NG4 = (NFC + 3) // 4
        Wgb = [P.buf(f"Wg{g}") for g in range(NG4)]
        Wub = [P.buf(f"Wu{g}") for g in range(NG4)]
        Wdb = [P.buf(f"Wd{g}") for g in range(NG4)]
        gbc = sb("gbc", [128, D], F32)
        C.gbcb = P.buf("gbc")
        xt = [sb(f"xt{i}", [128, 2, D], F32) for i in range(2)]
        xtb = [P.buf(f"xt{i}") for i in range(2)]
        C.xs = sb("xs", [128, 2, D], BF16)
        C.xsb = [P.buf(f"xs{s}") for s in range(2)]
        hT = [sb(f"hT{i}", [128, 8, TT], BF16) for i in range(2)]
        hTb = [P.buf(f"hT{i}") for i in range(2)]
        sg = [sb(f"sg{i}", [128, TT], F32) for i in range(2)]
        sgb = [P.buf(f"sg{i}") for i in range(2)]
        aT = [sb(f"aT{i}", [128, TT], BF16) for i in range(2)]
        aTb = [P.buf(f"aT{i}") for i in range(2)]
        tp = ps("tp", [128, 8, TT], BF16)
        tpb = P.buf("tp")
        gu = [ps(f"gu{i}", [128, 2, TT]) for i in range(2)]
        gub = [P.buf(f"gu{i}") for i in range(2)]
        yp = [[ps(f"y{s}{h}", [128, 512]) for h in range(2)] for s in range(2)]
        ypb = [[P.buf(f"y{s}{h}") for h in range(2)] for s in range(2)]

        P.dma(gbc[:, :], gain.partition_broadcast(128), w=[C.gbcb], sig="gbc")
        wgr = wg.rearrange("(kc p) f -> p kc f", p=128)
        wur = wu.rearrange("(kc p) f -> p kc f", p=128)
        wdr = wd.rearrange("(fc p) d -> p fc d", p=128)

        for g in range(NG4):
            f0 = 4 * g
            nf = min(4, NFC - f0)
            c0, c1 = f0 * 128, (f0 + nf) * 128
            P.dma(Wg[:, :, c0:c1], wgr[:, :, c0:c1], w=[Wgb[g]], sig=f"W{3 * g}", eng='pool')
            P.dma(Wu[:, :, c0:c1], wur[:, :, c0:c1], w=[Wub[g]], sig=f"W{3 * g + 1}", eng='pool')
            P.dma(Wd[:, f0:f0 + nf, :], wdr[:, f0:f0 + nf, :], w=[Wdb[g]], sig=f"W{3 * g + 2}", eng='pool')

        xin = x_in.rearrange("(t s p) d -> t p s d", s=2, p=128)
        xout = x_out.rearrange("(t s p) d -> t p s d", s=2, p=128)

        def load_x(t):
            P.dma(xt[t % 2][:, :, :], xin[t], w=[xtb[t % 2]], sig=f"xt{t % 2}")

        def norm(t):
            emit_norm_T(P, C, xt[t % 2], xtb[t % 2], gbc[:, :], hT[t % 2], hTb[t % 2], tp, tpb)

        def gu_mm(t, j):
            sl = j % 2
            h = hT[t % 2]
            for which, Wm, Wb in ((0, Wg, Wgb), (1, Wu, Wub)):
                for k in range(8):
                    P.op('pe', lambda e, k=k, which=which, Wm=Wm, sl=sl, h=h: e.matmul(
                        gu[sl][:, which, :], lhsT=Wm[:, k, j * 128:(j + 1) * 128], rhs=h[:, k, :],
                        start=(k == 0), stop=(k == 7)), r=[Wb[j // 4], hTb[t % 2]], w=[gub[sl]])

        def act_mul(t, j):
            sl = j % 2
            P.op('act', lambda e: e.activation(out=sg[sl][:, :], in_=gu[sl][:, 0, :], func=AF.Silu), r=[gub[sl]], w=[sgb[sl]])
            P.op('dve', lambda e: e.tensor_tensor(out=aT[sl][:, :], in0=sg[sl][:, :], in1=gu[sl][:, 1, :], op=ALU.mult),
                 r=[sgb[sl], gub[sl]], w=[aTb[sl]])

        def down_mm(t, j):
            sl = j % 2
            for s in range(2):
                for h in range(2):
                    P.op('pe', lambda e, s=s, h=h: e.matmul(
                        yp[s][h][:, :], lhsT=aT[sl][:, s * 128:(s + 1) * 128], rhs=Wd[:, j, h * 512:(h + 1) * 512],
                        start=(j == 0), stop=(j == NFC - 1)), r=[aTb[sl], Wdb[j // 4]], w=[ypb[s][h]])

        def resid(t):
            x = xt[t % 2]
            for s in range(2):
                for h in range(2):
                    P.op('dve', lambda e, s=s, h=h: e.scalar_tensor_tensor(
                        out=x[:, s, h * 512:(h + 1) * 512], in0=yp[s][h][:, :], scalar=0.5,
                        in1=x[:, s, h * 512:(h + 1) * 512], op0=ALU.mult, op1=ALU.add), r=[ypb[s][h], xtb[t % 2]], w=[xtb[t % 2]])
            P.dma(xout[t], x[:, :, :], r=[xtb[t % 2]], sig=f"xt{t % 2}")

        load_x(0)
        if ntile > 1:
            load_x(1)
        norm(0)
        for t in range(ntile):
            gu_mm(t, 0)
            for j in range(NFC):
                if j + 1 < NFC:
                    gu_mm(t, j + 1)
                if j == 6 and t + 1 < ntile:
                    norm(t + 1)
                act_mul(t, j)
                down_mm(t, j)
            resid(t)
            if t + 2 < ntile:
                load_x(t + 2)
        P.flush()


def phase_ple(P, C, x_in, x_out, p_in, wgate, wproj, gain_in, gain_out, S):
    nc = P.nc
    ntile = S // 128
    with ExitStack() as es:
        def sb(name, shape, dt):
            return es.enter_context(nc.sbuf_tensor(uname(name), shape, dt))

        def ps(name, shape, dt=F32):
            return es.enter_context(nc.psum_tensor(uname(name), shape, dt))
        Wg = sb("pWg", [128, 8, D], BF16)
        Wp = sb("pWp", [128, 2, D], BF16)
        Wgb = [P.buf(f"pWg{k}") for k in range(8)]
        Wpb = P.buf("pWp")
        gbc = sb("gbc", [128, D], F32)
        C.gbcb = P.buf("gbc")
        g4 = sb("g4", [128, D], F32)
        g4b = P.buf("g4")
        B = [ps(f"B{i}", [128, 512]) for i in range(8)]
        Bb = [P.buf(f"B{i}") for i in range(8)]

        def alloc_tile(i):
            T = Ctx()
            for nm, shape, dt in (("pb", [128, 256], BF16),
                                  ("pT", [128, 2, 128], BF16), ("hT", [128, 8, 128], BF16), ("sg0", [128, 512], F32),
                                  ("sg1", [128, 512], F32), ("ee", [128, D], F32), ("st", [128, 8], F32)):
                setattr(T, nm, sb(f"{nm}_{i}", shape, dt))
                setattr(T, nm + "b", P.buf(f"{nm}_{i}"))
            T.i = i
            T.N = norm_scratch(P, sb, i, 1)
            T.bk = (2 * i, 2 * i + 1)
            return T
        NW = 4
        TB = [alloc_tile(i) for i in range(NW)]
        xts = [sb(f"xt4_{i}", [128, 1, D], F32) for i in range(2 * NW)]
        xtsb = [P.buf(f"xt4_{i}") for i in range(2 * NW)]
        pts = [sb(f"pt4_{i}", [128, 256], F32) for i in range(2 * NW)]
        ptsb = [P.buf(f"pt4_{i}") for i in range(2 * NW)]

        P.dma(gbc[:, :], gain_in.partition_broadcast(128), w=[C.gbcb], sig="gbc")
        P.dma(g4[:, :], gain_out.partition_broadcast(128), w=[g4b], sig="g4")
        wgr = wgate.rearrange("(kc p) f -> p kc f", p=128)
        wpr = wproj.rearrange("(kc p) f -> p kc f", p=128)
        P.dma(Wg[:, :, :], wgr[:, :, :], w=Wgb, sig="W0", eng='pool')
        P.dma(Wp[:, :, :], wpr[:, :, :], w=[Wpb], sig="W1", eng='pool')

        xin = x_in.rearrange("(t s p) d -> t p s d", s=1, p=128)
        xout = x_out.rearrange("(t s p) d -> t p s d", s=1, p=128)
        pin = p_in.rearrange("(t p) d -> t p d", p=128)

        def load_x(t):
            P.dma(xts[t % (2 * NW)][:, :, :], xin[t], w=[xtsb[t % (2 * NW)]], sig=f"xt{t % (2 * NW)}")
            P.dma(pts[t % (2 * NW)][:, :], pin[t], w=[ptsb[t % (2 * NW)]], sig=f"pt{t % (2 * NW)}")

        def body(t):
            T = TB[t % NW]
            xt_, xtb_, pt_, ptb_ = xts[t % (2 * NW)], xtsb[t % (2 * NW)], pts[t % (2 * NW)], ptsb[t % (2 * NW)]
            a, b = T.bk
            tp = B[a].bitcast(BF16).rearrange("p (k t) -> p k t", k=8)
            tpp = B[b].bitcast(BF16)[:, 0:256].rearrange("p (k t) -> p k t", k=2)
            emit_norm_T(P, C, xt_, xtb_, gbc[:, :], T.hT, T.hTb, tp, Bb[a], nsub=1, N=T.N)
            P.op('act', lambda e: e.activation(out=T.pb[:, :], in_=pt_[:, :], func=AF.Copy), r=[ptb_], w=[T.pbb])
            for kc in range(2):
                P.op('pe', lambda e, kc=kc: e.transpose(out=tpp[:, kc, :], in_=T.pb[:, kc * 128:(kc + 1) * 128], identity=C.ident[:, :]),
                     r=[T.pbb, C.constb], w=[Bb[b]])
            P.op('dve', lambda e: e.tensor_copy(out=T.pT[:, :, :], in_=tpp), r=[Bb[b]], w=[T.pTb])
            sg = (T.sg0, T.sg1)
            sgb = (T.sg0b, T.sg1b)
            for hh in range(2):
                for k in range(8):
                    P.op('pe', lambda e, hh=hh, k=k: e.matmul(B[a][:, :], lhsT=T.hT[:, k, :], rhs=Wg[:, k, hh * 512:(hh + 1) * 512],
                                                              start=(k == 0), stop=(k == 7)), r=[T.hTb, Wgb[k]], w=[Bb[a]])
                for kc in range(2):
                    P.op('pe', lambda e, hh=hh, kc=kc: e.matmul(B[b][:, :], lhsT=T.pT[:, kc, :], rhs=Wp[:, kc, hh * 512:(hh + 1) * 512],
                                                                start=(kc == 0), stop=(kc == 1)), r=[T.pTb, Wpb], w=[Bb[b]])
                P.op('act', lambda e, hh=hh: e.activation(out=sg[hh][:, :], in_=B[a][:, :], func=AF.Sigmoid), r=[Bb[a]], w=[sgb[hh]])
                P.op('dve', lambda e, hh=hh: e.tensor_tensor(out=T.ee[:, hh * 512:(hh + 1) * 512], in0=sg[hh][:, :], in1=B[b][:, :],
                                                             op=ALU.mult), r=[sgb[hh], Bb[b]], w=[T.eeb])
            P.op('act', lambda e: e.activation(out=C.junk[:, :], in_=T.ee[:, :], func=AF.Square, accum_out=T.st[:, 0:1]),
                 r=[T.eeb], w=[T.stb, C.junkb])
            P.op('dve', lambda e: e.tensor_scalar(out=T.st[:, 1:2], in0=T.st[:, 0:1], scalar1=1.0 / D, scalar2=EPS, op0=ALU.mult,
                                                  op1=ALU.add), r=[T.stb], w=[T.stb])
            P.op('pool', lambda e: e.tensor_tensor(out=T.st[:, 2:3], in0=T.st[:, 1:2], in1=C.neghalf[:, 0:1], op=ALU.pow),
                 r=[T.stb, C.constb], w=[T.stb])
            P.op('dve', lambda e: e.scalar_tensor_tensor(out=T.ee[:, :], in0=T.ee[:, :], scalar=T.st[:, 2:3], in1=g4[:, :],
                                                         op0=ALU.mult, op1=ALU.mult), r=[T.eeb, T.stb, g4b], w=[T.eeb])
            P.op('dve', lambda e: e.tensor_tensor(out=xt_[:, 0, :], in0=xt_[:, 0, :], in1=T.ee[:, :], op=ALU.add),
                 r=[T.eeb, xtb_], w=[xtb_])
            P.dma(xout[t], xt_[:, :, :], r=[xtb_], sig=f"xt{t % (2 * NW)}")

        for t in range(min(2 * NW, ntile)):
            load_x(t)
        for t0 in range(0, ntile, NW):
            ts = list(range(t0, min(t0 + NW, ntile)))
            P.replay([P.capture(lambda t=t: body(t)) for t in ts])
            for t in range(t0 + 2 * NW, min(t0 + 3 * NW, ntile)):
                load_x(t)
        P.flush()

NH = 8
QK = 192
ATTN_SCALE = 192 ** -0.5
PI = 3.14159265358979


def phase_mla1(P, C, x_in, Wd_, l, j, S, scrQ):
    nc = P.nc
    W = Wd_
    ntile = S // 128
    QTn, QTr, KTn, KTr, Vs = scrQ
    with ExitStack() as es:
        def sb(name, shape, dt):
            return es.enter_context(nc.sbuf_tensor(uname(name), shape, dt))

        def ps(name, shape, dt=F32):
            return es.enter_context(nc.psum_tensor(uname(name), shape, dt))
        Win = sb("Win", [128, 8, 704], BF16)
        Wq = sb("Wq", [128, 3, 1536], BF16)
        Wkv = sb("Wkv", [128, 2, 2048], BF16)
        Winb, Wqb, Wkvb = P.buf("Win"), P.buf("Wq"), P.buf("Wkv")
        gbc = sb("gbc", [128, D], F32)
        C.gbcb = P.buf("gbc")
        qa = sb("qa", [128, 384], F32)
        kva = sb("kva", [128, 256], F32)
        qn_g = sb("qn_g", [128, 192], F32)
        kn_g = sb("kn_g", [128, 192], F32)
        invf = sb("invf", [128, 2, 32], F32)
        gb = P.buf("gains")
        xt = [sb(f"xt{i}", [128, 1, D], F32) for i in range(2)]
        xtb = [P.buf(f"xt{i}") for i in range(2)]
        posi = [sb(f"posi{i}", [128, 1], I32) for i in range(2)]
        posb = [P.buf(f"posi{i}") for i in range(2)]
        C.xs = sb("xs", [128, 1, D], BF16)
        C.xsb = [P.buf("xs0")]
        hT = [sb(f"hT{i}", [128, 8, 128], BF16) for i in range(2)]
        hTb = [P.buf(f"hT{i}") for i in range(2)]
        def alloc_tile(i):
            T = Ctx()
            for nm, shape, dt in (("zt", [128, 704], F32), ("cn", [128, 640], BF16), ("cT", [128, 5, 128], BF16),
                                  ("qf", [128, 8, 192], F32), ("qs", [128, 8, 192], F32), ("qb16", [128, 8, 192], BF16),
                                  ("kf", [128, 8, 128], F32), ("kn16", [128, 8, 192], BF16), ("vb", [128, 8, 128], BF16),
                                  ("sm", [128, 64], F32), ("cs", [128, 2, 32], F32), ("rt", [128, 4, 8, 32], F32)):
                setattr(T, nm, sb(f"{nm}_{i}", shape, dt))
                setattr(T, nm + "b", P.buf(f"{nm}_{i}"))
            T.ang = sb(f"ang_{i}", [128, 2, 32], F32)
            T.angi = sb(f"angi_{i}", [128, 2, 32], I32)
            T.angk = sb(f"angk_{i}", [128, 2, 32], F32)
            T.msk = sb(f"msk_{i}", [128, 2, 32], F32)
            T.angb = P.buf(f"ang_{i}")
            T.kg = sb(f"kg_{i}", [128, 64], F32)
            T.kr = sb(f"kr_{i}", [128, 64], F32)
            T.kgb = P.buf(f"kg_{i}")
            T.oTn = [sb(f"oTn{i}_{q}", [128, 8, 128], BF16) for q in range(2)]
            T.oTnb = [P.buf(f"oTn{i}_{q}") for q in range(2)]
            T.oTr = [sb(f"oTr{i}_{q}", [64, 8, 128], BF16) for q in range(2)]
            T.oTrb = [P.buf(f"oTr{i}_{q}") for q in range(2)]
            T.i = i
            return T
        TB = [alloc_tile(0), alloc_tile(1)]
        NS = [norm_scratch(P, sb, i, 1) for i in range(2)]
        B = [ps(f"B{i}", [128, 512]) for i in range(8)]
        Bb = [P.buf(f"B{i}") for i in range(8)]
        def bankset(par):
            ia, ib, ic, id_ = 4 * par, 4 * par + 1, 4 * par + 2, 4 * par + 3
            R = Ctx()
            R.tp = B[id_].bitcast(BF16).rearrange("p (k t) -> p k t", k=8)
            R.tpi = id_
            R.z = (ia, ib)
            R.tpc = B[ib].bitcast(BF16)[:, 384:1024].rearrange("p (k t) -> p k t", k=5)
            R.tpci = ib
            R.q = (ia, ic, id_)
            R.kv = (ib, ia, ic, id_)
            R.tq_n = B[ia].bitcast(BF16).rearrange("p (k t) -> p k t", k=8)
            R.tq_r = B[ib].bitcast(BF16).rearrange("p (k t) -> p k t", k=8)
            R.tqi = (ia, ib)
            return R
        BS = [bankset(0), bankset(1)]

        ng = W['norm_gains']
        P.dma(gbc[:, :], ng[l, 1, :].partition_broadcast(128), w=[gb], sig="gbc", par=True)
        C.gbcb = gb
        P.dma(qa[:, :], W['mla_q_a_norm'][j, :].partition_broadcast(128), w=[gb], sig="qa", par=True)
        P.dma(kva[:, :], W['mla_kv_a_norm'][j, :].partition_broadcast(128), w=[gb], sig="kva", par=True)
        P.dma(qn_g[:, :], W['mla_q_norm'][j, :].partition_broadcast(128), w=[gb], sig="qn_g", par=True)
        P.dma(kn_g[:, :], W['mla_k_norm'][j, :].partition_broadcast(128), w=[gb], sig="kn_g", par=True)
        P.dma(invf[:, :, :], W['invf'].partition_broadcast(128), w=[gb], sig="invf", par=True)
        win_r = W['mla_w_in'][j].rearrange("(kc p) f -> p kc f", p=128)
        P.dma(Win[:, :, :], win_r[:, :, :], w=[Winb], sig="W0", eng='pool')
        wq_r = W['mla_w_q_b'][j].rearrange("(kc p) f -> p kc f", p=128)
        P.dma(Wq[:, :, :], wq_r[:, :, :], w=[Wqb], sig="W1", eng='pool')
        wkv_r = W['mla_w_kv_b'][j].rearrange("(kc p) f -> p kc f", p=128)
        P.dma(Wkv[:, :, 0:1024], wkv_r[:, :, 0:1024], w=[Wkvb], sig="W2", eng='pool')
        P.dma(Wkv[:, :, 1024:2048], wkv_r[:, :, 1024:2048], w=[Wkvb], sig="W3", eng='pool')

        xin = x_in.rearrange("(t s p) d -> t p s d", s=1, p=128)
        pin = W['positions'].rearrange("(t p o) -> t p o", p=128, o=1)

        def load_x(t):
            P.dma(xt[t % 2][:, :, :], xin[t], w=[xtb[t % 2]], sig=f"xt{t % 2}")
            P.dma(posi[t % 2][:, :], pin[t], w=[posb[t % 2]], sig=f"posi{t % 2}")

        def norm(t):
            emit_norm_T(P, C, xt[t % 2], xtb[t % 2], gbc[:, :], hT[t % 2], hTb[t % 2], BS[t % 2].tp, Bb[BS[t % 2].tpi], nsub=1, N=NS[t % 2])

        def rope(T, src3, dst3, nh, tag):
            x1 = src3[:, :, 0:32]
            x2 = src3[:, :, 32:64]
            sinb = T.cs[:, 0:1, :].broadcast_to([128, nh, 32])
            cosb = T.cs[:, 1:2, :].broadcast_to([128, nh, 32])
            r = T.rt
            P.op('dve', lambda e: e.tensor_tensor(out=r[:, 0, 0:nh, :], in0=x1, in1=cosb, op=ALU.mult), r=[tag, T.csb], w=[T.rtb])
            P.op('dve', lambda e: e.tensor_tensor(out=r[:, 1, 0:nh, :], in0=x2, in1=sinb, op=ALU.mult), r=[tag, T.csb], w=[T.rtb])
            P.op('dve', lambda e: e.tensor_tensor(out=r[:, 2, 0:nh, :], in0=x2, in1=cosb, op=ALU.mult), r=[tag, T.csb], w=[T.rtb])
            P.op('dve', lambda e: e.tensor_tensor(out=r[:, 3, 0:nh, :], in0=x1, in1=sinb, op=ALU.mult), r=[tag, T.csb], w=[T.rtb])
            return r

        def body(t):
            T = TB[t % 2]
            R = BS[t % 2]
            h = hT[t % 2]
            norm(t)
            yield
            yield
            pf = T.sm[:, 40:41]
            P.op('dve', lambda e: e.tensor_copy(out=pf, in_=posi[t % 2][:, :]), r=[posb[t % 2]], w=[T.angb])
            P.op('dve', lambda e: e.tensor_scalar(out=T.ang[:, :, :], in0=invf[:, :, :], scalar1=pf, scalar2=None, op0=ALU.mult),
                 r=[T.angb, gb], w=[T.angb])
            P.op('dve', lambda e: e.tensor_scalar(out=T.ang[:, 1, :], in0=T.ang[:, 1, :], scalar1=PI / 2, scalar2=None, op0=ALU.add),
                 r=[T.angb], w=[T.angb])
            P.op('dve', lambda e: e.tensor_scalar(out=T.angk[:, :, :], in0=T.ang[:, :, :], scalar1=1.0 / (2 * PI), scalar2=None,
                                                  op0=ALU.mult), r=[T.angb], w=[T.angb])
            P.op('dve', lambda e: e.tensor_copy(out=T.angi[:, :, :], in_=T.angk[:, :, :]), r=[T.angb], w=[T.angb])
            P.op('dve', lambda e: e.tensor_copy(out=T.angk[:, :, :], in_=T.angi[:, :, :]), r=[T.angb], w=[T.angb])
            P.op('dve', lambda e: e.scalar_tensor_tensor(out=T.ang[:, :, :], in0=T.angk[:, :, :], scalar=-2 * PI, in1=T.ang[:, :, :],
                                                         op0=ALU.mult, op1=ALU.add), r=[T.angb], w=[T.angb])
            P.op('dve', lambda e: e.tensor_scalar(out=T.msk[:, :, :], in0=T.ang[:, :, :], scalar1=PI, scalar2=-2 * PI, op0=ALU.is_gt,
                                                  op1=ALU.mult), r=[T.angb], w=[T.angb])
            P.op('dve', lambda e: e.tensor_tensor(out=T.ang[:, :, :], in0=T.ang[:, :, :], in1=T.msk[:, :, :], op=ALU.add), r=[T.angb], w=[T.angb])
            P.op('dve', lambda e: e.tensor_scalar(out=T.msk[:, :, :], in0=T.ang[:, :, :], scalar1=-PI, scalar2=2 * PI, op0=ALU.is_lt,
                                                  op1=ALU.mult), r=[T.angb], w=[T.angb])
            P.op('dve', lambda e: e.tensor_tensor(out=T.ang[:, :, :], in0=T.ang[:, :, :], in1=T.msk[:, :, :], op=ALU.add), r=[T.angb], w=[T.angb])
            P.op('dve', lambda e: e.tensor_scalar(out=T.ang[:, :, :], in0=T.ang[:, :, :], scalar1=PI, scalar2=-PI, op0=ALU.min,
                                                  op1=ALU.max), r=[T.angb], w=[T.angb])
            P.op('act', lambda e: e.activation(out=T.cs[:, :, :], in_=T.ang[:, :, :], func=AF.Sin), r=[T.angb], w=[T.csb])
            yield
            for c0, c1, bk in ((0, 512, R.z[0]), (512, 704, R.z[1])):
                for k in range(8):
                    P.op('pe', lambda e, k=k, c0=c0, c1=c1, bk=bk: e.matmul(B[bk][:, 0:c1 - c0], lhsT=h[:, k, :], rhs=Win[:, k, c0:c1],
                                                                          start=(k == 0), stop=(k == 7)),
                         r=[hTb[t % 2], Winb], w=[Bb[bk]])
            P.op('act', lambda e: e.activation(out=T.zt[:, 0:512], in_=B[R.z[0]][:, :], func=AF.Copy), r=[Bb[R.z[0]]], w=[T.ztb])
            P.op('act', lambda e: e.activation(out=T.zt[:, 512:704], in_=B[R.z[1]][:, 0:192], func=AF.Copy), r=[Bb[R.z[1]]], w=[T.ztb])
            yield
            P.op('act', lambda e: e.activation(out=C.junk[:, 0:384], in_=T.zt[:, 0:384], func=AF.Square, accum_out=T.sm[:, 0:1]),
                 r=[T.ztb], w=[T.smb, C.junkb])
            P.op('act', lambda e: e.activation(out=C.junk[:, 0:256], in_=T.zt[:, 384:640], func=AF.Square, accum_out=T.sm[:, 1:2]),
                 r=[T.ztb], w=[T.smb, C.junkb])
            P.op('act', lambda e: e.activation(out=C.junk[:, 0:64], in_=T.zt[:, 640:704], func=AF.Square, accum_out=T.sm[:, 2:3]),
                 r=[T.ztb], w=[T.smb, C.junkb])
            P.op('dve', lambda e: e.tensor_scalar(out=T.sm[:, 4:5], in0=T.sm[:, 0:1], scalar1=1.0 / 384, scalar2=EPS, op0=ALU.mult,
                                                  op1=ALU.add), r=[T.smb], w=[T.smb])
            P.op('dve', lambda e: e.tensor_scalar(out=T.sm[:, 5:6], in0=T.sm[:, 1:2], scalar1=1.0 / 256, scalar2=EPS, op0=ALU.mult,
                                                  op1=ALU.add), r=[T.smb], w=[T.smb])
            P.op('pool', lambda e: e.tensor_tensor(out=T.sm[:, 6:8], in0=T.sm[:, 4:6], in1=C.neghalf[:, 0:2], op=ALU.pow),
                 r=[T.smb, C.constb], w=[T.smb])
            P.op('dve', lambda e: e.scalar_tensor_tensor(out=T.cn[:, 0:384], in0=T.zt[:, 0:384], scalar=T.sm[:, 6:7], in1=qa[:, :],
                                                         op0=ALU.mult, op1=ALU.mult), r=[T.ztb, T.smb, gb], w=[T.cnb])
            P.op('dve', lambda e: e.scalar_tensor_tensor(out=T.cn[:, 384:640], in0=T.zt[:, 384:640], scalar=T.sm[:, 7:8], in1=kva[:, :],
                                                         op0=ALU.mult, op1=ALU.mult), r=[T.ztb, T.smb, gb], w=[T.cnb])
            for k in range(5):
                P.op('pe', lambda e, k=k: e.transpose(out=R.tpc[:, k, :], in_=T.cn[:, k * 128:(k + 1) * 128], identity=C.ident[:, :]),
                     r=[T.cnb, C.constb], w=[Bb[R.tpci]])
            P.op('dve', lambda e: e.tensor_copy(out=T.cT[:, :, :], in_=R.tpc), r=[Bb[R.tpci]], w=[T.cTb])
            yield
            for c in range(3):
                for k in range(3):
                    P.op('pe', lambda e, c=c, k=k: e.matmul(B[R.q[c]][:, :], lhsT=T.cT[:, k, :], rhs=Wq[:, k, c * 512:(c + 1) * 512],
                                                            start=(k == 0), stop=(k == 2)), r=[T.cTb, Wqb], w=[Bb[R.q[c]]])
            qf2 = T.qf[:, :, :].rearrange("p h d -> p (h d)")
            for c in range(3):
                P.op('act', lambda e, c=c: e.activation(out=qf2[:, c * 512:(c + 1) * 512], in_=B[R.q[c]][:, :], func=AF.Copy),
                     r=[Bb[R.q[c]]], w=[T.qfb])
            yield
            kvb = R.kv
            for c in range(4):
                for k in range(2):
                    P.op('pe', lambda e, c=c, k=k: e.matmul(B[kvb[c]][:, :], lhsT=T.cT[:, 3 + k, :], rhs=Wkv[:, k, c * 512:(c + 1) * 512],
                                                            start=(k == 0), stop=(k == 1)), r=[T.cTb, Wkvb], w=[Bb[kvb[c]]])
            for c in range(4):
                kv3 = B[kvb[c]][:, :].rearrange("p (h d) -> p h d", h=2)
                P.op('act', lambda e, c=c, kv3=kv3: e.activation(out=T.kf[:, 2 * c:2 * c + 2, :], in_=kv3[:, :, 0:128], func=AF.Copy),
                     r=[Bb[kvb[c]]], w=[T.kfb])
                P.op('act', lambda e, c=c, kv3=kv3: e.activation(out=T.vb[:, 2 * c:2 * c + 2, :], in_=kv3[:, :, 128:256], func=AF.Copy),
                     r=[Bb[kvb[c]]], w=[T.vbb])
            P.dma(Vs[t * 128:(t + 1) * 128, :, :], T.vb[:, :, :], r=[T.vbb], sig=f"vb{T.i}")
            yield
            for hh in range(NH):
                P.op('act', lambda e, hh=hh: e.activation(out=C.junk[:, 0:192], in_=T.qf[:, hh, :], func=AF.Square,
                                                           accum_out=T.sm[:, 8 + hh:9 + hh]), r=[T.qfb], w=[T.smb, C.junkb])
            P.op('dve', lambda e: e.tensor_scalar(out=T.sm[:, 16:24], in0=T.sm[:, 8:16], scalar1=1.0 / QK, scalar2=EPS, op0=ALU.mult,
                                                  op1=ALU.add), r=[T.smb], w=[T.smb])
            P.op('pool', lambda e: e.tensor_tensor(out=T.sm[:, 24:32], in0=T.sm[:, 16:24], in1=C.neghalf[:, 0:8], op=ALU.pow),
                 r=[T.smb, C.constb], w=[T.smb])
            P.op('dve', lambda e: e.tensor_tensor(out=T.qs[:, :, :], in0=T.qf[:, :, :],
                                                  in1=T.sm[:, 24:32].unsqueeze(2).broadcast_to([128, NH, QK]), op=ALU.mult),
                 r=[T.qfb, T.smb], w=[T.qsb])
            P.op('dve', lambda e: e.tensor_tensor(out=T.qs[:, :, :], in0=T.qs[:, :, :], in1=qn_g[:, :].unsqueeze(1).broadcast_to([128, NH, QK]),
                                                  op=ALU.mult), r=[T.qsb, gb], w=[T.qsb])
            P.op('act', lambda e: e.activation(out=T.qb16[:, :, 0:128], in_=T.qs[:, :, 0:128], func=AF.Copy), r=[T.qsb], w=[T.qb16b])
            r = rope(T, T.qs[:, :, 128:192], None, NH, T.qsb)
            P.op('dve', lambda e: e.tensor_tensor(out=T.qb16[:, :, 128:160], in0=r[:, 0, :, :], in1=r[:, 1, :, :], op=ALU.subtract),
                 r=[T.rtb], w=[T.qb16b])
            P.op('dve', lambda e: e.tensor_tensor(out=T.qb16[:, :, 160:192], in0=r[:, 2, :, :], in1=r[:, 3, :, :], op=ALU.add),
                 r=[T.rtb], w=[T.qb16b])
            yield
            for hh in range(NH):
                P.op('act', lambda e, hh=hh: e.activation(out=C.junk[:, 0:128], in_=T.kf[:, hh, :], func=AF.Square,
                                                           accum_out=T.sm[:, 32 + hh:33 + hh]), r=[T.kfb], w=[T.smb, C.junkb])
            P.op('dve', lambda e: e.tensor_scalar(out=T.sm[:, 48:56], in0=T.sm[:, 32:40], scalar1=T.sm[:, 2:3], scalar2=1.0 / QK, op0=ALU.add,
                                                  op1=ALU.mult), r=[T.smb], w=[T.smb])
            P.op('dve', lambda e: e.tensor_scalar(out=T.sm[:, 48:56], in0=T.sm[:, 48:56], scalar1=EPS, scalar2=None, op0=ALU.add),
                 r=[T.smb], w=[T.smb])
            P.op('pool', lambda e: e.tensor_tensor(out=T.sm[:, 56:64], in0=T.sm[:, 48:56], in1=C.neghalf[:, 0:8], op=ALU.pow),
                 r=[T.smb, C.constb], w=[T.smb])
            P.op('dve', lambda e: e.tensor_tensor(out=T.kf[:, :, :], in0=T.kf[:, :, :],
                                                  in1=T.sm[:, 56:64].unsqueeze(2).broadcast_to([128, NH, 128]), op=ALU.mult),
                 r=[T.kfb, T.smb], w=[T.kfb])
            P.op('dve', lambda e: e.tensor_tensor(out=T.kn16[:, :, 0:128], in0=T.kf[:, :, :],
                                                  in1=kn_g[:, 0:128].unsqueeze(1).broadcast_to([128, NH, 128]), op=ALU.mult),
                 r=[T.kfb, gb], w=[T.kn16b])
            P.op('dve', lambda e: e.tensor_tensor(out=T.kg[:, :], in0=T.zt[:, 640:704], in1=kn_g[:, 128:192], op=ALU.mult),
                 r=[T.ztb, gb], w=[T.kgb])
            r = rope(T, T.kg[:, :].unsqueeze(1), None, 1, T.kgb)
            P.op('dve', lambda e: e.tensor_tensor(out=T.kr[:, 0:32], in0=r[:, 0, 0, :], in1=r[:, 1, 0, :], op=ALU.subtract),
                 r=[T.rtb], w=[T.kgb])
            P.op('dve', lambda e: e.tensor_tensor(out=T.kr[:, 32:64], in0=r[:, 2, 0, :], in1=r[:, 3, 0, :], op=ALU.add),
                 r=[T.rtb], w=[T.kgb])
            P.op('dve', lambda e: e.tensor_tensor(out=T.kn16[:, :, 128:192], in0=T.kr[:, :].unsqueeze(1).broadcast_to([128, NH, 64]),
                                                  in1=T.sm[:, 56:64].unsqueeze(2).broadcast_to([128, NH, 64]), op=ALU.mult),
                 r=[T.kgb, T.smb], w=[T.kn16b])
            yield
            for (src, srcb, dn, dr, sl) in ((T.qb16, T.qb16b, QTn, QTr, 0), (T.kn16, T.kn16b, KTn, KTr, 1)):
                for hh in range(NH):
                    P.op('pe', lambda e, hh=hh, src=src: e.transpose(out=R.tq_n[:, hh, :], in_=src[:, hh, 0:128], identity=C.ident[:, :]),
                         r=[srcb, C.constb], w=[Bb[R.tqi[0]]])
                for hh in range(NH):
                    P.op('pe', lambda e, hh=hh, src=src: e.transpose(out=R.tq_r[0:64, hh, :], in_=src[:, hh, 128:192],
                                                                     identity=C.ident[:, :]), r=[srcb, C.constb], w=[Bb[R.tqi[1]]])
                P.op('act', lambda e, sl=sl: e.activation(out=T.oTn[sl][:, :, :], in_=R.tq_n, func=AF.Copy), r=[Bb[R.tqi[0]]], w=[T.oTnb[sl]])
                P.op('dve', lambda e, sl=sl: e.tensor_copy(out=T.oTr[sl][:, :, :], in_=R.tq_r[0:64, :, :]), r=[Bb[R.tqi[1]]], w=[T.oTrb[sl]])
                P.dma(dn[:, :, t * 128:(t + 1) * 128].rearrange("h d t -> d h t"), T.oTn[sl][:, :, :], r=[T.oTnb[sl]], sig=f"oTn{T.i}_{sl}")
                P.dma(dr[:, :, t * 128:(t + 1) * 128].rearrange("h d t -> d h t"), T.oTr[sl][:, :, :], r=[T.oTrb[sl]], sig=f"oTr{T.i}_{sl}")

        load_x(0)
        if ntile > 1:
            load_x(1)
        for t0 in range(0, ntile, 2):
            lists = [P.capture(lambda t=t: [None for _ in body(t)]) for t in range(t0, min(t0 + 2, ntile))]
            P.replay(lists)
            for t in range(t0 + 2, min(t0 + 4, ntile)):
                load_x(t)
        P.flush()


def phase_mla2(P, C, x_in, x_out, Wd_, j, g, S, scrQ):
    nc = P.nc
    W = Wd_
    QTn, QTr, KTn, KTr, Vs = scrQ
    nkt = S // 128
    QB = 512 if S >= 512 else S
    nqb = S // QB
    nsubq = QB // 128
    HG = 4
    with ExitStack() as es:
        def sb(name, shape, dt):
            return es.enter_context(nc.sbuf_tensor(uname(name), shape, dt))

        def ps(name, shape, dt=F32):
            return es.enter_context(nc.psum_tensor(uname(name), shape, dt))
        Kn = sb("Kn", [128, HG, S], BF16)
        Kr = sb("Kr", [64, HG, S], BF16)
        Vt = sb("Vt", [128, nkt, HG, 128], BF16)
        Knb = [P.buf(f"Kn{h}") for h in range(HG)]
        Krb = [P.buf(f"Kr{h}") for h in range(HG)]
        Vtb = P.buf("Vt")
        Wo = sb("Wo", [128, HG, D], BF16)
        Wob = P.buf("Wo")
        ones = sb("ones", [128, 128], BF16)
        tri = sb("tri", [128, 128], BF16)
        trif = sb("trif", [128, 128], F32)
        cb = P.buf("mconst")
        xt = sb("xt", [128, nsubq, D], F32)
        xtb = P.buf("xt")
        Qn = [sb(f"Qn{i}", [128, HG, QB], BF16) for i in range(2)]
        Qr = [sb(f"Qr{i}", [64, HG, QB], BF16) for i in range(2)]
        Qb_ = [P.buf(f"Q{i}") for i in range(2)]
        pT = [sb(f"pT{i}", [128, QB], BF16) for i in range(3)]
        pTb = [P.buf(f"pT{i}") for i in range(3)]
        rc = sb("rc", [128, QB], F32)
        rcb = P.buf("rc")
        aT = [sb(f"aT{h}", [128, QB], BF16) for h in range(HG)]
        aTb = [P.buf(f"aT{h}") for h in range(HG)]
        sT = [ps(f"sT{i}", [128, 512]) for i in range(2)]
        sTb = [P.buf(f"sT{i}") for i in range(2)]
        oT = [ps(f"oT{i}", [128, 512]) for i in range(2)]
        oTb = [P.buf(f"oT{i}") for i in range(2)]
        lS = [ps(f"lS{i}", [128, 512]) for i in range(2)]
        lSb = [P.buf(f"lS{i}") for i in range(2)]
        Y = [ps(f"Y{i}", [128, 512]) for i in range(2)]
        Yb = [P.buf(f"Y{i}") for i in range(2)]

        P.op('pool', lambda e: e.memset(ones[:, :], 1.0), w=[cb])
        P.op('pool', lambda e: e.memset(trif[:, :], 1.0), w=[cb])
        P.op('pool', lambda e: e.affine_select(out=trif[:, :], in_=trif[:, :], pattern=[[1, 128]], compare_op=ALU.is_ge, fill=0.0,
                                               base=0, channel_multiplier=-1), r=[cb], w=[cb])
        P.op('pool', lambda e: e.tensor_copy(out=tri[:, :], in_=trif[:, :]), r=[cb], w=[cb])
        for h in range(HG):
            P.dma(Kn[:, h, :], KTn[g * HG + h, :, :], w=[Knb[h]], sig=f"Kn{h}")
            P.dma(Kr[:, h, :], KTr[g * HG + h, :, :], w=[Krb[h]], sig=f"Kr{h}")
        P.dma(Vt[:, :, :, :], Vs[:, g * HG:(g + 1) * HG, :].rearrange("(kt p) h e -> p kt h e", p=128), w=[Vtb], sig="Vt")
        wo_r = W['mla_w_out'][j][g * HG * 128:(g + 1) * HG * 128, :].rearrange("(h p) d -> p h d", p=128)
        P.dma(Wo[:, :, :], wo_r[:, :, :], w=[Wob], sig="W0", eng='pool')

        xin = x_in.rearrange("(b s p) d -> b p s d", s=nsubq, p=128)
        xout = x_out.rearrange("(b s p) d -> b p s d", s=nsubq, p=128)

        def load_q(b):
            sl = b % 2
            P.dma(Qn[sl][:, :, :], QTn[g * HG:(g + 1) * HG, :, b * QB:(b + 1) * QB].rearrange("h d t -> d h t"), w=[Qb_[sl]],
                  sig=f"Qn{sl}")
            P.dma(Qr[sl][:, :, :], QTr[g * HG:(g + 1) * HG, :, b * QB:(b + 1) * QB].rearrange("h d t -> d h t"), w=[Qb_[sl]],
                  sig=f"Qr{sl}")

        cnt = [0]

        def qk_mm(b, h, kt, ssl, qs_):
            r = kt - nsubq * b
            c0 = max(r, 0) * 128
            P.op('pe', lambda e: e.matmul(sT[ssl][:, c0:QB], lhsT=Kn[:, h, kt * 128:(kt + 1) * 128],
                                          rhs=Qn[qs_][:, h, c0:QB], start=True, stop=False),
                 r=[Knb[h], Qb_[qs_]], w=[sTb[ssl]])
            P.op('pe', lambda e: e.matmul(sT[ssl][:, c0:QB], lhsT=Kr[:, h, kt * 128:(kt + 1) * 128],
                                          rhs=Qr[qs_][:, h, c0:QB], start=False, stop=True),
                 r=[Krb[h], Qb_[qs_]], w=[sTb[ssl]])

        def pv_mm(b, h, kt, nk, ob, ssl, psl):
            r = kt - nsubq * b
            c0 = max(r, 0) * 128
            P.op('act', lambda e: e.activation(out=pT[psl][:, c0:QB], in_=sT[ssl][:, c0:QB], func=AF.Exp,
                                               scale=ATTN_SCALE), r=[sTb[ssl]], w=[pTb[psl]])
            if r >= 0:
                P.op('dve', lambda e: e.tensor_tensor(out=pT[psl][:, c0:c0 + 128], in0=pT[psl][:, c0:c0 + 128],
                                                      in1=tri[:, :], op=ALU.mult), r=[pTb[psl], cb], w=[pTb[psl]])
            P.op('pe', lambda e: e.matmul(oT[ob][:, c0:QB], lhsT=Vt[:, kt, h, :], rhs=pT[psl][:, c0:QB],
                                          start=(kt == 0), stop=(kt == nk - 1)),
                 r=[Vtb, pTb[psl]], w=[oTb[ob]])
            P.op('pe', lambda e: e.matmul(lS[ob][:, c0:QB], lhsT=ones[:, :], rhs=pT[psl][:, c0:QB],
                                          start=(kt == 0), stop=(kt == nk - 1)),
                 r=[cb, pTb[psl]], w=[lSb[ob]])

        def head_fin(b, h, ob):
            P.op('dve', lambda e: e.reciprocal(out=rc[:, :], in_=lS[ob][:, 0:QB]), r=[lSb[ob]], w=[rcb])
            P.op('dve', lambda e: e.tensor_tensor(out=aT[h][:, :], in0=oT[ob][:, 0:QB], in1=rc[:, :], op=ALU.mult),
                 r=[oTb[ob], rcb], w=[aTb[h]])

        def do_block(b):
            qs_ = b % 2
            nk = nsubq * (b + 1)
            pairs = [(h, kt) for h in range(HG) for kt in range(nk)]
            base = cnt[0]
            cnt[0] += len(pairs)
            qk_mm(b, pairs[0][0], pairs[0][1], base % 2, qs_)
            for i, (h, kt) in enumerate(pairs):
                if i + 1 < len(pairs):
                    qk_mm(b, pairs[i + 1][0], pairs[i + 1][1], (base + i + 1) % 2, qs_)
                ob = (b * HG + h) % 2
                pv_mm(b, h, kt, nk, ob, (base + i) % 2, (base + i) % 3)
                if kt == nk - 1:
                    head_fin(b, h, ob)

        def do_proj(b, sq, dh):
            for h in range(HG):
                P.op('pe', lambda e, h=h: e.matmul(Y[dh][:, :], lhsT=aT[h][:, sq * 128:(sq + 1) * 128],
                                                   rhs=Wo[:, h, dh * 512:(dh + 1) * 512], start=(h == 0),
                                                   stop=(h == HG - 1)), r=[aTb[h], Wob], w=[Yb[dh]])
            P.op('dve', lambda e: e.tensor_tensor(out=xt[:, sq, dh * 512:(dh + 1) * 512],
                                                  in0=xt[:, sq, dh * 512:(dh + 1) * 512], in1=Y[dh][:, :],
                                                  op=ALU.add), r=[Yb[dh], xtb], w=[xtb])

        load_q(0)
        for b in range(nqb):
            if b + 1 < nqb:
                load_q(b + 1)
            P.dma(xt[:, :, :], xin[b], w=[xtb], sig="xt")
            do_block(b)
            for sq in range(nsubq):
                for dh in range(2):
                    do_proj(b, sq, dh)
            P.dma(xout[b], xt[:, :, :], r=[xtb], sig="xt")
        P.flush()


def phase_even(P, C, x_in, x_out, Wd_, l, j, S):
    nc = P.nc
    W = Wd_
    ntile = S // 128
    with ExitStack() as es:
        def sb(name, shape, dt):
            return es.enter_context(nc.sbuf_tensor(uname(name), shape, dt))

        def ps(name, shape, dt=F32):
            return es.enter_context(nc.psum_tensor(uname(name), shape, dt))
        Win = sb("eWin", [128, 8, 3072], BF16)
        Winb = [P.buf(f"eWin{g}") for g in range(6)]
        Wo = sb("eWo", [128, 8, D], BF16)
        Wob = P.buf("eWo")
        gbc = sb("gbc", [128, D], F32)
        gb = P.buf("gains")
        C.gbcb = gb
        vnorm = sb("vnorm", [128, 512], F32)
        onorm = sb("onorm", [128, 128], F32)
        bs = sb("bs", [128, 8, 1], F32)
        lbr = sb("lbr", [128, 2, 4, 1], F32)
        lb = sb("lb", [128, 4], F32)
        oml = sb("oml", [128, 4], F32)
        wsf = sb("wsf", [128, 8, 128], F32)
        ws16 = sb("ws16", [128, 8, 128], BF16)
        WsT = sb("WsT", [128, 8, 128], BF16)
        trif = sb("trif", [128, 128], F32)
        bmask = sb("bmask", [128, 128], F32)
        rmask = sb("rmask", [128, 512], F32)
        cb = P.buf("econst")
        xt = [sb(f"xt{i}", [128, 1, D], F32) for i in range(4)]
        xtb = [P.buf(f"xt{i}") for i in range(4)]
        hT = [sb(f"hT{i}", [128, 8, 128], BF16) for i in range(2)]
        hTb = [P.buf(f"hT{i}") for i in range(2)]
        S32 = sb("S32", [128, 4, 128], F32)
        S16 = sb("S16", [128, 4, 128], BF16)
        S32b, S16b = P.buf("S32"), P.buf("S16")
        def alloc_tile(i):
            T = Ctx()
            for nm, shape, dt in (("f1", [128, 4, 128], F32), ("kk", [128, 4, 128], F32), ("lf", [128, 4, 128], F32),
                                  ("Bc", [128, 4, 128], F32), ("e1", [128, 4, 128], F32), ("e2", [128, 4, 128], F32),
                                  ("e3", [128, 4, 128], F32), ("eBC", [128, 4, 4, 1], F32), ("qf32", [128, 4, 128], F32),
                                  ("qt", [128, 4, 128], BF16), ("kt_", [128, 4, 128], BF16), ("kh", [128, 4, 128], BF16),
                                  ("khT", [128, 4, 128], BF16), ("v16", [128, 4, 128], BF16), ("gu_", [128, 512], F32),
                                  ("gv", [128, 8, 64], F32), ("sq", [128, 8, 64], F32), ("vn16", [128, 8, 64], BF16),
                                  ("sgl", [128, 512], F32), ("on", [128, 4, 128], F32), ("mT", [128, 8, 128], BF16),
                                  ("sm", [128, 64], F32)):
                setattr(T, nm, sb(f"{nm}_{i}", shape, dt))
                setattr(T, nm + "b", P.buf(f"{nm}_{i}"))
            T.PT = [sb(f"PT{h}_{i}", [128, 128], BF16) for h in range(4)]
            T.PTb = [P.buf(f"PT{h}_{i}") for h in range(4)]
            T.mixed = sb(f"mixed_{i}", [128, D], BF16)
            T.mixb = [P.buf(f"mixA_{i}"), P.buf(f"mixB_{i}")]
            return T
        TB = [alloc_tile(0), alloc_tile(1)]
        NS = [norm_scratch(P, sb, i, 1) for i in range(2)]
        B = [ps(f"B{i}", [128, 512]) for i in range(8)]
        Bb = [P.buf(f"B{i}") for i in range(8)]

        def b3(i, a):
            return B[i][:, :].rearrange("p (a t) -> p a t", a=a)

        def b16(i, a, n):
            return B[i].bitcast(BF16)[:, 0:a * n].rearrange("p (a t) -> p a t", a=a)

        ng = W['norm_gains']
        P.dma(gbc[:, :], ng[l, 1, :].partition_broadcast(128), w=[gb], sig="gbc", par=True)
        P.dma(vnorm[:, :], W['gmlp_v_norm'][j].rearrange("h d -> (h d)").partition_broadcast(128), w=[gb], sig="vnorm", par=True)
        P.dma(onorm[:, :], W['hgrn_out_norm'][j, :].partition_broadcast(128), w=[gb], sig="onorm", par=True)
        P.dma(bs[:, :, :], W['gmlp_b_s'][j].rearrange("h (t o) -> t h o", o=1), w=[gb], sig="bs", slow=True, par=True)
        P.dma(lbr[:, :, :, :], W['hgrn_lb_raw'].rearrange("r (h d o) -> d r h o", h=4, o=1), w=[gb], sig="lbr", slow=True, par=True)
        P.dma(wsf[:, :, :], W['gmlp_w_s'][j].rearrange("h t s -> t h s"), w=[gb], sig="wsf", par=True)
        P.op('pool', lambda e: e.memset(trif[:, :], 1.0), w=[cb])
        P.op('pool', lambda e: e.memset(bmask[:, :], 1.0), w=[cb])
        P.op('pool', lambda e: e.memset(rmask[:, :], 1.0), w=[cb])
        P.op('pool', lambda e: e.memset(rmask[:, :].rearrange("p (c t) -> p c t", t=32)[:, :, 0:1], 0.0), w=[cb])
        P.op('pool', lambda e: e.memset(S32[:, :, :], 0.0), w=[S32b])
        P.op('pool', lambda e: e.memset(S16[:, :, :], 0.0), w=[S16b])
        P.op('pool', lambda e: e.affine_select(out=trif[:, :], in_=trif[:, :], pattern=[[-1, 128]], compare_op=ALU.is_ge, fill=0.0,
                                               base=0, channel_multiplier=1), r=[cb], w=[cb])
        P.op('pool', lambda e: e.affine_select(out=bmask[:, :], in_=bmask[:, :], pattern=[[1, 128]], compare_op=ALU.is_ge, fill=0.0,
                                               base=0, channel_multiplier=-1), r=[cb], w=[cb])
        for cbk in range(1, 4):
            P.op('pool', lambda e, cbk=cbk: e.memset(bmask[0:32 * cbk, 32 * cbk:32 * cbk + 32], 0.0), r=[cb], w=[cb])
        P.op('dve', lambda e: e.tensor_tensor(out=ws16[:, :, :], in0=wsf[:, :, :], in1=trif[:, :].unsqueeze(1).broadcast_to([128, 8, 128]),
                                              op=ALU.mult), r=[gb, cb], w=[cb])
        tws = b16(7, 8, 128)
        for hh in range(8):
            P.op('pe', lambda e, hh=hh: e.transpose(out=tws[:, hh, :], in_=ws16[:, hh, :], identity=C.ident[:, :]), r=[cb, C.constb],
                 w=[Bb[7]])
        P.op('dve', lambda e: e.tensor_copy(out=WsT[:, :, :], in_=tws), r=[Bb[7]], w=[cb])
        if j == 0:
            P.op('pool', lambda e: e.memset(lb[:, :], 0.0), w=[cb])
        else:
            P.op('dve', lambda e: e.tensor_tensor(out=lb[:, :], in0=lbr[:, 1, :, 0], in1=lbr[:, 0, :, 0], op=ALU.subtract), r=[gb], w=[cb])
            P.op('act', lambda e: e.activation(out=lb[:, :], in_=lb[:, :], func=AF.Sigmoid), r=[cb], w=[cb])
            P.op('dve', lambda e: e.tensor_scalar(out=lb[:, :], in0=lb[:, :], scalar1=0.999, scalar2=0.0, op0=ALU.min, op1=ALU.max),
                 r=[cb], w=[cb])
        P.op('dve', lambda e: e.tensor_scalar(out=oml[:, :], in0=lb[:, :], scalar1=-1.0, scalar2=1.0, op0=ALU.mult, op1=ALU.add),
             r=[cb], w=[cb])
        win_r = W['even_w_in'][j].rearrange("(kc p) f -> p kc f", p=128)
        for g in (0, 1, 4, 5, 2, 3):
            P.dma(Win[:, :, g * 512:(g + 1) * 512], win_r[:, :, g * 512:(g + 1) * 512], w=[Winb[g]], sig=f"W{g}", eng='pool')
        wo_r = W['even_w_out'][j].rearrange("(kc p) f -> p kc f", p=128)
        P.dma(Wo[:, :, :], wo_r[:, :, :], w=[Wob], sig="W6", eng='pool')

        xin = x_in.rearrange("(t s p) d -> t p s d", s=1, p=128)
        xout = x_out.rearrange("(t s p) d -> t p s d", s=1, p=128)

        def load_x(t):
            P.dma(xt[t % 4][:, :, :], xin[t], w=[xtb[t % 4]], sig=f"xt{t % 4}")

        def bankset(par):
            R = Ctx()
            R.p = (4 * par, 4 * par + 1, 4 * par + 2)
            R.o = 4 * par + 3
            return R
        BS = [bankset(0), bankset(1)]

        def pre(t):
            T = TB[t % 2]
            R = BS[t % 2]
            p0, p1, p2 = R.p
            h = hT[t % 2]
            hb = hTb[t % 2]
            tp = B[p0].bitcast(BF16).rearrange("p (k t) -> p k t", k=8)
            emit_norm_T(P, C, xt[t % 4], xtb[t % 4], gbc[:, :], h, hb, tp, Bb[p0], nsub=1, N=NS[t % 2])

            def proj_tok(bk, c0):
                for k in range(8):
                    P.op('pe', lambda e, k=k: e.matmul(B[bk][:, :], lhsT=h[:, k, :], rhs=Win[:, k, c0:c0 + 512],
                                                       start=(k == 0), stop=(k == 7)), r=[hb, Winb[c0 // 512]], w=[Bb[bk]])

            def proj_feat(bk, c0):
                bv = b3(bk, 4)
                for hh in range(4):
                    for k in range(8):
                        P.op('pe', lambda e, k=k, hh=hh: e.matmul(bv[:, hh, :], lhsT=Win[:, k, c0 + hh * 128:c0 + (hh + 1) * 128],
                                                                  rhs=h[:, k, :], start=(k == 0), stop=(k == 7)),
                             r=[hb, Winb[c0 // 512]], w=[Bb[bk]])
            proj_tok(p1, 0)
            proj_tok(p2, 512)
            P.op('act', lambda e: e.activation(out=T.gu_[:, :], in_=B[p1][:, :], func=AF.Gelu_apprx_tanh), r=[Bb[p1]], w=[T.gu_b])
            P.op('act', lambda e: e.activation(out=T.gv[:, :, :], in_=b3(p2, 8), func=AF.Gelu_apprx_tanh), r=[Bb[p2]], w=[T.gvb])
            proj_tok(p0, 2048)
            proj_tok(p1, 2560)
            P.op('act', lambda e: e.activation(out=T.v16[:, :, :], in_=b3(p0, 4), func=AF.Copy), r=[Bb[p0]], w=[T.v16b])
            P.op('act', lambda e: e.activation(out=T.sgl[:, :], in_=B[p1][:, :], func=AF.Silu), r=[Bb[p1]], w=[T.sglb])
            proj_feat(p2, 1024)
            proj_feat(p0, 1536)
            P.op('act', lambda e: e.activation(out=T.qf32[:, :, :], in_=b3(p2, 4), func=AF.Copy), r=[Bb[p2]], w=[T.qf32b])
            P.op('act', lambda e: e.activation(out=T.f1[:, :, :], in_=b3(p0, 4), func=AF.Sigmoid), r=[Bb[p0]], w=[T.f1b])
            lb_bc = lb[:, :].unsqueeze(2).broadcast_to([128, 4, 128])
            oml_bc = oml[:, :].unsqueeze(2).broadcast_to([128, 4, 128])
            P.op('dve', lambda e: e.tensor_tensor(out=T.f1[:, :, :], in0=T.f1[:, :, :], in1=oml_bc, op=ALU.mult), r=[T.f1b, cb], w=[T.f1b])
            P.op('dve', lambda e: e.tensor_tensor(out=T.f1[:, :, :], in0=T.f1[:, :, :], in1=lb_bc, op=ALU.add), r=[T.f1b, cb], w=[T.f1b])
            P.op('dve', lambda e: e.tensor_scalar(out=T.kk[:, :, :], in0=T.f1[:, :, :], scalar1=-1.0, scalar2=1.0, op0=ALU.mult, op1=ALU.add),
                 r=[T.f1b], w=[T.kkb])
            P.op('dve', lambda e: e.tensor_scalar(out=T.lf[:, :, :], in0=T.f1[:, :, :], scalar1=1e-6, scalar2=None, op0=ALU.max),
                 r=[T.f1b], w=[T.lfb])
            P.op('act', lambda e: e.activation(out=T.lf[:, :, :], in_=T.lf[:, :, :], func=AF.Ln), r=[T.lfb], w=[T.lfb])
            P.op('dve', lambda e: e.tensor_tensor_scan(out=T.Bc[:, :, :].rearrange("p h t -> p (h t)"), data0=rmask[:, :],
                                                       data1=T.lf[:, :, :].rearrange("p h t -> p (h t)"), initial=0.0, op0=ALU.mult,
                                                       op1=ALU.add), r=[T.lfb, cb], w=[T.Bcb])
            P.op('act', lambda e: e.activation(out=T.e1[:, :, :], in_=T.Bc[:, :, :], func=AF.Exp), r=[T.Bcb], w=[T.e1b])
            P.op('dve', lambda e: e.tensor_tensor(out=T.qt[:, :, :], in0=T.qf32[:, :, :], in1=T.e1[:, :, :], op=ALU.mult),
                 r=[T.qf32b, T.e1b], w=[T.qtb])
            P.op('dve', lambda e: e.tensor_scalar(out=T.e2[:, :, :], in0=T.Bc[:, :, :], scalar1=-60.0, scalar2=None, op0=ALU.max),
                 r=[T.Bcb], w=[T.e2b])
            P.op('act', lambda e: e.activation(out=T.e2[:, :, :], in_=T.e2[:, :, :], func=AF.Exp, scale=-1.0), r=[T.e2b], w=[T.e2b])
            P.op('dve', lambda e: e.tensor_tensor(out=T.kt_[:, :, :], in0=T.kk[:, :, :], in1=T.e2[:, :, :], op=ALU.mult),
                 r=[T.kkb, T.e2b], w=[T.kt_b])
            B4 = T.Bc[:, :, :].rearrange("p h (c t) -> p h c t", t=32)
            BCl = B4[:, :, :, 31:32]
            P.op('dve', lambda e: e.tensor_tensor(out=T.e3[:, :, :].rearrange("p h (c t) -> p h c t", t=32),
                                                  in0=BCl.broadcast_to([128, 4, 4, 32]), in1=B4, op=ALU.subtract), r=[T.Bcb], w=[T.e3b])
            P.op('act', lambda e: e.activation(out=T.e3[:, :, :], in_=T.e3[:, :, :], func=AF.Exp), r=[T.e3b], w=[T.e3b])
            P.op('dve', lambda e: e.tensor_tensor(out=T.kh[:, :, :], in0=T.kk[:, :, :], in1=T.e3[:, :, :], op=ALU.mult),
                 r=[T.kkb, T.e3b], w=[T.khb])
            P.op('act', lambda e: e.activation(out=T.eBC[:, :, :, :], in_=BCl, func=AF.Exp), r=[T.Bcb], w=[T.eBCb])
            tk = b16(p1, 4, 128)
            for hh in range(4):
                P.op('pe', lambda e, hh=hh: e.transpose(out=tk[:, hh, :], in_=T.kh[:, hh, :], identity=C.ident[:, :]),
                     r=[T.khb, C.constb], w=[Bb[p1]])
            P.op('act', lambda e: e.activation(out=T.khT[:, :, :], in_=tk, func=AF.Copy), r=[Bb[p1]], w=[T.khTb])
            P.op('dve', lambda e: e.tensor_tensor(out=T.sq[:, :, :], in0=T.gv[:, :, :], in1=T.gv[:, :, :], op=ALU.mult), r=[T.gvb], w=[T.sqb])
            P.op('dve', lambda e: e.tensor_reduce(out=T.sm[:, 0:8], in_=T.sq[:, :, :], axis=AX.X, op=ALU.add), r=[T.sqb], w=[T.smb])
            P.op('dve', lambda e: e.tensor_scalar(out=T.sm[:, 8:16], in0=T.sm[:, 0:8], scalar1=1.0 / 64, scalar2=EPS, op0=ALU.mult,
                                                  op1=ALU.add), r=[T.smb], w=[T.smb])
            P.op('pool', lambda e: e.tensor_tensor(out=T.sm[:, 16:24], in0=T.sm[:, 8:16], in1=C.neghalf[:, 0:8], op=ALU.pow),
                 r=[T.smb, C.constb], w=[T.smb])
            P.op('dve', lambda e: e.tensor_tensor(out=T.sq[:, :, :], in0=T.gv[:, :, :],
                                                  in1=T.sm[:, 16:24].unsqueeze(2).broadcast_to([128, 8, 64]), op=ALU.mult),
                 r=[T.gvb, T.smb], w=[T.sqb])
            P.op('dve', lambda e: e.tensor_tensor(out=T.vn16[:, :, :], in0=T.sq[:, :, :],
                                                  in1=vnorm[:, :].rearrange("p (h d) -> p h d", h=8), op=ALU.mult),
                 r=[T.sqb, gb], w=[T.vn16b])
            Mv = b3(p2, 8)
            for hh in range(8):
                P.op('pe', lambda e, hh=hh: e.matmul(Mv[:, hh, :], lhsT=WsT[:, hh, :], rhs=T.vn16[:, hh, :], start=True, stop=True),
                     r=[cb, T.vn16b], w=[Bb[p2]])
            P.op('dve', lambda e: e.tensor_tensor(out=T.sq[:, :, :], in0=Mv, in1=bs[:, :, :].broadcast_to([128, 8, 64]), op=ALU.add),
                 r=[Bb[p2], gb], w=[T.sqb])
            P.op('dve', lambda e: e.tensor_tensor(out=T.mixed[:, 0:512], in0=T.sq[:, :, :].rearrange("p h d -> p (h d)"), in1=T.gu_[:, :],
                                                  op=ALU.mult), r=[T.sqb, T.gu_b], w=[T.mixb[0]])
            sc = b3(p0, 4)
            for hh in range(4):
                P.op('pe', lambda e, hh=hh: e.matmul(sc[:, hh, :], lhsT=T.kt_[:, hh, :], rhs=T.qt[:, hh, :], start=True, stop=True),
                     r=[T.kt_b, T.qtb], w=[Bb[p0]])
            for hh in range(4):
                P.op('dve', lambda e, hh=hh: e.tensor_tensor(out=T.PT[hh][:, :], in0=sc[:, hh, :], in1=bmask[:, :], op=ALU.mult),
                     r=[Bb[p0], cb], w=[T.PTb[hh]])
            O = b3(R.o, 4)
            for hh in range(4):
                P.op('pe', lambda e, hh=hh: e.matmul(O[:, hh, :], lhsT=T.PT[hh][:, :], rhs=T.v16[:, hh, :], start=(hh == 0), stop=False,
                                                     skip_group_check=True), r=[T.PTb[hh], T.v16b], w=[Bb[R.o]])

        def chain(t):
            T = TB[t % 2]
            R = BS[t % 2]
            p0, p1, p2 = R.p
            O = b3(R.o, 4)
            Ust = b3(p1, 4)
            for c in range(4):
                for hh in range(4):
                    P.op('pe', lambda e, hh=hh, c=c: e.matmul(O[32 * c:32 * c + 32, hh, :], lhsT=T.qt[:, hh, 32 * c:32 * c + 32],
                                                              rhs=S16[:, hh, :], start=False, stop=(c == 3), tile_position=(0, 32 * c),
                                                              skip_group_check=True), r=[T.qtb, S16b], w=[Bb[R.o]])
                for hh in range(4):
                    P.op('pe', lambda e, hh=hh, c=c: e.matmul(Ust[:, hh, :], lhsT=T.khT[32 * c:32 * c + 32, hh, :],
                                                              rhs=T.v16[32 * c:32 * c + 32, hh, :], start=True, stop=True,
                                                              tile_position=(32 * c, 0)), r=[T.khTb, T.v16b], w=[Bb[p1]])
                for hh in range(4):
                    P.op('dve', lambda e, hh=hh, c=c: e.scalar_tensor_tensor(out=S32[:, hh, :], in0=S32[:, hh, :], scalar=T.eBC[:, hh, c, :],
                                                                             in1=Ust[:, hh, :], op0=ALU.mult, op1=ALU.add),
                         r=[S32b, T.eBCb, Bb[p1]], w=[S32b])
                P.op('act', lambda e: e.activation(out=S16[:, :, :], in_=S32[:, :, :], func=AF.Copy), r=[S32b], w=[S16b])

        def post(t):
            T = TB[t % 2]
            R = BS[t % 2]
            p0, p1, p2 = R.p
            x = xt[t % 4]
            xb = xtb[t % 4]
            O = b3(R.o, 4)
            for hh in range(4):
                P.op('act', lambda e, hh=hh: e.activation(out=C.junk[:, 0:128], in_=O[:, hh, :], func=AF.Square,
                                                           accum_out=T.sm[:, 32 + hh:33 + hh]), r=[Bb[R.o]], w=[T.smb, C.junkb])
            P.op('dve', lambda e: e.tensor_scalar(out=T.sm[:, 40:44], in0=T.sm[:, 32:36], scalar1=1.0 / 128, scalar2=EPS, op0=ALU.mult,
                                                  op1=ALU.add), r=[T.smb], w=[T.smb])
            P.op('pool', lambda e: e.tensor_tensor(out=T.sm[:, 44:48], in0=T.sm[:, 40:44], in1=C.neghalf[:, 0:4], op=ALU.pow),
                 r=[T.smb, C.constb], w=[T.smb])
            for hh in range(4):
                P.op('act', lambda e, hh=hh: e.activation(out=T.on[:, hh, :], in_=O[:, hh, :], func=AF.Copy, scale=T.sm[:, 44 + hh:45 + hh]),
                     r=[Bb[R.o], T.smb], w=[T.onb])
            P.op('dve', lambda e: e.tensor_tensor(out=T.on[:, :, :], in0=T.on[:, :, :],
                                                  in1=onorm[:, :].unsqueeze(1).broadcast_to([128, 4, 128]), op=ALU.mult),
                 r=[T.onb, gb], w=[T.onb])
            P.op('dve', lambda e: e.tensor_tensor(out=T.mixed[:, 512:1024], in0=T.on[:, :, :].rearrange("p h d -> p (h d)"), in1=T.sgl[:, :],
                                                  op=ALU.mult), r=[T.onb, T.sglb], w=[T.mixb[1]])
            tm = b16(p2, 8, 128)
            for k in range(8):
                P.op('pe', lambda e, k=k: e.transpose(out=tm[:, k, :], in_=T.mixed[:, k * 128:(k + 1) * 128], identity=C.ident[:, :]),
                     r=[T.mixb[0], T.mixb[1], C.constb], w=[Bb[p2]])
            P.op('act', lambda e: e.activation(out=T.mT[:, :, :], in_=tm, func=AF.Copy), r=[Bb[p2]], w=[T.mTb])
            for dh, bk in ((0, p0), (1, p1)):
                for k in range(8):
                    P.op('pe', lambda e, k=k, dh=dh, bk=bk: e.matmul(B[bk][:, :], lhsT=T.mT[:, k, :], rhs=Wo[:, k, dh * 512:(dh + 1) * 512],
                                                                    start=(k == 0), stop=(k == 7)), r=[T.mTb, Wob], w=[Bb[bk]])
                P.op('dve', lambda e, dh=dh, bk=bk: e.tensor_tensor(out=x[:, 0, dh * 512:(dh + 1) * 512], in0=x[:, 0, dh * 512:(dh + 1) * 512],
                                                                   in1=B[bk][:, :], op=ALU.add), r=[Bb[bk], xb], w=[xb])
            P.dma(xout[t], x[:, :, :], r=[xb], sig=f"xt{t % 4}")

        for t in range(min(4, ntile)):
            load_x(t)
        for t0 in range(0, ntile, 2):
            ts = list(range(t0, min(t0 + 2, ntile)))
            P.replay([P.capture(lambda t=t: pre(t)) for t in ts])
            chain(ts[0])
            if len(ts) > 1:
                P.replay([P.capture(lambda: post(ts[0])), P.capture(lambda: chain(ts[1]))])
                post(ts[1])
            else:
                post(ts[0])
            for t in range(t0 + 4, min(t0 + 6, ntile)):
                load_x(t)
        P.flush()


def setup_consts(P, C, es, ident_dram):
    nc = P.nc

    def sb(name, shape, dt):
        return es.enter_context(nc.sbuf_tensor(uname(name), shape, dt))
    C.identf = sb("identf", [128, 128], F32)
    C.ident = sb("ident", [128, 128], BF16)
    C.neghalf = sb("neghalf", [128, 8], F32)
    C.junk = sb("junk", [128, D], BF16)
    C.ss = sb("ss", [128, 8], F32)
    C.vv = sb("vv", [128, 8], F32)
    C.rstd = sb("rstd", [128, 8], F32)
    b0 = P.buf("c0")
    P.dma(C.identf[:, :], ident_dram, w=[b0], sig="identf")
    P.op('dve', lambda e: e.tensor_copy(out=C.ident[:, :], in_=C.identf[:, :]), r=[b0], w=[b0])
    P.op('pool', lambda e: e.memset(C.neghalf[:, :], -0.5), w=[b0])
    P.flush()


def new_phase_bufs(P, C):
    C.constb = P.buf("const")
    C.junkb = P.buf("junk")
    C.ssb = [P.buf(f"ss{s}") for s in range(8)]
    C.vvb = P.buf("vv")
    C.rstdb = P.buf("rstd")
    C.vv2b = [P.buf("vv2_0"), P.buf("vv2_1")]
    C.rstd2b = [P.buf("rstd2_0"), P.buf("rstd2_1")]


def build_nc(S, plan):
    nc = bass.Bass("TRN2", target_bir_lowering=False)

    def din(name, shape, dt=F32):
        return nc.dram_tensor(name, list(shape), dt, kind="ExternalInput").ap()
    x = din("x", [S, D])
    ident = din("ident", [128, 128])
    W = {
        'norm_gains': din("norm_gains", [DEPTH, 5, D]),
        'ffn_w_gate': din("ffn_w_gate", [DEPTH, 2, D, DFF]),
        'ffn_w_up': din("ffn_w_up", [DEPTH, 2, D, DFF]),
        'ffn_w_down': din("ffn_w_down", [DEPTH, 2, DFF, D]),
        'ple_w_gate': din("ple_w_gate", [DEPTH, D, D]),
        'ple_w_proj': din("ple_w_proj", [DEPTH, 256, D]),
        'p': din("p", [DEPTH, S, 256]),
        'positions': din("positions", [S], I32),
        'invf': din("invf", [2, 32]),
        'mla_w_in': din("mla_w_in", [2, D, 704]),
        'mla_q_a_norm': din("mla_q_a_norm", [2, 384]),
        'mla_kv_a_norm': din("mla_kv_a_norm", [2, 256]),
        'mla_w_q_b': din("mla_w_q_b", [2, 384, 1536]),
        'mla_w_kv_b': din("mla_w_kv_b", [2, 256, 2048]),
        'mla_q_norm': din("mla_q_norm", [2, 192]),
        'mla_k_norm': din("mla_k_norm", [2, 192]),
        'mla_w_out': din("mla_w_out", [2, D, D]),
        'even_w_in': din("even_w_in", [2, D, 3072]),
        'gmlp_v_norm': din("gmlp_v_norm", [2, 8, 64]),
        'gmlp_w_s': din("gmlp_w_s", [2, 8, 128, 128]),
        'gmlp_b_s': din("gmlp_b_s", [2, 8, 128]),
        'hgrn_lb_raw': din("hgrn_lb_raw", [2, 512]),
        'hgrn_out_norm': din("hgrn_out_norm", [2, 128]),
        'even_w_out': din("even_w_out", [2, D, D]),
    }
    kd = "ExternalOutput" if DEBUG else "Internal"
    scrQ = (nc.dram_tensor("QTn", [NH, 128, S], BF16, kind=kd).ap(),
            nc.dram_tensor("QTr", [NH, 64, S], BF16, kind=kd).ap(),
            nc.dram_tensor("KTn", [NH, 128, S], BF16, kind=kd).ap(),
            nc.dram_tensor("KTr", [NH, 64, S], BF16, kind=kd).ap(),
            nc.dram_tensor("Vs", [S, NH, 128], BF16, kind=kd).ap())
    y = nc.dram_tensor("y", [S, D], F32, kind="ExternalOutput").ap()
    scr = [nc.dram_tensor(f"xscr{i}", [S, D], F32, kind="Internal").ap() for i in range(3)]
    P = Prog(nc)
    C = Ctx()
    with ExitStack() as es:
        new_phase_bufs(P, C)
        setup_consts(P, C, es, ident)
        cur = x
        si = 0
        for pi, ph in enumerate(plan):
            if pi == len(plan) - 1:
                dst = y
            else:
                dst = scr[si % 3]
                si += 1
            new_phase_bufs(P, C)
            if ph[0] == 'even':
                _, l = ph
                phase_even(P, C, cur, dst, W, l, l // 2, S)
            if ph[0] == 'mla':
                _, l = ph
                phase_mla1(P, C, cur, W, l, l // 2, S, scrQ)
                mid = scr[si % 3]
                si += 1
                new_phase_bufs(P, C)
                phase_mla2(P, C, cur, mid, W, l // 2, 0, S, scrQ)
                new_phase_bufs(P, C)
                phase_mla2(P, C, mid, dst, W, l // 2, 1, S, scrQ)
            if ph[0] == 'ffn':
                _, l, which = ph
                phase_ffn(P, C, cur, dst, W['ffn_w_gate'][l, which], W['ffn_w_up'][l, which], W['ffn_w_down'][l, which],
                          W['norm_gains'][l, 0 if which == 0 else 2, :], S)
            elif ph[0] == 'ple':
                _, l = ph
                phase_ple(P, C, cur, dst, W['p'][l], W['ple_w_gate'][l], W['ple_w_proj'][l], W['norm_gains'][l, 3, :],
                          W['norm_gains'][l, 4, :], S)
            cur = dst
    return nc


WEIGHT_KEYS = ['norm_gains', 'ffn_w_gate', 'ffn_w_up', 'ffn_w_down', 'ple_w_gate', 'ple_w_proj', 'even_w_in', 'gmlp_v_norm',
               'gmlp_w_s', 'gmlp_b_s', 'hgrn_lb_raw', 'hgrn_out_norm', 'even_w_out', 'mla_w_in', 'mla_q_a_norm', 'mla_kv_a_norm',
               'mla_w_q_b', 'mla_w_kv_b', 'mla_q_norm', 'mla_k_norm', 'mla_w_out']
N_CORES = 8
SEQ = 4096
FUSED = True


def layer_plan(l):
    return [('ffn', l, 0), ('even', l) if l % 2 == 0 else ('mla', l), ('ffn', l, 1), ('ple', l)]


def kernel(**inputs):
    x = np.ascontiguousarray(np.asarray(inputs['x'], dtype=np.float32))
    p = np.asarray(inputs['p'], dtype=np.float32)
    pos = np.asarray(inputs['positions']).astype(np.int32)
    Wt = {k: np.ascontiguousarray(np.asarray(inputs[k], dtype=np.float32)) for k in WEIGHT_KEYS}
    ident = np.eye(128, dtype=np.float32)
    invf = (10000.0 ** (-np.arange(0, 64, 2, dtype=np.float32) / 64)).astype(np.float32)
    invf2 = np.ascontiguousarray(np.stack([invf, invf]))
    B = x.shape[0]
    assert B == N_CORES and x.shape[1] == SEQ
    cur = [np.ascontiguousarray(x[b]) for b in range(B)]
    pb = [np.ascontiguousarray(p[:, b]) for b in range(B)]
    posb = [np.ascontiguousarray(pos[b]) for b in range(B)]
    plans = [sum([layer_plan(l) for l in range(DEPTH)], [])] if FUSED else [layer_plan(l) for l in range(DEPTH)]
    for plan in plans:
        nc = build_nc(SEQ, plan)
        in_maps = []
        for b in range(B):
            m = {"x": cur[b], "ident": ident, "invf": invf2, "p": pb[b], "positions": posb[b]}
            m.update(Wt)
            in_maps.append(m)
        res = run_bass_kernel_spmd(nc, in_maps, core_ids=list(range(N_CORES)))
        cur = [np.ascontiguousarray(np.asarray(res.results[b]["y"], dtype=np.float32)) for b in range(B)]
    return np.stack(cur, axis=0)
```

```python
import numpy as np
from contextlib import ExitStack
import concourse.bass as bass
import concourse.mybir as mybir
from concourse.bass_utils import run_bass_kernel_spmd

F32 = mybir.dt.float32
BF16 = mybir.dt.bfloat16
I32 = mybir.dt.int32
AF = mybir.ActivationFunctionType
ALU = mybir.AluOpType
AX = mybir.AxisListType

D = 1024
DFF = 2816
NFC = DFF // 128
DEPTH = 4
EPS = 1e-6
TT = 256
DEBUG = False
EV_DBG = ''
ENG_ATTR = {'sp': 'sync', 'act': 'scalar', 'pe': 'tensor', 'dve': 'vector', 'pool': 'gpsimd'}


class Buf:
    __slots__ = ('name', 'w', 'r', 'pw')

    def __init__(self, name):
        self.name = name
        self.w = None
        self.r = []
        self.pw = []


class Op:
    __slots__ = ('eng', 'fn', 'deps', 'needed', 'sem', 'val', 'is_dma')


class Prog:
    def __init__(self, nc):
        self.nc = nc
        self.ops = {e: [] for e in ENG_ATTR}
        self.eng_sem = {e: nc.alloc_semaphore(name=f"es_{e}") for e in ENG_ATTR}
        self.eng_cnt = {e: 0 for e in ENG_ATTR}
        self.dma_sem = {}
        self.order = []
        self.bufs = []
        self.dmas = []

    def buf(self, name):
        b = Buf(name)
        self.bufs.append(b)
        return b

    def _deps(self, eng, is_dma, r, w, par=False):
        deps = []

        def add(o, war=False):
            if o is None:
                return
            if not is_dma and not o.is_dma and o.eng == eng:
                if eng in ('pe', 'act'):
                    return
            if o not in deps:
                deps.append(o)
        for b in r:
            add(b.w)
            for o in b.pw:
                add(o)
        for b in w:
            if not par:
                add(b.w)
                for o in b.pw:
                    add(o)
            for o in b.r:
                add(o, war=True)
        return deps

    def _commit(self, op, r, w, par=False):
        for o in op.deps:
            o.needed = True
        for b in w:
            if par:
                b.pw.append(op)
                continue
            b.w = op
            b.pw = []
            b.r = []
        for b in r:
            if b.w is not op:
                b.r.append(op)
        self.ops[op.eng].append(op)
        self.order.append(op)

    def op(self, eng, fn, r=(), w=()):
        o = Op()
        o.eng = eng
        o.fn = fn
        o.is_dma = False
        o.needed = False
        o.sem = None
        o.val = None
        o.deps = self._deps(eng, False, r, w)
        self._commit(o, r, w)
        return o

    def dma(self, out, in_, r=(), w=(), sig=None, eng='sp', slow=False, par=False):
        if sig not in self.dma_sem:
            self.dma_sem[sig] = [self.nc.alloc_semaphore(name=f"ds_{sig}"), 0]
        o = Op()
        o.eng = eng
        if slow:
            o.fn = lambda e, out=out, in_=in_: e.dma_start(out=out, in_=in_, allow_slow_non_contiguous=True)
        else:
            o.fn = lambda e, out=out, in_=in_: e.dma_start(out=out, in_=in_)
        o.is_dma = True
        o.needed = True
        ent = self.dma_sem[sig]
        ent[1] += 16
        o.sem = ent[0]
        o.val = ent[1]
        o.deps = self._deps(eng, True, r, w, par)
        self._commit(o, r, w, par)
        self.dmas.append(o)
        return o

    def capture(self, fn):
        lst = []
        self.op = lambda *a, **k: lst.append((0, a, k))
        self.dma = lambda *a, **k: lst.append((1, a, k))
        try:
            fn()
        finally:
            del self.op
            del self.dma
        return lst

    def replay(self, lists):
        n = max(len(l) for l in lists)
        for i in range(n):
            for l in lists:
                if i < len(l):
                    kind, a, k = l[i]
                    (self.dma if kind else self.op)(*a, **k)

    def flush(self):
        nc = self.nc
        fin = Op()
        fin.eng = 'sp'
        fin.fn = None
        fin.is_dma = False
        fin.needed = False
        fin.sem = None
        fin.val = None
        fin.deps = list(self.dmas)
        self.ops['sp'].append(fin)
        self.order.append(fin)
        for o in self.order:
            if not o.is_dma and o.needed:
                self.eng_cnt[o.eng] += 1
                o.sem = self.eng_sem[o.eng]
                o.val = self.eng_cnt[o.eng]
        with nc.Block() as block:
            for en, attr in ENG_ATTR.items():
                ops = self.ops[en]

                def body(e, ops=ops):
                    seen = {}
                    for o in ops:
                        for d in o.deps:
                            k = id(d.sem)
                            if seen.get(k, 0) >= d.val:
                                continue
                            seen[k] = d.val
                            e.wait_ge(d.sem, d.val)
                        if o.fn is None:
                            continue
                        ins = o.fn(e)
                        if o.is_dma:
                            ins.then_inc(o.sem, 16)
                        elif o.needed:
                            ins.then_inc(o.sem, 1)
                getattr(block, attr)(body)
        self.ops = {e: [] for e in ENG_ATTR}
        self.order = []
        self.dmas = []
        for b in self.bufs:
            b.w = None
            b.r = []
            b.pw = []
        self.bufs = []


class Ctx:
    pass


_UID = [0]


def uname(name):
    _UID[0] += 1
    return f"{name}_{_UID[0]}"


def load_cast(P, C, dst, src, dstbuf, n):
    i = C.stage_i
    C.stage_i += 1
    slot = i % len(C.stage)
    st, sb = C.stage[slot], C.stage_b[slot]
    P.dma(st[:, 0:n], src, w=[sb], sig=f"stage{slot}")
    eng = 'dve' if (i % 2 == 0) else 'act'
    if eng == 'dve':
        P.op('dve', lambda e: e.tensor_copy(out=dst, in_=st[:, 0:n]), r=[sb], w=[dstbuf])
    else:
        P.op('act', lambda e: e.activation(out=dst, in_=st[:, 0:n], func=AF.Copy), r=[sb], w=[dstbuf])


def emit_norm_T(P, C, xt, xb, gbc, hT, hTb, tp, tpb, nsub=2, N=None):
    if N is None:
        N = C
    ss, ssb = N.ss, N.ssb
    for s in range(nsub):
        P.op('act', lambda e, s=s: e.activation(out=C.junk[:, :], in_=xt[:, s, :], func=AF.Square,
                                                 accum_out=ss[:, s:s + 1]), r=[xb], w=[ssb[s], C.junkb])
    P.op('dve', lambda e: e.tensor_scalar(out=N.vv[:, 0:nsub], in0=ss[:, 0:nsub], scalar1=1.0 / D, scalar2=EPS,
                                          op0=ALU.mult, op1=ALU.add), r=[ssb[s] for s in range(nsub)], w=[N.vvb])
    P.op('pool', lambda e: e.tensor_tensor(out=N.rstd[:, 0:nsub], in0=N.vv[:, 0:nsub], in1=C.neghalf[:, 0:nsub],
                                           op=ALU.pow), r=[N.vvb, C.constb], w=[N.rstdb])
    for s in range(nsub):
        P.op('dve', lambda e, s=s: e.scalar_tensor_tensor(out=N.xs[:, s, :], in0=xt[:, s, :], scalar=N.rstd[:, s:s + 1],
                                                          in1=gbc, op0=ALU.mult, op1=ALU.mult),
             r=[xb, N.rstdb, C.gbcb], w=[N.xsb[s]])
    for s in range(nsub):
        for k in range(8):
            P.op('pe', lambda e, s=s, k=k: e.transpose(out=tp[:, k, s * 128:(s + 1) * 128],
                                                       in_=N.xs[:, s, k * 128:(k + 1) * 128], identity=C.ident[:, :]),
                 r=[N.xsb[s], C.constb], w=[tpb])
    W = nsub * 128
    for hh in range(2):
        P.op('act', lambda e, hh=hh: e.activation(out=hT[:, hh * 4:(hh + 1) * 4, 0:W], in_=tp[:, hh * 4:(hh + 1) * 4, 0:W],
                                                  func=AF.Copy), r=[tpb], w=[hTb])


def norm_scratch(P, sbf, i, nsub):
    N = Ctx()
    N.ss = sbf(f"nss{i}", [128, 8], F32)
    N.vv = sbf(f"nvv{i}", [128, 8], F32)
    N.rstd = sbf(f"nrstd{i}", [128, 8], F32)
    N.xs = sbf(f"nxs{i}", [128, nsub, D], BF16)
    N.ssb = [P.buf(f"nss{i}_{s}") for s in range(8)]
    N.vvb = P.buf(f"nvv{i}")
    N.rstdb = P.buf(f"nrstd{i}")
    N.xsb = [P.buf(f"nxs{i}_{s}") for s in range(nsub)]
    return N

def phase_ffn(P, C, x_in, x_out, wg, wu, wd, gain, S):
    nc = P.nc
    ntile = S // TT
    with ExitStack() as es:
        def sb(name, shape, dt):
            return es.enter_context(nc.sbuf_tensor(uname(name), shape, dt))

        def ps(name, shape, dt=F32):
            return es.enter_context(nc.psum_tensor(uname(name), shape, dt))
        Wg = sb("Wg", [128, 8, DFF], BF16)
        Wu = sb("Wu", [128, 8, DFF], BF16)
        Wd = sb("Wd", [128, NFC, D], BF16)
        NG4 = (NFC + 3) // 4
        Wgb = [P.buf(f"Wg{g}") for g in range(NG4)]
        Wub = [P.buf(f"Wu{g}") for g in range(NG4)]
        Wdb = [P.buf(f"Wd{g}") for g in range(NG4)]
        gbc = sb("gbc", [128, D], F32)
        C.gbcb = P.buf("gbc")
        xt = [sb(f"xt{i}", [128, 2, D], F32) for i in range(2)]
        xtb = [P.buf(f"xt{i}") for i in range(2)]
        C.xs = sb("xs", [128, 2, D], BF16)
        C.xsb = [P.buf(f"xs{s}") for s in range(2)]
        hT = [sb(f"hT{i}", [128, 8, TT], BF16) for i in range(2)]
        hTb = [P.buf(f"hT{i}") for i in range(2)]
        sg = [sb(f"sg{i}", [128, TT], F32) for i in range(2)]
        sgb = [P.buf(f"sg{i}") for i in range(2)]
        aT = [sb(f"aT{i}", [128, TT], BF16) for i in range(2)]
        aTb = [P.buf(f"aT{i}") for i in range(2)]
        tp = ps("tp", [128, 8, TT], BF16)
        tpb = P.buf("tp")
        gu = [ps(f"gu{i}", [128, 2, TT]) for i in range(2)]
        gub = [P.buf(f"gu{i}") for i in range(2)]
        yp = [[ps(f"y{s}{h}", [128, 512]) for h in range(2)] for s in range(2)]
        ypb = [[P.buf(f"y{s}{h}") for h in range(2)] for s in range(2)]

        P.dma(gbc[:, :], gain.partition_broadcast(128), w=[C.gbcb], sig="gbc")
        wgr = wg.rearrange("(kc p) f -> p kc f", p=128)
        wur = wu.rearrange("(kc p) f -> p kc f", p=128)
        wdr = wd.rearrange("(fc p) d -> p fc d", p=128)

        for g in range(NG4):
            f0 = 4 * g
            nf = min(4, NFC - f0)
            c0, c1 = f0 * 128, (f0 + nf) * 128
            P.dma(Wg[:, :, c0:c1], wgr[:, :, c0:c1], w=[Wgb[g]], sig=f"W{3 * g}", eng='pool')
            P.dma(Wu[:, :, c0:c1], wur[:, :, c0:c1], w=[Wub[g]], sig=f"W{3 * g + 1}", eng='pool')
            P.dma(Wd[:, f0:f0 + nf, :], wdr[:, f0:f0 + nf, :], w=[Wdb[g]], sig=f"W{3 * g + 2}", eng='pool')

        xin = x_in.rearrange("(t s p) d -> t p s d", s=2, p=128)
        xout = x_out.rearrange("(t s p) d -> t p s d", s=2, p=128)

        def load_x(t):
            P.dma(xt[t % 2][:, :, :], xin[t], w=[xtb[t % 2]], sig=f"xt{t % 2}")

        def norm(t):
            emit_norm_T(P, C, xt[t % 2], xtb[t % 2], gbc[:, :], hT[t % 2], hTb[t % 2], tp, tpb)

        def gu_mm(t, j):
            sl = j % 2
            h = hT[t % 2]
            for which, Wm, Wb in ((0, Wg, Wgb), (1, Wu, Wub)):
                for k in range(8):
                    P.op('pe', lambda e, k=k, which=which, Wm=Wm, sl=sl, h=h: e.matmul(
                        gu[sl][:, which, :], lhsT=Wm[:, k, j * 128:(j + 1) * 128], rhs=h[:, k, :],
                        start=(k == 0), stop=(k == 7)), r=[Wb[j // 4], hTb[t % 2]], w=[gub[sl]])

        def act_mul(t, j):
            sl = j % 2
            P.op('act', lambda e: e.activation(out=sg[sl][:, :], in_=gu[sl][:, 0, :], func=AF.Silu), r=[gub[sl]], w=[sgb[sl]])
            P.op('dve', lambda e: e.tensor_tensor(out=aT[sl][:, :], in0=sg[sl][:, :], in1=gu[sl][:, 1, :], op=ALU.mult),
                 r=[sgb[sl], gub[sl]], w=[aTb[sl]])

        def down_mm(t, j):
            sl = j % 2
            for s in range(2):
                for h in range(2):
                    P.op('pe', lambda e, s=s, h=h: e.matmul(
                        yp[s][h][:, :], lhsT=aT[sl][:, s * 128:(s + 1) * 128], rhs=Wd[:, j, h * 512:(h + 1) * 512],
                        start=(j == 0), stop=(j == NFC - 1)), r=[aTb[sl], Wdb[j // 4]], w=[ypb[s][h]])

        def resid(t):
            x = xt[t % 2]
            for s in range(2):
                for h in range(2):
                    P.op('dve', lambda e, s=s, h=h: e.scalar_tensor_tensor(
                        out=x[:, s, h * 512:(h + 1) * 512], in0=yp[s][h][:, :], scalar=0.5,
                        in1=x[:, s, h * 512:(h + 1) * 512], op0=ALU.mult, op1=ALU.add), r=[ypb[s][h], xtb[t % 2]], w=[xtb[t % 2]])
            P.dma(xout[t], x[:, :, :], r=[xtb[t % 2]], sig=f"xt{t % 2}")

        load_x(0)
        if ntile > 1:
            load_x(1)
        norm(0)
        for t in range(ntile):
            gu_mm(t, 0)
            for j in range(NFC):
                if j + 1 < NFC:
                    gu_mm(t, j + 1)
                if j == 6 and t + 1 < ntile:
                    norm(t + 1)
                act_mul(t, j)
                down_mm(t, j)
            resid(t)
            if t + 2 < ntile:
                load_x(t + 2)
        P.flush()


def phase_ple(P, C, x_in, x_out, p_in, wgate, wproj, gain_in, gain_out, S):
    nc = P.nc
    ntile = S // 128
    with ExitStack() as es:
        def sb(name, shape, dt):
            return es.enter_context(nc.sbuf_tensor(uname(name), shape, dt))

        def ps(name, shape, dt=F32):
            return es.enter_context(nc.psum_tensor(uname(name), shape, dt))
        Wg = sb("pWg", [128, 8, D], BF16)
        Wp = sb("pWp", [128, 2, D], BF16)
        Wgb = [P.buf(f"pWg{k}") for k in range(8)]
        Wpb = P.buf("pWp")
        gbc = sb("gbc", [128, D], F32)
        C.gbcb = P.buf("gbc")
        g4 = sb("g4", [128, D], F32)
        g4b = P.buf("g4")
        B = [ps(f"B{i}", [128, 512]) for i in range(8)]
        Bb = [P.buf(f"B{i}") for i in range(8)]

        def alloc_tile(i):
            T = Ctx()
            for nm, shape, dt in (("pb", [128, 256], BF16),
                                  ("pT", [128, 2, 128], BF16), ("hT", [128, 8, 128], BF16), ("sg0", [128, 512], F32),
                                  ("sg1", [128, 512], F32), ("ee", [128, D], F32), ("st", [128, 8], F32)):
                setattr(T, nm, sb(f"{nm}_{i}", shape, dt))
                setattr(T, nm + "b", P.buf(f"{nm}_{i}"))
            T.i = i
            T.N = norm_scratch(P, sb, i, 1)
            T.bk = (2 * i, 2 * i + 1)
            return T
        NW = 4
        TB = [alloc_tile(i) for i in range(NW)]
        xts = [sb(f"xt4_{i}", [128, 1, D], F32) for i in range(2 * NW)]
        xtsb = [P.buf(f"xt4_{i}") for i in range(2 * NW)]
        pts = [sb(f"pt4_{i}", [128, 256], F32) for i in range(2 * NW)]
        ptsb = [P.buf(f"pt4_{i}") for i in range(2 * NW)]

        P.dma(gbc[:, :], gain_in.partition_broadcast(128), w=[C.gbcb], sig="gbc")
        P.dma(g4[:, :], gain_out.partition_broadcast(128), w=[g4b], sig="g4")
        wgr = wgate.rearrange("(kc p) f -> p kc f", p=128)
        wpr = wproj.rearrange("(kc p) f -> p kc f", p=128)
        P.dma(Wg[:, :, :], wgr[:, :, :], w=Wgb, sig="W0", eng='pool')
        P.dma(Wp[:, :, :], wpr[:, :, :], w=[Wpb], sig="W1", eng='pool')

        xin = x_in.rearrange("(t s p) d -> t p s d", s=1, p=128)
        xout = x_out.rearrange("(t s p) d -> t p s d", s=1, p=128)
        pin = p_in.rearrange("(t p) d -> t p d", p=128)

        def load_x(t):
            P.dma(xts[t % (2 * NW)][:, :, :], xin[t], w=[xtsb[t % (2 * NW)]], sig=f"xt{t % (2 * NW)}")
            P.dma(pts[t % (2 * NW)][:, :], pin[t], w=[ptsb[t % (2 * NW)]], sig=f"pt{t % (2 * NW)}")

        def body(t):
            T = TB[t % NW]
            xt_, xtb_, pt_, ptb_ = xts[t % (2 * NW)], xtsb[t % (2 * NW)], pts[t % (2 * NW)], ptsb[t % (2 * NW)]
            a, b = T.bk
            tp = B[a].bitcast(BF16).rearrange("p (k t) -> p k t", k=8)
            tpp = B[b].bitcast(BF16)[:, 0:256].rearrange("p (k t) -> p k t", k=2)
            emit_norm_T(P, C, xt_, xtb_, gbc[:, :], T.hT, T.hTb, tp, Bb[a], nsub=1, N=T.N)
            P.op('act', lambda e: e.activation(out=T.pb[:, :], in_=pt_[:, :], func=AF.Copy), r=[ptb_], w=[T.pbb])
            for kc in range(2):
                P.op('pe', lambda e, kc=kc: e.transpose(out=tpp[:, kc, :], in_=T.pb[:, kc * 128:(kc + 1) * 128], identity=C.ident[:, :]),
                     r=[T.pbb, C.constb], w=[Bb[b]])
            P.op('dve', lambda e: e.tensor_copy(out=T.pT[:, :, :], in_=tpp), r=[Bb[b]], w=[T.pTb])
            sg = (T.sg0, T.sg1)
            sgb = (T.sg0b, T.sg1b)
            for hh in range(2):
                for k in range(8):
                    P.op('pe', lambda e, hh=hh, k=k: e.matmul(B[a][:, :], lhsT=T.hT[:, k, :], rhs=Wg[:, k, hh * 512:(hh + 1) * 512],
                                                              start=(k == 0), stop=(k == 7)), r=[T.hTb, Wgb[k]], w=[Bb[a]])
                for kc in range(2):
                    P.op('pe', lambda e, hh=hh, kc=kc: e.matmul(B[b][:, :], lhsT=T.pT[:, kc, :], rhs=Wp[:, kc, hh * 512:(hh + 1) * 512],
                                                                start=(kc == 0), stop=(kc == 1)), r=[T.pTb, Wpb], w=[Bb[b]])
                P.op('act', lambda e, hh=hh: e.activation(out=sg[hh][:, :], in_=B[a][:, :], func=AF.Sigmoid), r=[Bb[a]], w=[sgb[hh]])
                P.op('dve', lambda e, hh=hh: e.tensor_tensor(out=T.ee[:, hh * 512:(hh + 1) * 512], in0=sg[hh][:, :], in1=B[b][:, :],
                                                             op=ALU.mult), r=[sgb[hh], Bb[b]], w=[T.eeb])
            P.op('act', lambda e: e.activation(out=C.junk[:, :], in_=T.ee[:, :], func=AF.Square, accum_out=T.st[:, 0:1]),
                 r=[T.eeb], w=[T.stb, C.junkb])
            P.op('dve', lambda e: e.tensor_scalar(out=T.st[:, 1:2], in0=T.st[:, 0:1], scalar1=1.0 / D, scalar2=EPS, op0=ALU.mult,
                                                  op1=ALU.add), r=[T.stb], w=[T.stb])
            P.op('pool', lambda e: e.tensor_tensor(out=T.st[:, 2:3], in0=T.st[:, 1:2], in1=C.neghalf[:, 0:1], op=ALU.pow),
                 r=[T.stb, C.constb], w=[T.stb])
            P.op('dve', lambda e: e.scalar_tensor_tensor(out=T.ee[:, :], in0=T.ee[:, :], scalar=T.st[:, 2:3], in1=g4[:, :],
                                                         op0=ALU.mult, op1=ALU.mult), r=[T.eeb, T.stb, g4b], w=[T.eeb])
            P.op('dve', lambda e: e.tensor_tensor(out=xt_[:, 0, :], in0=xt_[:, 0, :], in1=T.ee[:, :], op=ALU.add),
                 r=[T.eeb, xtb_], w=[xtb_])
            P.dma(xout[t], xt_[:, :, :], r=[xtb_], sig=f"xt{t % (2 * NW)}")

        for t in range(min(2 * NW, ntile)):
            load_x(t)
        for t0 in range(0, ntile, NW):
            ts = list(range(t0, min(t0 + NW, ntile)))
            P.replay([P.capture(lambda t=t: body(t)) for t in ts])
            for t in range(t0 + 2 * NW, min(t0 + 3 * NW, ntile)):
                load_x(t)
        P.flush()

NH = 8
QK = 192
ATTN_SCALE = 192 ** -0.5
PI = 3.14159265358979


def phase_mla1(P, C, x_in, Wd_, l, j, S, scrQ):
    nc = P.nc
    W = Wd_
    ntile = S // 128
    QTn, QTr, KTn, KTr, Vs = scrQ
    with ExitStack() as es:
        def sb(name, shape, dt):
            return es.enter_context(nc.sbuf_tensor(uname(name), shape, dt))

        def ps(name, shape, dt=F32):
            return es.enter_context(nc.psum_tensor(uname(name), shape, dt))
        Win = sb("Win", [128, 8, 704], BF16)
        Wq = sb("Wq", [128, 3, 1536], BF16)
        Wkv = sb("Wkv", [128, 2, 2048], BF16)
        Winb, Wqb, Wkvb = P.buf("Win"), P.buf("Wq"), P.buf("Wkv")
        gbc = sb("gbc", [128, D], F32)
        C.gbcb = P.buf("gbc")
        qa = sb("qa", [128, 384], F32)
        kva = sb("kva", [128, 256], F32)
        qn_g = sb("qn_g", [128, 192], F32)
        kn_g = sb("kn_g", [128, 192], F32)
        invf = sb("invf", [128, 2, 32], F32)
        gb = P.buf("gains")
        xt = [sb(f"xt{i}", [128, 1, D], F32) for i in range(2)]
        xtb = [P.buf(f"xt{i}") for i in range(2)]
        posi = [sb(f"posi{i}", [128, 1], I32) for i in range(2)]
        posb = [P.buf(f"posi{i}") for i in range(2)]
        C.xs = sb("xs", [128, 1, D], BF16)
        C.xsb = [P.buf("xs0")]
        hT = [sb(f"hT{i}", [128, 8, 128], BF16) for i in range(2)]
        hTb = [P.buf(f"hT{i}") for i in range(2)]
        def alloc_tile(i):
            T = Ctx()
            for nm, shape, dt in (("zt", [128, 704], F32), ("cn", [128, 640], BF16), ("cT", [128, 5, 128], BF16),
                                  ("qf", [128, 8, 192], F32), ("qs", [128, 8, 192], F32), ("qb16", [128, 8, 192], BF16),
                                  ("kf", [128, 8, 128], F32), ("kn16", [128, 8, 192], BF16), ("vb", [128, 8, 128], BF16),
                                  ("sm", [128, 64], F32), ("cs", [128, 2, 32], F32), ("rt", [128, 4, 8, 32], F32)):
                setattr(T, nm, sb(f"{nm}_{i}", shape, dt))
                setattr(T, nm + "b", P.buf(f"{nm}_{i}"))
            T.ang = sb(f"ang_{i}", [128, 2, 32], F32)
            T.angi = sb(f"angi_{i}", [128, 2, 32], I32)
            T.angk = sb(f"angk_{i}", [128, 2, 32], F32)
            T.msk = sb(f"msk_{i}", [128, 2, 32], F32)
            T.angb = P.buf(f"ang_{i}")
            T.kg = sb(f"kg_{i}", [128, 64], F32)
            T.kr = sb(f"kr_{i}", [128, 64], F32)
            T.kgb = P.buf(f"kg_{i}")
            T.oTn = [sb(f"oTn{i}_{q}", [128, 8, 128], BF16) for q in range(2)]
            T.oTnb = [P.buf(f"oTn{i}_{q}") for q in range(2)]
            T.oTr = [sb(f"oTr{i}_{q}", [64, 8, 128], BF16) for q in range(2)]
            T.oTrb = [P.buf(f"oTr{i}_{q}") for q in range(2)]
            T.i = i
            return T
        TB = [alloc_tile(0), alloc_tile(1)]
        NS = [norm_scratch(P, sb, i, 1) for i in range(2)]
        B = [ps(f"B{i}", [128, 512]) for i in range(8)]
        Bb = [P.buf(f"B{i}") for i in range(8)]
        def bankset(par):
            ia, ib, ic, id_ = 4 * par, 4 * par + 1, 4 * par + 2, 4 * par + 3
            R = Ctx()
            R.tp = B[id_].bitcast(BF16).rearrange("p (k t) -> p k t", k=8)
            R.tpi = id_
            R.z = (ia, ib)
            R.tpc = B[ib].bitcast(BF16)[:, 384:1024].rearrange("p (k t) -> p k t", k=5)
            R.tpci = ib
            R.q = (ia, ic, id_)
            R.kv = (ib, ia, ic, id_)
            R.tq_n = B[ia].bitcast(BF16).rearrange("p (k t) -> p k t", k=8)
            R.tq_r = B[ib].bitcast(BF16).rearrange("p (k t) -> p k t", k=8)
            R.tqi = (ia, ib)
            return R
        BS = [bankset(0), bankset(1)]

        ng = W['norm_gains']
        P.dma(gbc[:, :], ng[l, 1, :].partition_broadcast(128), w=[gb], sig="gbc", par=True)
        C.gbcb = gb
        P.dma(qa[:, :], W['mla_q_a_norm'][j, :].partition_broadcast(128), w=[gb], sig="qa", par=True)
        P.dma(kva[:, :], W['mla_kv_a_norm'][j, :].partition_broadcast(128), w=[gb], sig="kva", par=True)
        P.dma(qn_g[:, :], W['mla_q_norm'][j, :].partition_broadcast(128), w=[gb], sig="qn_g", par=True)
        P.dma(kn_g[:, :], W['mla_k_norm'][j, :].partition_broadcast(128), w=[gb], sig="kn_g", par=True)
        P.dma(invf[:, :, :], W['invf'].partition_broadcast(128), w=[gb], sig="invf", par=True)
        win_r = W['mla_w_in'][j].rearrange("(kc p) f -> p kc f", p=128)
        P.dma(Win[:, :, :], win_r[:, :, :], w=[Winb], sig="W0", eng='pool')
        wq_r = W['mla_w_q_b'][j].rearrange("(kc p) f -> p kc f", p=128)
        P.dma(Wq[:, :, :], wq_r[:, :, :], w=[Wqb], sig="W1", eng='pool')
        wkv_r = W['mla_w_kv_b'][j].rearrange("(kc p) f -> p kc f", p=128)
        P.dma(Wkv[:, :, 0:1024], wkv_r[:, :, 0:1024], w=[Wkvb], sig="W2", eng='pool')
        P.dma(Wkv[:, :, 1024:2048], wkv_r[:, :, 1024:2048], w=[Wkvb], sig="W3", eng='pool')

        xin = x_in.rearrange("(t s p) d -> t p s d", s=1, p=128)
        pin = W['positions'].rearrange("(t p o) -> t p o", p=128, o=1)

        def load_x(t):
            P.dma(xt[t % 2][:, :, :], xin[t], w=[xtb[t % 2]], sig=f"xt{t % 2}")
            P.dma(posi[t % 2][:, :], pin[t], w=[posb[t % 2]], sig=f"posi{t % 2}")

        def norm(t):
            emit_norm_T(P, C, xt[t % 2], xtb[t % 2], gbc[:, :], hT[t % 2], hTb[t % 2], BS[t % 2].tp, Bb[BS[t % 2].tpi], nsub=1, N=NS[t % 2])

        def rope(T, src3, dst3, nh, tag):
            x1 = src3[:, :, 0:32]
            x2 = src3[:, :, 32:64]
            sinb = T.cs[:, 0:1, :].broadcast_to([128, nh, 32])
            cosb = T.cs[:, 1:2, :].broadcast_to([128, nh, 32])
            r = T.rt
            P.op('dve', lambda e: e.tensor_tensor(out=r[:, 0, 0:nh, :], in0=x1, in1=cosb, op=ALU.mult), r=[tag, T.csb], w=[T.rtb])
            P.op('dve', lambda e: e.tensor_tensor(out=r[:, 1, 0:nh, :], in0=x2, in1=sinb, op=ALU.mult), r=[tag, T.csb], w=[T.rtb])
            P.op('dve', lambda e: e.tensor_tensor(out=r[:, 2, 0:nh, :], in0=x2, in1=cosb, op=ALU.mult), r=[tag, T.csb], w=[T.rtb])
            P.op('dve', lambda e: e.tensor_tensor(out=r[:, 3, 0:nh, :], in0=x1, in1=sinb, op=ALU.mult), r=[tag, T.csb], w=[T.rtb])
            return r

        def body(t):
            T = TB[t % 2]
            R = BS[t % 2]
            h = hT[t % 2]
            norm(t)
            yield
            yield
            pf = T.sm[:, 40:41]
            P.op('dve', lambda e: e.tensor_copy(out=pf, in_=posi[t % 2][:, :]), r=[posb[t % 2]], w=[T.angb])
            P.op('dve', lambda e: e.tensor_scalar(out=T.ang[:, :, :], in0=invf[:, :, :], scalar1=pf, scalar2=None, op0=ALU.mult),
                 r=[T.angb, gb], w=[T.angb])
            P.op('dve', lambda e: e.tensor_scalar(out=T.ang[:, 1, :], in0=T.ang[:, 1, :], scalar1=PI / 2, scalar2=None, op0=ALU.add),
                 r=[T.angb], w=[T.angb])
            P.op('dve', lambda e: e.tensor_scalar(out=T.angk[:, :, :], in0=T.ang[:, :, :], scalar1=1.0 / (2 * PI), scalar2=None,
                                                  op0=ALU.mult), r=[T.angb], w=[T.angb])
            P.op('dve', lambda e: e.tensor_copy(out=T.angi[:, :, :], in_=T.angk[:, :, :]), r=[T.angb], w=[T.angb])
            P.op('dve', lambda e: e.tensor_copy(out=T.angk[:, :, :], in_=T.angi[:, :, :]), r=[T.angb], w=[T.angb])
            P.op('dve', lambda e: e.scalar_tensor_tensor(out=T.ang[:, :, :], in0=T.angk[:, :, :], scalar=-2 * PI, in1=T.ang[:, :, :],
                                                         op0=ALU.mult, op1=ALU.add), r=[T.angb], w=[T.angb])
            P.op('dve', lambda e: e.tensor_scalar(out=T.msk[:, :, :], in0=T.ang[:, :, :], scalar1=PI, scalar2=-2 * PI, op0=ALU.is_gt,
                                                  op1=ALU.mult), r=[T.angb], w=[T.angb])
            P.op('dve', lambda e: e.tensor_tensor(out=T.ang[:, :, :], in0=T.ang[:, :, :], in1=T.msk[:, :, :], op=ALU.add), r=[T.angb], w=[T.angb])
            P.op('dve', lambda e: e.tensor_scalar(out=T.msk[:, :, :], in0=T.ang[:, :, :], scalar1=-PI, scalar2=2 * PI, op0=ALU.is_lt,
                                                  op1=ALU.mult), r=[T.angb], w=[T.angb])
            P.op('dve', lambda e: e.tensor_tensor(out=T.ang[:, :, :], in0=T.ang[:, :, :], in1=T.msk[:, :, :], op=ALU.add), r=[T.angb], w=[T.angb])
            P.op('dve', lambda e: e.tensor_scalar(out=T.ang[:, :, :], in0=T.ang[:, :, :], scalar1=PI, scalar2=-PI, op0=ALU.min,
                                                  op1=ALU.max), r=[T.angb], w=[T.angb])
            P.op('act', lambda e: e.activation(out=T.cs[:, :, :], in_=T.ang[:, :, :], func=AF.Sin), r=[T.angb], w=[T.csb])
            yield
            for c0, c1, bk in ((0, 512, R.z[0]), (512, 704, R.z[1])):
                for k in range(8):
                    P.op('pe', lambda e, k=k, c0=c0, c1=c1, bk=bk: e.matmul(B[bk][:, 0:c1 - c0], lhsT=h[:, k, :], rhs=Win[:, k, c0:c1],
                                                                          start=(k == 0), stop=(k == 7)),
                         r=[hTb[t % 2], Winb], w=[Bb[bk]])
            P.op('act', lambda e: e.activation(out=T.zt[:, 0:512], in_=B[R.z[0]][:, :], func=AF.Copy), r=[Bb[R.z[0]]], w=[T.ztb])
            P.op('act', lambda e: e.activation(out=T.zt[:, 512:704], in_=B[R.z[1]][:, 0:192], func=AF.Copy), r=[Bb[R.z[1]]], w=[T.ztb])
            yield
            P.op('act', lambda e: e.activation(out=C.junk[:, 0:384], in_=T.zt[:, 0:384], func=AF.Square, accum_out=T.sm[:, 0:1]),
                 r=[T.ztb], w=[T.smb, C.junkb])
            P.op('act', lambda e: e.activation(out=C.junk[:, 0:256], in_=T.zt[:, 384:640], func=AF.Square, accum_out=T.sm[:, 1:2]),
                 r=[T.ztb], w=[T.smb, C.junkb])
            P.op('act', lambda e: e.activation(out=C.junk[:, 0:64], in_=T.zt[:, 640:704], func=AF.Square, accum_out=T.sm[:, 2:3]),
                 r=[T.ztb], w=[T.smb, C.junkb])
            P.op('dve', lambda e: e.tensor_scalar(out=T.sm[:, 4:5], in0=T.sm[:, 0:1], scalar1=1.0 / 384, scalar2=EPS, op0=ALU.mult,
                                                  op1=ALU.add), r=[T.smb], w=[T.smb])
            P.op('dve', lambda e: e.tensor_scalar(out=T.sm[:, 5:6], in0=T.sm[:, 1:2], scalar1=1.0 / 256, scalar2=EPS, op0=ALU.mult,
                                                  op1=ALU.add), r=[T.smb], w=[T.smb])
            P.op('pool', lambda e: e.tensor_tensor(out=T.sm[:, 6:8], in0=T.sm[:, 4:6], in1=C.neghalf[:, 0:2], op=ALU.pow),
                 r=[T.smb, C.constb], w=[T.smb])
            P.op('dve', lambda e: e.scalar_tensor_tensor(out=T.cn[:, 0:384], in0=T.zt[:, 0:384], scalar=T.sm[:, 6:7], in1=qa[:, :],
                                                         op0=ALU.mult, op1=ALU.mult), r=[T.ztb, T.smb, gb], w=[T.cnb])
            P.op('dve', lambda e: e.scalar_tensor_tensor(out=T.cn[:, 384:640], in0=T.zt[:, 384:640], scalar=T.sm[:, 7:8], in1=kva[:, :],
                                                         op0=ALU.mult, op1=ALU.mult), r=[T.ztb, T.smb, gb], w=[T.cnb])
            for k in range(5):
                P.op('pe', lambda e, k=k: e.transpose(out=R.tpc[:, k, :], in_=T.cn[:, k * 128:(k + 1) * 128], identity=C.ident[:, :]),
                     r=[T.cnb, C.constb], w=[Bb[R.tpci]])
            P.op('dve', lambda e: e.tensor_copy(out=T.cT[:, :, :], in_=R.tpc), r=[Bb[R.tpci]], w=[T.cTb])
            yield
            for c in range(3):
                for k in range(3):
                    P.op('pe', lambda e, c=c, k=k: e.matmul(B[R.q[c]][:, :], lhsT=T.cT[:, k, :], rhs=Wq[:, k, c * 512:(c + 1) * 512],
                                                            start=(k == 0), stop=(k == 2)), r=[T.cTb, Wqb], w=[Bb[R.q[c]]])
            qf2 = T.qf[:, :, :].rearrange("p h d -> p (h d)")
            for c in range(3):
                P.op('act', lambda e, c=c: e.activation(out=qf2[:, c * 512:(c + 1) * 512], in_=B[R.q[c]][:, :], func=AF.Copy),
                     r=[Bb[R.q[c]]], w=[T.qfb])
            yield
            kvb = R.kv
            for c in range(4):
                for k in range(2):
                    P.op('pe', lambda e, c=c, k=k: e.matmul(B[kvb[c]][:, :], lhsT=T.cT[:, 3 + k, :], rhs=Wkv[:, k, c * 512:(c + 1) * 512],
                                                            start=(k == 0), stop=(k == 1)), r=[T.cTb, Wkvb], w=[Bb[kvb[c]]])
            for c in range(4):
                kv3 = B[kvb[c]][:, :].rearrange("p (h d) -> p h d", h=2)
                P.op('act', lambda e, c=c, kv3=kv3: e.activation(out=T.kf[:, 2 * c:2 * c + 2, :], in_=kv3[:, :, 0:128], func=AF.Copy),
                     r=[Bb[kvb[c]]], w=[T.kfb])
                P.op('act', lambda e, c=c, kv3=kv3: e.activation(out=T.vb[:, 2 * c:2 * c + 2, :], in_=kv3[:, :, 128:256], func=AF.Copy),
                     r=[Bb[kvb[c]]], w=[T.vbb])
            P.dma(Vs[t * 128:(t + 1) * 128, :, :], T.vb[:, :, :], r=[T.vbb], sig=f"vb{T.i}")
            yield
            for hh in range(NH):
                P.op('act', lambda e, hh=hh: e.activation(out=C.junk[:, 0:192], in_=T.qf[:, hh, :], func=AF.Square,
                                                           accum_out=T.sm[:, 8 + hh:9 + hh]), r=[T.qfb], w=[T.smb, C.junkb])
            P.op('dve', lambda e: e.tensor_scalar(out=T.sm[:, 16:24], in0=T.sm[:, 8:16], scalar1=1.0 / QK, scalar2=EPS, op0=ALU.mult,
                                                  op1=ALU.add), r=[T.smb], w=[T.smb])
            P.op('pool', lambda e: e.tensor_tensor(out=T.sm[:, 24:32], in0=T.sm[:, 16:24], in1=C.neghalf[:, 0:8], op=ALU.pow),
                 r=[T.smb, C.constb], w=[T.smb])
            P.op('dve', lambda e: e.tensor_tensor(out=T.qs[:, :, :], in0=T.qf[:, :, :],
                                                  in1=T.sm[:, 24:32].unsqueeze(2).broadcast_to([128, NH, QK]), op=ALU.mult),
                 r=[T.qfb, T.smb], w=[T.qsb])
            P.op('dve', lambda e: e.tensor_tensor(out=T.qs[:, :, :], in0=T.qs[:, :, :], in1=qn_g[:, :].unsqueeze(1).broadcast_to([128, NH, QK]),
                                                  op=ALU.mult), r=[T.qsb, gb], w=[T.qsb])
            P.op('act', lambda e: e.activation(out=T.qb16[:, :, 0:128], in_=T.qs[:, :, 0:128], func=AF.Copy), r=[T.qsb], w=[T.qb16b])
            r = rope(T, T.qs[:, :, 128:192], None, NH, T.qsb)
            P.op('dve', lambda e: e.tensor_tensor(out=T.qb16[:, :, 128:160], in0=r[:, 0, :, :], in1=r[:, 1, :, :], op=ALU.subtract),
                 r=[T.rtb], w=[T.qb16b])
            P.op('dve', lambda e: e.tensor_tensor(out=T.qb16[:, :, 160:192], in0=r[:, 2, :, :], in1=r[:, 3, :, :], op=ALU.add),
                 r=[T.rtb], w=[T.qb16b])
            yield
            for hh in range(NH):
                P.op('act', lambda e, hh=hh: e.activation(out=C.junk[:, 0:128], in_=T.kf[:, hh, :], func=AF.Square,
                                                           accum_out=T.sm[:, 32 + hh:33 + hh]), r=[T.kfb], w=[T.smb, C.junkb])
            P.op('dve', lambda e: e.tensor_scalar(out=T.sm[:, 48:56], in0=T.sm[:, 32:40], scalar1=T.sm[:, 2:3], scalar2=1.0 / QK, op0=ALU.add,
                                                  op1=ALU.mult), r=[T.smb], w=[T.smb])
            P.op('dve', lambda e: e.tensor_scalar(out=T.sm[:, 48:56], in0=T.sm[:, 48:56], scalar1=EPS, scalar2=None, op0=ALU.add),
                 r=[T.smb], w=[T.smb])
            P.op('pool', lambda e: e.tensor_tensor(out=T.sm[:, 56:64], in0=T.sm[:, 48:56], in1=C.neghalf[:, 0:8], op=ALU.pow),
                 r=[T.smb, C.constb], w=[T.smb])
            P.op('dve', lambda e: e.tensor_tensor(out=T.kf[:, :, :], in0=T.kf[:, :, :],
                                                  in1=T.sm[:, 56:64].unsqueeze(2).broadcast_to([128, NH, 128]), op=ALU.mult),
                 r=[T.kfb, T.smb], w=[T.kfb])
            P.op('dve', lambda e: e.tensor_tensor(out=T.kn16[:, :, 0:128], in0=T.kf[:, :, :],
                                                  in1=kn_g[:, 0:128].unsqueeze(1).broadcast_to([128, NH, 128]), op=ALU.mult),
                 r=[T.kfb, gb], w=[T.kn16b])
            P.op('dve', lambda e: e.tensor_tensor(out=T.kg[:, :], in0=T.zt[:, 640:704], in1=kn_g[:, 128:192], op=ALU.mult),
                 r=[T.ztb, gb], w=[T.kgb])
            r = rope(T, T.kg[:, :].unsqueeze(1), None, 1, T.kgb)
            P.op('dve', lambda e: e.tensor_tensor(out=T.kr[:, 0:32], in0=r[:, 0, 0, :], in1=r[:, 1, 0, :], op=ALU.subtract),
                 r=[T.rtb], w=[T.kgb])
            P.op('dve', lambda e: e.tensor_tensor(out=T.kr[:, 32:64], in0=r[:, 2, 0, :], in1=r[:, 3, 0, :], op=ALU.add),
                 r=[T.rtb], w=[T.kgb])
            P.op('dve', lambda e: e.tensor_tensor(out=T.kn16[:, :, 128:192], in0=T.kr[:, :].unsqueeze(1).broadcast_to([128, NH, 64]),
                                                  in1=T.sm[:, 56:64].unsqueeze(2).broadcast_to([128, NH, 64]), op=ALU.mult),
                 r=[T.kgb, T.smb], w=[T.kn16b])
            yield
            for (src, srcb, dn, dr, sl) in ((T.qb16, T.qb16b, QTn, QTr, 0), (T.kn16, T.kn16b, KTn, KTr, 1)):
                for hh in range(NH):
                    P.op('pe', lambda e, hh=hh, src=src: e.transpose(out=R.tq_n[:, hh, :], in_=src[:, hh, 0:128], identity=C.ident[:, :]),
                         r=[srcb, C.constb], w=[Bb[R.tqi[0]]])
                for hh in range(NH):
                    P.op('pe', lambda e, hh=hh, src=src: e.transpose(out=R.tq_r[0:64, hh, :], in_=src[:, hh, 128:192],
                                                                     identity=C.ident[:, :]), r=[srcb, C.constb], w=[Bb[R.tqi[1]]])
                P.op('act', lambda e, sl=sl: e.activation(out=T.oTn[sl][:, :, :], in_=R.tq_n, func=AF.Copy), r=[Bb[R.tqi[0]]], w=[T.oTnb[sl]])
                P.op('dve', lambda e, sl=sl: e.tensor_copy(out=T.oTr[sl][:, :, :], in_=R.tq_r[0:64, :, :]), r=[Bb[R.tqi[1]]], w=[T.oTrb[sl]])
                P.dma(dn[:, :, t * 128:(t + 1) * 128].rearrange("h d t -> d h t"), T.oTn[sl][:, :, :], r=[T.oTnb[sl]], sig=f"oTn{T.i}_{sl}")
                P.dma(dr[:, :, t * 128:(t + 1) * 128].rearrange("h d t -> d h t"), T.oTr[sl][:, :, :], r=[T.oTrb[sl]], sig=f"oTr{T.i}_{sl}")

        load_x(0)
        if ntile > 1:
            load_x(1)
        for t0 in range(0, ntile, 2):
            lists = [P.capture(lambda t=t: [None for _ in body(t)]) for t in range(t0, min(t0 + 2, ntile))]
            P.replay(lists)
            for t in range(t0 + 2, min(t0 + 4, ntile)):
                load_x(t)
        P.flush()


def phase_mla2(P, C, x_in, x_out, Wd_, j, g, S, scrQ):
    nc = P.nc
    W = Wd_
    QTn, QTr, KTn, KTr, Vs = scrQ
    nkt = S // 128
    QB = 512 if S >= 512 else S
    nqb = S // QB
    nsubq = QB // 128
    HG = 4
    with ExitStack() as es:
        def sb(name, shape, dt):
            return es.enter_context(nc.sbuf_tensor(uname(name), shape, dt))

        def ps(name, shape, dt=F32):
            return es.enter_context(nc.psum_tensor(uname(name), shape, dt))
        Kn = sb("Kn", [128, HG, S], BF16)
        Kr = sb("Kr", [64, HG, S], BF16)
        Vt = sb("Vt", [128, nkt, HG, 128], BF16)
        Knb = [P.buf(f"Kn{h}") for h in range(HG)]
        Krb = [P.buf(f"Kr{h}") for h in range(HG)]
        Vtb = P.buf("Vt")
        Wo = sb("Wo", [128, HG, D], BF16)
        Wob = P.buf("Wo")
        ones = sb("ones", [128, 128], BF16)
        tri = sb("tri", [128, 128], BF16)
        trif = sb("trif", [128, 128], F32)
        cb = P.buf("mconst")
        xt = sb("xt", [128, nsubq, D], F32)
        xtb = P.buf("xt")
        Qn = [sb(f"Qn{i}", [128, HG, QB], BF16) for i in range(2)]
        Qr = [sb(f"Qr{i}", [64, HG, QB], BF16) for i in range(2)]
        Qb_ = [P.buf(f"Q{i}") for i in range(2)]
        pT = [sb(f"pT{i}", [128, QB], BF16) for i in range(3)]
        pTb = [P.buf(f"pT{i}") for i in range(3)]
        rc = sb("rc", [128, QB], F32)
        rcb = P.buf("rc")
        aT = [sb(f"aT{h}", [128, QB], BF16) for h in range(HG)]
        aTb = [P.buf(f"aT{h}") for h in range(HG)]
        sT = [ps(f"sT{i}", [128, 512]) for i in range(2)]
        sTb = [P.buf(f"sT{i}") for i in range(2)]
        oT = [ps(f"oT{i}", [128, 512]) for i in range(2)]
        oTb = [P.buf(f"oT{i}") for i in range(2)]
        lS = [ps(f"lS{i}", [128, 512]) for i in range(2)]
        lSb = [P.buf(f"lS{i}") for i in range(2)]
        Y = [ps(f"Y{i}", [128, 512]) for i in range(2)]
        Yb = [P.buf(f"Y{i}") for i in range(2)]

        P.op('pool', lambda e: e.memset(ones[:, :], 1.0), w=[cb])
        P.op('pool', lambda e: e.memset(trif[:, :], 1.0), w=[cb])
        P.op('pool', lambda e: e.affine_select(out=trif[:, :], in_=trif[:, :], pattern=[[1, 128]], compare_op=ALU.is_ge, fill=0.0,
                                               base=0, channel_multiplier=-1), r=[cb], w=[cb])
        P.op('pool', lambda e: e.tensor_copy(out=tri[:, :], in_=trif[:, :]), r=[cb], w=[cb])
        for h in range(HG):
            P.dma(Kn[:, h, :], KTn[g * HG + h, :, :], w=[Knb[h]], sig=f"Kn{h}")
            P.dma(Kr[:, h, :], KTr[g * HG + h, :, :], w=[Krb[h]], sig=f"Kr{h}")
            if h == 0:
                P.dma(Vt[:, :, :, :], Vs[:, g * HG:(g + 1) * HG, :].rearrange("(kt p) h e -> p kt h e", p=128), w=[Vtb], sig="Vt")
        wo_r = W['mla_w_out'][j][g * HG * 128:(g + 1) * HG * 128, :].rearrange("(h p) d -> p h d", p=128)
        P.dma(Wo[:, :, :], wo_r[:, :, :], w=[Wob], sig="W0", eng='pool')

        xin = x_in.rearrange("(b s p) d -> b p s d", s=nsubq, p=128)
        xout = x_out.rearrange("(b s p) d -> b p s d", s=nsubq, p=128)

        def load_q(b):
            sl = b % 2
            P.dma(Qn[sl][:, :, :], QTn[g * HG:(g + 1) * HG, :, b * QB:(b + 1) * QB].rearrange("h d t -> d h t"), w=[Qb_[sl]],
                  sig=f"Qn{sl}")
            P.dma(Qr[sl][:, :, :], QTr[g * HG:(g + 1) * HG, :, b * QB:(b + 1) * QB].rearrange("h d t -> d h t"), w=[Qb_[sl]],
                  sig=f"Qr{sl}")

        cnt = [0]

        def qk_mm(b, h, kt, ssl, qs_):
            r = kt - nsubq * b
            c0 = max(r, 0) * 128
            P.op('pe', lambda e: e.matmul(sT[ssl][:, c0:QB], lhsT=Kn[:, h, kt * 128:(kt + 1) * 128],
                                          rhs=Qn[qs_][:, h, c0:QB], start=True, stop=False),
                 r=[Knb[h], Qb_[qs_]], w=[sTb[ssl]])
            P.op('pe', lambda e: e.matmul(sT[ssl][:, c0:QB], lhsT=Kr[:, h, kt * 128:(kt + 1) * 128],
                                          rhs=Qr[qs_][:, h, c0:QB], start=False, stop=True),
                 r=[Krb[h], Qb_[qs_]], w=[sTb[ssl]])

        def pv_mm(b, h, kt, nk, ob, ssl, psl):
            r = kt - nsubq * b
            c0 = max(r, 0) * 128
            P.op('act', lambda e: e.activation(out=pT[psl][:, c0:QB], in_=sT[ssl][:, c0:QB], func=AF.Exp,
                                               scale=ATTN_SCALE), r=[sTb[ssl]], w=[pTb[psl]])
            if r >= 0:
                P.op('dve', lambda e: e.tensor_tensor(out=pT[psl][:, c0:c0 + 128], in0=pT[psl][:, c0:c0 + 128],
                                                      in1=tri[:, :], op=ALU.mult), r=[pTb[psl], cb], w=[pTb[psl]])
            P.op('pe', lambda e: e.matmul(oT[ob][:, c0:QB], lhsT=Vt[:, kt, h, :], rhs=pT[psl][:, c0:QB],
                                          start=(kt == 0), stop=(kt == nk - 1)),
                 r=[Vtb, pTb[psl]], w=[oTb[ob]])
            P.op('pe', lambda e: e.matmul(lS[ob][:, c0:QB], lhsT=ones[:, :], rhs=pT[psl][:, c0:QB],
                                          start=(kt == 0), stop=(kt == nk - 1)),
                 r=[cb, pTb[psl]], w=[lSb[ob]])

        def head_fin(b, h, ob):
            P.op('dve', lambda e: e.reciprocal(out=rc[:, :], in_=lS[ob][:, 0:QB]), r=[lSb[ob]], w=[rcb])
            P.op('dve', lambda e: e.tensor_tensor(out=aT[h][:, :], in0=oT[ob][:, 0:QB], in1=rc[:, :], op=ALU.mult),
                 r=[oTb[ob], rcb], w=[aTb[h]])

        def do_block(b, first_issued):
            qs_ = b % 2
            nk = nsubq * (b + 1)
            pairs = [(h, kt) for h in range(HG) for kt in range(nk)]
            base = cnt[0]
            cnt[0] += len(pairs)
            if not first_issued:
                qk_mm(b, pairs[0][0], pairs[0][1], base % 2, qs_)
            for i, (h, kt) in enumerate(pairs):
                if i + 1 < len(pairs):
                    qk_mm(b, pairs[i + 1][0], pairs[i + 1][1], (base + i + 1) % 2, qs_)
                ob = (b * HG + h) % 2
                pv_mm(b, h, kt, nk, ob, (base + i) % 2, (base + i) % 3)
                if kt == nk - 1:
                    head_fin(b, h, ob)

        def do_proj(b, sq, dh):
            for h in range(HG):
                P.op('pe', lambda e, h=h: e.matmul(Y[dh][:, :], lhsT=aT[h][:, sq * 128:(sq + 1) * 128],
                                                   rhs=Wo[:, h, dh * 512:(dh + 1) * 512], start=(h == 0),
                                                   stop=(h == HG - 1)), r=[aTb[h], Wob], w=[Yb[dh]])
            P.op('dve', lambda e: e.tensor_tensor(out=xt[:, sq, dh * 512:(dh + 1) * 512],
                                                  in0=xt[:, sq, dh * 512:(dh + 1) * 512], in1=Y[dh][:, :],
                                                  op=ALU.add), r=[Yb[dh], xtb], w=[xtb])

        load_q(0)
        for b in range(nqb):
            if b + 1 < nqb:
                load_q(b + 1)
            P.dma(xt[:, :, :], xin[b], w=[xtb], sig="xt")
            do_block(b, b > 0)
            if b + 1 < nqb:
                qk_mm(b + 1, 0, 0, cnt[0] % 2, (b + 1) % 2)
            for sq in range(nsubq):
                for dh in range(2):
                    do_proj(b, sq, dh)
            P.dma(xout[b], xt[:, :, :], r=[xtb], sig="xt")
        P.flush()


def phase_even(P, C, x_in, x_out, Wd_, l, j, S):
    nc = P.nc
    W = Wd_
    ntile = S // 128
    with ExitStack() as es:
        def sb(name, shape, dt):
            return es.enter_context(nc.sbuf_tensor(uname(name), shape, dt))

        def ps(name, shape, dt=F32):
            return es.enter_context(nc.psum_tensor(uname(name), shape, dt))
        Win = sb("eWin", [128, 8, 3072], BF16)
        Winb = [P.buf(f"eWin{g}") for g in range(6)]
        Wo = sb("eWo", [128, 8, D], BF16)
        Wob = P.buf("eWo")
        gbc = sb("gbc", [128, D], F32)
        gb = P.buf("gains")
        C.gbcb = gb
        vnorm = sb("vnorm", [128, 512], F32)
        onorm = sb("onorm", [128, 128], F32)
        bs = sb("bs", [128, 8, 1], F32)
        lbr = sb("lbr", [128, 2, 4, 1], F32)
        lb = sb("lb", [128, 4], F32)
        oml = sb("oml", [128, 4], F32)
        wsf = sb("wsf", [128, 8, 128], F32)
        ws16 = sb("ws16", [128, 8, 128], BF16)
        WsT = sb("WsT", [128, 8, 128], BF16)
        trif = sb("trif", [128, 128], F32)
        bmask = sb("bmask", [128, 128], F32)
        rmask = sb("rmask", [128, 512], F32)
        cb = P.buf("econst")
        xt = [sb(f"xt{i}", [128, 1, D], F32) for i in range(4)]
        xtb = [P.buf(f"xt{i}") for i in range(4)]
        hT = [sb(f"hT{i}", [128, 8, 128], BF16) for i in range(2)]
        hTb = [P.buf(f"hT{i}") for i in range(2)]
        S32 = sb("S32", [128, 4, 128], F32)
        S16 = sb("S16", [128, 4, 128], BF16)
        S32b, S16b = P.buf("S32"), P.buf("S16")
        def alloc_tile(i):
            T = Ctx()
            for nm, shape, dt in (("f1", [128, 4, 128], F32), ("kk", [128, 4, 128], F32), ("lf", [128, 4, 128], F32),
                                  ("Bc", [128, 4, 128], F32), ("e1", [128, 4, 128], F32), ("e2", [128, 4, 128], F32),
                                  ("e3", [128, 4, 128], F32), ("eBC", [128, 4, 4, 1], F32), ("qf32", [128, 4, 128], F32),
                                  ("qt", [128, 4, 128], BF16), ("kt_", [128, 4, 128], BF16), ("kh", [128, 4, 128], BF16),
                                  ("khT", [128, 4, 128], BF16), ("v16", [128, 4, 128], BF16), ("gu_", [128, 512], F32),
                                  ("gv", [128, 8, 64], F32), ("sq", [128, 8, 64], F32), ("vn16", [128, 8, 64], BF16),
                                  ("sgl", [128, 512], F32), ("on", [128, 4, 128], F32), ("mT", [128, 8, 128], BF16),
                                  ("sm", [128, 64], F32)):
                setattr(T, nm, sb(f"{nm}_{i}", shape, dt))
                setattr(T, nm + "b", P.buf(f"{nm}_{i}"))
            T.PT = [sb(f"PT{h}_{i}", [128, 128], BF16) for h in range(4)]
            T.PTb = [P.buf(f"PT{h}_{i}") for h in range(4)]
            T.mixed = sb(f"mixed_{i}", [128, D], BF16)
            T.mixb = [P.buf(f"mixA_{i}"), P.buf(f"mixB_{i}")]
            return T
        TB = [alloc_tile(0), alloc_tile(1)]
        NS = [norm_scratch(P, sb, i, 1) for i in range(2)]
        B = [ps(f"B{i}", [128, 512]) for i in range(8)]
        Bb = [P.buf(f"B{i}") for i in range(8)]

        def b3(i, a):
            return B[i][:, :].rearrange("p (a t) -> p a t", a=a)

        def b16(i, a, n):
            return B[i].bitcast(BF16)[:, 0:a * n].rearrange("p (a t) -> p a t", a=a)

        ng = W['norm_gains']
        P.dma(gbc[:, :], ng[l, 1, :].partition_broadcast(128), w=[gb], sig="gbc", par=True)
        P.dma(vnorm[:, :], W['gmlp_v_norm'][j].rearrange("h d -> (h d)").partition_broadcast(128), w=[gb], sig="vnorm", par=True)
        P.dma(onorm[:, :], W['hgrn_out_norm'][j, :].partition_broadcast(128), w=[gb], sig="onorm", par=True)
        P.dma(bs[:, :, :], W['gmlp_b_s'][j].rearrange("h (t o) -> t h o", o=1), w=[gb], sig="bs", slow=True, par=True)
        P.dma(lbr[:, :, :, :], W['hgrn_lb_raw'].rearrange("r (h d o) -> d r h o", h=4, o=1), w=[gb], sig="lbr", slow=True, par=True)
        P.dma(wsf[:, :, :], W['gmlp_w_s'][j].rearrange("h t s -> t h s"), w=[gb], sig="wsf", par=True)
        P.op('pool', lambda e: e.memset(trif[:, :], 1.0), w=[cb])
        P.op('pool', lambda e: e.memset(bmask[:, :], 1.0), w=[cb])
        P.op('pool', lambda e: e.memset(rmask[:, :], 1.0), w=[cb])
        P.op('pool', lambda e: e.memset(rmask[:, :].rearrange("p (c t) -> p c t", t=32)[:, :, 0:1], 0.0), w=[cb])
        P.op('pool', lambda e: e.memset(S32[:, :, :], 0.0), w=[S32b])
        P.op('pool', lambda e: e.memset(S16[:, :, :], 0.0), w=[S16b])
        P.op('pool', lambda e: e.affine_select(out=trif[:, :], in_=trif[:, :], pattern=[[-1, 128]], compare_op=ALU.is_ge, fill=0.0,
                                               base=0, channel_multiplier=1), r=[cb], w=[cb])
        P.op('pool', lambda e: e.affine_select(out=bmask[:, :], in_=bmask[:, :], pattern=[[1, 128]], compare_op=ALU.is_ge, fill=0.0,
                                               base=0, channel_multiplier=-1), r=[cb], w=[cb])
        for cbk in range(1, 4):
            P.op('pool', lambda e, cbk=cbk: e.memset(bmask[0:32 * cbk, 32 * cbk:32 * cbk + 32], 0.0), r=[cb], w=[cb])
        P.op('dve', lambda e: e.tensor_tensor(out=ws16[:, :, :], in0=wsf[:, :, :], in1=trif[:, :].unsqueeze(1).broadcast_to([128, 8, 128]),
                                              op=ALU.mult), r=[gb, cb], w=[cb])
        tws = b16(7, 8, 128)
        for hh in range(8):
            P.op('pe', lambda e, hh=hh: e.transpose(out=tws[:, hh, :], in_=ws16[:, hh, :], identity=C.ident[:, :]), r=[cb, C.constb],
                 w=[Bb[7]])
        P.op('dve', lambda e: e.tensor_copy(out=WsT[:, :, :], in_=tws), r=[Bb[7]], w=[cb])
        if j == 0:
            P.op('pool', lambda e: e.memset(lb[:, :], 0.0), w=[cb])
        else:
            P.op('dve', lambda e: e.tensor_tensor(out=lb[:, :], in0=lbr[:, 1, :, 0], in1=lbr[:, 0, :, 0], op=ALU.subtract), r=[gb], w=[cb])
            P.op('act', lambda e: e.activation(out=lb[:, :], in_=lb[:, :], func=AF.Sigmoid), r=[cb], w=[cb])
            P.op('dve', lambda e: e.tensor_scalar(out=lb[:, :], in0=lb[:, :], scalar1=0.999, scalar2=0.0, op0=ALU.min, op1=ALU.max),
                 r=[cb], w=[cb])
        P.op('dve', lambda e: e.tensor_scalar(out=oml[:, :], in0=lb[:, :], scalar1=-1.0, scalar2=1.0, op0=ALU.mult, op1=ALU.add),
             r=[cb], w=[cb])
        win_r = W['even_w_in'][j].rearrange("(kc p) f -> p kc f", p=128)
        for g in (0, 1, 4, 5, 2, 3):
            P.dma(Win[:, :, g * 512:(g + 1) * 512], win_r[:, :, g * 512:(g + 1) * 512], w=[Winb[g]], sig=f"W{g}", eng='pool')
        wo_r = W['even_w_out'][j].rearrange("(kc p) f -> p kc f", p=128)
        P.dma(Wo[:, :, :], wo_r[:, :, :], w=[Wob], sig="W6", eng='pool')

        xin = x_in.rearrange("(t s p) d -> t p s d", s=1, p=128)
        xout = x_out.rearrange("(t s p) d -> t p s d", s=1, p=128)

        def load_x(t):
            P.dma(xt[t % 4][:, :, :], xin[t], w=[xtb[t % 4]], sig=f"xt{t % 4}")

        def bankset(par):
            R = Ctx()
            R.p = (4 * par, 4 * par + 1, 4 * par + 2)
            R.o = 4 * par + 3
            return R
        BS = [bankset(0), bankset(1)]

        def pre(t):
            T = TB[t % 2]
            R = BS[t % 2]
            p0, p1, p2 = R.p
            h = hT[t % 2]
            hb = hTb[t % 2]
            tp = B[p0].bitcast(BF16).rearrange("p (k t) -> p k t", k=8)
            emit_norm_T(P, C, xt[t % 4], xtb[t % 4], gbc[:, :], h, hb, tp, Bb[p0], nsub=1, N=NS[t % 2])

            def proj_tok(bk, c0):
                for k in range(8):
                    P.op('pe', lambda e, k=k: e.matmul(B[bk][:, :], lhsT=h[:, k, :], rhs=Win[:, k, c0:c0 + 512],
                                                       start=(k == 0), stop=(k == 7)), r=[hb, Winb[c0 // 512]], w=[Bb[bk]])

            def proj_feat(bk, c0):
                bv = b3(bk, 4)
                for hh in range(4):
                    for k in range(8):
                        P.op('pe', lambda e, k=k, hh=hh: e.matmul(bv[:, hh, :], lhsT=Win[:, k, c0 + hh * 128:c0 + (hh + 1) * 128],
                                                                  rhs=h[:, k, :], start=(k == 0), stop=(k == 7)),
                             r=[hb, Winb[c0 // 512]], w=[Bb[bk]])
            proj_tok(p1, 0)
            proj_tok(p2, 512)
            P.op('act', lambda e: e.activation(out=T.gu_[:, :], in_=B[p1][:, :], func=AF.Gelu_apprx_tanh), r=[Bb[p1]], w=[T.gu_b])
            P.op('act', lambda e: e.activation(out=T.gv[:, :, :], in_=b3(p2, 8), func=AF.Gelu_apprx_tanh), r=[Bb[p2]], w=[T.gvb])
            proj_tok(p0, 2048)
            proj_tok(p1, 2560)
            P.op('act', lambda e: e.activation(out=T.v16[:, :, :], in_=b3(p0, 4), func=AF.Copy), r=[Bb[p0]], w=[T.v16b])
            P.op('act', lambda e: e.activation(out=T.sgl[:, :], in_=B[p1][:, :], func=AF.Silu), r=[Bb[p1]], w=[T.sglb])
            proj_feat(p2, 1024)
            proj_feat(p0, 1536)
            P.op('act', lambda e: e.activation(out=T.qf32[:, :, :], in_=b3(p2, 4), func=AF.Copy), r=[Bb[p2]], w=[T.qf32b])
            P.op('act', lambda e: e.activation(out=T.f1[:, :, :], in_=b3(p0, 4), func=AF.Sigmoid), r=[Bb[p0]], w=[T.f1b])
            lb_bc = lb[:, :].unsqueeze(2).broadcast_to([128, 4, 128])
            oml_bc = oml[:, :].unsqueeze(2).broadcast_to([128, 4, 128])
            P.op('dve', lambda e: e.tensor_tensor(out=T.f1[:, :, :], in0=T.f1[:, :, :], in1=oml_bc, op=ALU.mult), r=[T.f1b, cb], w=[T.f1b])
            P.op('dve', lambda e: e.tensor_tensor(out=T.f1[:, :, :], in0=T.f1[:, :, :], in1=lb_bc, op=ALU.add), r=[T.f1b, cb], w=[T.f1b])
            P.op('dve', lambda e: e.tensor_scalar(out=T.kk[:, :, :], in0=T.f1[:, :, :], scalar1=-1.0, scalar2=1.0, op0=ALU.mult, op1=ALU.add),
                 r=[T.f1b], w=[T.kkb])
            P.op('dve', lambda e: e.tensor_scalar(out=T.lf[:, :, :], in0=T.f1[:, :, :], scalar1=1e-6, scalar2=None, op0=ALU.max),
                 r=[T.f1b], w=[T.lfb])
            P.op('act', lambda e: e.activation(out=T.lf[:, :, :], in_=T.lf[:, :, :], func=AF.Ln), r=[T.lfb], w=[T.lfb])
            P.op('dve', lambda e: e.tensor_tensor_scan(out=T.Bc[:, :, :].rearrange("p h t -> p (h t)"), data0=rmask[:, :],
                                                       data1=T.lf[:, :, :].rearrange("p h t -> p (h t)"), initial=0.0, op0=ALU.mult,
                                                       op1=ALU.add), r=[T.lfb, cb], w=[T.Bcb])
            P.op('act', lambda e: e.activation(out=T.e1[:, :, :], in_=T.Bc[:, :, :], func=AF.Exp), r=[T.Bcb], w=[T.e1b])
            P.op('dve', lambda e: e.tensor_tensor(out=T.qt[:, :, :], in0=T.qf32[:, :, :], in1=T.e1[:, :, :], op=ALU.mult),
                 r=[T.qf32b, T.e1b], w=[T.qtb])
            P.op('dve', lambda e: e.tensor_scalar(out=T.e2[:, :, :], in0=T.Bc[:, :, :], scalar1=-60.0, scalar2=None, op0=ALU.max),
                 r=[T.Bcb], w=[T.e2b])
            P.op('act', lambda e: e.activation(out=T.e2[:, :, :], in_=T.e2[:, :, :], func=AF.Exp, scale=-1.0), r=[T.e2b], w=[T.e2b])
            P.op('dve', lambda e: e.tensor_tensor(out=T.kt_[:, :, :], in0=T.kk[:, :, :], in1=T.e2[:, :, :], op=ALU.mult),
                 r=[T.kkb, T.e2b], w=[T.kt_b])
            B4 = T.Bc[:, :, :].rearrange("p h (c t) -> p h c t", t=32)
            BCl = B4[:, :, :, 31:32]
            P.op('dve', lambda e: e.tensor_tensor(out=T.e3[:, :, :].rearrange("p h (c t) -> p h c t", t=32),
                                                  in0=BCl.broadcast_to([128, 4, 4, 32]), in1=B4, op=ALU.subtract), r=[T.Bcb], w=[T.e3b])
            P.op('act', lambda e: e.activation(out=T.e3[:, :, :], in_=T.e3[:, :, :], func=AF.Exp), r=[T.e3b], w=[T.e3b])
            P.op('dve', lambda e: e.tensor_tensor(out=T.kh[:, :, :], in0=T.kk[:, :, :], in1=T.e3[:, :, :], op=ALU.mult),
                 r=[T.kkb, T.e3b], w=[T.khb])
            P.op('act', lambda e: e.activation(out=T.eBC[:, :, :, :], in_=BCl, func=AF.Exp), r=[T.Bcb], w=[T.eBCb])
            tk = b16(p1, 4, 128)
            for hh in range(4):
                P.op('pe', lambda e, hh=hh: e.transpose(out=tk[:, hh, :], in_=T.kh[:, hh, :], identity=C.ident[:, :]),
                     r=[T.khb, C.constb], w=[Bb[p1]])
            P.op('act', lambda e: e.activation(out=T.khT[:, :, :], in_=tk, func=AF.Copy), r=[Bb[p1]], w=[T.khTb])
            P.op('dve', lambda e: e.tensor_tensor(out=T.sq[:, :, :], in0=T.gv[:, :, :], in1=T.gv[:, :, :], op=ALU.mult), r=[T.gvb], w=[T.sqb])
            P.op('dve', lambda e: e.tensor_reduce(out=T.sm[:, 0:8], in_=T.sq[:, :, :], axis=AX.X, op=ALU.add), r=[T.sqb], w=[T.smb])
            P.op('dve', lambda e: e.tensor_scalar(out=T.sm[:, 8:16], in0=T.sm[:, 0:8], scalar1=1.0 / 64, scalar2=EPS, op0=ALU.mult,
                                                  op1=ALU.add), r=[T.smb], w=[T.smb])
            P.op('pool', lambda e: e.tensor_tensor(out=T.sm[:, 16:24], in0=T.sm[:, 8:16], in1=C.neghalf[:, 0:8], op=ALU.pow),
                 r=[T.smb, C.constb], w=[T.smb])
            P.op('dve', lambda e: e.tensor_tensor(out=T.sq[:, :, :], in0=T.gv[:, :, :],
                                                  in1=T.sm[:, 16:24].unsqueeze(2).broadcast_to([128, 8, 64]), op=ALU.mult),
                 r=[T.gvb, T.smb], w=[T.sqb])
            P.op('dve', lambda e: e.tensor_tensor(out=T.vn16[:, :, :], in0=T.sq[:, :, :],
                                                  in1=vnorm[:, :].rearrange("p (h d) -> p h d", h=8), op=ALU.mult),
                 r=[T.sqb, gb], w=[T.vn16b])
            Mv = b3(p2, 8)
            for hh in range(8):
                P.op('pe', lambda e, hh=hh: e.matmul(Mv[:, hh, :], lhsT=WsT[:, hh, :], rhs=T.vn16[:, hh, :], start=True, stop=True),
                     r=[cb, T.vn16b], w=[Bb[p2]])
            P.op('dve', lambda e: e.tensor_tensor(out=T.sq[:, :, :], in0=Mv, in1=bs[:, :, :].broadcast_to([128, 8, 64]), op=ALU.add),
                 r=[Bb[p2], gb], w=[T.sqb])
            P.op('dve', lambda e: e.tensor_tensor(out=T.mixed[:, 0:512], in0=T.sq[:, :, :].rearrange("p h d -> p (h d)"), in1=T.gu_[:, :],
                                                  op=ALU.mult), r=[T.sqb, T.gu_b], w=[T.mixb[0]])
            sc = b3(p0, 4)
            for hh in range(4):
                P.op('pe', lambda e, hh=hh: e.matmul(sc[:, hh, :], lhsT=T.kt_[:, hh, :], rhs=T.qt[:, hh, :], start=True, stop=True),
                     r=[T.kt_b, T.qtb], w=[Bb[p0]])
            for hh in range(4):
                P.op('dve', lambda e, hh=hh: e.tensor_tensor(out=T.PT[hh][:, :], in0=sc[:, hh, :], in1=bmask[:, :], op=ALU.mult),
                     r=[Bb[p0], cb], w=[T.PTb[hh]])
            O = b3(R.o, 4)
            for hh in range(4):
                P.op('pe', lambda e, hh=hh: e.matmul(O[:, hh, :], lhsT=T.PT[hh][:, :], rhs=T.v16[:, hh, :], start=(hh == 0), stop=False,
                                                     skip_group_check=True), r=[T.PTb[hh], T.v16b], w=[Bb[R.o]])

        def chain(t):
            T = TB[t % 2]
            R = BS[t % 2]
            p0, p1, p2 = R.p
            O = b3(R.o, 4)
            Ust = b3(p1, 4)
            for c in range(4):
                for hh in range(4):
                    P.op('pe', lambda e, hh=hh, c=c: e.matmul(O[32 * c:32 * c + 32, hh, :], lhsT=T.qt[:, hh, 32 * c:32 * c + 32],
                                                              rhs=S16[:, hh, :], start=False, stop=(c == 3), tile_position=(0, 32 * c),
                                                              skip_group_check=True), r=[T.qtb, S16b], w=[Bb[R.o]])
                for hh in range(4):
                    P.op('pe', lambda e, hh=hh, c=c: e.matmul(Ust[:, hh, :], lhsT=T.khT[32 * c:32 * c + 32, hh, :],
                                                              rhs=T.v16[32 * c:32 * c + 32, hh, :], start=True, stop=True,
                                                              tile_position=(32 * c, 0)), r=[T.khTb, T.v16b], w=[Bb[p1]])
                for hh in range(4):
                    P.op('dve', lambda e, hh=hh, c=c: e.scalar_tensor_tensor(out=S32[:, hh, :], in0=S32[:, hh, :], scalar=T.eBC[:, hh, c, :],
                                                                             in1=Ust[:, hh, :], op0=ALU.mult, op1=ALU.add),
                         r=[S32b, T.eBCb, Bb[p1]], w=[S32b])
                P.op('act', lambda e: e.activation(out=S16[:, :, :], in_=S32[:, :, :], func=AF.Copy), r=[S32b], w=[S16b])

        def post(t):
            T = TB[t % 2]
            R = BS[t % 2]
            p0, p1, p2 = R.p
            x = xt[t % 4]
            xb = xtb[t % 4]
            O = b3(R.o, 4)
            for hh in range(4):
                P.op('act', lambda e, hh=hh: e.activation(out=C.junk[:, 0:128], in_=O[:, hh, :], func=AF.Square,
                                                           accum_out=T.sm[:, 32 + hh:33 + hh]), r=[Bb[R.o]], w=[T.smb, C.junkb])
            P.op('dve', lambda e: e.tensor_scalar(out=T.sm[:, 40:44], in0=T.sm[:, 32:36], scalar1=1.0 / 128, scalar2=EPS, op0=ALU.mult,
                                                  op1=ALU.add), r=[T.smb], w=[T.smb])
            P.op('pool', lambda e: e.tensor_tensor(out=T.sm[:, 44:48], in0=T.sm[:, 40:44], in1=C.neghalf[:, 0:4], op=ALU.pow),
                 r=[T.smb, C.constb], w=[T.smb])
            for hh in range(4):
                P.op('act', lambda e, hh=hh: e.activation(out=T.on[:, hh, :], in_=O[:, hh, :], func=AF.Copy, scale=T.sm[:, 44 + hh:45 + hh]),
                     r=[Bb[R.o], T.smb], w=[T.onb])
            P.op('dve', lambda e: e.tensor_tensor(out=T.on[:, :, :], in0=T.on[:, :, :],
                                                  in1=onorm[:, :].unsqueeze(1).broadcast_to([128, 4, 128]), op=ALU.mult),
                 r=[T.onb, gb], w=[T.onb])
            P.op('dve', lambda e: e.tensor_tensor(out=T.mixed[:, 512:1024], in0=T.on[:, :, :].rearrange("p h d -> p (h d)"), in1=T.sgl[:, :],
                                                  op=ALU.mult), r=[T.onb, T.sglb], w=[T.mixb[1]])
            tm = b16(p2, 8, 128)
            for k in range(8):
                P.op('pe', lambda e, k=k: e.transpose(out=tm[:, k, :], in_=T.mixed[:, k * 128:(k + 1) * 128], identity=C.ident[:, :]),
                     r=[T.mixb[0], T.mixb[1], C.constb], w=[Bb[p2]])
            P.op('act', lambda e: e.activation(out=T.mT[:, :, :], in_=tm, func=AF.Copy), r=[Bb[p2]], w=[T.mTb])
            for dh, bk in ((0, p0), (1, p1)):
                for k in range(8):
                    P.op('pe', lambda e, k=k, dh=dh, bk=bk: e.matmul(B[bk][:, :], lhsT=T.mT[:, k, :], rhs=Wo[:, k, dh * 512:(dh + 1) * 512],
                                                                    start=(k == 0), stop=(k == 7)), r=[T.mTb, Wob], w=[Bb[bk]])
                P.op('dve', lambda e, dh=dh, bk=bk: e.tensor_tensor(out=x[:, 0, dh * 512:(dh + 1) * 512], in0=x[:, 0, dh * 512:(dh + 1) * 512],
                                                                   in1=B[bk][:, :], op=ALU.add), r=[Bb[bk], xb], w=[xb])
            P.dma(xout[t], x[:, :, :], r=[xb], sig=f"xt{t % 4}")

        for t in range(min(4, ntile)):
            load_x(t)
        for t0 in range(0, ntile, 2):
            ts = list(range(t0, min(t0 + 2, ntile)))
            P.replay([P.capture(lambda t=t: pre(t)) for t in ts])
            chain(ts[0])
            if len(ts) > 1:
                P.replay([P.capture(lambda: post(ts[0])), P.capture(lambda: chain(ts[1]))])
                post(ts[1])
            else:
                post(ts[0])
            for t in range(t0 + 4, min(t0 + 6, ntile)):
                load_x(t)
        P.flush()


def setup_consts(P, C, es, ident_dram):
    nc = P.nc

    def sb(name, shape, dt):
        return es.enter_context(nc.sbuf_tensor(uname(name), shape, dt))
    C.identf = sb("identf", [128, 128], F32)
    C.ident = sb("ident", [128, 128], BF16)
    C.neghalf = sb("neghalf", [128, 8], F32)
    C.junk = sb("junk", [128, D], BF16)
    C.ss = sb("ss", [128, 8], F32)
    C.vv = sb("vv", [128, 8], F32)
    C.rstd = sb("rstd", [128, 8], F32)
    b0 = P.buf("c0")
    P.dma(C.identf[:, :], ident_dram, w=[b0], sig="identf")
    P.op('dve', lambda e: e.tensor_copy(out=C.ident[:, :], in_=C.identf[:, :]), r=[b0], w=[b0])
    P.op('pool', lambda e: e.memset(C.neghalf[:, :], -0.5), w=[b0])
    P.flush()


def new_phase_bufs(P, C):
    C.constb = P.buf("const")
    C.junkb = P.buf("junk")
    C.ssb = [P.buf(f"ss{s}") for s in range(8)]
    C.vvb = P.buf("vv")
    C.rstdb = P.buf("rstd")
    C.vv2b = [P.buf("vv2_0"), P.buf("vv2_1")]
    C.rstd2b = [P.buf("rstd2_0"), P.buf("rstd2_1")]


def build_nc(S, plan):
    nc = bass.Bass("TRN2", target_bir_lowering=False)

    def din(name, shape, dt=F32):
        return nc.dram_tensor(name, list(shape), dt, kind="ExternalInput").ap()
    x = din("x", [S, D])
    ident = din("ident", [128, 128])
    W = {
        'norm_gains': din("norm_gains", [DEPTH, 5, D]),
        'ffn_w_gate': din("ffn_w_gate", [DEPTH, 2, D, DFF]),
        'ffn_w_up': din("ffn_w_up", [DEPTH, 2, D, DFF]),
        'ffn_w_down': din("ffn_w_down", [DEPTH, 2, DFF, D]),
        'ple_w_gate': din("ple_w_gate", [DEPTH, D, D]),
        'ple_w_proj': din("ple_w_proj", [DEPTH, 256, D]),
        'p': din("p", [DEPTH, S, 256]),
        'positions': din("positions", [S], I32),
        'invf': din("invf", [2, 32]),
        'mla_w_in': din("mla_w_in", [2, D, 704]),
        'mla_q_a_norm': din("mla_q_a_norm", [2, 384]),
        'mla_kv_a_norm': din("mla_kv_a_norm", [2, 256]),
        'mla_w_q_b': din("mla_w_q_b", [2, 384, 1536]),
        'mla_w_kv_b': din("mla_w_kv_b", [2, 256, 2048]),
        'mla_q_norm': din("mla_q_norm", [2, 192]),
        'mla_k_norm': din("mla_k_norm", [2, 192]),
        'mla_w_out': din("mla_w_out", [2, D, D]),
        'even_w_in': din("even_w_in", [2, D, 3072]),
        'gmlp_v_norm': din("gmlp_v_norm", [2, 8, 64]),
        'gmlp_w_s': din("gmlp_w_s", [2, 8, 128, 128]),
        'gmlp_b_s': din("gmlp_b_s", [2, 8, 128]),
        'hgrn_lb_raw': din("hgrn_lb_raw", [2, 512]),
        'hgrn_out_norm': din("hgrn_out_norm", [2, 128]),
        'even_w_out': din("even_w_out", [2, D, D]),
    }
    kd = "ExternalOutput" if DEBUG else "Internal"
    scrQ = (nc.dram_tensor("QTn", [NH, 128, S], BF16, kind=kd).ap(),
            nc.dram_tensor("QTr", [NH, 64, S], BF16, kind=kd).ap(),
            nc.dram_tensor("KTn", [NH, 128, S], BF16, kind=kd).ap(),
            nc.dram_tensor("KTr", [NH, 64, S], BF16, kind=kd).ap(),
            nc.dram_tensor("Vs", [S, NH, 128], BF16, kind=kd).ap())
    y = nc.dram_tensor("y", [S, D], F32, kind="ExternalOutput").ap()
    scr = [nc.dram_tensor(f"xscr{i}", [S, D], F32, kind="Internal").ap() for i in range(3)]
    P = Prog(nc)
    C = Ctx()
    with ExitStack() as es:
        new_phase_bufs(P, C)
        setup_consts(P, C, es, ident)
        cur = x
        si = 0
        for pi, ph in enumerate(plan):
            if pi == len(plan) - 1:
                dst = y
            else:
                dst = scr[si % 3]
                si += 1
            new_phase_bufs(P, C)
            if ph[0] == 'even':
                _, l = ph
                phase_even(P, C, cur, dst, W, l, l // 2, S)
            if ph[0] == 'mla':
                _, l = ph
                phase_mla1(P, C, cur, W, l, l // 2, S, scrQ)
                mid = scr[si % 3]
                si += 1
                new_phase_bufs(P, C)
                phase_mla2(P, C, cur, mid, W, l // 2, 0, S, scrQ)
                new_phase_bufs(P, C)
                phase_mla2(P, C, mid, dst, W, l // 2, 1, S, scrQ)
            if ph[0] == 'ffn':
                _, l, which = ph
                phase_ffn(P, C, cur, dst, W['ffn_w_gate'][l, which], W['ffn_w_up'][l, which], W['ffn_w_down'][l, which],
                          W['norm_gains'][l, 0 if which == 0 else 2, :], S)
            elif ph[0] == 'ple':
                _, l = ph
                phase_ple(P, C, cur, dst, W['p'][l], W['ple_w_gate'][l], W['ple_w_proj'][l], W['norm_gains'][l, 3, :],
                          W['norm_gains'][l, 4, :], S)
            cur = dst
    return nc


WEIGHT_KEYS = ['norm_gains', 'ffn_w_gate', 'ffn_w_up', 'ffn_w_down', 'ple_w_gate', 'ple_w_proj', 'even_w_in', 'gmlp_v_norm',
               'gmlp_w_s', 'gmlp_b_s', 'hgrn_lb_raw', 'hgrn_out_norm', 'even_w_out', 'mla_w_in', 'mla_q_a_norm', 'mla_kv_a_norm',
               'mla_w_q_b', 'mla_w_kv_b', 'mla_q_norm', 'mla_k_norm', 'mla_w_out']
N_CORES = 8
SEQ = 4096
FUSED = True


def layer_plan(l):
    return [('ffn', l, 0), ('even', l) if l % 2 == 0 else ('mla', l), ('ffn', l, 1), ('ple', l)]


def kernel(**inputs):
    x = np.ascontiguousarray(np.asarray(inputs['x'], dtype=np.float32))
    p = np.asarray(inputs['p'], dtype=np.float32)
    pos = np.asarray(inputs['positions']).astype(np.int32)
    Wt = {k: np.ascontiguousarray(np.asarray(inputs[k], dtype=np.float32)) for k in WEIGHT_KEYS}
    ident = np.eye(128, dtype=np.float32)
    invf = (10000.0 ** (-np.arange(0, 64, 2, dtype=np.float32) / 64)).astype(np.float32)
    invf2 = np.ascontiguousarray(np.stack([invf, invf]))
    B = x.shape[0]
    assert B == N_CORES and x.shape[1] == SEQ
    cur = [np.ascontiguousarray(x[b]) for b in range(B)]
    pb = [np.ascontiguousarray(p[:, b]) for b in range(B)]
    posb = [np.ascontiguousarray(pos[b]) for b in range(B)]
    plans = [sum([layer_plan(l) for l in range(DEPTH)], [])] if FUSED else [layer_plan(l) for l in range(DEPTH)]
    for plan in plans:
        nc = build_nc(SEQ, plan)
        in_maps = []
        for b in range(B):
            m = {"x": cur[b], "ident": ident, "invf": invf2, "p": pb[b], "positions": posb[b]}
            m.update(Wt)
            in_maps.append(m)
        res = run_bass_kernel_spmd(nc, in_maps, core_ids=list(range(N_CORES)))
        cur = [np.ascontiguousarray(np.asarray(res.results[b]["y"], dtype=np.float32)) for b in range(B)]
    return np.stack(cur, axis=0)
```

```python
import numpy as np
from contextlib import ExitStack
import concourse.bass as bass
import concourse.mybir as mybir
from concourse.bass_utils import run_bass_kernel_spmd

F32 = mybir.dt.float32
BF16 = mybir.dt.bfloat16
I32 = mybir.dt.int32
AF = mybir.ActivationFunctionType
ALU = mybir.AluOpType
AX = mybir.AxisListType

D = 1024
DFF = 2816
NFC = DFF // 128
DEPTH = 4
EPS = 1e-6
TT = 256
DEBUG = False
EV_DBG = ''
ENG_ATTR = {'sp': 'sync', 'act': 'scalar', 'pe': 'tensor', 'dve': 'vector', 'pool': 'gpsimd'}


class Buf:
    __slots__ = ('name', 'w', 'r', 'pw')

    def __init__(self, name):
        self.name = name
        self.w = None
        self.r = []
        self.pw = []


class Op:
    __slots__ = ('eng', 'fn', 'deps', 'needed', 'sem', 'val', 'is_dma')


class Prog:
    def __init__(self, nc):
        self.nc = nc
        self.ops = {e: [] for e in ENG_ATTR}
        self.eng_sem = {e: nc.alloc_semaphore(name=f"es_{e}") for e in ENG_ATTR}
        self.eng_cnt = {e: 0 for e in ENG_ATTR}
        self.dma_sem = {}
        self.order = []
        self.bufs = []
        self.dmas = []

    def buf(self, name):
        b = Buf(name)
        self.bufs.append(b)
        return b

    def _deps(self, eng, is_dma, r, w, par=False):
        deps = []

        def add(o, war=False):
            if o is None:
                return
            if not is_dma and not o.is_dma and o.eng == eng:
                if eng in ('pe', 'act'):
                    return
            if o not in deps:
                deps.append(o)
        for b in r:
            add(b.w)
            for o in b.pw:
                add(o)
        for b in w:
            if not par:
                add(b.w)
                for o in b.pw:
                    add(o)
            for o in b.r:
                add(o, war=True)
        return deps

    def _commit(self, op, r, w, par=False):
        for o in op.deps:
            o.needed = True
        for b in w:
            if par:
                b.pw.append(op)
                continue
            b.w = op
            b.pw = []
            b.r = []
        for b in r:
            if b.w is not op:
                b.r.append(op)
        self.ops[op.eng].append(op)
        self.order.append(op)

    def op(self, eng, fn, r=(), w=()):
        o = Op()
        o.eng = eng
        o.fn = fn
        o.is_dma = False
        o.needed = False
        o.sem = None
        o.val = None
        o.deps = self._deps(eng, False, r, w)
        self._commit(o, r, w)
        return o

    def dma(self, out, in_, r=(), w=(), sig=None, eng='sp', slow=False, par=False):
        if sig not in self.dma_sem:
            self.dma_sem[sig] = [self.nc.alloc_semaphore(name=f"ds_{sig}"), 0]
        o = Op()
        o.eng = eng
        if slow:
            o.fn = lambda e, out=out, in_=in_: e.dma_start(out=out, in_=in_, allow_slow_non_contiguous=True)
        else:
            o.fn = lambda e, out=out, in_=in_: e.dma_start(out=out, in_=in_)
        o.is_dma = True
        o.needed = True
        ent = self.dma_sem[sig]
        ent[1] += 16
        o.sem = ent[0]
        o.val = ent[1]
        o.deps = self._deps(eng, True, r, w, par)
        self._commit(o, r, w, par)
        self.dmas.append(o)
        return o

    def capture(self, fn):
        lst = []
        self.op = lambda *a, **k: lst.append((0, a, k))
        self.dma = lambda *a, **k: lst.append((1, a, k))
        try:
            fn()
        finally:
            del self.op
            del self.dma
        return lst

    def replay(self, lists):
        n = max(len(l) for l in lists)
        for i in range(n):
            for l in lists:
                if i < len(l):
                    kind, a, k = l[i]
                    (self.dma if kind else self.op)(*a, **k)

    def flush(self):
        nc = self.nc
        fin = Op()
        fin.eng = 'sp'
        fin.fn = None
        fin.is_dma = False
        fin.needed = False
        fin.sem = None
        fin.val = None
        fin.deps = list(self.dmas)
        self.ops['sp'].append(fin)
        self.order.append(fin)
        for o in self.order:
            if not o.is_dma and o.needed:
                self.eng_cnt[o.eng] += 1
                o.sem = self.eng_sem[o.eng]
                o.val = self.eng_cnt[o.eng]
        with nc.Block() as block:
            for en, attr in ENG_ATTR.items():
                ops = self.ops[en]

                def body(e, ops=ops):
                    seen = {}
                    for o in ops:
                        for d in o.deps:
                            k = id(d.sem)
                            if seen.get(k, 0) >= d.val:
                                continue
                            seen[k] = d.val
                            e.wait_ge(d.sem, d.val)
                        if o.fn is None:
                            continue
                        ins = o.fn(e)
                        if o.is_dma:
                            ins.then_inc(o.sem, 16)
                        elif o.needed:
                            ins.then_inc(o.sem, 1)
                getattr(block, attr)(body)
        self.ops = {e: [] for e in ENG_ATTR}
        self.order = []
        self.dmas = []
        for b in self.bufs:
            b.w = None
            b.r = []
            b.pw = []
        self.bufs = []


class Ctx:
    pass


_UID = [0]


def uname(name):
    _UID[0] += 1
    return f"{name}_{_UID[0]}"


def load_cast(P, C, dst, src, dstbuf, n):
    i = C.stage_i
    C.stage_i += 1
    slot = i % len(C.stage)
    st, sb = C.stage[slot], C.stage_b[slot]
    P.dma(st[:, 0:n], src, w=[sb], sig=f"stage{slot}")
    eng = 'dve' if (i % 2 == 0) else 'act'
    if eng == 'dve':
        P.op('dve', lambda e: e.tensor_copy(out=dst, in_=st[:, 0:n]), r=[sb], w=[dstbuf])
    else:
        P.op('act', lambda e: e.activation(out=dst, in_=st[:, 0:n], func=AF.Copy), r=[sb], w=[dstbuf])


def emit_norm_T(P, C, xt, xb, gbc, hT, hTb, tp, tpb, nsub=2, N=None):
    if N is None:
        N = C
    ss, ssb = N.ss, N.ssb
    for s in range(nsub):
        P.op('act', lambda e, s=s: e.activation(out=C.junk[:, :], in_=xt[:, s, :], func=AF.Square,
                                                 accum_out=ss[:, s:s + 1]), r=[xb], w=[ssb[s], C.junkb])
    P.op('dve', lambda e: e.tensor_scalar(out=N.vv[:, 0:nsub], in0=ss[:, 0:nsub], scalar1=1.0 / D, scalar2=EPS,
                                          op0=ALU.mult, op1=ALU.add), r=[ssb[s] for s in range(nsub)], w=[N.vvb])
    P.op('pool', lambda e: e.tensor_tensor(out=N.rstd[:, 0:nsub], in0=N.vv[:, 0:nsub], in1=C.neghalf[:, 0:nsub],
                                           op=ALU.pow), r=[N.vvb, C.constb], w=[N.rstdb])
    for s in range(nsub):
        P.op('dve', lambda e, s=s: e.scalar_tensor_tensor(out=N.xs[:, s, :], in0=xt[:, s, :], scalar=N.rstd[:, s:s + 1],
                                                          in1=gbc, op0=ALU.mult, op1=ALU.mult),
             r=[xb, N.rstdb, C.gbcb], w=[N.xsb[s]])
    for s in range(nsub):
        for k in range(8):
            P.op('pe', lambda e, s=s, k=k: e.transpose(out=tp[:, k, s * 128:(s + 1) * 128],
                                                       in_=N.xs[:, s, k * 128:(k + 1) * 128], identity=C.ident[:, :]),
                 r=[N.xsb[s], C.constb], w=[tpb])
    W = nsub * 128
    for hh in range(2):
        P.op('act', lambda e, hh=hh: e.activation(out=hT[:, hh * 4:(hh + 1) * 4, 0:W], in_=tp[:, hh * 4:(hh + 1) * 4, 0:W],
                                                  func=AF.Copy), r=[tpb], w=[hTb])


def norm_scratch(P, sbf, i, nsub):
    N = Ctx()
    N.ss = sbf(f"nss{i}", [128, 8], F32)
    N.vv = sbf(f"nvv{i}", [128, 8], F32)
    N.rstd = sbf(f"nrstd{i}", [128, 8], F32)
    N.xs = sbf(f"nxs{i}", [128, nsub, D], BF16)
    N.ssb = [P.buf(f"nss{i}_{s}") for s in range(8)]
    N.vvb = P.buf(f"nvv{i}")
    N.rstdb = P.buf(f"nrstd{i}")
    N.xsb = [P.buf(f"nxs{i}_{s}") for s in range(nsub)]
    return N

def phase_ffn(P, C, x_in, x_out, wg, wu, wd, gain, S):
    nc = P.nc
    ntile = S // TT
    with ExitStack() as es:
        def sb(name, shape, dt):
            return es.enter_context(nc.sbuf_tensor(uname(name), shape, dt))

        def ps(name, shape, dt=F32):
            return es.enter_context(nc.psum_tensor(uname(name), shape, dt))
        Wg = sb("Wg", [128, 8, DFF], BF16)
        Wu = sb("Wu", [128, 8, DFF], BF16)
        Wd = sb("Wd", [128, NFC, D], BF16)
        NG4 = (NFC + 3) // 4
        Wgb = [P.buf(f"Wg{g}") for g in range(NG4)]
        Wub = [P.buf(f"Wu{g}") for g in range(NG4)]
        Wdb = [P.buf(f"Wd{g}") for g in range(NG4)]
        gbc = sb("gbc", [128, D], F32)
        C.gbcb = P.buf("gbc")
        xt = [sb(f"xt{i}", [128, 2, D], F32) for i in range(2)]
        xtb = [P.buf(f"xt{i}") for i in range(2)]
        C.xs = sb("xs", [128, 2, D], BF16)
        C.xsb = [P.buf(f"xs{s}") for s in range(2)]
        hT = [sb(f"hT{i}", [128, 8, TT], BF16) for i in range(2)]
        hTb = [P.buf(f"hT{i}") for i in range(2)]
        sg = [sb(f"sg{i}", [128, TT], F32) for i in range(2)]
        sgb = [P.buf(f"sg{i}") for i in range(2)]
        aT = [sb(f"aT{i}", [128, TT], BF16) for i in range(2)]
        aTb = [P.buf(f"aT{i}") for i in range(2)]
        tp = ps("tp", [128, 8, TT], BF16)
        tpb = P.buf("tp")
        gu = [ps(f"gu{i}", [128, 2, TT]) for i in range(2)]
        gub = [P.buf(f"gu{i}") for i in range(2)]
        yp = [[ps(f"y{s}{h}", [128, 512]) for h in range(2)] for s in range(2)]
        ypb = [[P.buf(f"y{s}{h}") for h in range(2)] for s in range(2)]

        P.dma(gbc[:, :], gain.partition_broadcast(128), w=[C.gbcb], sig="gbc")
        wgr = wg.rearrange("(kc p) f -> p kc f", p=128)
        wur = wu.rearrange("(kc p) f -> p kc f", p=128)
        wdr = wd.rearrange("(fc p) d -> p fc d", p=128)

        for g in range(NG4):
            f0 = 4 * g
            nf = min(4, NFC - f0)
            c0, c1 = f0 * 128, (f0 + nf) * 128
            P.dma(Wg[:, :, c0:c1], wgr[:, :, c0:c1], w=[Wgb[g]], sig=f"W{3 * g}", eng='pool')
            P.dma(Wu[:, :, c0:c1], wur[:, :, c0:c1], w=[Wub[g]], sig=f"W{3 * g + 1}", eng='pool')
            P.dma(Wd[:, f0:f0 + nf, :], wdr[:, f0:f0 + nf, :], w=[Wdb[g]], sig=f"W{3 * g + 2}", eng='pool')

        xin = x_in.rearrange("(t s p) d -> t p s d", s=2, p=128)
        xout = x_out.rearrange("(t s p) d -> t p s d", s=2, p=128)

        def load_x(t):
            P.dma(xt[t % 2][:, :, :], xin[t], w=[xtb[t % 2]], sig=f"xt{t % 2}")

        def norm(t):
            emit_norm_T(P, C, xt[t % 2], xtb[t % 2], gbc[:, :], hT[t % 2], hTb[t % 2], tp, tpb)

        def gu_mm(t, j):
            sl = j % 2
            h = hT[t % 2]
            for which, Wm, Wb in ((0, Wg, Wgb), (1, Wu, Wub)):
                for k in range(8):
                    P.op('pe', lambda e, k=k, which=which, Wm=Wm, sl=sl, h=h: e.matmul(
                        gu[sl][:, which, :], lhsT=Wm[:, k, j * 128:(j + 1) * 128], rhs=h[:, k, :],
                        start=(k == 0), stop=(k == 7)), r=[Wb[j // 4], hTb[t % 2]], w=[gub[sl]])

        def act_mul(t, j):
            sl = j % 2
            P.op('act', lambda e: e.activation(out=sg[sl][:, :], in_=gu[sl][:, 0, :], func=AF.Silu), r=[gub[sl]], w=[sgb[sl]])
            P.op('dve', lambda e: e.tensor_tensor(out=aT[sl][:, :], in0=sg[sl][:, :], in1=gu[sl][:, 1, :], op=ALU.mult),
                 r=[sgb[sl], gub[sl]], w=[aTb[sl]])

        def down_mm(t, j):
            sl = j % 2
            for s in range(2):
                for h in range(2):
                    P.op('pe', lambda e, s=s, h=h: e.matmul(
                        yp[s][h][:, :], lhsT=aT[sl][:, s * 128:(s + 1) * 128], rhs=Wd[:, j, h * 512:(h + 1) * 512],
                        start=(j == 0), stop=(j == NFC - 1)), r=[aTb[sl], Wdb[j // 4]], w=[ypb[s][h]])

        def resid(t):
            x = xt[t % 2]
            for s in range(2):
                for h in range(2):
                    P.op('dve', lambda e, s=s, h=h: e.scalar_tensor_tensor(
                        out=x[:, s, h * 512:(h + 1) * 512], in0=yp[s][h][:, :], scalar=0.5,
                        in1=x[:, s, h * 512:(h + 1) * 512], op0=ALU.mult, op1=ALU.add), r=[ypb[s][h], xtb[t % 2]], w=[xtb[t % 2]])
            P.dma(xout[t], x[:, :, :], r=[xtb[t % 2]], sig=f"xt{t % 2}")

        load_x(0)
        if ntile > 1:
            load_x(1)
        norm(0)
        for t in range(ntile):
            gu_mm(t, 0)
            for j in range(NFC):
                if j + 1 < NFC:
                    gu_mm(t, j + 1)
                if j == 6 and t + 1 < ntile:
                    norm(t + 1)
                act_mul(t, j)
                down_mm(t, j)
            resid(t)
            if t + 2 < ntile:
                load_x(t + 2)
        P.flush()


def phase_ple(P, C, x_in, x_out, p_in, wgate, wproj, gain_in, gain_out, S):
    nc = P.nc
    ntile = S // 128
    with ExitStack() as es:
        def sb(name, shape, dt):
            return es.enter_context(nc.sbuf_tensor(uname(name), shape, dt))

        def ps(name, shape, dt=F32):
            return es.enter_context(nc.psum_tensor(uname(name), shape, dt))
        Wg = sb("pWg", [128, 8, D], BF16)
        Wp = sb("pWp", [128, 2, D], BF16)
        Wgb = [P.buf(f"pWg{k}") for k in range(8)]
        Wpb = P.buf("pWp")
        gbc = sb("gbc", [128, D], F32)
        C.gbcb = P.buf("gbc")
        g4 = sb("g4", [128, D], F32)
        g4b = P.buf("g4")
        B = [ps(f"B{i}", [128, 512]) for i in range(8)]
        Bb = [P.buf(f"B{i}") for i in range(8)]

        def alloc_tile(i):
            T = Ctx()
            for nm, shape, dt in (("pb", [128, 256], BF16),
                                  ("pT", [128, 2, 128], BF16), ("hT", [128, 8, 128], BF16), ("sg0", [128, 512], F32),
                                  ("sg1", [128, 512], F32), ("ee", [128, D], F32), ("st", [128, 8], F32)):
                setattr(T, nm, sb(f"{nm}_{i}", shape, dt))
                setattr(T, nm + "b", P.buf(f"{nm}_{i}"))
            T.i = i
            T.N = norm_scratch(P, sb, i, 1)
            T.bk = (2 * i, 2 * i + 1)
            return T
        NW = 4
        TB = [alloc_tile(i) for i in range(NW)]
        xts = [sb(f"xt4_{i}", [128, 1, D], F32) for i in range(2 * NW)]
        xtsb = [P.buf(f"xt4_{i}") for i in range(2 * NW)]
        pts = [sb(f"pt4_{i}", [128, 256], F32) for i in range(2 * NW)]
        ptsb = [P.buf(f"pt4_{i}") for i in range(2 * NW)]

        P.dma(gbc[:, :], gain_in.partition_broadcast(128), w=[C.gbcb], sig="gbc")
        P.dma(g4[:, :], gain_out.partition_broadcast(128), w=[g4b], sig="g4")
        wgr = wgate.rearrange("(kc p) f -> p kc f", p=128)
        wpr = wproj.rearrange("(kc p) f -> p kc f", p=128)
        P.dma(Wg[:, :, :], wgr[:, :, :], w=Wgb, sig="W0", eng='pool')
        P.dma(Wp[:, :, :], wpr[:, :, :], w=[Wpb], sig="W1", eng='pool')

        xin = x_in.rearrange("(t s p) d -> t p s d", s=1, p=128)
        xout = x_out.rearrange("(t s p) d -> t p s d", s=1, p=128)
        pin = p_in.rearrange("(t p) d -> t p d", p=128)

        def load_x(t):
            P.dma(xts[t % (2 * NW)][:, :, :], xin[t], w=[xtsb[t % (2 * NW)]], sig=f"xt{t % (2 * NW)}")
            P.dma(pts[t % (2 * NW)][:, :], pin[t], w=[ptsb[t % (2 * NW)]], sig=f"pt{t % (2 * NW)}")

        def body(t):
            T = TB[t % NW]
            xt_, xtb_, pt_, ptb_ = xts[t % (2 * NW)], xtsb[t % (2 * NW)], pts[t % (2 * NW)], ptsb[t % (2 * NW)]
            a, b = T.bk
            tp = B[a].bitcast(BF16).rearrange("p (k t) -> p k t", k=8)
            tpp = B[b].bitcast(BF16)[:, 0:256].rearrange("p (k t) -> p k t", k=2)
            emit_norm_T(P, C, xt_, xtb_, gbc[:, :], T.hT, T.hTb, tp, Bb[a], nsub=1, N=T.N)
            P.op('act', lambda e: e.activation(out=T.pb[:, :], in_=pt_[:, :], func=AF.Copy), r=[ptb_], w=[T.pbb])
            for kc in range(2):
                P.op('pe', lambda e, kc=kc: e.transpose(out=tpp[:, kc, :], in_=T.pb[:, kc * 128:(kc + 1) * 128], identity=C.ident[:, :]),
                     r=[T.pbb, C.constb], w=[Bb[b]])
            P.op('dve', lambda e: e.tensor_copy(out=T.pT[:, :, :], in_=tpp), r=[Bb[b]], w=[T.pTb])
            sg = (T.sg0, T.sg1)
            sgb = (T.sg0b, T.sg1b)
            for hh in range(2):
                for k in range(8):
                    P.op('pe', lambda e, hh=hh, k=k: e.matmul(B[a][:, :], lhsT=T.hT[:, k, :], rhs=Wg[:, k, hh * 512:(hh + 1) * 512],
                                                              start=(k == 0), stop=(k == 7)), r=[T.hTb, Wgb[k]], w=[Bb[a]])
                for kc in range(2):
                    P.op('pe', lambda e, hh=hh, kc=kc: e.matmul(B[b][:, :], lhsT=T.pT[:, kc, :], rhs=Wp[:, kc, hh * 512:(hh + 1) * 512],
                                                                start=(kc == 0), stop=(kc == 1)), r=[T.pTb, Wpb], w=[Bb[b]])
                P.op('act', lambda e, hh=hh: e.activation(out=sg[hh][:, :], in_=B[a][:, :], func=AF.Sigmoid), r=[Bb[a]], w=[sgb[hh]])
                P.op('dve', lambda e, hh=hh: e.tensor_tensor(out=T.ee[:, hh * 512:(hh + 1) * 512], in0=sg[hh][:, :], in1=B[b][:, :],
                                                             op=ALU.mult), r=[sgb[hh], Bb[b]], w=[T.eeb])
            P.op('act', lambda e: e.activation(out=C.junk[:, :], in_=T.ee[:, :], func=AF.Square, accum_out=T.st[:, 0:1]),
                 r=[T.eeb], w=[T.stb, C.junkb])
            P.op('dve', lambda e: e.tensor_scalar(out=T.st[:, 1:2], in0=T.st[:, 0:1], scalar1=1.0 / D, scalar2=EPS, op0=ALU.mult,
                                                  op1=ALU.add), r=[T.stb], w=[T.stb])
            P.op('pool', lambda e: e.tensor_tensor(out=T.st[:, 2:3], in0=T.st[:, 1:2], in1=C.neghalf[:, 0:1], op=ALU.pow),
                 r=[T.stb, C.constb], w=[T.stb])
            P.op('dve', lambda e: e.scalar_tensor_tensor(out=T.ee[:, :], in0=T.ee[:, :], scalar=T.st[:, 2:3], in1=g4[:, :],
                                                         op0=ALU.mult, op1=ALU.mult), r=[T.eeb, T.stb, g4b], w=[T.eeb])
            P.op('dve', lambda e: e.tensor_tensor(out=xt_[:, 0, :], in0=xt_[:, 0, :], in1=T.ee[:, :], op=ALU.add),
                 r=[T.eeb, xtb_], w=[xtb_])
            P.dma(xout[t], xt_[:, :, :], r=[xtb_], sig=f"xt{t % (2 * NW)}")

        for t in range(min(2 * NW, ntile)):
            load_x(t)
        for t0 in range(0, ntile, NW):
            ts = list(range(t0, min(t0 + NW, ntile)))
            P.replay([P.capture(lambda t=t: body(t)) for t in ts])
            for t in range(t0 + 2 * NW, min(t0 + 3 * NW, ntile)):
                load_x(t)
        P.flush()

NH = 8
QK = 192
ATTN_SCALE = 192 ** -0.5
PI = 3.14159265358979


def phase_mla1(P, C, x_in, Wd_, l, j, S, scrQ):
    nc = P.nc
    W = Wd_
    ntile = S // 128
    QTn, QTr, KTn, KTr, Vs = scrQ
    with ExitStack() as es:
        def sb(name, shape, dt):
            return es.enter_context(nc.sbuf_tensor(uname(name), shape, dt))

        def ps(name, shape, dt=F32):
            return es.enter_context(nc.psum_tensor(uname(name), shape, dt))
        Win = sb("Win", [128, 8, 704], BF16)
        Wq = sb("Wq", [128, 3, 1536], BF16)
        Wkv = sb("Wkv", [128, 2, 2048], BF16)
        Winb, Wqb, Wkvb = P.buf("Win"), P.buf("Wq"), P.buf("Wkv")
        gbc = sb("gbc", [128, D], F32)
        C.gbcb = P.buf("gbc")
        qa = sb("qa", [128, 384], F32)
        kva = sb("kva", [128, 256], F32)
        qn_g = sb("qn_g", [128, 192], F32)
        kn_g = sb("kn_g", [128, 192], F32)
        invf = sb("invf", [128, 2, 32], F32)
        gb = P.buf("gains")
        xt = [sb(f"xt{i}", [128, 1, D], F32) for i in range(2)]
        xtb = [P.buf(f"xt{i}") for i in range(2)]
        posi = [sb(f"posi{i}", [128, 1], I32) for i in range(2)]
        posb = [P.buf(f"posi{i}") for i in range(2)]
        C.xs = sb("xs", [128, 1, D], BF16)
        C.xsb = [P.buf("xs0")]
        hT = [sb(f"hT{i}", [128, 8, 128], BF16) for i in range(2)]
        hTb = [P.buf(f"hT{i}") for i in range(2)]
        def alloc_tile(i):
            T = Ctx()
            for nm, shape, dt in (("zt", [128, 704], F32), ("cn", [128, 640], BF16), ("cT", [128, 5, 128], BF16),
                                  ("qf", [128, 8, 192], F32), ("qs", [128, 8, 192], F32), ("qb16", [128, 8, 192], BF16),
                                  ("kf", [128, 8, 128], F32), ("kn16", [128, 8, 192], BF16), ("vb", [128, 8, 128], BF16),
                                  ("sm", [128, 64], F32), ("cs", [128, 2, 32], F32), ("rt", [128, 4, 8, 32], F32)):
                setattr(T, nm, sb(f"{nm}_{i}", shape, dt))
                setattr(T, nm + "b", P.buf(f"{nm}_{i}"))
            T.ang = sb(f"ang_{i}", [128, 2, 32], F32)
            T.angi = sb(f"angi_{i}", [128, 2, 32], I32)
            T.angk = sb(f"angk_{i}", [128, 2, 32], F32)
            T.msk = sb(f"msk_{i}", [128, 2, 32], F32)
            T.angb = P.buf(f"ang_{i}")
            T.kg = sb(f"kg_{i}", [128, 64], F32)
            T.kr = sb(f"kr_{i}", [128, 64], F32)
            T.kgb = P.buf(f"kg_{i}")
            T.oTn = [sb(f"oTn{i}_{q}", [128, 8, 128], BF16) for q in range(2)]
            T.oTnb = [P.buf(f"oTn{i}_{q}") for q in range(2)]
            T.oTr = [sb(f"oTr{i}_{q}", [64, 8, 128], BF16) for q in range(2)]
            T.oTrb = [P.buf(f"oTr{i}_{q}") for q in range(2)]
            T.i = i
            return T
        TB = [alloc_tile(0), alloc_tile(1)]
        NS = [norm_scratch(P, sb, i, 1) for i in range(2)]
        B = [ps(f"B{i}", [128, 512]) for i in range(8)]
        Bb = [P.buf(f"B{i}") for i in range(8)]
        def bankset(par):
            ia, ib, ic, id_ = 4 * par, 4 * par + 1, 4 * par + 2, 4 * par + 3
            R = Ctx()
            R.tp = B[id_].bitcast(BF16).rearrange("p (k t) -> p k t", k=8)
            R.tpi = id_
            R.z = (ia, ib)
            R.tpc = B[ib].bitcast(BF16)[:, 384:1024].rearrange("p (k t) -> p k t", k=5)
            R.tpci = ib
            R.q = (ia, ic, id_)
            R.kv = (ib, ia, ic, id_)
            R.tq_n = B[ia].bitcast(BF16).rearrange("p (k t) -> p k t", k=8)
            R.tq_r = B[ib].bitcast(BF16).rearrange("p (k t) -> p k t", k=8)
            R.tqi = (ia, ib)
            return R
        BS = [bankset(0), bankset(1)]

        ng = W['norm_gains']
        P.dma(gbc[:, :], ng[l, 1, :].partition_broadcast(128), w=[gb], sig="gbc", par=True)
        C.gbcb = gb
        P.dma(qa[:, :], W['mla_q_a_norm'][j, :].partition_broadcast(128), w=[gb], sig="qa", par=True)
        P.dma(kva[:, :], W['mla_kv_a_norm'][j, :].partition_broadcast(128), w=[gb], sig="kva", par=True)
        P.dma(qn_g[:, :], W['mla_q_norm'][j, :].partition_broadcast(128), w=[gb], sig="qn_g", par=True)
        P.dma(kn_g[:, :], W['mla_k_norm'][j, :].partition_broadcast(128), w=[gb], sig="kn_g", par=True)
        P.dma(invf[:, :, :], W['invf'].partition_broadcast(128), w=[gb], sig="invf", par=True)
        win_r = W['mla_w_in'][j].rearrange("(kc p) f -> p kc f", p=128)
        P.dma(Win[:, :, :], win_r[:, :, :], w=[Winb], sig="W0", eng='pool')
        wq_r = W['mla_w_q_b'][j].rearrange("(kc p) f -> p kc f", p=128)
        P.dma(Wq[:, :, :], wq_r[:, :, :], w=[Wqb], sig="W1", eng='pool')
        wkv_r = W['mla_w_kv_b'][j].rearrange("(kc p) f -> p kc f", p=128)
        P.dma(Wkv[:, :, 0:1024], wkv_r[:, :, 0:1024], w=[Wkvb], sig="W2", eng='pool')
        P.dma(Wkv[:, :, 1024:2048], wkv_r[:, :, 1024:2048], w=[Wkvb], sig="W3", eng='pool')

        xin = x_in.rearrange("(t s p) d -> t p s d", s=1, p=128)
        pin = W['positions'].rearrange("(t p o) -> t p o", p=128, o=1)

        def load_x(t):
            P.dma(xt[t % 2][:, :, :], xin[t], w=[xtb[t % 2]], sig=f"xt{t % 2}")
            P.dma(posi[t % 2][:, :], pin[t], w=[posb[t % 2]], sig=f"posi{t % 2}")

        def norm(t):
            emit_norm_T(P, C, xt[t % 2], xtb[t % 2], gbc[:, :], hT[t % 2], hTb[t % 2], BS[t % 2].tp, Bb[BS[t % 2].tpi], nsub=1, N=NS[t % 2])

        def rope(T, src3, dst3, nh, tag):
            x1 = src3[:, :, 0:32]
            x2 = src3[:, :, 32:64]
            sinb = T.cs[:, 0:1, :].broadcast_to([128, nh, 32])
            cosb = T.cs[:, 1:2, :].broadcast_to([128, nh, 32])
            r = T.rt
            P.op('dve', lambda e: e.tensor_tensor(out=r[:, 0, 0:nh, :], in0=x1, in1=cosb, op=ALU.mult), r=[tag, T.csb], w=[T.rtb])
            P.op('dve', lambda e: e.tensor_tensor(out=r[:, 1, 0:nh, :], in0=x2, in1=sinb, op=ALU.mult), r=[tag, T.csb], w=[T.rtb])
            P.op('dve', lambda e: e.tensor_tensor(out=r[:, 2, 0:nh, :], in0=x2, in1=cosb, op=ALU.mult), r=[tag, T.csb], w=[T.rtb])
            P.op('dve', lambda e: e.tensor_tensor(out=r[:, 3, 0:nh, :], in0=x1, in1=sinb, op=ALU.mult), r=[tag, T.csb], w=[T.rtb])
            return r

        def body(t):
            T = TB[t % 2]
            R = BS[t % 2]
            h = hT[t % 2]
            norm(t)
            yield
            yield
            pf = T.sm[:, 40:41]
            P.op('dve', lambda e: e.tensor_copy(out=pf, in_=posi[t % 2][:, :]), r=[posb[t % 2]], w=[T.angb])
            P.op('dve', lambda e: e.tensor_scalar(out=T.ang[:, :, :], in0=invf[:, :, :], scalar1=pf, scalar2=None, op0=ALU.mult),
                 r=[T.angb, gb], w=[T.angb])
            P.op('dve', lambda e: e.tensor_scalar(out=T.ang[:, 1, :], in0=T.ang[:, 1, :], scalar1=PI / 2, scalar2=None, op0=ALU.add),
                 r=[T.angb], w=[T.angb])
            P.op('dve', lambda e: e.tensor_scalar(out=T.angk[:, :, :], in0=T.ang[:, :, :], scalar1=1.0 / (2 * PI), scalar2=None,
                                                  op0=ALU.mult), r=[T.angb], w=[T.angb])
            P.op('dve', lambda e: e.tensor_copy(out=T.angi[:, :, :], in_=T.angk[:, :, :]), r=[T.angb], w=[T.angb])
            P.op('dve', lambda e: e.tensor_copy(out=T.angk[:, :, :], in_=T.angi[:, :, :]), r=[T.angb], w=[T.angb])
            P.op('dve', lambda e: e.scalar_tensor_tensor(out=T.ang[:, :, :], in0=T.angk[:, :, :], scalar=-2 * PI, in1=T.ang[:, :, :],
                                                         op0=ALU.mult, op1=ALU.add), r=[T.angb], w=[T.angb])
            P.op('dve', lambda e: e.tensor_scalar(out=T.msk[:, :, :], in0=T.ang[:, :, :], scalar1=PI, scalar2=-2 * PI, op0=ALU.is_gt,
                                                  op1=ALU.mult), r=[T.angb], w=[T.angb])
            P.op('dve', lambda e: e.tensor_tensor(out=T.ang[:, :, :], in0=T.ang[:, :, :], in1=T.msk[:, :, :], op=ALU.add), r=[T.angb], w=[T.angb])
            P.op('dve', lambda e: e.tensor_scalar(out=T.msk[:, :, :], in0=T.ang[:, :, :], scalar1=-PI, scalar2=2 * PI, op0=ALU.is_lt,
                                                  op1=ALU.mult), r=[T.angb], w=[T.angb])
            P.op('dve', lambda e: e.tensor_tensor(out=T.ang[:, :, :], in0=T.ang[:, :, :], in1=T.msk[:, :, :], op=ALU.add), r=[T.angb], w=[T.angb])
            P.op('dve', lambda e: e.tensor_scalar(out=T.ang[:, :, :], in0=T.ang[:, :, :], scalar1=PI, scalar2=-PI, op0=ALU.min,
                                                  op1=ALU.max), r=[T.angb], w=[T.angb])
            P.op('act', lambda e: e.activation(out=T.cs[:, :, :], in_=T.ang[:, :, :], func=AF.Sin), r=[T.angb], w=[T.csb])
            yield
            for c0, c1, bk in ((0, 512, R.z[0]), (512, 704, R.z[1])):
                for k in range(8):
                    P.op('pe', lambda e, k=k, c0=c0, c1=c1, bk=bk: e.matmul(B[bk][:, 0:c1 - c0], lhsT=h[:, k, :], rhs=Win[:, k, c0:c1],
                                                                          start=(k == 0), stop=(k == 7)),
                         r=[hTb[t % 2], Winb], w=[Bb[bk]])
            P.op('act', lambda e: e.activation(out=T.zt[:, 0:512], in_=B[R.z[0]][:, :], func=AF.Copy), r=[Bb[R.z[0]]], w=[T.ztb])
            P.op('act', lambda e: e.activation(out=T.zt[:, 512:704], in_=B[R.z[1]][:, 0:192], func=AF.Copy), r=[Bb[R.z[1]]], w=[T.ztb])
            yield
            P.op('act', lambda e: e.activation(out=C.junk[:, 0:384], in_=T.zt[:, 0:384], func=AF.Square, accum_out=T.sm[:, 0:1]),
                 r=[T.ztb], w=[T.smb, C.junkb])
            P.op('act', lambda e: e.activation(out=C.junk[:, 0:256], in_=T.zt[:, 384:640], func=AF.Square, accum_out=T.sm[:, 1:2]),
                 r=[T.ztb], w=[T.smb, C.junkb])
            P.op('act', lambda e: e.activation(out=C.junk[:, 0:64], in_=T.zt[:, 640:704], func=AF.Square, accum_out=T.sm[:, 2:3]),
                 r=[T.ztb], w=[T.smb, C.junkb])
            P.op('dve', lambda e: e.tensor_scalar(out=T.sm[:, 4:5], in0=T.sm[:, 0:1], scalar1=1.0 / 384, scalar2=EPS, op0=ALU.mult,
                                                  op1=ALU.add), r=[T.smb], w=[T.smb])
            P.op('dve', lambda e: e.tensor_scalar(out=T.sm[:, 5:6], in0=T.sm[:, 1:2], scalar1=1.0 / 256, scalar2=EPS, op0=ALU.mult,
                                                  op1=ALU.add), r=[T.smb], w=[T.smb])
            P.op('pool', lambda e: e.tensor_tensor(out=T.sm[:, 6:8], in0=T.sm[:, 4:6], in1=C.neghalf[:, 0:2], op=ALU.pow),
                 r=[T.smb, C.constb], w=[T.smb])
            P.op('dve', lambda e: e.scalar_tensor_tensor(out=T.cn[:, 0:384], in0=T.zt[:, 0:384], scalar=T.sm[:, 6:7], in1=qa[:, :],
                                                         op0=ALU.mult, op1=ALU.mult), r=[T.ztb, T.smb, gb], w=[T.cnb])
            P.op('dve', lambda e: e.scalar_tensor_tensor(out=T.cn[:, 384:640], in0=T.zt[:, 384:640], scalar=T.sm[:, 7:8], in1=kva[:, :],
                                                         op0=ALU.mult, op1=ALU.mult), r=[T.ztb, T.smb, gb], w=[T.cnb])
            for k in range(5):
                P.op('pe', lambda e, k=k: e.transpose(out=R.tpc[:, k, :], in_=T.cn[:, k * 128:(k + 1) * 128], identity=C.ident[:, :]),
                     r=[T.cnb, C.constb], w=[Bb[R.tpci]])
            P.op('dve', lambda e: e.tensor_copy(out=T.cT[:, :, :], in_=R.tpc), r=[Bb[R.tpci]], w=[T.cTb])
            yield
            for c in range(3):
                for k in range(3):
                    P.op('pe', lambda e, c=c, k=k: e.matmul(B[R.q[c]][:, :], lhsT=T.cT[:, k, :], rhs=Wq[:, k, c * 512:(c + 1) * 512],
                                                            start=(k == 0), stop=(k == 2)), r=[T.cTb, Wqb], w=[Bb[R.q[c]]])
            qf2 = T.qf[:, :, :].rearrange("p h d -> p (h d)")
            for c in range(3):
                P.op('act', lambda e, c=c: e.activation(out=qf2[:, c * 512:(c + 1) * 512], in_=B[R.q[c]][:, :], func=AF.Copy),
                     r=[Bb[R.q[c]]], w=[T.qfb])
            yield
            kvb = R.kv
            for c in range(4):
                for k in range(2):
                    P.op('pe', lambda e, c=c, k=k: e.matmul(B[kvb[c]][:, :], lhsT=T.cT[:, 3 + k, :], rhs=Wkv[:, k, c * 512:(c + 1) * 512],
                                                            start=(k == 0), stop=(k == 1)), r=[T.cTb, Wkvb], w=[Bb[kvb[c]]])
            for c in range(4):
                kv3 = B[kvb[c]][:, :].rearrange("p (h d) -> p h d", h=2)
                P.op('act', lambda e, c=c, kv3=kv3: e.activation(out=T.kf[:, 2 * c:2 * c + 2, :], in_=kv3[:, :, 0:128], func=AF.Copy),
                     r=[Bb[kvb[c]]], w=[T.kfb])
                P.op('act', lambda e, c=c, kv3=kv3: e.activation(out=T.vb[:, 2 * c:2 * c + 2, :], in_=kv3[:, :, 128:256], func=AF.Copy),
                     r=[Bb[kvb[c]]], w=[T.vbb])
            P.dma(Vs[t * 128:(t + 1) * 128, :, :], T.vb[:, :, :], r=[T.vbb], sig=f"vb{T.i}")
            yield
            for hh in range(NH):
                P.op('act', lambda e, hh=hh: e.activation(out=C.junk[:, 0:192], in_=T.qf[:, hh, :], func=AF.Square,
                                                           accum_out=T.sm[:, 8 + hh:9 + hh]), r=[T.qfb], w=[T.smb, C.junkb])
            P.op('dve', lambda e: e.tensor_scalar(out=T.sm[:, 16:24], in0=T.sm[:, 8:16], scalar1=1.0 / QK, scalar2=EPS, op0=ALU.mult,
                                                  op1=ALU.add), r=[T.smb], w=[T.smb])
            P.op('pool', lambda e: e.tensor_tensor(out=T.sm[:, 24:32], in0=T.sm[:, 16:24], in1=C.neghalf[:, 0:8], op=ALU.pow),
                 r=[T.smb, C.constb], w=[T.smb])
            P.op('dve', lambda e: e.tensor_tensor(out=T.qs[:, :, :], in0=T.qf[:, :, :],
                                                  in1=T.sm[:, 24:32].unsqueeze(2).broadcast_to([128, NH, QK]), op=ALU.mult),
                 r=[T.qfb, T.smb], w=[T.qsb])
            P.op('dve', lambda e: e.tensor_tensor(out=T.qs[:, :, :], in0=T.qs[:, :, :], in1=qn_g[:, :].unsqueeze(1).broadcast_to([128, NH, QK]),
                                                  op=ALU.mult), r=[T.qsb, gb], w=[T.qsb])
            P.op('act', lambda e: e.activation(out=T.qb16[:, :, 0:128], in_=T.qs[:, :, 0:128], func=AF.Copy), r=[T.qsb], w=[T.qb16b])
            r = rope(T, T.qs[:, :, 128:192], None, NH, T.qsb)
            P.op('dve', lambda e: e.tensor_tensor(out=T.qb16[:, :, 128:160], in0=r[:, 0, :, :], in1=r[:, 1, :, :], op=ALU.subtract),
                 r=[T.rtb], w=[T.qb16b])
            P.op('dve', lambda e: e.tensor_tensor(out=T.qb16[:, :, 160:192], in0=r[:, 2, :, :], in1=r[:, 3, :, :], op=ALU.add),
                 r=[T.rtb], w=[T.qb16b])
            yield
            for hh in range(NH):
                P.op('act', lambda e, hh=hh: e.activation(out=C.junk[:, 0:128], in_=T.kf[:, hh, :], func=AF.Square,
                                                           accum_out=T.sm[:, 32 + hh:33 + hh]), r=[T.kfb], w=[T.smb, C.junkb])
            P.op('dve', lambda e: e.tensor_scalar(out=T.sm[:, 48:56], in0=T.sm[:, 32:40], scalar1=T.sm[:, 2:3], scalar2=1.0 / QK, op0=ALU.add,
                                                  op1=ALU.mult), r=[T.smb], w=[T.smb])
            P.op('dve', lambda e: e.tensor_scalar(out=T.sm[:, 48:56], in0=T.sm[:, 48:56], scalar1=EPS, scalar2=None, op0=ALU.add),
                 r=[T.smb], w=[T.smb])
            P.op('pool', lambda e: e.tensor_tensor(out=T.sm[:, 56:64], in0=T.sm[:, 48:56], in1=C.neghalf[:, 0:8], op=ALU.pow),
                 r=[T.smb, C.constb], w=[T.smb])
            P.op('dve', lambda e: e.tensor_tensor(out=T.kf[:, :, :], in0=T.kf[:, :, :],
                                                  in1=T.sm[:, 56:64].unsqueeze(2).broadcast_to([128, NH, 128]), op=ALU.mult),
                 r=[T.kfb, T.smb], w=[T.kfb])
            P.op('dve', lambda e: e.tensor_tensor(out=T.kn16[:, :, 0:128], in0=T.kf[:, :, :],
                                                  in1=kn_g[:, 0:128].unsqueeze(1).broadcast_to([128, NH, 128]), op=ALU.mult),
                 r=[T.kfb, gb], w=[T.kn16b])
            P.op('dve', lambda e: e.tensor_tensor(out=T.kg[:, :], in0=T.zt[:, 640:704], in1=kn_g[:, 128:192], op=ALU.mult),
                 r=[T.ztb, gb], w=[T.kgb])
            r = rope(T, T.kg[:, :].unsqueeze(1), None, 1, T.kgb)
            P.op('dve', lambda e: e.tensor_tensor(out=T.kr[:, 0:32], in0=r[:, 0, 0, :], in1=r[:, 1, 0, :], op=ALU.subtract),
                 r=[T.rtb], w=[T.kgb])
            P.op('dve', lambda e: e.tensor_tensor(out=T.kr[:, 32:64], in0=r[:, 2, 0, :], in1=r[:, 3, 0, :], op=ALU.add),
                 r=[T.rtb], w=[T.kgb])
            P.op('dve', lambda e: e.tensor_tensor(out=T.kn16[:, :, 128:192], in0=T.kr[:, :].unsqueeze(1).broadcast_to([128, NH, 64]),
                                                  in1=T.sm[:, 56:64].unsqueeze(2).broadcast_to([128, NH, 64]), op=ALU.mult),
                 r=[T.kgb, T.smb], w=[T.kn16b])
            yield
            for (src, srcb, dn, dr, sl) in ((T.qb16, T.qb16b, QTn, QTr, 0), (T.kn16, T.kn16b, KTn, KTr, 1)):
                for hh in range(NH):
                    P.op('pe', lambda e, hh=hh, src=src: e.transpose(out=R.tq_n[:, hh, :], in_=src[:, hh, 0:128], identity=C.ident[:, :]),
                         r=[srcb, C.constb], w=[Bb[R.tqi[0]]])
                for hh in range(NH):
                    P.op('pe', lambda e, hh=hh, src=src: e.transpose(out=R.tq_r[0:64, hh, :], in_=src[:, hh, 128:192],
                                                                     identity=C.ident[:, :]), r=[srcb, C.constb], w=[Bb[R.tqi[1]]])
                P.op('act', lambda e, sl=sl: e.activation(out=T.oTn[sl][:, :, :], in_=R.tq_n, func=AF.Copy), r=[Bb[R.tqi[0]]], w=[T.oTnb[sl]])
                P.op('dve', lambda e, sl=sl: e.tensor_copy(out=T.oTr[sl][:, :, :], in_=R.tq_r[0:64, :, :]), r=[Bb[R.tqi[1]]], w=[T.oTrb[sl]])
                P.dma(dn[:, :, t * 128:(t + 1) * 128].rearrange("h d t -> d h t"), T.oTn[sl][:, :, :], r=[T.oTnb[sl]], sig=f"oTn{T.i}_{sl}")
                P.dma(dr[:, :, t * 128:(t + 1) * 128].rearrange("h d t -> d h t"), T.oTr[sl][:, :, :], r=[T.oTrb[sl]], sig=f"oTr{T.i}_{sl}")

        load_x(0)
        if ntile > 1:
            load_x(1)
        for t0 in range(0, ntile, 2):
            lists = [P.capture(lambda t=t: [None for _ in body(t)]) for t in range(t0, min(t0 + 2, ntile))]
            P.replay(lists)
            for t in range(t0 + 2, min(t0 + 4, ntile)):
                load_x(t)
        P.flush()


def phase_mla2(P, C, x_in, x_out, Wd_, j, g, S, scrQ):
    nc = P.nc
    W = Wd_
    QTn, QTr, KTn, KTr, Vs = scrQ
    nkt = S // 128
    QB = 512 if S >= 512 else S
    nqb = S // QB
    nsubq = QB // 128
    HG = 4
    with ExitStack() as es:
        def sb(name, shape, dt):
            return es.enter_context(nc.sbuf_tensor(uname(name), shape, dt))

        def ps(name, shape, dt=F32):
            return es.enter_context(nc.psum_tensor(uname(name), shape, dt))
        Kn = sb("Kn", [128, HG, S], BF16)
        Kr = sb("Kr", [64, HG, S], BF16)
        Vt = sb("Vt", [128, nkt, HG, 128], BF16)
        Knb = [P.buf(f"Kn{h}") for h in range(HG)]
        Krb = [P.buf(f"Kr{h}") for h in range(HG)]
        Vtb = P.buf("Vt")
        Wo = sb("Wo", [128, HG, D], BF16)
        Wob = P.buf("Wo")
        ones = sb("ones", [128, 128], BF16)
        tri = sb("tri", [128, 128], BF16)
        trif = sb("trif", [128, 128], F32)
        cb = P.buf("mconst")
        xt = sb("xt", [128, nsubq, D], F32)
        xtb = P.buf("xt")
        Qn = [sb(f"Qn{i}", [128, HG, QB], BF16) for i in range(2)]
        Qr = [sb(f"Qr{i}", [64, HG, QB], BF16) for i in range(2)]
        Qb_ = [P.buf(f"Q{i}") for i in range(2)]
        pT = [sb(f"pT{i}", [128, QB], BF16) for i in range(3)]
        pTb = [P.buf(f"pT{i}") for i in range(3)]
        rc = sb("rc", [128, QB], F32)
        rcb = P.buf("rc")
        aT = [sb(f"aT{h}", [128, QB], BF16) for h in range(HG)]
        aTb = [P.buf(f"aT{h}") for h in range(HG)]
        sT = [ps(f"sT{i}", [128, 512]) for i in range(2)]
        sTb = [P.buf(f"sT{i}") for i in range(2)]
        oT = [ps(f"oT{i}", [128, 512]) for i in range(2)]
        oTb = [P.buf(f"oT{i}") for i in range(2)]
        lS = [ps(f"lS{i}", [128, 512]) for i in range(2)]
        lSb = [P.buf(f"lS{i}") for i in range(2)]
        Y = [ps(f"Y{i}", [128, 512]) for i in range(2)]
        Yb = [P.buf(f"Y{i}") for i in range(2)]

        P.op('pool', lambda e: e.memset(ones[:, :], 1.0), w=[cb])
        P.op('pool', lambda e: e.memset(trif[:, :], 1.0), w=[cb])
        P.op('pool', lambda e: e.affine_select(out=trif[:, :], in_=trif[:, :], pattern=[[1, 128]], compare_op=ALU.is_ge, fill=0.0,
                                               base=0, channel_multiplier=-1), r=[cb], w=[cb])
        P.op('pool', lambda e: e.tensor_copy(out=tri[:, :], in_=trif[:, :]), r=[cb], w=[cb])
        for h in range(HG):
            P.dma(Kn[:, h, :], KTn[g * HG + h, :, :], w=[Knb[h]], sig=f"Kn{h}")
            P.dma(Kr[:, h, :], KTr[g * HG + h, :, :], w=[Krb[h]], sig=f"Kr{h}")
            if h == 0:
                P.dma(Vt[:, :, :, :], Vs[:, g * HG:(g + 1) * HG, :].rearrange("(kt p) h e -> p kt h e", p=128), w=[Vtb], sig="Vt")
        wo_r = W['mla_w_out'][j][g * HG * 128:(g + 1) * HG * 128, :].rearrange("(h p) d -> p h d", p=128)
        P.dma(Wo[:, :, :], wo_r[:, :, :], w=[Wob], sig="W0", eng='pool')

        xin = x_in.rearrange("(b s p) d -> b p s d", s=nsubq, p=128)
        xout = x_out.rearrange("(b s p) d -> b p s d", s=nsubq, p=128)

        def load_q(b):
            sl = b % 2
            P.dma(Qn[sl][:, :, :], QTn[g * HG:(g + 1) * HG, :, b * QB:(b + 1) * QB].rearrange("h d t -> d h t"), w=[Qb_[sl]],
                  sig=f"Qn{sl}")
            P.dma(Qr[sl][:, :, :], QTr[g * HG:(g + 1) * HG, :, b * QB:(b + 1) * QB].rearrange("h d t -> d h t"), w=[Qb_[sl]],
                  sig=f"Qr{sl}")

        cnt = [0]

        def qk_mm(b, h, kt, ssl, qs_):
            r = kt - nsubq * b
            c0 = max(r, 0) * 128
            P.op('pe', lambda e: e.matmul(sT[ssl][:, c0:QB], lhsT=Kn[:, h, kt * 128:(kt + 1) * 128],
                                          rhs=Qn[qs_][:, h, c0:QB], start=True, stop=False),
                 r=[Knb[h], Qb_[qs_]], w=[sTb[ssl]])
            P.op('pe', lambda e: e.matmul(sT[ssl][:, c0:QB], lhsT=Kr[:, h, kt * 128:(kt + 1) * 128],
                                          rhs=Qr[qs_][:, h, c0:QB], start=False, stop=True),
                 r=[Krb[h], Qb_[qs_]], w=[sTb[ssl]])

        def pv_mm(b, h, kt, nk, ob, ssl, psl):
            r = kt - nsubq * b
            c0 = max(r, 0) * 128
            P.op('act', lambda e: e.activation(out=pT[psl][:, c0:QB], in_=sT[ssl][:, c0:QB], func=AF.Exp,
                                               scale=ATTN_SCALE), r=[sTb[ssl]], w=[pTb[psl]])
            if r >= 0:
                P.op('dve', lambda e: e.tensor_tensor(out=pT[psl][:, c0:c0 + 128], in0=pT[psl][:, c0:c0 + 128],
                                                      in1=tri[:, :], op=ALU.mult), r=[pTb[psl], cb], w=[pTb[psl]])
            P.op('pe', lambda e: e.matmul(oT[ob][:, c0:QB], lhsT=Vt[:, kt, h, :], rhs=pT[psl][:, c0:QB],
                                          start=(kt == 0), stop=(kt == nk - 1)),
                 r=[Vtb, pTb[psl]], w=[oTb[ob]])
            P.op('pe', lambda e: e.matmul(lS[ob][:, c0:QB], lhsT=ones[:, :], rhs=pT[psl][:, c0:QB],
                                          start=(kt == 0), stop=(kt == nk - 1)),
                 r=[cb, pTb[psl]], w=[lSb[ob]])

        def head_fin(b, h, ob):
            P.op('dve', lambda e: e.reciprocal(out=rc[:, :], in_=lS[ob][:, 0:QB]), r=[lSb[ob]], w=[rcb])
            P.op('dve', lambda e: e.tensor_tensor(out=aT[h][:, :], in0=oT[ob][:, 0:QB], in1=rc[:, :], op=ALU.mult),
                 r=[oTb[ob], rcb], w=[aTb[h]])

        def do_block(b, first_issued):
            qs_ = b % 2
            nk = nsubq * (b + 1)
            pairs = [(h, kt) for h in range(HG) for kt in range(nk)]
            base = cnt[0]
            cnt[0] += len(pairs)
            if not first_issued:
                qk_mm(b, pairs[0][0], pairs[0][1], base % 2, qs_)
            for i, (h, kt) in enumerate(pairs):
                if i + 1 < len(pairs):
                    qk_mm(b, pairs[i + 1][0], pairs[i + 1][1], (base + i + 1) % 2, qs_)
                ob = (b * HG + h) % 2
                pv_mm(b, h, kt, nk, ob, (base + i) % 2, (base + i) % 3)
                if kt == nk - 1:
                    head_fin(b, h, ob)

        def do_proj(b, sq, dh):
            for h in range(HG):
                P.op('pe', lambda e, h=h: e.matmul(Y[dh][:, :], lhsT=aT[h][:, sq * 128:(sq + 1) * 128],
                                                   rhs=Wo[:, h, dh * 512:(dh + 1) * 512], start=(h == 0),
                                                   stop=(h == HG - 1)), r=[aTb[h], Wob], w=[Yb[dh]])
            P.op('dve', lambda e: e.tensor_tensor(out=xt[:, sq, dh * 512:(dh + 1) * 512],
                                                  in0=xt[:, sq, dh * 512:(dh + 1) * 512], in1=Y[dh][:, :],
                                                  op=ALU.add), r=[Yb[dh], xtb], w=[xtb])

        load_q(0)
        for b in range(nqb):
            if b + 1 < nqb:
                load_q(b + 1)
            P.dma(xt[:, :, :], xin[b], w=[xtb], sig="xt")
            do_block(b, b > 0)
            if b + 1 < nqb:
                qk_mm(b + 1, 0, 0, cnt[0] % 2, (b + 1) % 2)
            for sq in range(nsubq):
                for dh in range(2):
                    do_proj(b, sq, dh)
            P.dma(xout[b], xt[:, :, :], r=[xtb], sig="xt")
        P.flush()


def phase_even(P, C, x_in, x_out, Wd_, l, j, S):
    nc = P.nc
    W = Wd_
    ntile = S // 128
    with ExitStack() as es:
        def sb(name, shape, dt):
            return es.enter_context(nc.sbuf_tensor(uname(name), shape, dt))

        def ps(name, shape, dt=F32):
            return es.enter_context(nc.psum_tensor(uname(name), shape, dt))
        Win = sb("eWin", [128, 8, 3072], BF16)
        Winb = [P.buf(f"eWin{g}") for g in range(6)]
        Wo = sb("eWo", [128, 8, D], BF16)
        Wob = P.buf("eWo")
        gbc = sb("gbc", [128, D], F32)
        gb = P.buf("gains")
        C.gbcb = gb
        vnorm = sb("vnorm", [128, 512], F32)
        onorm = sb("onorm", [128, 128], F32)
        bs = sb("bs", [128, 8, 1], F32)
        lbr = sb("lbr", [128, 2, 4, 1], F32)
        lb = sb("lb", [128, 4], F32)
        oml = sb("oml", [128, 4], F32)
        wsf = sb("wsf", [128, 8, 128], F32)
        ws16 = sb("ws16", [128, 8, 128], BF16)
        WsT = sb("WsT", [128, 8, 128], BF16)
        trif = sb("trif", [128, 128], F32)
        bmask = sb("bmask", [128, 128], F32)
        rmask = sb("rmask", [128, 512], F32)
        cb = P.buf("econst")
        xt = [sb(f"xt{i}", [128, 1, D], F32) for i in range(4)]
        xtb = [P.buf(f"xt{i}") for i in range(4)]
        hT = [sb(f"hT{i}", [128, 8, 128], BF16) for i in range(2)]
        hTb = [P.buf(f"hT{i}") for i in range(2)]
        S32 = sb("S32", [128, 4, 128], F32)
        S16 = sb("S16", [128, 4, 128], BF16)
        S32b, S16b = P.buf("S32"), P.buf("S16")
        def alloc_tile(i):
            T = Ctx()
            for nm, shape, dt in (("f1", [128, 4, 128], F32), ("kk", [128, 4, 128], F32), ("lf", [128, 4, 128], F32),
                                  ("Bc", [128, 4, 128], F32), ("e1", [128, 4, 128], F32), ("e2", [128, 4, 128], F32),
                                  ("e3", [128, 4, 128], F32), ("eBC", [128, 4, 4, 1], F32), ("qf32", [128, 4, 128], F32),
                                  ("qt", [128, 4, 128], BF16), ("kt_", [128, 4, 128], BF16), ("kh", [128, 4, 128], BF16),
                                  ("khT", [128, 4, 128], BF16), ("v16", [128, 4, 128], BF16), ("gu_", [128, 512], F32),
                                  ("gv", [128, 8, 64], F32), ("sq", [128, 8, 64], F32), ("vn16", [128, 8, 64], BF16),
                                  ("sgl", [128, 512], F32), ("on", [128, 4, 128], F32), ("mT", [128, 8, 128], BF16),
                                  ("sm", [128, 64], F32)):
                setattr(T, nm, sb(f"{nm}_{i}", shape, dt))
                setattr(T, nm + "b", P.buf(f"{nm}_{i}"))
            T.PT = [sb(f"PT{h}_{i}", [128, 128], BF16) for h in range(4)]
            T.PTb = [P.buf(f"PT{h}_{i}") for h in range(4)]
            T.mixed = sb(f"mixed_{i}", [128, D], BF16)
            T.mixb = [P.buf(f"mixA_{i}"), P.buf(f"mixB_{i}")]
            return T
        TB = [alloc_tile(0), alloc_tile(1)]
        NS = [norm_scratch(P, sb, i, 1) for i in range(2)]
        B = [ps(f"B{i}", [128, 512]) for i in range(8)]
        Bb = [P.buf(f"B{i}") for i in range(8)]

        def b3(i, a):
            return B[i][:, :].rearrange("p (a t) -> p a t", a=a)

        def b16(i, a, n):
            return B[i].bitcast(BF16)[:, 0:a * n].rearrange("p (a t) -> p a t", a=a)

        ng = W['norm_gains']
        P.dma(gbc[:, :], ng[l, 1, :].partition_broadcast(128), w=[gb], sig="gbc", par=True)
        P.dma(vnorm[:, :], W['gmlp_v_norm'][j].rearrange("h d -> (h d)").partition_broadcast(128), w=[gb], sig="vnorm", par=True)
        P.dma(onorm[:, :], W['hgrn_out_norm'][j, :].partition_broadcast(128), w=[gb], sig="onorm", par=True)
        P.dma(bs[:, :, :], W['gmlp_b_s'][j].rearrange("h (t o) -> t h o", o=1), w=[gb], sig="bs", slow=True, par=True)
        P.dma(lbr[:, :, :, :], W['hgrn_lb_raw'].rearrange("r (h d o) -> d r h o", h=4, o=1), w=[gb], sig="lbr", slow=True, par=True)
        P.dma(wsf[:, :, :], W['gmlp_w_s'][j].rearrange("h t s -> t h s"), w=[gb], sig="wsf", par=True)
        P.op('pool', lambda e: e.memset(trif[:, :], 1.0), w=[cb])
        P.op('pool', lambda e: e.memset(bmask[:, :], 1.0), w=[cb])
        P.op('pool', lambda e: e.memset(rmask[:, :], 1.0), w=[cb])
        P.op('pool', lambda e: e.memset(rmask[:, :].rearrange("p (c t) -> p c t", t=32)[:, :, 0:1], 0.0), w=[cb])
        P.op('pool', lambda e: e.memset(S32[:, :, :], 0.0), w=[S32b])
        P.op('pool', lambda e: e.memset(S16[:, :, :], 0.0), w=[S16b])
        P.op('pool', lambda e: e.affine_select(out=trif[:, :], in_=trif[:, :], pattern=[[-1, 128]], compare_op=ALU.is_ge, fill=0.0,
                                               base=0, channel_multiplier=1), r=[cb], w=[cb])
        P.op('pool', lambda e: e.affine_select(out=bmask[:, :], in_=bmask[:, :], pattern=[[1, 128]], compare_op=ALU.is_ge, fill=0.0,
                                               base=0, channel_multiplier=-1), r=[cb], w=[cb])
        for cbk in range(1, 4):
            P.op('pool', lambda e, cbk=cbk: e.memset(bmask[0:32 * cbk, 32 * cbk:32 * cbk + 32], 0.0), r=[cb], w=[cb])
        P.op('dve', lambda e: e.tensor_tensor(out=ws16[:, :, :], in0=wsf[:, :, :], in1=trif[:, :].unsqueeze(1).broadcast_to([128, 8, 128]),
                                              op=ALU.mult), r=[gb, cb], w=[cb])
        tws = b16(7, 8, 128)
        for hh in range(8):
            P.op('pe', lambda e, hh=hh: e.transpose(out=tws[:, hh, :], in_=ws16[:, hh, :], identity=C.ident[:, :]), r=[cb, C.constb],
                 w=[Bb[7]])
        P.op('dve', lambda e: e.tensor_copy(out=WsT[:, :, :], in_=tws), r=[Bb[7]], w=[cb])
        if j == 0:
            P.op('pool', lambda e: e.memset(lb[:, :], 0.0), w=[cb])
        else:
            P.op('dve', lambda e: e.tensor_tensor(out=lb[:, :], in0=lbr[:, 1, :, 0], in1=lbr[:, 0, :, 0], op=ALU.subtract), r=[gb], w=[cb])
            P.op('act', lambda e: e.activation(out=lb[:, :], in_=lb[:, :], func=AF.Sigmoid), r=[cb], w=[cb])
            P.op('dve', lambda e: e.tensor_scalar(out=lb[:, :], in0=lb[:, :], scalar1=0.999, scalar2=0.0, op0=ALU.min, op1=ALU.max),
                 r=[cb], w=[cb])
        P.op('dve', lambda e: e.tensor_scalar(out=oml[:, :], in0=lb[:, :], scalar1=-1.0, scalar2=1.0, op0=ALU.mult, op1=ALU.add),
             r=[cb], w=[cb])
        win_r = W['even_w_in'][j].rearrange("(kc p) f -> p kc f", p=128)
        for g in (0, 1, 4, 5, 2, 3):
            P.dma(Win[:, :, g * 512:(g + 1) * 512], win_r[:, :, g * 512:(g + 1) * 512], w=[Winb[g]], sig=f"W{g}", eng='pool')
        wo_r = W['even_w_out'][j].rearrange("(kc p) f -> p kc f", p=128)
        P.dma(Wo[:, :, :], wo_r[:, :, :], w=[Wob], sig="W6", eng='pool')

        xin = x_in.rearrange("(t s p) d -> t p s d", s=1, p=128)
        xout = x_out.rearrange("(t s p) d -> t p s d", s=1, p=128)

        def load_x(t):
            P.dma(xt[t % 4][:, :, :], xin[t], w=[xtb[t % 4]], sig=f"xt{t % 4}")

        def bankset(par):
            R = Ctx()
            R.p = (4 * par, 4 * par + 1, 4 * par + 2)
            R.o = 4 * par + 3
            return R
        BS = [bankset(0), bankset(1)]

        def pre_norm(t):
            p0 = BS[t % 2].p[0]
            tp = B[p0].bitcast(BF16).rearrange("p (k t) -> p k t", k=8)
            emit_norm_T(P, C, xt[t % 4], xtb[t % 4], gbc[:, :], hT[t % 2], hTb[t % 2], tp, Bb[p0], nsub=1, N=NS[t % 2])

        def pre(t):
            T = TB[t % 2]
            R = BS[t % 2]
            p0, p1, p2 = R.p
            h = hT[t % 2]
            hb = hTb[t % 2]

            def proj_tok(bk, c0):
                for k in range(8):
                    P.op('pe', lambda e, k=k: e.matmul(B[bk][:, :], lhsT=h[:, k, :], rhs=Win[:, k, c0:c0 + 512],
                                                       start=(k == 0), stop=(k == 7)), r=[hb, Winb[c0 // 512]], w=[Bb[bk]])

            def proj_feat(bk, c0):
                bv = b3(bk, 4)
                for hh in range(4):
                    for k in range(8):
                        P.op('pe', lambda e, k=k, hh=hh: e.matmul(bv[:, hh, :], lhsT=Win[:, k, c0 + hh * 128:c0 + (hh + 1) * 128],
                                                                  rhs=h[:, k, :], start=(k == 0), stop=(k == 7)),
                             r=[hb, Winb[c0 // 512]], w=[Bb[bk]])
            proj_tok(p1, 0)
            proj_tok(p2, 512)
            P.op('act', lambda e: e.activation(out=T.gu_[:, :], in_=B[p1][:, :], func=AF.Gelu_apprx_tanh), r=[Bb[p1]], w=[T.gu_b])
            P.op('act', lambda e: e.activation(out=T.gv[:, :, :], in_=b3(p2, 8), func=AF.Gelu_apprx_tanh), r=[Bb[p2]], w=[T.gvb])
            proj_tok(p0, 2048)
            proj_tok(p1, 2560)
            P.op('act', lambda e: e.activation(out=T.v16[:, :, :], in_=b3(p0, 4), func=AF.Copy), r=[Bb[p0]], w=[T.v16b])
            P.op('act', lambda e: e.activation(out=T.sgl[:, :], in_=B[p1][:, :], func=AF.Silu), r=[Bb[p1]], w=[T.sglb])
            proj_feat(p2, 1024)
            proj_feat(p0, 1536)
            P.op('act', lambda e: e.activation(out=T.qf32[:, :, :], in_=b3(p2, 4), func=AF.Copy), r=[Bb[p2]], w=[T.qf32b])
            P.op('act', lambda e: e.activation(out=T.f1[:, :, :], in_=b3(p0, 4), func=AF.Sigmoid), r=[Bb[p0]], w=[T.f1b])
            lb_bc = lb[:, :].unsqueeze(2).broadcast_to([128, 4, 128])
            oml_bc = oml[:, :].unsqueeze(2).broadcast_to([128, 4, 128])
            P.op('dve', lambda e: e.tensor_tensor(out=T.f1[:, :, :], in0=T.f1[:, :, :], in1=oml_bc, op=ALU.mult), r=[T.f1b, cb], w=[T.f1b])
            P.op('dve', lambda e: e.tensor_tensor(out=T.f1[:, :, :], in0=T.f1[:, :, :], in1=lb_bc, op=ALU.add), r=[T.f1b, cb], w=[T.f1b])
            P.op('dve', lambda e: e.tensor_scalar(out=T.kk[:, :, :], in0=T.f1[:, :, :], scalar1=-1.0, scalar2=1.0, op0=ALU.mult, op1=ALU.add),
                 r=[T.f1b], w=[T.kkb])
            P.op('dve', lambda e: e.tensor_scalar(out=T.lf[:, :, :], in0=T.f1[:, :, :], scalar1=1e-6, scalar2=None, op0=ALU.max),
                 r=[T.f1b], w=[T.lfb])
            P.op('act', lambda e: e.activation(out=T.lf[:, :, :], in_=T.lf[:, :, :], func=AF.Ln), r=[T.lfb], w=[T.lfb])
            P.op('dve', lambda e: e.tensor_tensor_scan(out=T.Bc[:, :, :].rearrange("p h t -> p (h t)"), data0=rmask[:, :],
                                                       data1=T.lf[:, :, :].rearrange("p h t -> p (h t)"), initial=0.0, op0=ALU.mult,
                                                       op1=ALU.add), r=[T.lfb, cb], w=[T.Bcb])
            P.op('act', lambda e: e.activation(out=T.e1[:, :, :], in_=T.Bc[:, :, :], func=AF.Exp), r=[T.Bcb], w=[T.e1b])
            P.op('dve', lambda e: e.tensor_tensor(out=T.qt[:, :, :], in0=T.qf32[:, :, :], in1=T.e1[:, :, :], op=ALU.mult),
                 r=[T.qf32b, T.e1b], w=[T.qtb])
            P.op('dve', lambda e: e.tensor_scalar(out=T.e2[:, :, :], in0=T.Bc[:, :, :], scalar1=-60.0, scalar2=None, op0=ALU.max),
                 r=[T.Bcb], w=[T.e2b])
            P.op('act', lambda e: e.activation(out=T.e2[:, :, :], in_=T.e2[:, :, :], func=AF.Exp, scale=-1.0), r=[T.e2b], w=[T.e2b])
            P.op('dve', lambda e: e.tensor_tensor(out=T.kt_[:, :, :], in0=T.kk[:, :, :], in1=T.e2[:, :, :], op=ALU.mult),
                 r=[T.kkb, T.e2b], w=[T.kt_b])
            B4 = T.Bc[:, :, :].rearrange("p h (c t) -> p h c t", t=32)
            BCl = B4[:, :, :, 31:32]
            P.op('dve', lambda e: e.tensor_tensor(out=T.e3[:, :, :].rearrange("p h (c t) -> p h c t", t=32),
                                                  in0=BCl.broadcast_to([128, 4, 4, 32]), in1=B4, op=ALU.subtract), r=[T.Bcb], w=[T.e3b])
            P.op('act', lambda e: e.activation(out=T.e3[:, :, :], in_=T.e3[:, :, :], func=AF.Exp), r=[T.e3b], w=[T.e3b])
            P.op('dve', lambda e: e.tensor_tensor(out=T.kh[:, :, :], in0=T.kk[:, :, :], in1=T.e3[:, :, :], op=ALU.mult),
                 r=[T.kkb, T.e3b], w=[T.khb])
            P.op('act', lambda e: e.activation(out=T.eBC[:, :, :, :], in_=BCl, func=AF.Exp), r=[T.Bcb], w=[T.eBCb])
            tk = b16(p1, 4, 128)
            for hh in range(4):
                P.op('pe', lambda e, hh=hh: e.transpose(out=tk[:, hh, :], in_=T.kh[:, hh, :], identity=C.ident[:, :]),
                     r=[T.khb, C.constb], w=[Bb[p1]])
            P.op('act', lambda e: e.activation(out=T.khT[:, :, :], in_=tk, func=AF.Copy), r=[Bb[p1]], w=[T.khTb])
            P.op('dve', lambda e: e.tensor_tensor(out=T.sq[:, :, :], in0=T.gv[:, :, :], in1=T.gv[:, :, :], op=ALU.mult), r=[T.gvb], w=[T.sqb])
            P.op('dve', lambda e: e.tensor_reduce(out=T.sm[:, 0:8], in_=T.sq[:, :, :], axis=AX.X, op=ALU.add), r=[T.sqb], w=[T.smb])
            P.op('dve', lambda e: e.tensor_scalar(out=T.sm[:, 8:16], in0=T.sm[:, 0:8], scalar1=1.0 / 64, scalar2=EPS, op0=ALU.mult,
                                                  op1=ALU.add), r=[T.smb], w=[T.smb])
            P.op('pool', lambda e: e.tensor_tensor(out=T.sm[:, 16:24], in0=T.sm[:, 8:16], in1=C.neghalf[:, 0:8], op=ALU.pow),
                 r=[T.smb, C.constb], w=[T.smb])
            P.op('dve', lambda e: e.tensor_tensor(out=T.sq[:, :, :], in0=T.gv[:, :, :],
                                                  in1=T.sm[:, 16:24].unsqueeze(2).broadcast_to([128, 8, 64]), op=ALU.mult),
                 r=[T.gvb, T.smb], w=[T.sqb])
            P.op('dve', lambda e: e.tensor_tensor(out=T.vn16[:, :, :], in0=T.sq[:, :, :],
                                                  in1=vnorm[:, :].rearrange("p (h d) -> p h d", h=8), op=ALU.mult),
                 r=[T.sqb, gb], w=[T.vn16b])
            Mv = b3(p2, 8)
            for hh in range(8):
                P.op('pe', lambda e, hh=hh: e.matmul(Mv[:, hh, :], lhsT=WsT[:, hh, :], rhs=T.vn16[:, hh, :], start=True, stop=True),
                     r=[cb, T.vn16b], w=[Bb[p2]])
            P.op('dve', lambda e: e.tensor_tensor(out=T.sq[:, :, :], in0=Mv, in1=bs[:, :, :].broadcast_to([128, 8, 64]), op=ALU.add),
                 r=[Bb[p2], gb], w=[T.sqb])
            P.op('dve', lambda e: e.tensor_tensor(out=T.mixed[:, 0:512], in0=T.sq[:, :, :].rearrange("p h d -> p (h d)"), in1=T.gu_[:, :],
                                                  op=ALU.mult), r=[T.sqb, T.gu_b], w=[T.mixb[0]])
            sc = b3(p0, 4)
            for hh in range(4):
                P.op('pe', lambda e, hh=hh: e.matmul(sc[:, hh, :], lhsT=T.kt_[:, hh, :], rhs=T.qt[:, hh, :], start=True, stop=True),
                     r=[T.kt_b, T.qtb], w=[Bb[p0]])
            for hh in range(4):
                P.op('dve', lambda e, hh=hh: e.tensor_tensor(out=T.PT[hh][:, :], in0=sc[:, hh, :], in1=bmask[:, :], op=ALU.mult),
                     r=[Bb[p0], cb], w=[T.PTb[hh]])
            O = b3(R.o, 4)
            for hh in range(4):
                P.op('pe', lambda e, hh=hh: e.matmul(O[:, hh, :], lhsT=T.PT[hh][:, :], rhs=T.v16[:, hh, :], start=(hh == 0), stop=False,
                                                     skip_group_check=True), r=[T.PTb[hh], T.v16b], w=[Bb[R.o]])

        def chain(t):
            T = TB[t % 2]
            R = BS[t % 2]
            p0, p1, p2 = R.p
            O = b3(R.o, 4)
            Ust = b3(p1, 4)
            for c in range(4):
                for hh in range(4):
                    P.op('pe', lambda e, hh=hh, c=c: e.matmul(O[32 * c:32 * c + 32, hh, :], lhsT=T.qt[:, hh, 32 * c:32 * c + 32],
                                                              rhs=S16[:, hh, :], start=False, stop=(c == 3), tile_position=(0, 32 * c),
                                                              skip_group_check=True), r=[T.qtb, S16b], w=[Bb[R.o]])
                for hh in range(4):
                    P.op('pe', lambda e, hh=hh, c=c: e.matmul(Ust[:, hh, :], lhsT=T.khT[32 * c:32 * c + 32, hh, :],
                                                              rhs=T.v16[32 * c:32 * c + 32, hh, :], start=True, stop=True,
                                                              tile_position=(32 * c, 0)), r=[T.khTb, T.v16b], w=[Bb[p1]])
                for hh in range(4):
                    P.op('dve', lambda e, hh=hh, c=c: e.scalar_tensor_tensor(out=S32[:, hh, :], in0=S32[:, hh, :], scalar=T.eBC[:, hh, c, :],
                                                                             in1=Ust[:, hh, :], op0=ALU.mult, op1=ALU.add),
                         r=[S32b, T.eBCb, Bb[p1]], w=[S32b])
                P.op('act', lambda e: e.activation(out=S16[:, :, :], in_=S32[:, :, :], func=AF.Copy), r=[S32b], w=[S16b])

        def post(t):
            T = TB[t % 2]
            R = BS[t % 2]
            p0, p1, p2 = R.p
            x = xt[t % 4]
            xb = xtb[t % 4]
            O = b3(R.o, 4)
            for hh in range(4):
                P.op('act', lambda e, hh=hh: e.activation(out=C.junk[:, 0:128], in_=O[:, hh, :], func=AF.Square,
                                                           accum_out=T.sm[:, 32 + hh:33 + hh]), r=[Bb[R.o]], w=[T.smb, C.junkb])
            P.op('dve', lambda e: e.tensor_scalar(out=T.sm[:, 40:44], in0=T.sm[:, 32:36], scalar1=1.0 / 128, scalar2=EPS, op0=ALU.mult,
                                                  op1=ALU.add), r=[T.smb], w=[T.smb])
            P.op('pool', lambda e: e.tensor_tensor(out=T.sm[:, 44:48], in0=T.sm[:, 40:44], in1=C.neghalf[:, 0:4], op=ALU.pow),
                 r=[T.smb, C.constb], w=[T.smb])
            for hh in range(4):
                P.op('act', lambda e, hh=hh: e.activation(out=T.on[:, hh, :], in_=O[:, hh, :], func=AF.Copy, scale=T.sm[:, 44 + hh:45 + hh]),
                     r=[Bb[R.o], T.smb], w=[T.onb])
            P.op('dve', lambda e: e.tensor_tensor(out=T.on[:, :, :], in0=T.on[:, :, :],
                                                  in1=onorm[:, :].unsqueeze(1).broadcast_to([128, 4, 128]), op=ALU.mult),
                 r=[T.onb, gb], w=[T.onb])
            P.op('dve', lambda e: e.tensor_tensor(out=T.mixed[:, 512:1024], in0=T.on[:, :, :].rearrange("p h d -> p (h d)"), in1=T.sgl[:, :],
                                                  op=ALU.mult), r=[T.onb, T.sglb], w=[T.mixb[1]])
            tm = b16(p2, 8, 128)
            for k in range(8):
                P.op('pe', lambda e, k=k: e.transpose(out=tm[:, k, :], in_=T.mixed[:, k * 128:(k + 1) * 128], identity=C.ident[:, :]),
                     r=[T.mixb[0], T.mixb[1], C.constb], w=[Bb[p2]])
            P.op('act', lambda e: e.activation(out=T.mT[:, :, :], in_=tm, func=AF.Copy), r=[Bb[p2]], w=[T.mTb])
            for dh, bk in ((0, p0), (1, p1)):
                for k in range(8):
                    P.op('pe', lambda e, k=k, dh=dh, bk=bk: e.matmul(B[bk][:, :], lhsT=T.mT[:, k, :], rhs=Wo[:, k, dh * 512:(dh + 1) * 512],
                                                                    start=(k == 0), stop=(k == 7)), r=[T.mTb, Wob], w=[Bb[bk]])
                P.op('dve', lambda e, dh=dh, bk=bk: e.tensor_tensor(out=x[:, 0, dh * 512:(dh + 1) * 512], in0=x[:, 0, dh * 512:(dh + 1) * 512],
                                                                   in1=B[bk][:, :], op=ALU.add), r=[Bb[bk], xb], w=[xb])
            P.dma(xout[t], x[:, :, :], r=[xb], sig=f"xt{t % 4}")

        for t in range(min(4, ntile)):
            load_x(t)
        P.replay([P.capture(lambda t=t: pre_norm(t)) for t in range(min(2, ntile))])
        for t0 in range(0, ntile, 2):
            ts = list(range(t0, min(t0 + 2, ntile)))
            nxt = list(range(t0 + 2, min(t0 + 4, ntile)))
            P.replay([P.capture(lambda t=t: pre(t)) for t in ts])
            chain(ts[0])
            if len(ts) > 1:
                P.replay([P.capture(lambda: post(ts[0])), P.capture(lambda: chain(ts[1]))])
                if nxt:
                    P.replay([P.capture(lambda t=t: pre_norm(t)) for t in nxt])
                post(ts[1])
            else:
                post(ts[0])
            for t in range(t0 + 4, min(t0 + 6, ntile)):
                load_x(t)
        P.flush()


def setup_consts(P, C, es, ident_dram):
    nc = P.nc

    def sb(name, shape, dt):
        return es.enter_context(nc.sbuf_tensor(uname(name), shape, dt))
    C.identf = sb("identf", [128, 128], F32)
    C.ident = sb("ident", [128, 128], BF16)
    C.neghalf = sb("neghalf", [128, 8], F32)
    C.junk = sb("junk", [128, D], BF16)
    C.ss = sb("ss", [128, 8], F32)
    C.vv = sb("vv", [128, 8], F32)
    C.rstd = sb("rstd", [128, 8], F32)
    b0 = P.buf("c0")
    P.dma(C.identf[:, :], ident_dram, w=[b0], sig="identf")
    P.op('dve', lambda e: e.tensor_copy(out=C.ident[:, :], in_=C.identf[:, :]), r=[b0], w=[b0])
    P.op('pool', lambda e: e.memset(C.neghalf[:, :], -0.5), w=[b0])
    P.flush()


def new_phase_bufs(P, C):
    C.constb = P.buf("const")
    C.junkb = P.buf("junk")
    C.ssb = [P.buf(f"ss{s}") for s in range(8)]
    C.vvb = P.buf("vv")
    C.rstdb = P.buf("rstd")
    C.vv2b = [P.buf("vv2_0"), P.buf("vv2_1")]
    C.rstd2b = [P.buf("rstd2_0"), P.buf("rstd2_1")]


def build_nc(S, plan):
    nc = bass.Bass("TRN2", target_bir_lowering=False)

    def din(name, shape, dt=F32):
        return nc.dram_tensor(name, list(shape), dt, kind="ExternalInput").ap()
    x = din("x", [S, D])
    ident = din("ident", [128, 128])
    W = {
        'norm_gains': din("norm_gains", [DEPTH, 5, D]),
        'ffn_w_gate': din("ffn_w_gate", [DEPTH, 2, D, DFF]),
        'ffn_w_up': din("ffn_w_up", [DEPTH, 2, D, DFF]),
        'ffn_w_down': din("ffn_w_down", [DEPTH, 2, DFF, D]),
        'ple_w_gate': din("ple_w_gate", [DEPTH, D, D]),
        'ple_w_proj': din("ple_w_proj", [DEPTH, 256, D]),
        'p': din("p", [DEPTH, S, 256]),
        'positions': din("positions", [S], I32),
        'invf': din("invf", [2, 32]),
        'mla_w_in': din("mla_w_in", [2, D, 704]),
        'mla_q_a_norm': din("mla_q_a_norm", [2, 384]),
        'mla_kv_a_norm': din("mla_kv_a_norm", [2, 256]),
        'mla_w_q_b': din("mla_w_q_b", [2, 384, 1536]),
        'mla_w_kv_b': din("mla_w_kv_b", [2, 256, 2048]),
        'mla_q_norm': din("mla_q_norm", [2, 192]),
        'mla_k_norm': din("mla_k_norm", [2, 192]),
        'mla_w_out': din("mla_w_out", [2, D, D]),
        'even_w_in': din("even_w_in", [2, D, 3072]),
        'gmlp_v_norm': din("gmlp_v_norm", [2, 8, 64]),
        'gmlp_w_s': din("gmlp_w_s", [2, 8, 128, 128]),
        'gmlp_b_s': din("gmlp_b_s", [2, 8, 128]),
        'hgrn_lb_raw': din("hgrn_lb_raw", [2, 512]),
        'hgrn_out_norm': din("hgrn_out_norm", [2, 128]),
        'even_w_out': din("even_w_out", [2, D, D]),
    }
    kd = "ExternalOutput" if DEBUG else "Internal"
    scrQ = (nc.dram_tensor("QTn", [NH, 128, S], BF16, kind=kd).ap(),
            nc.dram_tensor("QTr", [NH, 64, S], BF16, kind=kd).ap(),
            nc.dram_tensor("KTn", [NH, 128, S], BF16, kind=kd).ap(),
            nc.dram_tensor("KTr", [NH, 64, S], BF16, kind=kd).ap(),
            nc.dram_tensor("Vs", [S, NH, 128], BF16, kind=kd).ap())
    y = nc.dram_tensor("y", [S, D], F32, kind="ExternalOutput").ap()
    scr = [nc.dram_tensor(f"xscr{i}", [S, D], F32, kind="Internal").ap() for i in range(3)]
    P = Prog(nc)
    C = Ctx()
    with ExitStack() as es:
        new_phase_bufs(P, C)
        setup_consts(P, C, es, ident)
        cur = x
        si = 0
        for pi, ph in enumerate(plan):
            if pi == len(plan) - 1:
                dst = y
            else:
                dst = scr[si % 3]
                si += 1
            new_phase_bufs(P, C)
            if ph[0] == 'even':
                _, l = ph
                phase_even(P, C, cur, dst, W, l, l // 2, S)
            if ph[0] == 'mla':
                _, l = ph
                phase_mla1(P, C, cur, W, l, l // 2, S, scrQ)
                mid = scr[si % 3]
                si += 1
                new_phase_bufs(P, C)
                phase_mla2(P, C, cur, mid, W, l // 2, 0, S, scrQ)
                new_phase_bufs(P, C)
                phase_mla2(P, C, mid, dst, W, l // 2, 1, S, scrQ)
            if ph[0] == 'ffn':
                _, l, which = ph
                phase_ffn(P, C, cur, dst, W['ffn_w_gate'][l, which], W['ffn_w_up'][l, which], W['ffn_w_down'][l, which],
                          W['norm_gains'][l, 0 if which == 0 else 2, :], S)
            elif ph[0] == 'ple':
                _, l = ph
                phase_ple(P, C, cur, dst, W['p'][l], W['ple_w_gate'][l], W['ple_w_proj'][l], W['norm_gains'][l, 3, :],
                          W['norm_gains'][l, 4, :], S)
            cur = dst
    return nc


WEIGHT_KEYS = ['norm_gains', 'ffn_w_gate', 'ffn_w_up', 'ffn_w_down', 'ple_w_gate', 'ple_w_proj', 'even_w_in', 'gmlp_v_norm',
               'gmlp_w_s', 'gmlp_b_s', 'hgrn_lb_raw', 'hgrn_out_norm', 'even_w_out', 'mla_w_in', 'mla_q_a_norm', 'mla_kv_a_norm',
               'mla_w_q_b', 'mla_w_kv_b', 'mla_q_norm', 'mla_k_norm', 'mla_w_out']
N_CORES = 8
SEQ = 4096
FUSED = True


def layer_plan(l):
    return [('ffn', l, 0), ('even', l) if l % 2 == 0 else ('mla', l), ('ffn', l, 1), ('ple', l)]


def kernel(**inputs):
    x = np.ascontiguousarray(np.asarray(inputs['x'], dtype=np.float32))
    p = np.asarray(inputs['p'], dtype=np.float32)
    pos = np.asarray(inputs['positions']).astype(np.int32)
    Wt = {k: np.ascontiguousarray(np.asarray(inputs[k], dtype=np.float32)) for k in WEIGHT_KEYS}
    ident = np.eye(128, dtype=np.float32)
    invf = (10000.0 ** (-np.arange(0, 64, 2, dtype=np.float32) / 64)).astype(np.float32)
    invf2 = np.ascontiguousarray(np.stack([invf, invf]))
    B = x.shape[0]
    assert B == N_CORES and x.shape[1] == SEQ
    cur = [np.ascontiguousarray(x[b]) for b in range(B)]
    pb = [np.ascontiguousarray(p[:, b]) for b in range(B)]
    posb = [np.ascontiguousarray(pos[b]) for b in range(B)]
    plans = [sum([layer_plan(l) for l in range(DEPTH)], [])] if FUSED else [layer_plan(l) for l in range(DEPTH)]
    for plan in plans:
        nc = build_nc(SEQ, plan)
        in_maps = []
        for b in range(B):
            m = {"x": cur[b], "ident": ident, "invf": invf2, "p": pb[b], "positions": posb[b]}
            m.update(Wt)
            in_maps.append(m)
        res = run_bass_kernel_spmd(nc, in_maps, core_ids=list(range(N_CORES)))
        cur = [np.ascontiguousarray(np.asarray(res.results[b]["y"], dtype=np.float32)) for b in range(B)]
    return np.stack(cur, axis=0)
```

```python
import numpy as np
from contextlib import ExitStack
import concourse.bass as bass
import concourse.mybir as mybir
from concourse.bass_utils import run_bass_kernel_spmd

F32 = mybir.dt.float32
BF16 = mybir.dt.bfloat16
I32 = mybir.dt.int32
AF = mybir.ActivationFunctionType
ALU = mybir.AluOpType
AX = mybir.AxisListType

D = 1024
DFF = 2816
NFC = DFF // 128
DEPTH = 4
EPS = 1e-6
TT = 256
DEBUG = False
EV_DBG = ''
ENG_ATTR = {'sp': 'sync', 'act': 'scalar', 'pe': 'tensor', 'dve': 'vector', 'pool': 'gpsimd'}


class Buf:
    __slots__ = ('name', 'w', 'r', 'pw')

    def __init__(self, name):
        self.name = name
        self.w = None
        self.r = []
        self.pw = []


class Op:
    __slots__ = ('eng', 'fn', 'deps', 'needed', 'sem', 'val', 'is_dma')


class Prog:
    def __init__(self, nc):
        self.nc = nc
        self.ops = {e: [] for e in ENG_ATTR}
        self.eng_sem = {e: nc.alloc_semaphore(name=f"es_{e}") for e in ENG_ATTR}
        self.eng_cnt = {e: 0 for e in ENG_ATTR}
        self.dma_sem = {}
        self.order = []
        self.bufs = []
        self.dmas = []

    def buf(self, name):
        b = Buf(name)
        self.bufs.append(b)
        return b

    def _deps(self, eng, is_dma, r, w, par=False):
        deps = []

        def add(o, war=False):
            if o is None:
                return
            if not is_dma and not o.is_dma and o.eng == eng:
                if eng in ('pe', 'act'):
                    return
            if o not in deps:
                deps.append(o)
        for b in r:
            add(b.w)
            for o in b.pw:
                add(o)
        for b in w:
            if not par:
                add(b.w)
                for o in b.pw:
                    add(o)
            for o in b.r:
                add(o, war=True)
        return deps

    def _commit(self, op, r, w, par=False):
        for o in op.deps:
            o.needed = True
        for b in w:
            if par:
                b.pw.append(op)
                continue
            b.w = op
            b.pw = []
            b.r = []
        for b in r:
            if b.w is not op:
                b.r.append(op)
        self.ops[op.eng].append(op)
        self.order.append(op)

    def op(self, eng, fn, r=(), w=()):
        o = Op()
        o.eng = eng
        o.fn = fn
        o.is_dma = False
        o.needed = False
        o.sem = None
        o.val = None
        o.deps = self._deps(eng, False, r, w)
        self._commit(o, r, w)
        return o

    def dma(self, out, in_, r=(), w=(), sig=None, eng='sp', slow=False, par=False):
        if sig not in self.dma_sem:
            self.dma_sem[sig] = [self.nc.alloc_semaphore(name=f"ds_{sig}"), 0]
        o = Op()
        o.eng = eng
        if slow:
            o.fn = lambda e, out=out, in_=in_: e.dma_start(out=out, in_=in_, allow_slow_non_contiguous=True)
        else:
            o.fn = lambda e, out=out, in_=in_: e.dma_start(out=out, in_=in_)
        o.is_dma = True
        o.needed = True
        ent = self.dma_sem[sig]
        ent[1] += 16
        o.sem = ent[0]
        o.val = ent[1]
        o.deps = self._deps(eng, True, r, w, par)
        self._commit(o, r, w, par)
        self.dmas.append(o)
        return o

    def capture(self, fn):
        lst = []
        self.op = lambda *a, **k: lst.append((0, a, k))
        self.dma = lambda *a, **k: lst.append((1, a, k))
        try:
            fn()
        finally:
            del self.op
            del self.dma
        return lst

    def replay(self, lists):
        n = max(len(l) for l in lists)
        for i in range(n):
            for l in lists:
                if i < len(l):
                    kind, a, k = l[i]
                    (self.dma if kind else self.op)(*a, **k)

    def flush(self):
        nc = self.nc
        fin = Op()
        fin.eng = 'sp'
        fin.fn = None
        fin.is_dma = False
        fin.needed = False
        fin.sem = None
        fin.val = None
        fin.deps = list(self.dmas)
        self.ops['sp'].append(fin)
        self.order.append(fin)
        for o in self.order:
            if not o.is_dma and o.needed:
                self.eng_cnt[o.eng] += 1
                o.sem = self.eng_sem[o.eng]
                o.val = self.eng_cnt[o.eng]
        with nc.Block() as block:
            for en, attr in ENG_ATTR.items():
                ops = self.ops[en]

                def body(e, ops=ops):
                    seen = {}
                    for o in ops:
                        for d in o.deps:
                            k = id(d.sem)
                            if seen.get(k, 0) >= d.val:
                                continue
                            seen[k] = d.val
                            e.wait_ge(d.sem, d.val)
                        if o.fn is None:
                            continue
                        ins = o.fn(e)
                        if o.is_dma:
                            ins.then_inc(o.sem, 16)
                        elif o.needed:
                            ins.then_inc(o.sem, 1)
                getattr(block, attr)(body)
        self.ops = {e: [] for e in ENG_ATTR}
        self.order = []
        self.dmas = []
        for b in self.bufs:
            b.w = None
            b.r = []
            b.pw = []
        self.bufs = []


class Ctx:
    pass


_UID = [0]


def uname(name):
    _UID[0] += 1
    return f"{name}_{_UID[0]}"


def load_cast(P, C, dst, src, dstbuf, n):
    i = C.stage_i
    C.stage_i += 1
    slot = i % len(C.stage)
    st, sb = C.stage[slot], C.stage_b[slot]
    P.dma(st[:, 0:n], src, w=[sb], sig=f"stage{slot}")
    eng = 'dve' if (i % 2 == 0) else 'act'
    if eng == 'dve':
        P.op('dve', lambda e: e.tensor_copy(out=dst, in_=st[:, 0:n]), r=[sb], w=[dstbuf])
    else:
        P.op('act', lambda e: e.activation(out=dst, in_=st[:, 0:n], func=AF.Copy), r=[sb], w=[dstbuf])


def emit_norm_T(P, C, xt, xb, gbc, hT, hTb, tp, tpb, nsub=2, N=None):
    if N is None:
        N = C
    ss, ssb = N.ss, N.ssb
    for s in range(nsub):
        P.op('act', lambda e, s=s: e.activation(out=C.junk[:, :], in_=xt[:, s, :], func=AF.Square,
                                                 accum_out=ss[:, s:s + 1]), r=[xb], w=[ssb[s], C.junkb])
    P.op('dve', lambda e: e.tensor_scalar(out=N.vv[:, 0:nsub], in0=ss[:, 0:nsub], scalar1=1.0 / D, scalar2=EPS,
                                          op0=ALU.mult, op1=ALU.add), r=[ssb[s] for s in range(nsub)], w=[N.vvb])
    P.op('pool', lambda e: e.tensor_tensor(out=N.rstd[:, 0:nsub], in0=N.vv[:, 0:nsub], in1=C.neghalf[:, 0:nsub],
                                           op=ALU.pow), r=[N.vvb, C.constb], w=[N.rstdb])
    for s in range(nsub):
        P.op('dve', lambda e, s=s: e.scalar_tensor_tensor(out=N.xs[:, s, :], in0=xt[:, s, :], scalar=N.rstd[:, s:s + 1],
                                                          in1=gbc, op0=ALU.mult, op1=ALU.mult),
             r=[xb, N.rstdb, C.gbcb], w=[N.xsb[s]])
    for s in range(nsub):
        for k in range(8):
            P.op('pe', lambda e, s=s, k=k: e.transpose(out=tp[:, k, s * 128:(s + 1) * 128],
                                                       in_=N.xs[:, s, k * 128:(k + 1) * 128], identity=C.ident[:, :]),
                 r=[N.xsb[s], C.constb], w=[tpb])
    W = nsub * 128
    for hh in range(2):
        P.op('act', lambda e, hh=hh: e.activation(out=hT[:, hh * 4:(hh + 1) * 4, 0:W], in_=tp[:, hh * 4:(hh + 1) * 4, 0:W],
                                                  func=AF.Copy), r=[tpb], w=[hTb])


def norm_scratch(P, sbf, i, nsub):
    N = Ctx()
    N.ss = sbf(f"nss{i}", [128, 8], F32)
    N.vv = sbf(f"nvv{i}", [128, 8], F32)
    N.rstd = sbf(f"nrstd{i}", [128, 8], F32)
    N.xs = sbf(f"nxs{i}", [128, nsub, D], BF16)
    N.ssb = [P.buf(f"nss{i}_{s}") for s in range(8)]
    N.vvb = P.buf(f"nvv{i}")
    N.rstdb = P.buf(f"nrstd{i}")
    N.xsb = [P.buf(f"nxs{i}_{s}") for s in range(nsub)]
    return N

def phase_ffn(P, C, x_in, x_out, wg, wu, wd, gain, S):
    nc = P.nc
    ntile = S // TT
    with ExitStack() as es:
        def sb(name, shape, dt):
            return es.enter_context(nc.sbuf_tensor(uname(name), shape, dt))

        def ps(name, shape, dt=F32):
            return es.enter_context(nc.psum_tensor(uname(name), shape, dt))
        Wg = sb("Wg", [128, 8, DFF], BF16)
        Wu = sb("Wu", [128, 8, DFF], BF16)
        Wd = sb("Wd", [128, NFC, D], BF16)
        NG4 = (NFC + 3) // 4
        Wgb = [P.buf(f"Wg{g}") for g in range(NG4)]
        Wub = [P.buf(f"Wu{g}") for g in range(NG4)]
        Wdb = [P.buf(f"Wd{g}") for g in range(NG4)]
        gbc = sb("gbc", [128, D], F32)
        C.gbcb = P.buf("gbc")
        xt = [sb(f"xt{i}", [128, 2, D], F32) for i in range(2)]
        xtb = [P.buf(f"xt{i}") for i in range(2)]
        C.xs = sb("xs", [128, 2, D], BF16)
        C.xsb = [P.buf(f"xs{s}") for s in range(2)]
        hT = [sb(f"hT{i}", [128, 8, TT], BF16) for i in range(2)]
        hTb = [P.buf(f"hT{i}") for i in range(2)]
        sg = [sb(f"sg{i}", [128, TT], F32) for i in range(2)]
        sgb = [P.buf(f"sg{i}") for i in range(2)]
        aT = [sb(f"aT{i}", [128, TT], BF16) for i in range(2)]
        aTb = [P.buf(f"aT{i}") for i in range(2)]
        tp = ps("tp", [128, 8, TT], BF16)
        tpb = P.buf("tp")
        gu = [ps(f"gu{i}", [128, 2, TT]) for i in range(2)]
        gub = [P.buf(f"gu{i}") for i in range(2)]
        yp = [[ps(f"y{s}{h}", [128, 512]) for h in range(2)] for s in range(2)]
        ypb = [[P.buf(f"y{s}{h}") for h in range(2)] for s in range(2)]

        P.dma(gbc[:, :], gain.partition_broadcast(128), w=[C.gbcb], sig="gbc")
        wgr = wg.rearrange("(kc p) f -> p kc f", p=128)
        wur = wu.rearrange("(kc p) f -> p kc f", p=128)
        wdr = wd.rearrange("(fc p) d -> p fc d", p=128)

        for g in range(NG4):
            f0 = 4 * g
            nf = min(4, NFC - f0)
            c0, c1 = f0 * 128, (f0 + nf) * 128
            P.dma(Wg[:, :, c0:c1], wgr[:, :, c0:c1], w=[Wgb[g]], sig=f"W{3 * g}", eng='pool')
            P.dma(Wu[:, :, c0:c1], wur[:, :, c0:c1], w=[Wub[g]], sig=f"W{3 * g + 1}", eng='pool')
            P.dma(Wd[:, f0:f0 + nf, :], wdr[:, f0:f0 + nf, :], w=[Wdb[g]], sig=f"W{3 * g + 2}", eng='pool')

        xin = x_in.rearrange("(t s p) d -> t p s d", s=2, p=128)
        xout = x_out.rearrange("(t s p) d -> t p s d", s=2, p=128)

        def load_x(t):
            P.dma(xt[t % 2][:, :, :], xin[t], w=[xtb[t % 2]], sig=f"xt{t % 2}")

        def norm(t):
            emit_norm_T(P, C, xt[t % 2], xtb[t % 2], gbc[:, :], hT[t % 2], hTb[t % 2], tp, tpb)

        def gu_mm(t, j):
            sl = j % 2
            h = hT[t % 2]
            for which, Wm, Wb in ((0, Wg, Wgb), (1, Wu, Wub)):
                for k in range(8):
                    P.op('pe', lambda e, k=k, which=which, Wm=Wm, sl=sl, h=h: e.matmul(
                        gu[sl][:, which, :], lhsT=Wm[:, k, j * 128:(j + 1) * 128], rhs=h[:, k, :],
                        start=(k == 0), stop=(k == 7)), r=[Wb[j // 4], hTb[t % 2]], w=[gub[sl]])

        def act_mul(t, j):
            sl = j % 2
            P.op('act', lambda e: e.activation(out=sg[sl][:, :], in_=gu[sl][:, 0, :], func=AF.Silu), r=[gub[sl]], w=[sgb[sl]])
            P.op('dve', lambda e: e.tensor_tensor(out=aT[sl][:, :], in0=sg[sl][:, :], in1=gu[sl][:, 1, :], op=ALU.mult),
                 r=[sgb[sl], gub[sl]], w=[aTb[sl]])

        def down_mm(t, j):
            sl = j % 2
            for s in range(2):
                for h in range(2):
                    P.op('pe', lambda e, s=s, h=h: e.matmul(
                        yp[s][h][:, :], lhsT=aT[sl][:, s * 128:(s + 1) * 128], rhs=Wd[:, j, h * 512:(h + 1) * 512],
                        start=(j == 0), stop=(j == NFC - 1)), r=[aTb[sl], Wdb[j // 4]], w=[ypb[s][h]])

        def resid(t):
            x = xt[t % 2]
            for s in range(2):
                for h in range(2):
                    P.op('dve', lambda e, s=s, h=h: e.scalar_tensor_tensor(
                        out=x[:, s, h * 512:(h + 1) * 512], in0=yp[s][h][:, :], scalar=0.5,
                        in1=x[:, s, h * 512:(h + 1) * 512], op0=ALU.mult, op1=ALU.add), r=[ypb[s][h], xtb[t % 2]], w=[xtb[t % 2]])
            P.dma(xout[t], x[:, :, :], r=[xtb[t % 2]], sig=f"xt{t % 2}")

        load_x(0)
        if ntile > 1:
            load_x(1)
        norm(0)
        for t in range(ntile):
            gu_mm(t, 0)
            for j in range(NFC):
                if j + 1 < NFC:
                    gu_mm(t, j + 1)
                if j == 6 and t + 1 < ntile:
                    norm(t + 1)
                act_mul(t, j)
                down_mm(t, j)
            resid(t)
            if t + 2 < ntile:
                load_x(t + 2)
        P.flush()


def phase_ple(P, C, x_in, x_out, p_in, wgate, wproj, gain_in, gain_out, S):
    nc = P.nc
    ntile = S // 128
    with ExitStack() as es:
        def sb(name, shape, dt):
            return es.enter_context(nc.sbuf_tensor(uname(name), shape, dt))

        def ps(name, shape, dt=F32):
            return es.enter_context(nc.psum_tensor(uname(name), shape, dt))
        Wg = sb("pWg", [128, 8, D], BF16)
        Wp = sb("pWp", [128, 2, D], BF16)
        Wgb = [P.buf(f"pWg{k}") for k in range(8)]
        Wpb = P.buf("pWp")
        gbc = sb("gbc", [128, D], F32)
        C.gbcb = P.buf("gbc")
        g4 = sb("g4", [128, D], F32)
        g4b = P.buf("g4")
        B = [ps(f"B{i}", [128, 512]) for i in range(8)]
        Bb = [P.buf(f"B{i}") for i in range(8)]

        def alloc_tile(i):
            T = Ctx()
            for nm, shape, dt in (("pb", [128, 256], BF16),
                                  ("pT", [128, 2, 128], BF16), ("hT", [128, 8, 128], BF16), ("sg0", [128, 512], F32),
                                  ("sg1", [128, 512], F32), ("ee", [128, D], F32), ("st", [128, 8], F32)):
                setattr(T, nm, sb(f"{nm}_{i}", shape, dt))
                setattr(T, nm + "b", P.buf(f"{nm}_{i}"))
            T.i = i
            T.N = norm_scratch(P, sb, i, 1)
            T.bk = (2 * i, 2 * i + 1)
            return T
        NW = 4
        TB = [alloc_tile(i) for i in range(NW)]
        xts = [sb(f"xt4_{i}", [128, 1, D], F32) for i in range(2 * NW)]
        xtsb = [P.buf(f"xt4_{i}") for i in range(2 * NW)]
        pts = [sb(f"pt4_{i}", [128, 256], F32) for i in range(2 * NW)]
        ptsb = [P.buf(f"pt4_{i}") for i in range(2 * NW)]

        P.dma(gbc[:, :], gain_in.partition_broadcast(128), w=[C.gbcb], sig="gbc")
        P.dma(g4[:, :], gain_out.partition_broadcast(128), w=[g4b], sig="g4")
        wgr = wgate.rearrange("(kc p) f -> p kc f", p=128)
        wpr = wproj.rearrange("(kc p) f -> p kc f", p=128)
        P.dma(Wg[:, :, :], wgr[:, :, :], w=Wgb, sig="W0", eng='pool')
        P.dma(Wp[:, :, :], wpr[:, :, :], w=[Wpb], sig="W1", eng='pool')

        xin = x_in.rearrange("(t s p) d -> t p s d", s=1, p=128)
        xout = x_out.rearrange("(t s p) d -> t p s d", s=1, p=128)
        pin = p_in.rearrange("(t p) d -> t p d", p=128)

        def load_x(t):
            P.dma(xts[t % (2 * NW)][:, :, :], xin[t], w=[xtsb[t % (2 * NW)]], sig=f"xt{t % (2 * NW)}")
            P.dma(pts[t % (2 * NW)][:, :], pin[t], w=[ptsb[t % (2 * NW)]], sig=f"pt{t % (2 * NW)}")

        def body(t):
            T = TB[t % NW]
            xt_, xtb_, pt_, ptb_ = xts[t % (2 * NW)], xtsb[t % (2 * NW)], pts[t % (2 * NW)], ptsb[t % (2 * NW)]
            a, b = T.bk
            tp = B[a].bitcast(BF16).rearrange("p (k t) -> p k t", k=8)
            tpp = B[b].bitcast(BF16)[:, 0:256].rearrange("p (k t) -> p k t", k=2)
            emit_norm_T(P, C, xt_, xtb_, gbc[:, :], T.hT, T.hTb, tp, Bb[a], nsub=1, N=T.N)
            P.op('act', lambda e: e.activation(out=T.pb[:, :], in_=pt_[:, :], func=AF.Copy), r=[ptb_], w=[T.pbb])
            for kc in range(2):
                P.op('pe', lambda e, kc=kc: e.transpose(out=tpp[:, kc, :], in_=T.pb[:, kc * 128:(kc + 1) * 128], identity=C.ident[:, :]),
                     r=[T.pbb, C.constb], w=[Bb[b]])
            P.op('dve', lambda e: e.tensor_copy(out=T.pT[:, :, :], in_=tpp), r=[Bb[b]], w=[T.pTb])
            sg = (T.sg0, T.sg1)
            sgb = (T.sg0b, T.sg1b)
            for hh in range(2):
                for k in range(8):
                    P.op('pe', lambda e, hh=hh, k=k: e.matmul(B[a][:, :], lhsT=T.hT[:, k, :], rhs=Wg[:, k, hh * 512:(hh + 1) * 512],
                                                              start=(k == 0), stop=(k == 7)), r=[T.hTb, Wgb[k]], w=[Bb[a]])
                for kc in range(2):
                    P.op('pe', lambda e, hh=hh, kc=kc: e.matmul(B[b][:, :], lhsT=T.pT[:, kc, :], rhs=Wp[:, kc, hh * 512:(hh + 1) * 512],
                                                                start=(kc == 0), stop=(kc == 1)), r=[T.pTb, Wpb], w=[Bb[b]])
                P.op('act', lambda e, hh=hh: e.activation(out=sg[hh][:, :], in_=B[a][:, :], func=AF.Sigmoid), r=[Bb[a]], w=[sgb[hh]])
                P.op('dve', lambda e, hh=hh: e.tensor_tensor(out=T.ee[:, hh * 512:(hh + 1) * 512], in0=sg[hh][:, :], in1=B[b][:, :],
                                                             op=ALU.mult), r=[sgb[hh], Bb[b]], w=[T.eeb])
            P.op('act', lambda e: e.activation(out=C.junk[:, :], in_=T.ee[:, :], func=AF.Square, accum_out=T.st[:, 0:1]),
                 r=[T.eeb], w=[T.stb, C.junkb])
            P.op('dve', lambda e: e.tensor_scalar(out=T.st[:, 1:2], in0=T.st[:, 0:1], scalar1=1.0 / D, scalar2=EPS, op0=ALU.mult,
                                                  op1=ALU.add), r=[T.stb], w=[T.stb])
            P.op('pool', lambda e: e.tensor_tensor(out=T.st[:, 2:3], in0=T.st[:, 1:2], in1=C.neghalf[:, 0:1], op=ALU.pow),
                 r=[T.stb, C.constb], w=[T.stb])
            P.op('dve', lambda e: e.scalar_tensor_tensor(out=T.ee[:, :], in0=T.ee[:, :], scalar=T.st[:, 2:3], in1=g4[:, :],
                                                         op0=ALU.mult, op1=ALU.mult), r=[T.eeb, T.stb, g4b], w=[T.eeb])
            P.op('dve', lambda e: e.tensor_tensor(out=xt_[:, 0, :], in0=xt_[:, 0, :], in1=T.ee[:, :], op=ALU.add),
                 r=[T.eeb, xtb_], w=[xtb_])
            P.dma(xout[t], xt_[:, :, :], r=[xtb_], sig=f"xt{t % (2 * NW)}")

        for t in range(min(2 * NW, ntile)):
            load_x(t)
        for t0 in range(0, ntile, NW):
            ts = list(range(t0, min(t0 + NW, ntile)))
            P.replay([P.capture(lambda t=t: body(t)) for t in ts])
            for t in range(t0 + 2 * NW, min(t0 + 3 * NW, ntile)):
                load_x(t)
        P.flush()

NH = 8
QK = 192
ATTN_SCALE = 192 ** -0.5
PI = 3.14159265358979


def phase_mla1(P, C, x_in, Wd_, l, j, S, scrQ):
    nc = P.nc
    W = Wd_
    ntile = S // 128
    QTn, QTr, KTn, KTr, Vs = scrQ
    with ExitStack() as es:
        def sb(name, shape, dt):
            return es.enter_context(nc.sbuf_tensor(uname(name), shape, dt))

        def ps(name, shape, dt=F32):
            return es.enter_context(nc.psum_tensor(uname(name), shape, dt))
        Win = sb("Win", [128, 8, 704], BF16)
        Wq = sb("Wq", [128, 3, 1536], BF16)
        Wkv = sb("Wkv", [128, 2, 2048], BF16)
        Winb, Wqb, Wkvb = P.buf("Win"), P.buf("Wq"), P.buf("Wkv")
        gbc = sb("gbc", [128, D], F32)
        C.gbcb = P.buf("gbc")
        qa = sb("qa", [128, 384], F32)
        kva = sb("kva", [128, 256], F32)
        qn_g = sb("qn_g", [128, 192], F32)
        kn_g = sb("kn_g", [128, 192], F32)
        invf = sb("invf", [128, 2, 32], F32)
        gb = P.buf("gains")
        xt = [sb(f"xt{i}", [128, 1, D], F32) for i in range(4)]
        xtb = [P.buf(f"xt{i}") for i in range(4)]
        posi = [sb(f"posi{i}", [128, 1], I32) for i in range(4)]
        posb = [P.buf(f"posi{i}") for i in range(4)]
        C.xs = sb("xs", [128, 1, D], BF16)
        C.xsb = [P.buf("xs0")]
        hT = [sb(f"hT{i}", [128, 8, 128], BF16) for i in range(2)]
        hTb = [P.buf(f"hT{i}") for i in range(2)]
        def alloc_tile(i):
            T = Ctx()
            for nm, shape, dt in (("zt", [128, 704], F32), ("cn", [128, 640], BF16), ("cT", [128, 5, 128], BF16),
                                  ("qf", [128, 8, 192], F32), ("qs", [128, 8, 192], F32), ("qb16", [128, 8, 192], BF16),
                                  ("kf", [128, 8, 128], F32), ("kn16", [128, 8, 192], BF16), ("vb", [128, 8, 128], BF16),
                                  ("sm", [128, 64], F32), ("cs", [128, 2, 32], F32), ("rt", [128, 4, 8, 32], F32)):
                setattr(T, nm, sb(f"{nm}_{i}", shape, dt))
                setattr(T, nm + "b", P.buf(f"{nm}_{i}"))
            T.ang = sb(f"ang_{i}", [128, 2, 32], F32)
            T.angi = sb(f"angi_{i}", [128, 2, 32], I32)
            T.angk = sb(f"angk_{i}", [128, 2, 32], F32)
            T.msk = sb(f"msk_{i}", [128, 2, 32], F32)
            T.angb = P.buf(f"ang_{i}")
            T.kg = sb(f"kg_{i}", [128, 64], F32)
            T.kr = sb(f"kr_{i}", [128, 64], F32)
            T.kgb = P.buf(f"kg_{i}")
            T.oTn = [sb(f"oTn{i}_{q}", [128, 8, 128], BF16) for q in range(2)]
            T.oTnb = [P.buf(f"oTn{i}_{q}") for q in range(2)]
            T.oTr = [sb(f"oTr{i}_{q}", [64, 8, 128], BF16) for q in range(2)]
            T.oTrb = [P.buf(f"oTr{i}_{q}") for q in range(2)]
            T.i = i
            return T
        TB = [alloc_tile(0), alloc_tile(1)]
        NS = [norm_scratch(P, sb, i, 1) for i in range(2)]
        B = [ps(f"B{i}", [128, 512]) for i in range(8)]
        Bb = [P.buf(f"B{i}") for i in range(8)]
        def bankset(par):
            ia, ib, ic, id_ = 4 * par, 4 * par + 1, 4 * par + 2, 4 * par + 3
            R = Ctx()
            R.tp = B[id_].bitcast(BF16).rearrange("p (k t) -> p k t", k=8)
            R.tpi = id_
            R.z = (ia, ib)
            R.tpc = B[ib].bitcast(BF16)[:, 384:1024].rearrange("p (k t) -> p k t", k=5)
            R.tpci = ib
            R.q = (ia, ic, id_)
            R.kv = (ib, ia, ic, id_)
            R.tq_n = B[ia].bitcast(BF16).rearrange("p (k t) -> p k t", k=8)
            R.tq_r = B[ib].bitcast(BF16).rearrange("p (k t) -> p k t", k=8)
            R.tqi = (ia, ib)
            return R
        BS = [bankset(0), bankset(1)]

        ng = W['norm_gains']
        P.dma(gbc[:, :], ng[l, 1, :].partition_broadcast(128), w=[gb], sig="gbc", par=True)
        C.gbcb = gb
        P.dma(qa[:, :], W['mla_q_a_norm'][j, :].partition_broadcast(128), w=[gb], sig="qa", par=True)
        P.dma(kva[:, :], W['mla_kv_a_norm'][j, :].partition_broadcast(128), w=[gb], sig="kva", par=True)
        P.dma(qn_g[:, :], W['mla_q_norm'][j, :].partition_broadcast(128), w=[gb], sig="qn_g", par=True)
        P.dma(kn_g[:, :], W['mla_k_norm'][j, :].partition_broadcast(128), w=[gb], sig="kn_g", par=True)
        P.dma(invf[:, :, :], W['invf'].partition_broadcast(128), w=[gb], sig="invf", par=True)
        win_r = W['mla_w_in'][j].rearrange("(kc p) f -> p kc f", p=128)
        P.dma(Win[:, :, :], win_r[:, :, :], w=[Winb], sig="W0", eng='pool')
        wq_r = W['mla_w_q_b'][j].rearrange("(kc p) f -> p kc f", p=128)
        P.dma(Wq[:, :, :], wq_r[:, :, :], w=[Wqb], sig="W1", eng='pool')
        wkv_r = W['mla_w_kv_b'][j].rearrange("(kc p) f -> p kc f", p=128)
        P.dma(Wkv[:, :, 0:1024], wkv_r[:, :, 0:1024], w=[Wkvb], sig="W2", eng='pool')
        P.dma(Wkv[:, :, 1024:2048], wkv_r[:, :, 1024:2048], w=[Wkvb], sig="W3", eng='pool')

        xin = x_in.rearrange("(t s p) d -> t p s d", s=1, p=128)
        pin = W['positions'].rearrange("(t p o) -> t p o", p=128, o=1)

        def load_x(t):
            P.dma(xt[t % 4][:, :, :], xin[t], w=[xtb[t % 4]], sig=f"xt{t % 4}")
            P.dma(posi[t % 4][:, :], pin[t], w=[posb[t % 4]], sig=f"posi{t % 4}")

        def norm(t):
            emit_norm_T(P, C, xt[t % 4], xtb[t % 4], gbc[:, :], hT[t % 2], hTb[t % 2], BS[t % 2].tp, Bb[BS[t % 2].tpi], nsub=1, N=NS[t % 2])

        def rope(T, src3, dst3, nh, tag):
            x1 = src3[:, :, 0:32]
            x2 = src3[:, :, 32:64]
            sinb = T.cs[:, 0:1, :].broadcast_to([128, nh, 32])
            cosb = T.cs[:, 1:2, :].broadcast_to([128, nh, 32])
            r = T.rt
            P.op('dve', lambda e: e.tensor_tensor(out=r[:, 0, 0:nh, :], in0=x1, in1=cosb, op=ALU.mult), r=[tag, T.csb], w=[T.rtb])
            P.op('dve', lambda e: e.tensor_tensor(out=r[:, 1, 0:nh, :], in0=x2, in1=sinb, op=ALU.mult), r=[tag, T.csb], w=[T.rtb])
            P.op('dve', lambda e: e.tensor_tensor(out=r[:, 2, 0:nh, :], in0=x2, in1=cosb, op=ALU.mult), r=[tag, T.csb], w=[T.rtb])
            P.op('dve', lambda e: e.tensor_tensor(out=r[:, 3, 0:nh, :], in0=x1, in1=sinb, op=ALU.mult), r=[tag, T.csb], w=[T.rtb])
            return r

        def body(t):
            T = TB[t % 2]
            R = BS[t % 2]
            h = hT[t % 2]
            norm(t)
            yield
            yield
            pf = T.sm[:, 40:41]
            P.op('dve', lambda e: e.tensor_copy(out=pf, in_=posi[t % 4][:, :]), r=[posb[t % 4]], w=[T.angb])
            P.op('dve', lambda e: e.tensor_scalar(out=T.ang[:, :, :], in0=invf[:, :, :], scalar1=pf, scalar2=None, op0=ALU.mult),
                 r=[T.angb, gb], w=[T.angb])
            P.op('dve', lambda e: e.tensor_scalar(out=T.ang[:, 1, :], in0=T.ang[:, 1, :], scalar1=PI / 2, scalar2=None, op0=ALU.add),
                 r=[T.angb], w=[T.angb])
            P.op('dve', lambda e: e.tensor_scalar(out=T.angk[:, :, :], in0=T.ang[:, :, :], scalar1=1.0 / (2 * PI), scalar2=None,
                                                  op0=ALU.mult), r=[T.angb], w=[T.angb])
            P.op('dve', lambda e: e.tensor_copy(out=T.angi[:, :, :], in_=T.angk[:, :, :]), r=[T.angb], w=[T.angb])
            P.op('dve', lambda e: e.tensor_copy(out=T.angk[:, :, :], in_=T.angi[:, :, :]), r=[T.angb], w=[T.angb])
            P.op('dve', lambda e: e.scalar_tensor_tensor(out=T.ang[:, :, :], in0=T.angk[:, :, :], scalar=-2 * PI, in1=T.ang[:, :, :],
                                                         op0=ALU.mult, op1=ALU.add), r=[T.angb], w=[T.angb])
            P.op('dve', lambda e: e.tensor_scalar(out=T.msk[:, :, :], in0=T.ang[:, :, :], scalar1=PI, scalar2=-2 * PI, op0=ALU.is_gt,
                                                  op1=ALU.mult), r=[T.angb], w=[T.angb])
            P.op('dve', lambda e: e.tensor_tensor(out=T.ang[:, :, :], in0=T.ang[:, :, :], in1=T.msk[:, :, :], op=ALU.add), r=[T.angb], w=[T.angb])
            P.op('dve', lambda e: e.tensor_scalar(out=T.msk[:, :, :], in0=T.ang[:, :, :], scalar1=-PI, scalar2=2 * PI, op0=ALU.is_lt,
                                                  op1=ALU.mult), r=[T.angb], w=[T.angb])
            P.op('dve', lambda e: e.tensor_tensor(out=T.ang[:, :, :], in0=T.ang[:, :, :], in1=T.msk[:, :, :], op=ALU.add), r=[T.angb], w=[T.angb])
            P.op('dve', lambda e: e.tensor_scalar(out=T.ang[:, :, :], in0=T.ang[:, :, :], scalar1=PI, scalar2=-PI, op0=ALU.min,
                                                  op1=ALU.max), r=[T.angb], w=[T.angb])
            P.op('act', lambda e: e.activation(out=T.cs[:, :, :], in_=T.ang[:, :, :], func=AF.Sin), r=[T.angb], w=[T.csb])
            yield
            for c0, c1, bk in ((0, 512, R.z[0]), (512, 704, R.z[1])):
                for k in range(8):
                    P.op('pe', lambda e, k=k, c0=c0, c1=c1, bk=bk: e.matmul(B[bk][:, 0:c1 - c0], lhsT=h[:, k, :], rhs=Win[:, k, c0:c1],
                                                                          start=(k == 0), stop=(k == 7)),
                         r=[hTb[t % 2], Winb], w=[Bb[bk]])
            P.op('act', lambda e: e.activation(out=T.zt[:, 0:512], in_=B[R.z[0]][:, :], func=AF.Copy), r=[Bb[R.z[0]]], w=[T.ztb])
            P.op('act', lambda e: e.activation(out=T.zt[:, 512:704], in_=B[R.z[1]][:, 0:192], func=AF.Copy), r=[Bb[R.z[1]]], w=[T.ztb])
            yield
            P.op('act', lambda e: e.activation(out=C.junk[:, 0:384], in_=T.zt[:, 0:384], func=AF.Square, accum_out=T.sm[:, 0:1]),
                 r=[T.ztb], w=[T.smb, C.junkb])
            P.op('act', lambda e: e.activation(out=C.junk[:, 0:256], in_=T.zt[:, 384:640], func=AF.Square, accum_out=T.sm[:, 1:2]),
                 r=[T.ztb], w=[T.smb, C.junkb])
            P.op('act', lambda e: e.activation(out=C.junk[:, 0:64], in_=T.zt[:, 640:704], func=AF.Square, accum_out=T.sm[:, 2:3]),
                 r=[T.ztb], w=[T.smb, C.junkb])
            P.op('dve', lambda e: e.tensor_scalar(out=T.sm[:, 4:5], in0=T.sm[:, 0:1], scalar1=1.0 / 384, scalar2=EPS, op0=ALU.mult,
                                                  op1=ALU.add), r=[T.smb], w=[T.smb])
            P.op('dve', lambda e: e.tensor_scalar(out=T.sm[:, 5:6], in0=T.sm[:, 1:2], scalar1=1.0 / 256, scalar2=EPS, op0=ALU.mult,
                                                  op1=ALU.add), r=[T.smb], w=[T.smb])
            P.op('pool', lambda e: e.tensor_tensor(out=T.sm[:, 6:8], in0=T.sm[:, 4:6], in1=C.neghalf[:, 0:2], op=ALU.pow),
                 r=[T.smb, C.constb], w=[T.smb])
            P.op('dve', lambda e: e.scalar_tensor_tensor(out=T.cn[:, 0:384], in0=T.zt[:, 0:384], scalar=T.sm[:, 6:7], in1=qa[:, :],
                                                         op0=ALU.mult, op1=ALU.mult), r=[T.ztb, T.smb, gb], w=[T.cnb])
            P.op('dve', lambda e: e.scalar_tensor_tensor(out=T.cn[:, 384:640], in0=T.zt[:, 384:640], scalar=T.sm[:, 7:8], in1=kva[:, :],
                                                         op0=ALU.mult, op1=ALU.mult), r=[T.ztb, T.smb, gb], w=[T.cnb])
            for k in range(5):
                P.op('pe', lambda e, k=k: e.transpose(out=R.tpc[:, k, :], in_=T.cn[:, k * 128:(k + 1) * 128], identity=C.ident[:, :]),
                     r=[T.cnb, C.constb], w=[Bb[R.tpci]])
            P.op('dve', lambda e: e.tensor_copy(out=T.cT[:, :, :], in_=R.tpc), r=[Bb[R.tpci]], w=[T.cTb])
            yield
            for c in range(3):
                for k in range(3):
                    P.op('pe', lambda e, c=c, k=k: e.matmul(B[R.q[c]][:, :], lhsT=T.cT[:, k, :], rhs=Wq[:, k, c * 512:(c + 1) * 512],
                                                            start=(k == 0), stop=(k == 2)), r=[T.cTb, Wqb], w=[Bb[R.q[c]]])
            qf2 = T.qf[:, :, :].rearrange("p h d -> p (h d)")
            for c in range(3):
                P.op('act', lambda e, c=c: e.activation(out=qf2[:, c * 512:(c + 1) * 512], in_=B[R.q[c]][:, :], func=AF.Copy),
                     r=[Bb[R.q[c]]], w=[T.qfb])
            yield
            kvb = R.kv
            for c in range(4):
                for k in range(2):
                    P.op('pe', lambda e, c=c, k=k: e.matmul(B[kvb[c]][:, :], lhsT=T.cT[:, 3 + k, :], rhs=Wkv[:, k, c * 512:(c + 1) * 512],
                                                            start=(k == 0), stop=(k == 1)), r=[T.cTb, Wkvb], w=[Bb[kvb[c]]])
            for c in range(4):
                kv3 = B[kvb[c]][:, :].rearrange("p (h d) -> p h d", h=2)
                P.op('act', lambda e, c=c, kv3=kv3: e.activation(out=T.kf[:, 2 * c:2 * c + 2, :], in_=kv3[:, :, 0:128], func=AF.Copy),
                     r=[Bb[kvb[c]]], w=[T.kfb])
                P.op('act', lambda e, c=c, kv3=kv3: e.activation(out=T.vb[:, 2 * c:2 * c + 2, :], in_=kv3[:, :, 128:256], func=AF.Copy),
                     r=[Bb[kvb[c]]], w=[T.vbb])
            P.dma(Vs[t * 128:(t + 1) * 128, :, :], T.vb[:, :, :], r=[T.vbb], sig=f"vb{T.i}")
            yield
            for hh in range(NH):
                P.op('act', lambda e, hh=hh: e.activation(out=C.junk[:, 0:192], in_=T.qf[:, hh, :], func=AF.Square,
                                                           accum_out=T.sm[:, 8 + hh:9 + hh]), r=[T.qfb], w=[T.smb, C.junkb])
            P.op('dve', lambda e: e.tensor_scalar(out=T.sm[:, 16:24], in0=T.sm[:, 8:16], scalar1=1.0 / QK, scalar2=EPS, op0=ALU.mult,
                                                  op1=ALU.add), r=[T.smb], w=[T.smb])
            P.op('pool', lambda e: e.tensor_tensor(out=T.sm[:, 24:32], in0=T.sm[:, 16:24], in1=C.neghalf[:, 0:8], op=ALU.pow),
                 r=[T.smb, C.constb], w=[T.smb])
            P.op('dve', lambda e: e.tensor_tensor(out=T.qs[:, :, :], in0=T.qf[:, :, :],
                                                  in1=T.sm[:, 24:32].unsqueeze(2).broadcast_to([128, NH, QK]), op=ALU.mult),
                 r=[T.qfb, T.smb], w=[T.qsb])
            P.op('dve', lambda e: e.tensor_tensor(out=T.qs[:, :, :], in0=T.qs[:, :, :], in1=qn_g[:, :].unsqueeze(1).broadcast_to([128, NH, QK]),
                                                  op=ALU.mult), r=[T.qsb, gb], w=[T.qsb])
            P.op('act', lambda e: e.activation(out=T.qb16[:, :, 0:128], in_=T.qs[:, :, 0:128], func=AF.Copy), r=[T.qsb], w=[T.qb16b])
            r = rope(T, T.qs[:, :, 128:192], None, NH, T.qsb)
            P.op('dve', lambda e: e.tensor_tensor(out=T.qb16[:, :, 128:160], in0=r[:, 0, :, :], in1=r[:, 1, :, :], op=ALU.subtract),
                 r=[T.rtb], w=[T.qb16b])
            P.op('dve', lambda e: e.tensor_tensor(out=T.qb16[:, :, 160:192], in0=r[:, 2, :, :], in1=r[:, 3, :, :], op=ALU.add),
                 r=[T.rtb], w=[T.qb16b])
            yield
            for hh in range(NH):
                P.op('act', lambda e, hh=hh: e.activation(out=C.junk[:, 0:128], in_=T.kf[:, hh, :], func=AF.Square,
                                                           accum_out=T.sm[:, 32 + hh:33 + hh]), r=[T.kfb], w=[T.smb, C.junkb])
            P.op('dve', lambda e: e.tensor_scalar(out=T.sm[:, 48:56], in0=T.sm[:, 32:40], scalar1=T.sm[:, 2:3], scalar2=1.0 / QK, op0=ALU.add,
                                                  op1=ALU.mult), r=[T.smb], w=[T.smb])
            P.op('dve', lambda e: e.tensor_scalar(out=T.sm[:, 48:56], in0=T.sm[:, 48:56], scalar1=EPS, scalar2=None, op0=ALU.add),
                 r=[T.smb], w=[T.smb])
            P.op('pool', lambda e: e.tensor_tensor(out=T.sm[:, 56:64], in0=T.sm[:, 48:56], in1=C.neghalf[:, 0:8], op=ALU.pow),
                 r=[T.smb, C.constb], w=[T.smb])
            P.op('dve', lambda e: e.tensor_tensor(out=T.kf[:, :, :], in0=T.kf[:, :, :],
                                                  in1=T.sm[:, 56:64].unsqueeze(2).broadcast_to([128, NH, 128]), op=ALU.mult),
                 r=[T.kfb, T.smb], w=[T.kfb])
            P.op('dve', lambda e: e.tensor_tensor(out=T.kn16[:, :, 0:128], in0=T.kf[:, :, :],
                                                  in1=kn_g[:, 0:128].unsqueeze(1).broadcast_to([128, NH, 128]), op=ALU.mult),
                 r=[T.kfb, gb], w=[T.kn16b])
            P.op('dve', lambda e: e.tensor_tensor(out=T.kg[:, :], in0=T.zt[:, 640:704], in1=kn_g[:, 128:192], op=ALU.mult),
                 r=[T.ztb, gb], w=[T.kgb])
            r = rope(T, T.kg[:, :].unsqueeze(1), None, 1, T.kgb)
            P.op('dve', lambda e: e.tensor_tensor(out=T.kr[:, 0:32], in0=r[:, 0, 0, :], in1=r[:, 1, 0, :], op=ALU.subtract),
                 r=[T.rtb], w=[T.kgb])
            P.op('dve', lambda e: e.tensor_tensor(out=T.kr[:, 32:64], in0=r[:, 2, 0, :], in1=r[:, 3, 0, :], op=ALU.add),
                 r=[T.rtb], w=[T.kgb])
            P.op('dve', lambda e: e.tensor_tensor(out=T.kn16[:, :, 128:192], in0=T.kr[:, :].unsqueeze(1).broadcast_to([128, NH, 64]),
                                                  in1=T.sm[:, 56:64].unsqueeze(2).broadcast_to([128, NH, 64]), op=ALU.mult),
                 r=[T.kgb, T.smb], w=[T.kn16b])
            yield
            for (src, srcb, dn, dr, sl) in ((T.qb16, T.qb16b, QTn, QTr, 0), (T.kn16, T.kn16b, KTn, KTr, 1)):
                for hh in range(NH):
                    P.op('pe', lambda e, hh=hh, src=src: e.transpose(out=R.tq_n[:, hh, :], in_=src[:, hh, 0:128], identity=C.ident[:, :]),
                         r=[srcb, C.constb], w=[Bb[R.tqi[0]]])
                for hh in range(NH):
                    P.op('pe', lambda e, hh=hh, src=src: e.transpose(out=R.tq_r[0:64, hh, :], in_=src[:, hh, 128:192],
                                                                     identity=C.ident[:, :]), r=[srcb, C.constb], w=[Bb[R.tqi[1]]])
                P.op('act', lambda e, sl=sl: e.activation(out=T.oTn[sl][:, :, :], in_=R.tq_n, func=AF.Copy), r=[Bb[R.tqi[0]]], w=[T.oTnb[sl]])
                P.op('dve', lambda e, sl=sl: e.tensor_copy(out=T.oTr[sl][:, :, :], in_=R.tq_r[0:64, :, :]), r=[Bb[R.tqi[1]]], w=[T.oTrb[sl]])
                P.dma(dn[:, :, t * 128:(t + 1) * 128].rearrange("h d t -> d h t"), T.oTn[sl][:, :, :], r=[T.oTnb[sl]], sig=f"oTn{T.i}_{sl}")
                P.dma(dr[:, :, t * 128:(t + 1) * 128].rearrange("h d t -> d h t"), T.oTr[sl][:, :, :], r=[T.oTrb[sl]], sig=f"oTr{T.i}_{sl}")

        for t in range(min(4, ntile)):
            load_x(t)
        for t0 in range(0, ntile, 2):
            lists = [P.capture(lambda t=t: [None for _ in body(t)]) for t in range(t0, min(t0 + 2, ntile))]
            P.replay(lists)
            for t in range(t0 + 4, min(t0 + 6, ntile)):
                load_x(t)
        P.flush()


def phase_mla2(P, C, x_in, x_out, Wd_, j, g, S, scrQ):
    nc = P.nc
    W = Wd_
    QTn, QTr, KTn, KTr, Vs = scrQ
    nkt = S // 128
    QB = 512 if S >= 512 else S
    nqb = S // QB
    nsubq = QB // 128
    HG = 4
    with ExitStack() as es:
        def sb(name, shape, dt):
            return es.enter_context(nc.sbuf_tensor(uname(name), shape, dt))

        def ps(name, shape, dt=F32):
            return es.enter_context(nc.psum_tensor(uname(name), shape, dt))
        Kn = sb("Kn", [128, HG, S], BF16)
        Kr = sb("Kr", [64, HG, S], BF16)
        Vt = sb("Vt", [128, nkt, HG, 128], BF16)
        Knb = [P.buf(f"Kn{h}") for h in range(HG)]
        Krb = [P.buf(f"Kr{h}") for h in range(HG)]
        Vtb = P.buf("Vt")
        Wo = sb("Wo", [128, HG, D], BF16)
        Wob = P.buf("Wo")
        ones = sb("ones", [128, 128], BF16)
        tri = sb("tri", [128, 128], BF16)
        trif = sb("trif", [128, 128], F32)
        cb = P.buf("mconst")
        xt = sb("xt", [128, nsubq, D], F32)
        xtb = P.buf("xt")
        Qn = [sb(f"Qn{i}", [128, HG, QB], BF16) for i in range(2)]
        Qr = [sb(f"Qr{i}", [64, HG, QB], BF16) for i in range(2)]
        Qb_ = [P.buf(f"Q{i}") for i in range(2)]
        pT = [sb(f"pT{i}", [128, QB], BF16) for i in range(3)]
        pTb = [P.buf(f"pT{i}") for i in range(3)]
        rc = sb("rc", [128, QB], F32)
        rcb = P.buf("rc")
        aT = [sb(f"aT{h}", [128, QB], BF16) for h in range(HG)]
        aTb = [P.buf(f"aT{h}") for h in range(HG)]
        sT = [ps(f"sT{i}", [128, 512]) for i in range(2)]
        sTb = [P.buf(f"sT{i}") for i in range(2)]
        oT = [ps(f"oT{i}", [128, 512]) for i in range(2)]
        oTb = [P.buf(f"oT{i}") for i in range(2)]
        lS = [ps(f"lS{i}", [128, 512]) for i in range(2)]
        lSb = [P.buf(f"lS{i}") for i in range(2)]
        Y = [ps(f"Y{i}", [128, 512]) for i in range(2)]
        Yb = [P.buf(f"Y{i}") for i in range(2)]

        P.op('pool', lambda e: e.memset(ones[:, :], 1.0), w=[cb])
        P.op('pool', lambda e: e.memset(trif[:, :], 1.0), w=[cb])
        P.op('pool', lambda e: e.affine_select(out=trif[:, :], in_=trif[:, :], pattern=[[1, 128]], compare_op=ALU.is_ge, fill=0.0,
                                               base=0, channel_multiplier=-1), r=[cb], w=[cb])
        P.op('pool', lambda e: e.tensor_copy(out=tri[:, :], in_=trif[:, :]), r=[cb], w=[cb])
        for h in range(HG):
            P.dma(Kn[:, h, :], KTn[g * HG + h, :, :], w=[Knb[h]], sig=f"Kn{h}")
            P.dma(Kr[:, h, :], KTr[g * HG + h, :, :], w=[Krb[h]], sig=f"Kr{h}")
            if h == 0:
                P.dma(Vt[:, :, :, :], Vs[:, g * HG:(g + 1) * HG, :].rearrange("(kt p) h e -> p kt h e", p=128), w=[Vtb], sig="Vt")
        wo_r = W['mla_w_out'][j][g * HG * 128:(g + 1) * HG * 128, :].rearrange("(h p) d -> p h d", p=128)
        P.dma(Wo[:, :, :], wo_r[:, :, :], w=[Wob], sig="W0", eng='pool')

        xin = x_in.rearrange("(b s p) d -> b p s d", s=nsubq, p=128)
        xout = x_out.rearrange("(b s p) d -> b p s d", s=nsubq, p=128)

        def load_q(b):
            sl = b % 2
            P.dma(Qn[sl][:, :, :], QTn[g * HG:(g + 1) * HG, :, b * QB:(b + 1) * QB].rearrange("h d t -> d h t"), w=[Qb_[sl]],
                  sig=f"Qn{sl}")
            P.dma(Qr[sl][:, :, :], QTr[g * HG:(g + 1) * HG, :, b * QB:(b + 1) * QB].rearrange("h d t -> d h t"), w=[Qb_[sl]],
                  sig=f"Qr{sl}")

        cnt = [0]

        def qk_mm(b, h, kt, ssl, qs_):
            r = kt - nsubq * b
            c0 = max(r, 0) * 128
            P.op('pe', lambda e: e.matmul(sT[ssl][:, c0:QB], lhsT=Kn[:, h, kt * 128:(kt + 1) * 128],
                                          rhs=Qn[qs_][:, h, c0:QB], start=True, stop=False),
                 r=[Knb[h], Qb_[qs_]], w=[sTb[ssl]])
            P.op('pe', lambda e: e.matmul(sT[ssl][:, c0:QB], lhsT=Kr[:, h, kt * 128:(kt + 1) * 128],
                                          rhs=Qr[qs_][:, h, c0:QB], start=False, stop=True),
                 r=[Krb[h], Qb_[qs_]], w=[sTb[ssl]])

        def pv_mm(b, h, kt, nk, ob, ssl, psl):
            r = kt - nsubq * b
            c0 = max(r, 0) * 128
            P.op('act', lambda e: e.activation(out=pT[psl][:, c0:QB], in_=sT[ssl][:, c0:QB], func=AF.Exp,
                                               scale=ATTN_SCALE), r=[sTb[ssl]], w=[pTb[psl]])
            if r >= 0:
                P.op('dve', lambda e: e.tensor_tensor(out=pT[psl][:, c0:c0 + 128], in0=pT[psl][:, c0:c0 + 128],
                                                      in1=tri[:, :], op=ALU.mult), r=[pTb[psl], cb], w=[pTb[psl]])
            P.op('pe', lambda e: e.matmul(oT[ob][:, c0:QB], lhsT=Vt[:, kt, h, :], rhs=pT[psl][:, c0:QB],
                                          start=(kt == 0), stop=(kt == nk - 1)),
                 r=[Vtb, pTb[psl]], w=[oTb[ob]])
            P.op('pe', lambda e: e.matmul(lS[ob][:, c0:QB], lhsT=ones[:, :], rhs=pT[psl][:, c0:QB],
                                          start=(kt == 0), stop=(kt == nk - 1)),
                 r=[cb, pTb[psl]], w=[lSb[ob]])

        def head_fin(b, h, ob):
            P.op('dve', lambda e: e.reciprocal(out=rc[:, :], in_=lS[ob][:, 0:QB]), r=[lSb[ob]], w=[rcb])
            P.op('dve', lambda e: e.tensor_tensor(out=aT[h][:, :], in0=oT[ob][:, 0:QB], in1=rc[:, :], op=ALU.mult),
                 r=[oTb[ob], rcb], w=[aTb[h]])

        def do_block(b, first_issued):
            qs_ = b % 2
            nk = nsubq * (b + 1)
            pairs = [(h, kt) for h in range(HG) for kt in range(nk)]
            base = cnt[0]
            cnt[0] += len(pairs)
            if not first_issued:
                qk_mm(b, pairs[0][0], pairs[0][1], base % 2, qs_)
            for i, (h, kt) in enumerate(pairs):
                if i + 1 < len(pairs):
                    qk_mm(b, pairs[i + 1][0], pairs[i + 1][1], (base + i + 1) % 2, qs_)
                ob = (b * HG + h) % 2
                pv_mm(b, h, kt, nk, ob, (base + i) % 2, (base + i) % 3)
                if kt == nk - 1:
                    head_fin(b, h, ob)

        def do_proj(b, sq, dh):
            for h in range(HG):
                P.op('pe', lambda e, h=h: e.matmul(Y[dh][:, :], lhsT=aT[h][:, sq * 128:(sq + 1) * 128],
                                                   rhs=Wo[:, h, dh * 512:(dh + 1) * 512], start=(h == 0),
                                                   stop=(h == HG - 1)), r=[aTb[h], Wob], w=[Yb[dh]])
            P.op('dve', lambda e: e.tensor_tensor(out=xt[:, sq, dh * 512:(dh + 1) * 512],
                                                  in0=xt[:, sq, dh * 512:(dh + 1) * 512], in1=Y[dh][:, :],
                                                  op=ALU.add), r=[Yb[dh], xtb], w=[xtb])

        load_q(0)
        for b in range(nqb):
            if b + 1 < nqb:
                load_q(b + 1)
            P.dma(xt[:, :, :], xin[b], w=[xtb], sig="xt")
            do_block(b, b > 0)
            if b + 1 < nqb:
                qk_mm(b + 1, 0, 0, cnt[0] % 2, (b + 1) % 2)
            for sq in range(nsubq):
                for dh in range(2):
                    do_proj(b, sq, dh)
            P.dma(xout[b], xt[:, :, :], r=[xtb], sig="xt")
        P.flush()


def phase_even(P, C, x_in, x_out, Wd_, l, j, S):
    nc = P.nc
    W = Wd_
    ntile = S // 128
    with ExitStack() as es:
        def sb(name, shape, dt):
            return es.enter_context(nc.sbuf_tensor(uname(name), shape, dt))

        def ps(name, shape, dt=F32):
            return es.enter_context(nc.psum_tensor(uname(name), shape, dt))
        Win = sb("eWin", [128, 8, 3072], BF16)
        Winb = [P.buf(f"eWin{g}") for g in range(6)]
        Wo = sb("eWo", [128, 8, D], BF16)
        Wob = P.buf("eWo")
        gbc = sb("gbc", [128, D], F32)
        gb = P.buf("gains")
        C.gbcb = gb
        vnorm = sb("vnorm", [128, 512], F32)
        onorm = sb("onorm", [128, 128], F32)
        bs = sb("bs", [128, 8, 1], F32)
        lbr = sb("lbr", [128, 2, 4, 1], F32)
        lb = sb("lb", [128, 4], F32)
        oml = sb("oml", [128, 4], F32)
        wsf = sb("wsf", [128, 8, 128], F32)
        ws16 = sb("ws16", [128, 8, 128], BF16)
        WsT = sb("WsT", [128, 8, 128], BF16)
        trif = sb("trif", [128, 128], F32)
        bmask = sb("bmask", [128, 128], F32)
        rmask = sb("rmask", [128, 512], F32)
        cb = P.buf("econst")
        xt = [sb(f"xt{i}", [128, 1, D], F32) for i in range(4)]
        xtb = [P.buf(f"xt{i}") for i in range(4)]
        hT = [sb(f"hT{i}", [128, 8, 128], BF16) for i in range(2)]
        hTb = [P.buf(f"hT{i}") for i in range(2)]
        S32 = sb("S32", [128, 4, 128], F32)
        S16 = sb("S16", [128, 4, 128], BF16)
        S32b, S16b = P.buf("S32"), P.buf("S16")
        def alloc_tile(i):
            T = Ctx()
            for nm, shape, dt in (("f1", [128, 4, 128], F32), ("kk", [128, 4, 128], F32), ("lf", [128, 4, 128], F32),
                                  ("Bc", [128, 4, 128], F32), ("e1", [128, 4, 128], F32), ("e2", [128, 4, 128], F32),
                                  ("e3", [128, 4, 128], F32), ("eBC", [128, 4, 4, 1], F32), ("qf32", [128, 4, 128], F32),
                                  ("qt", [128, 4, 128], BF16), ("kt_", [128, 4, 128], BF16), ("kh", [128, 4, 128], BF16),
                                  ("khT", [128, 4, 128], BF16), ("v16", [128, 4, 128], BF16), ("gu_", [128, 512], F32),
                                  ("gv", [128, 8, 64], F32), ("sq", [128, 8, 64], F32), ("vn16", [128, 8, 64], BF16),
                                  ("sgl", [128, 512], F32), ("on", [128, 4, 128], F32), ("mT", [128, 8, 128], BF16),
                                  ("sm", [128, 64], F32)):
                setattr(T, nm, sb(f"{nm}_{i}", shape, dt))
                setattr(T, nm + "b", P.buf(f"{nm}_{i}"))
            T.PT = [sb(f"PT{h}_{i}", [128, 128], BF16) for h in range(4)]
            T.PTb = [P.buf(f"PT{h}_{i}") for h in range(4)]
            T.mixed = sb(f"mixed_{i}", [128, D], BF16)
            T.mixb = [P.buf(f"mixA_{i}"), P.buf(f"mixB_{i}")]
            return T
        TB = [alloc_tile(0), alloc_tile(1)]
        NS = [norm_scratch(P, sb, i, 1) for i in range(2)]
        B = [ps(f"B{i}", [128, 512]) for i in range(8)]
        Bb = [P.buf(f"B{i}") for i in range(8)]

        def b3(i, a):
            return B[i][:, :].rearrange("p (a t) -> p a t", a=a)

        def b16(i, a, n):
            return B[i].bitcast(BF16)[:, 0:a * n].rearrange("p (a t) -> p a t", a=a)

        ng = W['norm_gains']
        P.dma(gbc[:, :], ng[l, 1, :].partition_broadcast(128), w=[gb], sig="gbc", par=True)
        P.dma(vnorm[:, :], W['gmlp_v_norm'][j].rearrange("h d -> (h d)").partition_broadcast(128), w=[gb], sig="vnorm", par=True)
        P.dma(onorm[:, :], W['hgrn_out_norm'][j, :].partition_broadcast(128), w=[gb], sig="onorm", par=True)
        P.dma(bs[:, :, :], W['gmlp_b_s'][j].rearrange("h (t o) -> t h o", o=1), w=[gb], sig="bs", slow=True, par=True)
        P.dma(lbr[:, :, :, :], W['hgrn_lb_raw'].rearrange("r (h d o) -> d r h o", h=4, o=1), w=[gb], sig="lbr", slow=True, par=True)
        P.dma(wsf[:, :, :], W['gmlp_w_s'][j].rearrange("h t s -> t h s"), w=[gb], sig="wsf", par=True)
        P.op('pool', lambda e: e.memset(trif[:, :], 1.0), w=[cb])
        P.op('pool', lambda e: e.memset(bmask[:, :], 1.0), w=[cb])
        P.op('pool', lambda e: e.memset(rmask[:, :], 1.0), w=[cb])
        P.op('pool', lambda e: e.memset(rmask[:, :].rearrange("p (c t) -> p c t", t=32)[:, :, 0:1], 0.0), w=[cb])
        P.op('pool', lambda e: e.memset(S32[:, :, :], 0.0), w=[S32b])
        P.op('pool', lambda e: e.memset(S16[:, :, :], 0.0), w=[S16b])
        P.op('pool', lambda e: e.affine_select(out=trif[:, :], in_=trif[:, :], pattern=[[-1, 128]], compare_op=ALU.is_ge, fill=0.0,
                                               base=0, channel_multiplier=1), r=[cb], w=[cb])
        P.op('pool', lambda e: e.affine_select(out=bmask[:, :], in_=bmask[:, :], pattern=[[1, 128]], compare_op=ALU.is_ge, fill=0.0,
                                               base=0, channel_multiplier=-1), r=[cb], w=[cb])
        for cbk in range(1, 4):
            P.op('pool', lambda e, cbk=cbk: e.memset(bmask[0:32 * cbk, 32 * cbk:32 * cbk + 32], 0.0), r=[cb], w=[cb])
        P.op('dve', lambda e: e.tensor_tensor(out=ws16[:, :, :], in0=wsf[:, :, :], in1=trif[:, :].unsqueeze(1).broadcast_to([128, 8, 128]),
                                              op=ALU.mult), r=[gb, cb], w=[cb])
        tws = b16(7, 8, 128)
        for hh in range(8):
            P.op('pe', lambda e, hh=hh: e.transpose(out=tws[:, hh, :], in_=ws16[:, hh, :], identity=C.ident[:, :]), r=[cb, C.constb],
                 w=[Bb[7]])
        P.op('dve', lambda e: e.tensor_copy(out=WsT[:, :, :], in_=tws), r=[Bb[7]], w=[cb])
        if j == 0:
            P.op('pool', lambda e: e.memset(lb[:, :], 0.0), w=[cb])
        else:
            P.op('dve', lambda e: e.tensor_tensor(out=lb[:, :], in0=lbr[:, 1, :, 0], in1=lbr[:, 0, :, 0], op=ALU.subtract), r=[gb], w=[cb])
            P.op('act', lambda e: e.activation(out=lb[:, :], in_=lb[:, :], func=AF.Sigmoid), r=[cb], w=[cb])
            P.op('dve', lambda e: e.tensor_scalar(out=lb[:, :], in0=lb[:, :], scalar1=0.999, scalar2=0.0, op0=ALU.min, op1=ALU.max),
                 r=[cb], w=[cb])
        P.op('dve', lambda e: e.tensor_scalar(out=oml[:, :], in0=lb[:, :], scalar1=-1.0, scalar2=1.0, op0=ALU.mult, op1=ALU.add),
             r=[cb], w=[cb])
        win_r = W['even_w_in'][j].rearrange("(kc p) f -> p kc f", p=128)
        for g in (0, 1, 4, 5, 2, 3):
            P.dma(Win[:, :, g * 512:(g + 1) * 512], win_r[:, :, g * 512:(g + 1) * 512], w=[Winb[g]], sig=f"W{g}", eng='pool')
        wo_r = W['even_w_out'][j].rearrange("(kc p) f -> p kc f", p=128)
        P.dma(Wo[:, :, :], wo_r[:, :, :], w=[Wob], sig="W6", eng='pool')

        xin = x_in.rearrange("(t s p) d -> t p s d", s=1, p=128)
        xout = x_out.rearrange("(t s p) d -> t p s d", s=1, p=128)

        def load_x(t):
            P.dma(xt[t % 4][:, :, :], xin[t], w=[xtb[t % 4]], sig=f"xt{t % 4}")

        def bankset(par):
            R = Ctx()
            R.p = (4 * par, 4 * par + 1, 4 * par + 2)
            R.o = 4 * par + 3
            return R
        BS = [bankset(0), bankset(1)]

        def pre_norm(t):
            p0 = BS[t % 2].p[0]
            tp = B[p0].bitcast(BF16).rearrange("p (k t) -> p k t", k=8)
            emit_norm_T(P, C, xt[t % 4], xtb[t % 4], gbc[:, :], hT[t % 2], hTb[t % 2], tp, Bb[p0], nsub=1, N=NS[t % 2])

        def pre(t):
            T = TB[t % 2]
            R = BS[t % 2]
            p0, p1, p2 = R.p
            h = hT[t % 2]
            hb = hTb[t % 2]

            def proj_tok(bk, c0):
                for k in range(8):
                    P.op('pe', lambda e, k=k: e.matmul(B[bk][:, :], lhsT=h[:, k, :], rhs=Win[:, k, c0:c0 + 512],
                                                       start=(k == 0), stop=(k == 7)), r=[hb, Winb[c0 // 512]], w=[Bb[bk]])

            def proj_feat(bk, c0):
                bv = b3(bk, 4)
                for hh in range(4):
                    for k in range(8):
                        P.op('pe', lambda e, k=k, hh=hh: e.matmul(bv[:, hh, :], lhsT=Win[:, k, c0 + hh * 128:c0 + (hh + 1) * 128],
                                                                  rhs=h[:, k, :], start=(k == 0), stop=(k == 7)),
                             r=[hb, Winb[c0 // 512]], w=[Bb[bk]])
            proj_tok(p1, 0)
            proj_tok(p2, 512)
            P.op('act', lambda e: e.activation(out=T.gu_[:, :], in_=B[p1][:, :], func=AF.Gelu_apprx_tanh), r=[Bb[p1]], w=[T.gu_b])
            P.op('act', lambda e: e.activation(out=T.gv[:, :, :], in_=b3(p2, 8), func=AF.Gelu_apprx_tanh), r=[Bb[p2]], w=[T.gvb])
            proj_tok(p0, 2048)
            proj_tok(p1, 2560)
            P.op('act', lambda e: e.activation(out=T.v16[:, :, :], in_=b3(p0, 4), func=AF.Copy), r=[Bb[p0]], w=[T.v16b])
            P.op('act', lambda e: e.activation(out=T.sgl[:, :], in_=B[p1][:, :], func=AF.Silu), r=[Bb[p1]], w=[T.sglb])
            proj_feat(p2, 1024)
            proj_feat(p0, 1536)
            P.op('act', lambda e: e.activation(out=T.qf32[:, :, :], in_=b3(p2, 4), func=AF.Copy), r=[Bb[p2]], w=[T.qf32b])
            P.op('act', lambda e: e.activation(out=T.f1[:, :, :], in_=b3(p0, 4), func=AF.Sigmoid), r=[Bb[p0]], w=[T.f1b])
            lb_bc = lb[:, :].unsqueeze(2).broadcast_to([128, 4, 128])
            oml_bc = oml[:, :].unsqueeze(2).broadcast_to([128, 4, 128])
            P.op('dve', lambda e: e.tensor_tensor(out=T.f1[:, :, :], in0=T.f1[:, :, :], in1=oml_bc, op=ALU.mult), r=[T.f1b, cb], w=[T.f1b])
            P.op('dve', lambda e: e.tensor_tensor(out=T.f1[:, :, :], in0=T.f1[:, :, :], in1=lb_bc, op=ALU.add), r=[T.f1b, cb], w=[T.f1b])
            P.op('dve', lambda e: e.tensor_scalar(out=T.kk[:, :, :], in0=T.f1[:, :, :], scalar1=-1.0, scalar2=1.0, op0=ALU.mult, op1=ALU.add),
                 r=[T.f1b], w=[T.kkb])
            P.op('dve', lambda e: e.tensor_scalar(out=T.lf[:, :, :], in0=T.f1[:, :, :], scalar1=1e-6, scalar2=None, op0=ALU.max),
                 r=[T.f1b], w=[T.lfb])
            P.op('act', lambda e: e.activation(out=T.lf[:, :, :], in_=T.lf[:, :, :], func=AF.Ln), r=[T.lfb], w=[T.lfb])
            P.op('dve', lambda e: e.tensor_tensor_scan(out=T.Bc[:, :, :].rearrange("p h t -> p (h t)"), data0=rmask[:, :],
                                                       data1=T.lf[:, :, :].rearrange("p h t -> p (h t)"), initial=0.0, op0=ALU.mult,
                                                       op1=ALU.add), r=[T.lfb, cb], w=[T.Bcb])
            P.op('act', lambda e: e.activation(out=T.e1[:, :, :], in_=T.Bc[:, :, :], func=AF.Exp), r=[T.Bcb], w=[T.e1b])
            P.op('dve', lambda e: e.tensor_tensor(out=T.qt[:, :, :], in0=T.qf32[:, :, :], in1=T.e1[:, :, :], op=ALU.mult),
                 r=[T.qf32b, T.e1b], w=[T.qtb])
            P.op('dve', lambda e: e.tensor_scalar(out=T.e2[:, :, :], in0=T.Bc[:, :, :], scalar1=-60.0, scalar2=None, op0=ALU.max),
                 r=[T.Bcb], w=[T.e2b])
            P.op('act', lambda e: e.activation(out=T.e2[:, :, :], in_=T.e2[:, :, :], func=AF.Exp, scale=-1.0), r=[T.e2b], w=[T.e2b])
            P.op('dve', lambda e: e.tensor_tensor(out=T.kt_[:, :, :], in0=T.kk[:, :, :], in1=T.e2[:, :, :], op=ALU.mult),
                 r=[T.kkb, T.e2b], w=[T.kt_b])
            B4 = T.Bc[:, :, :].rearrange("p h (c t) -> p h c t", t=32)
            BCl = B4[:, :, :, 31:32]
            P.op('dve', lambda e: e.tensor_tensor(out=T.e3[:, :, :].rearrange("p h (c t) -> p h c t", t=32),
                                                  in0=BCl.broadcast_to([128, 4, 4, 32]), in1=B4, op=ALU.subtract), r=[T.Bcb], w=[T.e3b])
            P.op('act', lambda e: e.activation(out=T.e3[:, :, :], in_=T.e3[:, :, :], func=AF.Exp), r=[T.e3b], w=[T.e3b])
            P.op('dve', lambda e: e.tensor_tensor(out=T.kh[:, :, :], in0=T.kk[:, :, :], in1=T.e3[:, :, :], op=ALU.mult),
                 r=[T.kkb, T.e3b], w=[T.khb])
            P.op('act', lambda e: e.activation(out=T.eBC[:, :, :, :], in_=BCl, func=AF.Exp), r=[T.Bcb], w=[T.eBCb])
            tk = b16(p1, 4, 128)
            for hh in range(4):
                P.op('pe', lambda e, hh=hh: e.transpose(out=tk[:, hh, :], in_=T.kh[:, hh, :], identity=C.ident[:, :]),
                     r=[T.khb, C.constb], w=[Bb[p1]])
            P.op('act', lambda e: e.activation(out=T.khT[:, :, :], in_=tk, func=AF.Copy), r=[Bb[p1]], w=[T.khTb])
            P.op('dve', lambda e: e.tensor_tensor(out=T.sq[:, :, :], in0=T.gv[:, :, :], in1=T.gv[:, :, :], op=ALU.mult), r=[T.gvb], w=[T.sqb])
            P.op('dve', lambda e: e.tensor_reduce(out=T.sm[:, 0:8], in_=T.sq[:, :, :], axis=AX.X, op=ALU.add), r=[T.sqb], w=[T.smb])
            P.op('dve', lambda e: e.tensor_scalar(out=T.sm[:, 8:16], in0=T.sm[:, 0:8], scalar1=1.0 / 64, scalar2=EPS, op0=ALU.mult,
                                                  op1=ALU.add), r=[T.smb], w=[T.smb])
            P.op('pool', lambda e: e.tensor_tensor(out=T.sm[:, 16:24], in0=T.sm[:, 8:16], in1=C.neghalf[:, 0:8], op=ALU.pow),
                 r=[T.smb, C.constb], w=[T.smb])
            P.op('dve', lambda e: e.tensor_tensor(out=T.sq[:, :, :], in0=T.gv[:, :, :],
                                                  in1=T.sm[:, 16:24].unsqueeze(2).broadcast_to([128, 8, 64]), op=ALU.mult),
                 r=[T.gvb, T.smb], w=[T.sqb])
            P.op('dve', lambda e: e.tensor_tensor(out=T.vn16[:, :, :], in0=T.sq[:, :, :],
                                                  in1=vnorm[:, :].rearrange("p (h d) -> p h d", h=8), op=ALU.mult),
                 r=[T.sqb, gb], w=[T.vn16b])
            Mv = b3(p2, 8)
            for hh in range(8):
                P.op('pe', lambda e, hh=hh: e.matmul(Mv[:, hh, :], lhsT=WsT[:, hh, :], rhs=T.vn16[:, hh, :], start=True, stop=True),
                     r=[cb, T.vn16b], w=[Bb[p2]])
            P.op('dve', lambda e: e.tensor_tensor(out=T.sq[:, :, :], in0=Mv, in1=bs[:, :, :].broadcast_to([128, 8, 64]), op=ALU.add),
                 r=[Bb[p2], gb], w=[T.sqb])
            P.op('dve', lambda e: e.tensor_tensor(out=T.mixed[:, 0:512], in0=T.sq[:, :, :].rearrange("p h d -> p (h d)"), in1=T.gu_[:, :],
                                                  op=ALU.mult), r=[T.sqb, T.gu_b], w=[T.mixb[0]])
            sc = b3(p0, 4)
            for hh in range(4):
                P.op('pe', lambda e, hh=hh: e.matmul(sc[:, hh, :], lhsT=T.kt_[:, hh, :], rhs=T.qt[:, hh, :], start=True, stop=True),
                     r=[T.kt_b, T.qtb], w=[Bb[p0]])
            for hh in range(4):
                P.op('dve', lambda e, hh=hh: e.tensor_tensor(out=T.PT[hh][:, :], in0=sc[:, hh, :], in1=bmask[:, :], op=ALU.mult),
                     r=[Bb[p0], cb], w=[T.PTb[hh]])
            O = b3(R.o, 4)
            for hh in range(4):
                P.op('pe', lambda e, hh=hh: e.matmul(O[:, hh, :], lhsT=T.PT[hh][:, :], rhs=T.v16[:, hh, :], start=(hh == 0), stop=False,
                                                     skip_group_check=True), r=[T.PTb[hh], T.v16b], w=[Bb[R.o]])

        def chain(t):
            T = TB[t % 2]
            R = BS[t % 2]
            p0, p1, p2 = R.p
            O = b3(R.o, 4)
            Ust = b3(p1, 4)
            for c in range(4):
                for hh in range(4):
                    P.op('pe', lambda e, hh=hh, c=c: e.matmul(O[32 * c:32 * c + 32, hh, :], lhsT=T.qt[:, hh, 32 * c:32 * c + 32],
                                                              rhs=S16[:, hh, :], start=False, stop=(c == 3), tile_position=(0, 32 * c),
                                                              skip_group_check=True), r=[T.qtb, S16b], w=[Bb[R.o]])
                for hh in range(4):
                    P.op('pe', lambda e, hh=hh, c=c: e.matmul(Ust[:, hh, :], lhsT=T.khT[32 * c:32 * c + 32, hh, :],
                                                              rhs=T.v16[32 * c:32 * c + 32, hh, :], start=True, stop=True,
                                                              tile_position=(32 * c, 0)), r=[T.khTb, T.v16b], w=[Bb[p1]])
                for hh in range(4):
                    P.op('dve', lambda e, hh=hh, c=c: e.scalar_tensor_tensor(out=S32[:, hh, :], in0=S32[:, hh, :], scalar=T.eBC[:, hh, c, :],
                                                                             in1=Ust[:, hh, :], op0=ALU.mult, op1=ALU.add),
                         r=[S32b, T.eBCb, Bb[p1]], w=[S32b])
                P.op('act', lambda e: e.activation(out=S16[:, :, :], in_=S32[:, :, :], func=AF.Copy), r=[S32b], w=[S16b])

        def post(t):
            T = TB[t % 2]
            R = BS[t % 2]
            p0, p1, p2 = R.p
            x = xt[t % 4]
            xb = xtb[t % 4]
            O = b3(R.o, 4)
            for hh in range(4):
                P.op('act', lambda e, hh=hh: e.activation(out=C.junk[:, 0:128], in_=O[:, hh, :], func=AF.Square,
                                                           accum_out=T.sm[:, 32 + hh:33 + hh]), r=[Bb[R.o]], w=[T.smb, C.junkb])
            P.op('dve', lambda e: e.tensor_scalar(out=T.sm[:, 40:44], in0=T.sm[:, 32:36], scalar1=1.0 / 128, scalar2=EPS, op0=ALU.mult,
                                                  op1=ALU.add), r=[T.smb], w=[T.smb])
            P.op('pool', lambda e: e.tensor_tensor(out=T.sm[:, 44:48], in0=T.sm[:, 40:44], in1=C.neghalf[:, 0:4], op=ALU.pow),
                 r=[T.smb, C.constb], w=[T.smb])
            for hh in range(4):
                P.op('act', lambda e, hh=hh: e.activation(out=T.on[:, hh, :], in_=O[:, hh, :], func=AF.Copy, scale=T.sm[:, 44 + hh:45 + hh]),
                     r=[Bb[R.o], T.smb], w=[T.onb])
            P.op('dve', lambda e: e.tensor_tensor(out=T.on[:, :, :], in0=T.on[:, :, :],
                                                  in1=onorm[:, :].unsqueeze(1).broadcast_to([128, 4, 128]), op=ALU.mult),
                 r=[T.onb, gb], w=[T.onb])
            P.op('dve', lambda e: e.tensor_tensor(out=T.mixed[:, 512:1024], in0=T.on[:, :, :].rearrange("p h d -> p (h d)"), in1=T.sgl[:, :],
                                                  op=ALU.mult), r=[T.onb, T.sglb], w=[T.mixb[1]])
            tm = b16(p2, 8, 128)
            for k in range(8):
                P.op('pe', lambda e, k=k: e.transpose(out=tm[:, k, :], in_=T.mixed[:, k * 128:(k + 1) * 128], identity=C.ident[:, :]),
                     r=[T.mixb[0], T.mixb[1], C.constb], w=[Bb[p2]])
            P.op('act', lambda e: e.activation(out=T.mT[:, :, :], in_=tm, func=AF.Copy), r=[Bb[p2]], w=[T.mTb])
            for dh, bk in ((0, p0), (1, p1)):
                for k in range(8):
                    P.op('pe', lambda e, k=k, dh=dh, bk=bk: e.matmul(B[bk][:, :], lhsT=T.mT[:, k, :], rhs=Wo[:, k, dh * 512:(dh + 1) * 512],
                                                                    start=(k == 0), stop=(k == 7)), r=[T.mTb, Wob], w=[Bb[bk]])
                P.op('dve', lambda e, dh=dh, bk=bk: e.tensor_tensor(out=x[:, 0, dh * 512:(dh + 1) * 512], in0=x[:, 0, dh * 512:(dh + 1) * 512],
                                                                   in1=B[bk][:, :], op=ALU.add), r=[Bb[bk], xb], w=[xb])
            P.dma(xout[t], x[:, :, :], r=[xb], sig=f"xt{t % 4}")

        for t in range(min(4, ntile)):
            load_x(t)
        P.replay([P.capture(lambda t=t: pre_norm(t)) for t in range(min(2, ntile))])
        for t0 in range(0, ntile, 2):
            ts = list(range(t0, min(t0 + 2, ntile)))
            nxt = list(range(t0 + 2, min(t0 + 4, ntile)))
            P.replay([P.capture(lambda t=t: pre(t)) for t in ts])
            chain(ts[0])
            if len(ts) > 1:
                P.replay([P.capture(lambda: post(ts[0])), P.capture(lambda: chain(ts[1]))])
                if nxt:
                    P.replay([P.capture(lambda t=t: pre_norm(t)) for t in nxt])
                post(ts[1])
            else:
                post(ts[0])
            for t in range(t0 + 4, min(t0 + 6, ntile)):
                load_x(t)
        P.flush()


def setup_consts(P, C, es, ident_dram):
    nc = P.nc

    def sb(name, shape, dt):
        return es.enter_context(nc.sbuf_tensor(uname(name), shape, dt))
    C.identf = sb("identf", [128, 128], F32)
    C.ident = sb("ident", [128, 128], BF16)
    C.neghalf = sb("neghalf", [128, 8], F32)
    C.junk = sb("junk", [128, D], BF16)
    C.ss = sb("ss", [128, 8], F32)
    C.vv = sb("vv", [128, 8], F32)
    C.rstd = sb("rstd", [128, 8], F32)
    b0 = P.buf("c0")
    P.dma(C.identf[:, :], ident_dram, w=[b0], sig="identf")
    P.op('dve', lambda e: e.tensor_copy(out=C.ident[:, :], in_=C.identf[:, :]), r=[b0], w=[b0])
    P.op('pool', lambda e: e.memset(C.neghalf[:, :], -0.5), w=[b0])
    P.flush()


def new_phase_bufs(P, C):
    C.constb = P.buf("const")
    C.junkb = P.buf("junk")
    C.ssb = [P.buf(f"ss{s}") for s in range(8)]
    C.vvb = P.buf("vv")
    C.rstdb = P.buf("rstd")
    C.vv2b = [P.buf("vv2_0"), P.buf("vv2_1")]
    C.rstd2b = [P.buf("rstd2_0"), P.buf("rstd2_1")]


def build_nc(S, plan):
    nc = bass.Bass("TRN2", target_bir_lowering=False)

    def din(name, shape, dt=F32):
        return nc.dram_tensor(name, list(shape), dt, kind="ExternalInput").ap()
    x = din("x", [S, D])
    ident = din("ident", [128, 128])
    W = {
        'norm_gains': din("norm_gains", [DEPTH, 5, D]),
        'ffn_w_gate': din("ffn_w_gate", [DEPTH, 2, D, DFF]),
        'ffn_w_up': din("ffn_w_up", [DEPTH, 2, D, DFF]),
        'ffn_w_down': din("ffn_w_down", [DEPTH, 2, DFF, D]),
        'ple_w_gate': din("ple_w_gate", [DEPTH, D, D]),
        'ple_w_proj': din("ple_w_proj", [DEPTH, 256, D]),
        'p': din("p", [DEPTH, S, 256]),
        'positions': din("positions", [S], I32),
        'invf': din("invf", [2, 32]),
        'mla_w_in': din("mla_w_in", [2, D, 704]),
        'mla_q_a_norm': din("mla_q_a_norm", [2, 384]),
        'mla_kv_a_norm': din("mla_kv_a_norm", [2, 256]),
        'mla_w_q_b': din("mla_w_q_b", [2, 384, 1536]),
        'mla_w_kv_b': din("mla_w_kv_b", [2, 256, 2048]),
        'mla_q_norm': din("mla_q_norm", [2, 192]),
        'mla_k_norm': din("mla_k_norm", [2, 192]),
        'mla_w_out': din("mla_w_out", [2, D, D]),
        'even_w_in': din("even_w_in", [2, D, 3072]),
        'gmlp_v_norm': din("gmlp_v_norm", [2, 8, 64]),
        'gmlp_w_s': din("gmlp_w_s", [2, 8, 128, 128]),
        'gmlp_b_s': din("gmlp_b_s", [2, 8, 128]),
        'hgrn_lb_raw': din("hgrn_lb_raw", [2, 512]),
        'hgrn_out_norm': din("hgrn_out_norm", [2, 128]),
        'even_w_out': din("even_w_out", [2, D, D]),
    }
    kd = "ExternalOutput" if DEBUG else "Internal"
    scrQ = (nc.dram_tensor("QTn", [NH, 128, S], BF16, kind=kd).ap(),
            nc.dram_tensor("QTr", [NH, 64, S], BF16, kind=kd).ap(),
            nc.dram_tensor("KTn", [NH, 128, S], BF16, kind=kd).ap(),
            nc.dram_tensor("KTr", [NH, 64, S], BF16, kind=kd).ap(),
            nc.dram_tensor("Vs", [S, NH, 128], BF16, kind=kd).ap())
    y = nc.dram_tensor("y", [S, D], F32, kind="ExternalOutput").ap()
    scr = [nc.dram_tensor(f"xscr{i}", [S, D], F32, kind="Internal").ap() for i in range(3)]
    P = Prog(nc)
    C = Ctx()
    with ExitStack() as es:
        new_phase_bufs(P, C)
        setup_consts(P, C, es, ident)
        cur = x
        si = 0
        for pi, ph in enumerate(plan):
            if pi == len(plan) - 1:
                dst = y
            else:
                dst = scr[si % 3]
                si += 1
            new_phase_bufs(P, C)
            if ph[0] == 'even':
                _, l = ph
                phase_even(P, C, cur, dst, W, l, l // 2, S)
            if ph[0] == 'mla':
                _, l = ph
                phase_mla1(P, C, cur, W, l, l // 2, S, scrQ)
                mid = scr[si % 3]
                si += 1
                new_phase_bufs(P, C)
                phase_mla2(P, C, cur, mid, W, l // 2, 0, S, scrQ)
                new_phase_bufs(P, C)
                phase_mla2(P, C, mid, dst, W, l // 2, 1, S, scrQ)
            if ph[0] == 'ffn':
                _, l, which = ph
                phase_ffn(P, C, cur, dst, W['ffn_w_gate'][l, which], W['ffn_w_up'][l, which], W['ffn_w_down'][l, which],
                          W['norm_gains'][l, 0 if which == 0 else 2, :], S)
            elif ph[0] == 'ple':
                _, l = ph
                phase_ple(P, C, cur, dst, W['p'][l], W['ple_w_gate'][l], W['ple_w_proj'][l], W['norm_gains'][l, 3, :],
                          W['norm_gains'][l, 4, :], S)
            cur = dst
    return nc


WEIGHT_KEYS = ['norm_gains', 'ffn_w_gate', 'ffn_w_up', 'ffn_w_down', 'ple_w_gate', 'ple_w_proj', 'even_w_in', 'gmlp_v_norm',
               'gmlp_w_s', 'gmlp_b_s', 'hgrn_lb_raw', 'hgrn_out_norm', 'even_w_out', 'mla_w_in', 'mla_q_a_norm', 'mla_kv_a_norm',
               'mla_w_q_b', 'mla_w_kv_b', 'mla_q_norm', 'mla_k_norm', 'mla_w_out']
N_CORES = 8
SEQ = 4096
FUSED = True


def layer_plan(l):
    return [('ffn', l, 0), ('even', l) if l % 2 == 0 else ('mla', l), ('ffn', l, 1), ('ple', l)]


def kernel(**inputs):
    x = np.ascontiguousarray(np.asarray(inputs['x'], dtype=np.float32))
    p = np.asarray(inputs['p'], dtype=np.float32)
    pos = np.asarray(inputs['positions']).astype(np.int32)
    Wt = {k: np.ascontiguousarray(np.asarray(inputs[k], dtype=np.float32)) for k in WEIGHT_KEYS}
    ident = np.eye(128, dtype=np.float32)
    invf = (10000.0 ** (-np.arange(0, 64, 2, dtype=np.float32) / 64)).astype(np.float32)
    invf2 = np.ascontiguousarray(np.stack([invf, invf]))
    B = x.shape[0]
    assert B == N_CORES and x.shape[1] == SEQ
    cur = [np.ascontiguousarray(x[b]) for b in range(B)]
    pb = [np.ascontiguousarray(p[:, b]) for b in range(B)]
    posb = [np.ascontiguousarray(pos[b]) for b in range(B)]
    plans = [sum([layer_plan(l) for l in range(DEPTH)], [])] if FUSED else [layer_plan(l) for l in range(DEPTH)]
    for plan in plans:
        nc = build_nc(SEQ, plan)
        in_maps = []
        for b in range(B):
            m = {"x": cur[b], "ident": ident, "invf": invf2, "p": pb[b], "positions": posb[b]}
            m.update(Wt)
            in_maps.append(m)
        res = run_bass_kernel_spmd(nc, in_maps, core_ids=list(range(N_CORES)))
        cur = [np.ascontiguousarray(np.asarray(res.results[b]["y"], dtype=np.float32)) for b in range(B)]
    return np.stack(cur, axis=0)
```

```python
import numpy as np
from contextlib import ExitStack
import concourse.bass as bass
import concourse.mybir as mybir
from concourse.bass_utils import run_bass_kernel_spmd

F32 = mybir.dt.float32
BF16 = mybir.dt.bfloat16
I32 = mybir.dt.int32
AF = mybir.ActivationFunctionType
ALU = mybir.AluOpType
AX = mybir.AxisListType

D = 1024
DFF = 2816
NFC = DFF // 128
DEPTH = 4
EPS = 1e-6
TT = 256
DEBUG = False
EV_DBG = ''
ENG_ATTR = {'sp': 'sync', 'act': 'scalar', 'pe': 'tensor', 'dve': 'vector', 'pool': 'gpsimd'}


class Buf:
    __slots__ = ('name', 'w', 'r', 'pw')

    def __init__(self, name):
        self.name = name
        self.w = None
        self.r = []
        self.pw = []


class Op:
    __slots__ = ('eng', 'fn', 'deps', 'needed', 'sem', 'val', 'is_dma')


class Prog:
    def __init__(self, nc):
        self.nc = nc
        self.ops = {e: [] for e in ENG_ATTR}
        self.eng_sem = {e: nc.alloc_semaphore(name=f"es_{e}") for e in ENG_ATTR}
        self.eng_cnt = {e: 0 for e in ENG_ATTR}
        self.dma_sem = {}
        self.order = []
        self.bufs = []
        self.dmas = []

    def buf(self, name):
        b = Buf(name)
        self.bufs.append(b)
        return b

    def _deps(self, eng, is_dma, r, w, par=False):
        deps = []

        def add(o, war=False):
            if o is None:
                return
            if not is_dma and not o.is_dma and o.eng == eng:
                if eng in ('pe', 'act'):
                    return
            if o not in deps:
                deps.append(o)
        for b in r:
            add(b.w)
            for o in b.pw:
                add(o)
        for b in w:
            if not par:
                add(b.w)
                for o in b.pw:
                    add(o)
            for o in b.r:
                add(o, war=True)
        return deps

    def _commit(self, op, r, w, par=False):
        for o in op.deps:
            o.needed = True
        for b in w:
            if par:
                b.pw.append(op)
                continue
            b.w = op
            b.pw = []
            b.r = []
        for b in r:
            if b.w is not op:
                b.r.append(op)
        self.ops[op.eng].append(op)
        self.order.append(op)

    def op(self, eng, fn, r=(), w=()):
        o = Op()
        o.eng = eng
        o.fn = fn
        o.is_dma = False
        o.needed = False
        o.sem = None
        o.val = None
        o.deps = self._deps(eng, False, r, w)
        self._commit(o, r, w)
        return o

    def dma(self, out, in_, r=(), w=(), sig=None, eng='sp', slow=False, par=False):
        if sig not in self.dma_sem:
            self.dma_sem[sig] = [self.nc.alloc_semaphore(name=f"ds_{sig}"), 0]
        o = Op()
        o.eng = eng
        if slow:
            o.fn = lambda e, out=out, in_=in_: e.dma_start(out=out, in_=in_, allow_slow_non_contiguous=True)
        else:
            o.fn = lambda e, out=out, in_=in_: e.dma_start(out=out, in_=in_)
        o.is_dma = True
        o.needed = True
        ent = self.dma_sem[sig]
        ent[1] += 16
        o.sem = ent[0]
        o.val = ent[1]
        o.deps = self._deps(eng, True, r, w, par)
        self._commit(o, r, w, par)
        self.dmas.append(o)
        return o

    def capture(self, fn):
        lst = []
        self.op = lambda *a, **k: lst.append((0, a, k))
        self.dma = lambda *a, **k: lst.append((1, a, k))
        try:
            fn()
        finally:
            del self.op
            del self.dma
        return lst

    def replay(self, lists):
        n = max(len(l) for l in lists)
        for i in range(n):
            for l in lists:
                if i < len(l):
                    kind, a, k = l[i]
                    (self.dma if kind else self.op)(*a, **k)

    def flush(self):
        nc = self.nc
        fin = Op()
        fin.eng = 'sp'
        fin.fn = None
        fin.is_dma = False
        fin.needed = False
        fin.sem = None
        fin.val = None
        fin.deps = list(self.dmas)
        self.ops['sp'].append(fin)
        self.order.append(fin)
        for o in self.order:
            if not o.is_dma and o.needed:
                self.eng_cnt[o.eng] += 1
                o.sem = self.eng_sem[o.eng]
                o.val = self.eng_cnt[o.eng]
        with nc.Block() as block:
            for en, attr in ENG_ATTR.items():
                ops = self.ops[en]

                def body(e, ops=ops):
                    seen = {}
                    for o in ops:
                        for d in o.deps:
                            k = id(d.sem)
                            if seen.get(k, 0) >= d.val:
                                continue
                            seen[k] = d.val
                            e.wait_ge(d.sem, d.val)
                        if o.fn is None:
                            continue
                        ins = o.fn(e)
                        if o.is_dma:
                            ins.then_inc(o.sem, 16)
                        elif o.needed:
                            ins.then_inc(o.sem, 1)
                getattr(block, attr)(body)
        self.ops = {e: [] for e in ENG_ATTR}
        self.order = []
        self.dmas = []
        for b in self.bufs:
            b.w = None
            b.r = []
            b.pw = []
        self.bufs = []


class Ctx:
    pass


_UID = [0]


def uname(name):
    _UID[0] += 1
    return f"{name}_{_UID[0]}"


def load_cast(P, C, dst, src, dstbuf, n):
    i = C.stage_i
    C.stage_i += 1
    slot = i % len(C.stage)
    st, sb = C.stage[slot], C.stage_b[slot]
    P.dma(st[:, 0:n], src, w=[sb], sig=f"stage{slot}")
    eng = 'dve' if (i % 2 == 0) else 'act'
    if eng == 'dve':
        P.op('dve', lambda e: e.tensor_copy(out=dst, in_=st[:, 0:n]), r=[sb], w=[dstbuf])
    else:
        P.op('act', lambda e: e.activation(out=dst, in_=st[:, 0:n], func=AF.Copy), r=[sb], w=[dstbuf])


def emit_norm_T(P, C, xt, xb, gbc, hT, hTb, tp, tpb, nsub=2, N=None):
    if N is None:
        N = C
    ss, ssb = N.ss, N.ssb
    for s in range(nsub):
        P.op('act', lambda e, s=s: e.activation(out=C.junk[:, :], in_=xt[:, s, :], func=AF.Square,
                                                 accum_out=ss[:, s:s + 1]), r=[xb], w=[ssb[s], C.junkb])
    P.op('dve', lambda e: e.tensor_scalar(out=N.vv[:, 0:nsub], in0=ss[:, 0:nsub], scalar1=1.0 / D, scalar2=EPS,
                                          op0=ALU.mult, op1=ALU.add), r=[ssb[s] for s in range(nsub)], w=[N.vvb])
    P.op('pool', lambda e: e.tensor_tensor(out=N.rstd[:, 0:nsub], in0=N.vv[:, 0:nsub], in1=C.neghalf[:, 0:nsub],
                                           op=ALU.pow), r=[N.vvb, C.constb], w=[N.rstdb])
    for s in range(nsub):
        P.op('dve', lambda e, s=s: e.scalar_tensor_tensor(out=N.xs[:, s, :], in0=xt[:, s, :], scalar=N.rstd[:, s:s + 1],
                                                          in1=gbc, op0=ALU.mult, op1=ALU.mult),
             r=[xb, N.rstdb, C.gbcb], w=[N.xsb[s]])
    for s in range(nsub):
        for k in range(8):
            P.op('pe', lambda e, s=s, k=k: e.transpose(out=tp[:, k, s * 128:(s + 1) * 128],
                                                       in_=N.xs[:, s, k * 128:(k + 1) * 128], identity=C.ident[:, :]),
                 r=[N.xsb[s], C.constb], w=[tpb])
    W = nsub * 128
    for hh in range(2):
        P.op('act', lambda e, hh=hh: e.activation(out=hT[:, hh * 4:(hh + 1) * 4, 0:W], in_=tp[:, hh * 4:(hh + 1) * 4, 0:W],
                                                  func=AF.Copy), r=[tpb], w=[hTb])


def norm_scratch(P, sbf, i, nsub):
    N = Ctx()
    N.ss = sbf(f"nss{i}", [128, 8], F32)
    N.vv = sbf(f"nvv{i}", [128, 8], F32)
    N.rstd = sbf(f"nrstd{i}", [128, 8], F32)
    N.xs = sbf(f"nxs{i}", [128, nsub, D], BF16)
    N.ssb = [P.buf(f"nss{i}_{s}") for s in range(8)]
    N.vvb = P.buf(f"nvv{i}")
    N.rstdb = P.buf(f"nrstd{i}")
    N.xsb = [P.buf(f"nxs{i}_{s}") for s in range(nsub)]
    return N

def phase_ffn(P, C, x_in, x_out, wg, wu, wd, gain, S):
    nc = P.nc
    ntile = S // TT
    with ExitStack() as es:
        def sb(name, shape, dt):
            return es.enter_context(nc.sbuf_tensor(uname(name), shape, dt))

        def ps(name, shape, dt=F32):
            return es.enter_context(nc.psum_tensor(uname(name), shape, dt))
        Wg = sb("Wg", [128, 8, DFF], BF16)
        Wu = sb("Wu", [128, 8, DFF], BF16)
        Wd = sb("Wd", [128, NFC, D], BF16)
        NG4 = (NFC + 3) // 4
        Wgb = [P.buf(f"Wg{g}") for g in range(NG4)]
        Wub = [P.buf(f"Wu{g}") for g in range(NG4)]
        Wdb = [P.buf(f"Wd{g}") for g in range(NG4)]
        gbc = sb("gbc", [128, D], F32)
        C.gbcb = P.buf("gbc")
        xt = [sb(f"xt{i}", [128, 2, D], F32) for i in range(2)]
        xtb = [P.buf(f"xt{i}") for i in range(2)]
        C.xs = sb("xs", [128, 2, D], BF16)
        C.xsb = [P.buf(f"xs{s}") for s in range(2)]
        hT = [sb(f"hT{i}", [128, 8, TT], BF16) for i in range(2)]
        hTb = [P.buf(f"hT{i}") for i in range(2)]
        sg = [sb(f"sg{i}", [128, TT], F32) for i in range(3)]
        sgb = [P.buf(f"sg{i}") for i in range(3)]
        aT = [sb(f"aT{i}", [128, TT], BF16) for i in range(3)]
        aTb = [P.buf(f"aT{i}") for i in range(3)]
        tp = ps("tp", [128, 8, TT], BF16)
        tpb = P.buf("tp")
        gu = [ps(f"gu{i}", [128, 2, TT]) for i in range(2)]
        gub = [P.buf(f"gu{i}") for i in range(2)]
        gu.append(tp.bitcast(F32)[:, 4:8, :].rearrange("p (a b) t -> p a (b t)", a=2))
        gub.append(tpb)
        yp = [[ps(f"y{s}{h}", [128, 512]) for h in range(2)] for s in range(2)]
        ypb = [[P.buf(f"y{s}{h}") for h in range(2)] for s in range(2)]

        P.dma(gbc[:, :], gain.partition_broadcast(128), w=[C.gbcb], sig="gbc")
        wgr = wg.rearrange("(kc p) f -> p kc f", p=128)
        wur = wu.rearrange("(kc p) f -> p kc f", p=128)
        wdr = wd.rearrange("(fc p) d -> p fc d", p=128)

        for g in range(NG4):
            f0 = 4 * g
            nf = min(4, NFC - f0)
            c0, c1 = f0 * 128, (f0 + nf) * 128
            P.dma(Wg[:, :, c0:c1], wgr[:, :, c0:c1], w=[Wgb[g]], sig=f"W{3 * g}", eng='pool')
            P.dma(Wu[:, :, c0:c1], wur[:, :, c0:c1], w=[Wub[g]], sig=f"W{3 * g + 1}", eng='pool')
            P.dma(Wd[:, f0:f0 + nf, :], wdr[:, f0:f0 + nf, :], w=[Wdb[g]], sig=f"W{3 * g + 2}", eng='pool')

        xin = x_in.rearrange("(t s p) d -> t p s d", s=2, p=128)
        xout = x_out.rearrange("(t s p) d -> t p s d", s=2, p=128)

        def load_x(t):
            P.dma(xt[t % 2][:, :, :], xin[t], w=[xtb[t % 2]], sig=f"xt{t % 2}")

        def norm(t):
            emit_norm_T(P, C, xt[t % 2], xtb[t % 2], gbc[:, :], hT[t % 2], hTb[t % 2], tp, tpb)

        def gu_mm(t, j):
            sl = j % 3
            h = hT[t % 2]
            for which, Wm, Wb in ((0, Wg, Wgb), (1, Wu, Wub)):
                for k in range(8):
                    P.op('pe', lambda e, k=k, which=which, Wm=Wm, sl=sl, h=h: e.matmul(
                        gu[sl][:, which, :], lhsT=Wm[:, k, j * 128:(j + 1) * 128], rhs=h[:, k, :],
                        start=(k == 0), stop=(k == 7)), r=[Wb[j // 4], hTb[t % 2]], w=[gub[sl]])

        def act_mul(t, j):
            sl = j % 3
            P.op('act', lambda e: e.activation(out=sg[sl][:, :], in_=gu[sl][:, 0, :], func=AF.Silu), r=[gub[sl]], w=[sgb[sl]])
            P.op('dve', lambda e: e.tensor_tensor(out=aT[sl][:, :], in0=sg[sl][:, :], in1=gu[sl][:, 1, :], op=ALU.mult),
                 r=[sgb[sl], gub[sl]], w=[aTb[sl]])

        def down_mm(t, j):
            sl = j % 3
            for s in range(2):
                for h in range(2):
                    P.op('pe', lambda e, s=s, h=h: e.matmul(
                        yp[s][h][:, :], lhsT=aT[sl][:, s * 128:(s + 1) * 128], rhs=Wd[:, j, h * 512:(h + 1) * 512],
                        start=(j == 0), stop=(j == NFC - 1)), r=[aTb[sl], Wdb[j // 4]], w=[ypb[s][h]])

        def resid(t):
            x = xt[t % 2]
            for s in range(2):
                for h in range(2):
                    P.op('dve', lambda e, s=s, h=h: e.scalar_tensor_tensor(
                        out=x[:, s, h * 512:(h + 1) * 512], in0=yp[s][h][:, :], scalar=0.5,
                        in1=x[:, s, h * 512:(h + 1) * 512], op0=ALU.mult, op1=ALU.add), r=[ypb[s][h], xtb[t % 2]], w=[xtb[t % 2]])
            P.dma(xout[t], x[:, :, :], r=[xtb[t % 2]], sig=f"xt{t % 2}")

        load_x(0)
        if ntile > 1:
            load_x(1)
        norm(0)
        for t in range(ntile):
            gu_mm(t, 0)
            for j in range(NFC):
                if j + 1 < NFC:
                    gu_mm(t, j + 1)
                if j == 6 and t + 1 < ntile:
                    norm(t + 1)
                act_mul(t, j)
                down_mm(t, j)
            resid(t)
            if t + 2 < ntile:
                load_x(t + 2)
        P.flush()


def phase_ple(P, C, x_in, x_out, p_in, wgate, wproj, gain_in, gain_out, S):
    nc = P.nc
    ntile = S // 128
    with ExitStack() as es:
        def sb(name, shape, dt):
            return es.enter_context(nc.sbuf_tensor(uname(name), shape, dt))

        def ps(name, shape, dt=F32):
            return es.enter_context(nc.psum_tensor(uname(name), shape, dt))
        Wg = sb("pWg", [128, 8, D], BF16)
        Wp = sb("pWp", [128, 2, D], BF16)
        Wgb = [P.buf(f"pWg{k}") for k in range(8)]
        Wpb = P.buf("pWp")
        gbc = sb("gbc", [128, D], F32)
        C.gbcb = P.buf("gbc")
        g4 = sb("g4", [128, D], F32)
        g4b = P.buf("g4")
        B = [ps(f"B{i}", [128, 512]) for i in range(8)]
        Bb = [P.buf(f"B{i}") for i in range(8)]

        def alloc_tile(i):
            T = Ctx()
            for nm, shape, dt in (("pb", [128, 256], BF16),
                                  ("pT", [128, 2, 128], BF16), ("hT", [128, 8, 128], BF16), ("sg0", [128, 512], F32),
                                  ("sg1", [128, 512], F32), ("ee", [128, D], F32), ("st", [128, 8], F32)):
                setattr(T, nm, sb(f"{nm}_{i}", shape, dt))
                setattr(T, nm + "b", P.buf(f"{nm}_{i}"))
            T.i = i
            T.N = norm_scratch(P, sb, i, 1)
            T.bk = (2 * i, 2 * i + 1)
            return T
        NW = 4
        TB = [alloc_tile(i) for i in range(NW)]
        xts = [sb(f"xt4_{i}", [128, 1, D], F32) for i in range(2 * NW)]
        xtsb = [P.buf(f"xt4_{i}") for i in range(2 * NW)]
        pts = [sb(f"pt4_{i}", [128, 256], F32) for i in range(2 * NW)]
        ptsb = [P.buf(f"pt4_{i}") for i in range(2 * NW)]

        P.dma(gbc[:, :], gain_in.partition_broadcast(128), w=[C.gbcb], sig="gbc")
        P.dma(g4[:, :], gain_out.partition_broadcast(128), w=[g4b], sig="g4")
        wgr = wgate.rearrange("(kc p) f -> p kc f", p=128)
        wpr = wproj.rearrange("(kc p) f -> p kc f", p=128)
        P.dma(Wg[:, :, :], wgr[:, :, :], w=Wgb, sig="W0", eng='pool')
        P.dma(Wp[:, :, :], wpr[:, :, :], w=[Wpb], sig="W1", eng='pool')

        xin = x_in.rearrange("(t s p) d -> t p s d", s=1, p=128)
        xout = x_out.rearrange("(t s p) d -> t p s d", s=1, p=128)
        pin = p_in.rearrange("(t p) d -> t p d", p=128)

        def load_x(t):
            P.dma(xts[t % (2 * NW)][:, :, :], xin[t], w=[xtsb[t % (2 * NW)]], sig=f"xt{t % (2 * NW)}")
            P.dma(pts[t % (2 * NW)][:, :], pin[t], w=[ptsb[t % (2 * NW)]], sig=f"pt{t % (2 * NW)}")

        def body(t):
            T = TB[t % NW]
            xt_, xtb_, pt_, ptb_ = xts[t % (2 * NW)], xtsb[t % (2 * NW)], pts[t % (2 * NW)], ptsb[t % (2 * NW)]
            a, b = T.bk
            tp = B[a].bitcast(BF16).rearrange("p (k t) -> p k t", k=8)
            tpp = B[b].bitcast(BF16)[:, 0:256].rearrange("p (k t) -> p k t", k=2)
            emit_norm_T(P, C, xt_, xtb_, gbc[:, :], T.hT, T.hTb, tp, Bb[a], nsub=1, N=T.N)
            P.op('act', lambda e: e.activation(out=T.pb[:, :], in_=pt_[:, :], func=AF.Copy), r=[ptb_], w=[T.pbb])
            for kc in range(2):
                P.op('pe', lambda e, kc=kc: e.transpose(out=tpp[:, kc, :], in_=T.pb[:, kc * 128:(kc + 1) * 128], identity=C.ident[:, :]),
                     r=[T.pbb, C.constb], w=[Bb[b]])
            P.op('dve', lambda e: e.tensor_copy(out=T.pT[:, :, :], in_=tpp), r=[Bb[b]], w=[T.pTb])
            sg = (T.sg0, T.sg1)
            sgb = (T.sg0b, T.sg1b)
            for hh in range(2):
                for k in range(8):
                    P.op('pe', lambda e, hh=hh, k=k: e.matmul(B[a][:, :], lhsT=T.hT[:, k, :], rhs=Wg[:, k, hh * 512:(hh + 1) * 512],
                                                              start=(k == 0), stop=(k == 7)), r=[T.hTb, Wgb[k]], w=[Bb[a]])
                for kc in range(2):
                    P.op('pe', lambda e, hh=hh, kc=kc: e.matmul(B[b][:, :], lhsT=T.pT[:, kc, :], rhs=Wp[:, kc, hh * 512:(hh + 1) * 512],
                                                                start=(kc == 0), stop=(kc == 1)), r=[T.pTb, Wpb], w=[Bb[b]])
                P.op('act', lambda e, hh=hh: e.activation(out=sg[hh][:, :], in_=B[a][:, :], func=AF.Sigmoid), r=[Bb[a]], w=[sgb[hh]])
                P.op('dve', lambda e, hh=hh: e.tensor_tensor(out=T.ee[:, hh * 512:(hh + 1) * 512], in0=sg[hh][:, :], in1=B[b][:, :],
                                                             op=ALU.mult), r=[sgb[hh], Bb[b]], w=[T.eeb])
            P.op('act', lambda e: e.activation(out=C.junk[:, :], in_=T.ee[:, :], func=AF.Square, accum_out=T.st[:, 0:1]),
                 r=[T.eeb], w=[T.stb, C.junkb])
            P.op('dve', lambda e: e.tensor_scalar(out=T.st[:, 1:2], in0=T.st[:, 0:1], scalar1=1.0 / D, scalar2=EPS, op0=ALU.mult,
                                                  op1=ALU.add), r=[T.stb], w=[T.stb])
            P.op('pool', lambda e: e.tensor_tensor(out=T.st[:, 2:3], in0=T.st[:, 1:2], in1=C.neghalf[:, 0:1], op=ALU.pow),
                 r=[T.stb, C.constb], w=[T.stb])
            P.op('dve', lambda e: e.scalar_tensor_tensor(out=T.ee[:, :], in0=T.ee[:, :], scalar=T.st[:, 2:3], in1=g4[:, :],
                                                         op0=ALU.mult, op1=ALU.mult), r=[T.eeb, T.stb, g4b], w=[T.eeb])
            P.op('dve', lambda e: e.tensor_tensor(out=xt_[:, 0, :], in0=xt_[:, 0, :], in1=T.ee[:, :], op=ALU.add),
                 r=[T.eeb, xtb_], w=[xtb_])
            P.dma(xout[t], xt_[:, :, :], r=[xtb_], sig=f"xt{t % (2 * NW)}")

        for t in range(min(2 * NW, ntile)):
            load_x(t)
        for t0 in range(0, ntile, NW):
            ts = list(range(t0, min(t0 + NW, ntile)))
            P.replay([P.capture(lambda t=t: body(t)) for t in ts])
            for t in range(t0 + 2 * NW, min(t0 + 3 * NW, ntile)):
                load_x(t)
        P.flush()

NH = 8
QK = 192
ATTN_SCALE = 192 ** -0.5
PI = 3.14159265358979


def phase_mla1(P, C, x_in, Wd_, l, j, S, scrQ):
    nc = P.nc
    W = Wd_
    ntile = S // 128
    QTn, QTr, KTn, KTr, Vs = scrQ
    with ExitStack() as es:
        def sb(name, shape, dt):
            return es.enter_context(nc.sbuf_tensor(uname(name), shape, dt))

        def ps(name, shape, dt=F32):
            return es.enter_context(nc.psum_tensor(uname(name), shape, dt))
        Win = sb("Win", [128, 8, 704], BF16)
        Wq = sb("Wq", [128, 3, 1536], BF16)
        Wkv = sb("Wkv", [128, 2, 2048], BF16)
        Winb, Wqb, Wkvb = P.buf("Win"), P.buf("Wq"), P.buf("Wkv")
        gbc = sb("gbc", [128, D], F32)
        C.gbcb = P.buf("gbc")
        qa = sb("qa", [128, 384], F32)
        kva = sb("kva", [128, 256], F32)
        qn_g = sb("qn_g", [128, 192], F32)
        kn_g = sb("kn_g", [128, 192], F32)
        invf = sb("invf", [128, 2, 32], F32)
        gb = P.buf("gains")
        xt = [sb(f"xt{i}", [128, 1, D], F32) for i in range(4)]
        xtb = [P.buf(f"xt{i}") for i in range(4)]
        posi = [sb(f"posi{i}", [128, 1], I32) for i in range(4)]
        posb = [P.buf(f"posi{i}") for i in range(4)]
        C.xs = sb("xs", [128, 1, D], BF16)
        C.xsb = [P.buf("xs0")]
        hT = [sb(f"hT{i}", [128, 8, 128], BF16) for i in range(2)]
        hTb = [P.buf(f"hT{i}") for i in range(2)]
        def alloc_tile(i):
            T = Ctx()
            for nm, shape, dt in (("zt", [128, 704], F32), ("cn", [128, 640], BF16), ("cT", [128, 5, 128], BF16),
                                  ("qf", [128, 8, 192], F32), ("qs", [128, 8, 192], F32), ("qb16", [128, 8, 192], BF16),
                                  ("kf", [128, 8, 128], F32), ("kn16", [128, 8, 192], BF16), ("vb", [128, 8, 128], BF16),
                                  ("sm", [128, 64], F32), ("cs", [128, 2, 32], F32), ("rt", [128, 4, 8, 32], F32)):
                setattr(T, nm, sb(f"{nm}_{i}", shape, dt))
                setattr(T, nm + "b", P.buf(f"{nm}_{i}"))
            T.ang = sb(f"ang_{i}", [128, 2, 32], F32)
            T.angi = sb(f"angi_{i}", [128, 2, 32], I32)
            T.angk = sb(f"angk_{i}", [128, 2, 32], F32)
            T.msk = sb(f"msk_{i}", [128, 2, 32], F32)
            T.angb = P.buf(f"ang_{i}")
            T.kg = sb(f"kg_{i}", [128, 64], F32)
            T.kr = sb(f"kr_{i}", [128, 64], F32)
            T.kgb = P.buf(f"kg_{i}")
            T.oTn = [sb(f"oTn{i}_{q}", [128, 8, 128], BF16) for q in range(2)]
            T.oTnb = [P.buf(f"oTn{i}_{q}") for q in range(2)]
            T.oTr = [sb(f"oTr{i}_{q}", [64, 8, 128], BF16) for q in range(2)]
            T.oTrb = [P.buf(f"oTr{i}_{q}") for q in range(2)]
            T.i = i
            return T
        TB = [alloc_tile(0), alloc_tile(1)]
        NS = [norm_scratch(P, sb, i, 1) for i in range(2)]
        B = [ps(f"B{i}", [128, 512]) for i in range(8)]
        Bb = [P.buf(f"B{i}") for i in range(8)]
        def bankset(par):
            ia, ib, ic, id_ = 4 * par, 4 * par + 1, 4 * par + 2, 4 * par + 3
            R = Ctx()
            R.tp = B[id_].bitcast(BF16).rearrange("p (k t) -> p k t", k=8)
            R.tpi = id_
            R.z = (ia, ib)
            R.tpc = B[ib].bitcast(BF16)[:, 384:1024].rearrange("p (k t) -> p k t", k=5)
            R.tpci = ib
            R.q = (ia, ic, id_)
            R.kv = (ib, ia, ic, id_)
            R.tq_n = B[ia].bitcast(BF16).rearrange("p (k t) -> p k t", k=8)
            R.tq_r = B[ib].bitcast(BF16).rearrange("p (k t) -> p k t", k=8)
            R.tqi = (ia, ib)
            return R
        BS = [bankset(0), bankset(1)]

        ng = W['norm_gains']
        P.dma(gbc[:, :], ng[l, 1, :].partition_broadcast(128), w=[gb], sig="gbc", par=True)
        C.gbcb = gb
        P.dma(qa[:, :], W['mla_q_a_norm'][j, :].partition_broadcast(128), w=[gb], sig="qa", par=True)
        P.dma(kva[:, :], W['mla_kv_a_norm'][j, :].partition_broadcast(128), w=[gb], sig="kva", par=True)
        P.dma(qn_g[:, :], W['mla_q_norm'][j, :].partition_broadcast(128), w=[gb], sig="qn_g", par=True)
        P.dma(kn_g[:, :], W['mla_k_norm'][j, :].partition_broadcast(128), w=[gb], sig="kn_g", par=True)
        P.dma(invf[:, :, :], W['invf'].partition_broadcast(128), w=[gb], sig="invf", par=True)
        win_r = W['mla_w_in'][j].rearrange("(kc p) f -> p kc f", p=128)
        P.dma(Win[:, :, :], win_r[:, :, :], w=[Winb], sig="W0", eng='pool')
        wq_r = W['mla_w_q_b'][j].rearrange("(kc p) f -> p kc f", p=128)
        P.dma(Wq[:, :, :], wq_r[:, :, :], w=[Wqb], sig="W1", eng='pool')
        wkv_r = W['mla_w_kv_b'][j].rearrange("(kc p) f -> p kc f", p=128)
        P.dma(Wkv[:, :, 0:1024], wkv_r[:, :, 0:1024], w=[Wkvb], sig="W2", eng='pool')
        P.dma(Wkv[:, :, 1024:2048], wkv_r[:, :, 1024:2048], w=[Wkvb], sig="W3", eng='pool')

        xin = x_in.rearrange("(t s p) d -> t p s d", s=1, p=128)
        pin = W['positions'].rearrange("(t p o) -> t p o", p=128, o=1)

        def load_x(t):
            P.dma(xt[t % 4][:, :, :], xin[t], w=[xtb[t % 4]], sig=f"xt{t % 4}")
            P.dma(posi[t % 4][:, :], pin[t], w=[posb[t % 4]], sig=f"posi{t % 4}")

        def norm(t):
            emit_norm_T(P, C, xt[t % 4], xtb[t % 4], gbc[:, :], hT[t % 2], hTb[t % 2], BS[t % 2].tp, Bb[BS[t % 2].tpi], nsub=1, N=NS[t % 2])

        def rope(T, src3, dst3, nh, tag):
            x1 = src3[:, :, 0:32]
            x2 = src3[:, :, 32:64]
            sinb = T.cs[:, 0:1, :].broadcast_to([128, nh, 32])
            cosb = T.cs[:, 1:2, :].broadcast_to([128, nh, 32])
            r = T.rt
            P.op('dve', lambda e: e.tensor_tensor(out=r[:, 0, 0:nh, :], in0=x1, in1=cosb, op=ALU.mult), r=[tag, T.csb], w=[T.rtb])
            P.op('dve', lambda e: e.tensor_tensor(out=r[:, 1, 0:nh, :], in0=x2, in1=sinb, op=ALU.mult), r=[tag, T.csb], w=[T.rtb])
            P.op('dve', lambda e: e.tensor_tensor(out=r[:, 2, 0:nh, :], in0=x2, in1=cosb, op=ALU.mult), r=[tag, T.csb], w=[T.rtb])
            P.op('dve', lambda e: e.tensor_tensor(out=r[:, 3, 0:nh, :], in0=x1, in1=sinb, op=ALU.mult), r=[tag, T.csb], w=[T.rtb])
            return r

        def body(t):
            T = TB[t % 2]
            R = BS[t % 2]
            h = hT[t % 2]
            norm(t)
            yield
            yield
            pf = T.sm[:, 40:41]
            P.op('dve', lambda e: e.tensor_copy(out=pf, in_=posi[t % 4][:, :]), r=[posb[t % 4]], w=[T.angb])
            P.op('dve', lambda e: e.tensor_scalar(out=T.ang[:, :, :], in0=invf[:, :, :], scalar1=pf, scalar2=None, op0=ALU.mult),
                 r=[T.angb, gb], w=[T.angb])
            P.op('dve', lambda e: e.tensor_scalar(out=T.ang[:, 1, :], in0=T.ang[:, 1, :], scalar1=PI / 2, scalar2=None, op0=ALU.add),
                 r=[T.angb], w=[T.angb])
            P.op('dve', lambda e: e.tensor_scalar(out=T.angk[:, :, :], in0=T.ang[:, :, :], scalar1=1.0 / (2 * PI), scalar2=None,
                                                  op0=ALU.mult), r=[T.angb], w=[T.angb])
            P.op('dve', lambda e: e.tensor_copy(out=T.angi[:, :, :], in_=T.angk[:, :, :]), r=[T.angb], w=[T.angb])
            P.op('dve', lambda e: e.tensor_copy(out=T.angk[:, :, :], in_=T.angi[:, :, :]), r=[T.angb], w=[T.angb])
            P.op('dve', lambda e: e.scalar_tensor_tensor(out=T.ang[:, :, :], in0=T.angk[:, :, :], scalar=-2 * PI, in1=T.ang[:, :, :],
                                                         op0=ALU.mult, op1=ALU.add), r=[T.angb], w=[T.angb])
            P.op('dve', lambda e: e.tensor_scalar(out=T.msk[:, :, :], in0=T.ang[:, :, :], scalar1=PI, scalar2=-2 * PI, op0=ALU.is_gt,
                                                  op1=ALU.mult), r=[T.angb], w=[T.angb])
            P.op('dve', lambda e: e.tensor_tensor(out=T.ang[:, :, :], in0=T.ang[:, :, :], in1=T.msk[:, :, :], op=ALU.add), r=[T.angb], w=[T.angb])
            P.op('dve', lambda e: e.tensor_scalar(out=T.msk[:, :, :], in0=T.ang[:, :, :], scalar1=-PI, scalar2=2 * PI, op0=ALU.is_lt,
                                                  op1=ALU.mult), r=[T.angb], w=[T.angb])
            P.op('dve', lambda e: e.tensor_tensor(out=T.ang[:, :, :], in0=T.ang[:, :, :], in1=T.msk[:, :, :], op=ALU.add), r=[T.angb], w=[T.angb])
            P.op('dve', lambda e: e.tensor_scalar(out=T.ang[:, :, :], in0=T.ang[:, :, :], scalar1=PI, scalar2=-PI, op0=ALU.min,
                                                  op1=ALU.max), r=[T.angb], w=[T.angb])
            P.op('act', lambda e: e.activation(out=T.cs[:, :, :], in_=T.ang[:, :, :], func=AF.Sin), r=[T.angb], w=[T.csb])
            yield
            for c0, c1, bk in ((0, 512, R.z[0]), (512, 704, R.z[1])):
                for k in range(8):
                    P.op('pe', lambda e, k=k, c0=c0, c1=c1, bk=bk: e.matmul(B[bk][:, 0:c1 - c0], lhsT=h[:, k, :], rhs=Win[:, k, c0:c1],
                                                                          start=(k == 0), stop=(k == 7)),
                         r=[hTb[t % 2], Winb], w=[Bb[bk]])
            P.op('act', lambda e: e.activation(out=T.zt[:, 0:512], in_=B[R.z[0]][:, :], func=AF.Copy), r=[Bb[R.z[0]]], w=[T.ztb])
            P.op('act', lambda e: e.activation(out=T.zt[:, 512:704], in_=B[R.z[1]][:, 0:192], func=AF.Copy), r=[Bb[R.z[1]]], w=[T.ztb])
            yield
            P.op('act', lambda e: e.activation(out=C.junk[:, 0:384], in_=T.zt[:, 0:384], func=AF.Square, accum_out=T.sm[:, 0:1]),
                 r=[T.ztb], w=[T.smb, C.junkb])
            P.op('act', lambda e: e.activation(out=C.junk[:, 0:256], in_=T.zt[:, 384:640], func=AF.Square, accum_out=T.sm[:, 1:2]),
                 r=[T.ztb], w=[T.smb, C.junkb])
            P.op('act', lambda e: e.activation(out=C.junk[:, 0:64], in_=T.zt[:, 640:704], func=AF.Square, accum_out=T.sm[:, 2:3]),
                 r=[T.ztb], w=[T.smb, C.junkb])
            P.op('dve', lambda e: e.tensor_scalar(out=T.sm[:, 4:5], in0=T.sm[:, 0:1], scalar1=1.0 / 384, scalar2=EPS, op0=ALU.mult,
                                                  op1=ALU.add), r=[T.smb], w=[T.smb])
            P.op('dve', lambda e: e.tensor_scalar(out=T.sm[:, 5:6], in0=T.sm[:, 1:2], scalar1=1.0 / 256, scalar2=EPS, op0=ALU.mult,
                                                  op1=ALU.add), r=[T.smb], w=[T.smb])
            P.op('pool', lambda e: e.tensor_tensor(out=T.sm[:, 6:8], in0=T.sm[:, 4:6], in1=C.neghalf[:, 0:2], op=ALU.pow),
                 r=[T.smb, C.constb], w=[T.smb])
            P.op('dve', lambda e: e.scalar_tensor_tensor(out=T.cn[:, 0:384], in0=T.zt[:, 0:384], scalar=T.sm[:, 6:7], in1=qa[:, :],
                                                         op0=ALU.mult, op1=ALU.mult), r=[T.ztb, T.smb, gb], w=[T.cnb])
            P.op('dve', lambda e: e.scalar_tensor_tensor(out=T.cn[:, 384:640], in0=T.zt[:, 384:640], scalar=T.sm[:, 7:8], in1=kva[:, :],
                                                         op0=ALU.mult, op1=ALU.mult), r=[T.ztb, T.smb, gb], w=[T.cnb])
            for k in range(5):
                P.op('pe', lambda e, k=k: e.transpose(out=R.tpc[:, k, :], in_=T.cn[:, k * 128:(k + 1) * 128], identity=C.ident[:, :]),
                     r=[T.cnb, C.constb], w=[Bb[R.tpci]])
            P.op('dve', lambda e: e.tensor_copy(out=T.cT[:, :, :], in_=R.tpc), r=[Bb[R.tpci]], w=[T.cTb])
            yield
            for c in range(3):
                for k in range(3):
                    P.op('pe', lambda e, c=c, k=k: e.matmul(B[R.q[c]][:, :], lhsT=T.cT[:, k, :], rhs=Wq[:, k, c * 512:(c + 1) * 512],
                                                            start=(k == 0), stop=(k == 2)), r=[T.cTb, Wqb], w=[Bb[R.q[c]]])
            qf2 = T.qf[:, :, :].rearrange("p h d -> p (h d)")
            for c in range(3):
                P.op('act', lambda e, c=c: e.activation(out=qf2[:, c * 512:(c + 1) * 512], in_=B[R.q[c]][:, :], func=AF.Copy),
                     r=[Bb[R.q[c]]], w=[T.qfb])
            yield
            kvb = R.kv
            for c in range(4):
                for k in range(2):
                    P.op('pe', lambda e, c=c, k=k: e.matmul(B[kvb[c]][:, :], lhsT=T.cT[:, 3 + k, :], rhs=Wkv[:, k, c * 512:(c + 1) * 512],
                                                            start=(k == 0), stop=(k == 1)), r=[T.cTb, Wkvb], w=[Bb[kvb[c]]])
            for c in range(4):
                kv3 = B[kvb[c]][:, :].rearrange("p (h d) -> p h d", h=2)
                P.op('act', lambda e, c=c, kv3=kv3: e.activation(out=T.kf[:, 2 * c:2 * c + 2, :], in_=kv3[:, :, 0:128], func=AF.Copy),
                     r=[Bb[kvb[c]]], w=[T.kfb])
                P.op('act', lambda e, c=c, kv3=kv3: e.activation(out=T.vb[:, 2 * c:2 * c + 2, :], in_=kv3[:, :, 128:256], func=AF.Copy),
                     r=[Bb[kvb[c]]], w=[T.vbb])
            P.dma(Vs[t * 128:(t + 1) * 128, :, :], T.vb[:, :, :], r=[T.vbb], sig=f"vb{T.i}")
            yield
            for hh in range(NH):
                P.op('act', lambda e, hh=hh: e.activation(out=C.junk[:, 0:192], in_=T.qf[:, hh, :], func=AF.Square,
                                                           accum_out=T.sm[:, 8 + hh:9 + hh]), r=[T.qfb], w=[T.smb, C.junkb])
            P.op('dve', lambda e: e.tensor_scalar(out=T.sm[:, 16:24], in0=T.sm[:, 8:16], scalar1=1.0 / QK, scalar2=EPS, op0=ALU.mult,
                                                  op1=ALU.add), r=[T.smb], w=[T.smb])
            P.op('pool', lambda e: e.tensor_tensor(out=T.sm[:, 24:32], in0=T.sm[:, 16:24], in1=C.neghalf[:, 0:8], op=ALU.pow),
                 r=[T.smb, C.constb], w=[T.smb])
            P.op('dve', lambda e: e.tensor_tensor(out=T.qs[:, :, :], in0=T.qf[:, :, :],
                                                  in1=T.sm[:, 24:32].unsqueeze(2).broadcast_to([128, NH, QK]), op=ALU.mult),
                 r=[T.qfb, T.smb], w=[T.qsb])
            P.op('dve', lambda e: e.tensor_tensor(out=T.qs[:, :, :], in0=T.qs[:, :, :], in1=qn_g[:, :].unsqueeze(1).broadcast_to([128, NH, QK]),
                                                  op=ALU.mult), r=[T.qsb, gb], w=[T.qsb])
            P.op('act', lambda e: e.activation(out=T.qb16[:, :, 0:128], in_=T.qs[:, :, 0:128], func=AF.Copy), r=[T.qsb], w=[T.qb16b])
            r = rope(T, T.qs[:, :, 128:192], None, NH, T.qsb)
            P.op('dve', lambda e: e.tensor_tensor(out=T.qb16[:, :, 128:160], in0=r[:, 0, :, :], in1=r[:, 1, :, :], op=ALU.subtract),
                 r=[T.rtb], w=[T.qb16b])
            P.op('dve', lambda e: e.tensor_tensor(out=T.qb16[:, :, 160:192], in0=r[:, 2, :, :], in1=r[:, 3, :, :], op=ALU.add),
                 r=[T.rtb], w=[T.qb16b])
            yield
            for hh in range(NH):
                P.op('act', lambda e, hh=hh: e.activation(out=C.junk[:, 0:128], in_=T.kf[:, hh, :], func=AF.Square,
                                                           accum_out=T.sm[:, 32 + hh:33 + hh]), r=[T.kfb], w=[T.smb, C.junkb])
            P.op('dve', lambda e: e.tensor_scalar(out=T.sm[:, 48:56], in0=T.sm[:, 32:40], scalar1=T.sm[:, 2:3], scalar2=1.0 / QK, op0=ALU.add,
                                                  op1=ALU.mult), r=[T.smb], w=[T.smb])
            P.op('dve', lambda e: e.tensor_scalar(out=T.sm[:, 48:56], in0=T.sm[:, 48:56], scalar1=EPS, scalar2=None, op0=ALU.add),
                 r=[T.smb], w=[T.smb])
            P.op('pool', lambda e: e.tensor_tensor(out=T.sm[:, 56:64], in0=T.sm[:, 48:56], in1=C.neghalf[:, 0:8], op=ALU.pow),
                 r=[T.smb, C.constb], w=[T.smb])
            P.op('dve', lambda e: e.tensor_tensor(out=T.kf[:, :, :], in0=T.kf[:, :, :],
                                                  in1=T.sm[:, 56:64].unsqueeze(2).broadcast_to([128, NH, 128]), op=ALU.mult),
                 r=[T.kfb, T.smb], w=[T.kfb])
            P.op('dve', lambda e: e.tensor_tensor(out=T.kn16[:, :, 0:128], in0=T.kf[:, :, :],
                                                  in1=kn_g[:, 0:128].unsqueeze(1).broadcast_to([128, NH, 128]), op=ALU.mult),
                 r=[T.kfb, gb], w=[T.kn16b])
            P.op('dve', lambda e: e.tensor_tensor(out=T.kg[:, :], in0=T.zt[:, 640:704], in1=kn_g[:, 128:192], op=ALU.mult),
                 r=[T.ztb, gb], w=[T.kgb])
            r = rope(T, T.kg[:, :].unsqueeze(1), None, 1, T.kgb)
            P.op('dve', lambda e: e.tensor_tensor(out=T.kr[:, 0:32], in0=r[:, 0, 0, :], in1=r[:, 1, 0, :], op=ALU.subtract),
                 r=[T.rtb], w=[T.kgb])
            P.op('dve', lambda e: e.tensor_tensor(out=T.kr[:, 32:64], in0=r[:, 2, 0, :], in1=r[:, 3, 0, :], op=ALU.add),
                 r=[T.rtb], w=[T.kgb])
            P.op('dve', lambda e: e.tensor_tensor(out=T.kn16[:, :, 128:192], in0=T.kr[:, :].unsqueeze(1).broadcast_to([128, NH, 64]),
                                                  in1=T.sm[:, 56:64].unsqueeze(2).broadcast_to([128, NH, 64]), op=ALU.mult),
                 r=[T.kgb, T.smb], w=[T.kn16b])
            yield
            for (src, srcb, dn, dr, sl) in ((T.qb16, T.qb16b, QTn, QTr, 0), (T.kn16, T.kn16b, KTn, KTr, 1)):
                for hh in range(NH):
                    P.op('pe', lambda e, hh=hh, src=src: e.transpose(out=R.tq_n[:, hh, :], in_=src[:, hh, 0:128], identity=C.ident[:, :]),
                         r=[srcb, C.constb], w=[Bb[R.tqi[0]]])
                for hh in range(NH):
                    P.op('pe', lambda e, hh=hh, src=src: e.transpose(out=R.tq_r[0:64, hh, :], in_=src[:, hh, 128:192],
                                                                     identity=C.ident[:, :]), r=[srcb, C.constb], w=[Bb[R.tqi[1]]])
                P.op('act', lambda e, sl=sl: e.activation(out=T.oTn[sl][:, :, :], in_=R.tq_n, func=AF.Copy), r=[Bb[R.tqi[0]]], w=[T.oTnb[sl]])
                P.op('dve', lambda e, sl=sl: e.tensor_copy(out=T.oTr[sl][:, :, :], in_=R.tq_r[0:64, :, :]), r=[Bb[R.tqi[1]]], w=[T.oTrb[sl]])
                P.dma(dn[:, :, t * 128:(t + 1) * 128].rearrange("h d t -> d h t"), T.oTn[sl][:, :, :], r=[T.oTnb[sl]], sig=f"oTn{T.i}_{sl}")
                P.dma(dr[:, :, t * 128:(t + 1) * 128].rearrange("h d t -> d h t"), T.oTr[sl][:, :, :], r=[T.oTrb[sl]], sig=f"oTr{T.i}_{sl}")

        for t in range(min(4, ntile)):
            load_x(t)
        for t0 in range(0, ntile, 2):
            lists = [P.capture(lambda t=t: [None for _ in body(t)]) for t in range(t0, min(t0 + 2, ntile))]
            P.replay(lists)
            for t in range(t0 + 4, min(t0 + 6, ntile)):
                load_x(t)
        P.flush()


def phase_mla2(P, C, x_in, x_out, Wd_, j, g, S, scrQ):
    nc = P.nc
    W = Wd_
    QTn, QTr, KTn, KTr, Vs = scrQ
    nkt = S // 128
    QB = 512 if S >= 512 else S
    nqb = S // QB
    nsubq = QB // 128
    HG = 4
    with ExitStack() as es:
        def sb(name, shape, dt):
            return es.enter_context(nc.sbuf_tensor(uname(name), shape, dt))

        def ps(name, shape, dt=F32):
            return es.enter_context(nc.psum_tensor(uname(name), shape, dt))
        Kn = sb("Kn", [128, HG, S], BF16)
        Kr = sb("Kr", [64, HG, S], BF16)
        Vt = sb("Vt", [128, nkt, HG, 128], BF16)
        Knb = [P.buf(f"Kn{h}") for h in range(HG)]
        Krb = [P.buf(f"Kr{h}") for h in range(HG)]
        Vtb = P.buf("Vt")
        Wo = sb("Wo", [128, HG, D], BF16)
        Wob = P.buf("Wo")
        ones = sb("ones", [128, 128], BF16)
        tri = sb("tri", [128, 128], BF16)
        trif = sb("trif", [128, 128], F32)
        cb = P.buf("mconst")
        xt = sb("xt", [128, nsubq, D], F32)
        xtb = P.buf("xt")
        Qn = [sb(f"Qn{i}", [128, HG, QB], BF16) for i in range(2)]
        Qr = [sb(f"Qr{i}", [64, HG, QB], BF16) for i in range(2)]
        Qb_ = [P.buf(f"Q{i}") for i in range(2)]
        pT = [sb(f"pT{i}", [128, QB], BF16) for i in range(3)]
        pTb = [P.buf(f"pT{i}") for i in range(3)]
        rc = sb("rc", [128, QB], F32)
        rcb = P.buf("rc")
        aT = [sb(f"aT{h}", [128, QB], BF16) for h in range(HG)]
        aTb = [P.buf(f"aT{h}") for h in range(HG)]
        sT = [ps(f"sT{i}", [128, 512]) for i in range(2)]
        sTb = [P.buf(f"sT{i}") for i in range(2)]
        oT = [ps(f"oT{i}", [128, 512]) for i in range(2)]
        oTb = [P.buf(f"oT{i}") for i in range(2)]
        lS = [ps(f"lS{i}", [128, 512]) for i in range(2)]
        lSb = [P.buf(f"lS{i}") for i in range(2)]
        Y = [ps(f"Y{i}", [128, 512]) for i in range(2)]
        Yb = [P.buf(f"Y{i}") for i in range(2)]

        P.op('pool', lambda e: e.memset(ones[:, :], 1.0), w=[cb])
        P.op('pool', lambda e: e.memset(trif[:, :], 1.0), w=[cb])
        P.op('pool', lambda e: e.affine_select(out=trif[:, :], in_=trif[:, :], pattern=[[1, 128]], compare_op=ALU.is_ge, fill=0.0,
                                               base=0, channel_multiplier=-1), r=[cb], w=[cb])
        P.op('pool', lambda e: e.tensor_copy(out=tri[:, :], in_=trif[:, :]), r=[cb], w=[cb])
        for h in range(HG):
            P.dma(Kn[:, h, :], KTn[g * HG + h, :, :], w=[Knb[h]], sig=f"Kn{h}")
            P.dma(Kr[:, h, :], KTr[g * HG + h, :, :], w=[Krb[h]], sig=f"Kr{h}")
            if h == 0:
                P.dma(Vt[:, :, :, :], Vs[:, g * HG:(g + 1) * HG, :].rearrange("(kt p) h e -> p kt h e", p=128), w=[Vtb], sig="Vt")
        wo_r = W['mla_w_out'][j][g * HG * 128:(g + 1) * HG * 128, :].rearrange("(h p) d -> p h d", p=128)
        P.dma(Wo[:, :, :], wo_r[:, :, :], w=[Wob], sig="W0", eng='pool')

        xin = x_in.rearrange("(b s p) d -> b p s d", s=nsubq, p=128)
        xout = x_out.rearrange("(b s p) d -> b p s d", s=nsubq, p=128)

        def load_q(b):
            sl = b % 2
            P.dma(Qn[sl][:, :, :], QTn[g * HG:(g + 1) * HG, :, b * QB:(b + 1) * QB].rearrange("h d t -> d h t"), w=[Qb_[sl]],
                  sig=f"Qn{sl}")
            P.dma(Qr[sl][:, :, :], QTr[g * HG:(g + 1) * HG, :, b * QB:(b + 1) * QB].rearrange("h d t -> d h t"), w=[Qb_[sl]],
                  sig=f"Qr{sl}")

        cnt = [0]

        def qk_mm(b, h, kt, ssl, qs_):
            r = kt - nsubq * b
            c0 = max(r, 0) * 128
            P.op('pe', lambda e: e.matmul(sT[ssl][:, c0:QB], lhsT=Kn[:, h, kt * 128:(kt + 1) * 128],
                                          rhs=Qn[qs_][:, h, c0:QB], start=True, stop=False),
                 r=[Knb[h], Qb_[qs_]], w=[sTb[ssl]])
            P.op('pe', lambda e: e.matmul(sT[ssl][:, c0:QB], lhsT=Kr[:, h, kt * 128:(kt + 1) * 128],
                                          rhs=Qr[qs_][:, h, c0:QB], start=False, stop=True),
                 r=[Krb[h], Qb_[qs_]], w=[sTb[ssl]])

        def pv_mm(b, h, kt, nk, ob, ssl, psl):
            r = kt - nsubq * b
            c0 = max(r, 0) * 128
            P.op('act', lambda e: e.activation(out=pT[psl][:, c0:QB], in_=sT[ssl][:, c0:QB], func=AF.Exp,
                                               scale=ATTN_SCALE), r=[sTb[ssl]], w=[pTb[psl]])
            if r >= 0:
                P.op('dve', lambda e: e.tensor_tensor(out=pT[psl][:, c0:c0 + 128], in0=pT[psl][:, c0:c0 + 128],
                                                      in1=tri[:, :], op=ALU.mult), r=[pTb[psl], cb], w=[pTb[psl]])
            P.op('pe', lambda e: e.matmul(oT[ob][:, c0:QB], lhsT=Vt[:, kt, h, :], rhs=pT[psl][:, c0:QB],
                                          start=(kt == 0), stop=(kt == nk - 1)),
                 r=[Vtb, pTb[psl]], w=[oTb[ob]])
            P.op('pe', lambda e: e.matmul(lS[ob][:, c0:QB], lhsT=ones[:, :], rhs=pT[psl][:, c0:QB],
                                          start=(kt == 0), stop=(kt == nk - 1)),
                 r=[cb, pTb[psl]], w=[lSb[ob]])

        def head_fin(b, h, ob):
            P.op('dve', lambda e: e.reciprocal(out=rc[:, :], in_=lS[ob][:, 0:QB]), r=[lSb[ob]], w=[rcb])
            P.op('dve', lambda e: e.tensor_tensor(out=aT[h][:, :], in0=oT[ob][:, 0:QB], in1=rc[:, :], op=ALU.mult),
                 r=[oTb[ob], rcb], w=[aTb[h]])

        def do_block(b, first_issued):
            qs_ = b % 2
            nk = nsubq * (b + 1)
            pairs = [(h, kt) for h in range(HG) for kt in range(nk)]
            base = cnt[0]
            cnt[0] += len(pairs)
            if not first_issued:
                qk_mm(b, pairs[0][0], pairs[0][1], base % 2, qs_)
            for i, (h, kt) in enumerate(pairs):
                if i + 1 < len(pairs):
                    qk_mm(b, pairs[i + 1][0], pairs[i + 1][1], (base + i + 1) % 2, qs_)
                ob = (b * HG + h) % 2
                pv_mm(b, h, kt, nk, ob, (base + i) % 2, (base + i) % 3)
                if kt == nk - 1:
                    head_fin(b, h, ob)

        def do_proj(b, sq, dh):
            for h in range(HG):
                P.op('pe', lambda e, h=h: e.matmul(Y[dh][:, :], lhsT=aT[h][:, sq * 128:(sq + 1) * 128],
                                                   rhs=Wo[:, h, dh * 512:(dh + 1) * 512], start=(h == 0),
                                                   stop=(h == HG - 1)), r=[aTb[h], Wob], w=[Yb[dh]])
            P.op('dve', lambda e: e.tensor_tensor(out=xt[:, sq, dh * 512:(dh + 1) * 512],
                                                  in0=xt[:, sq, dh * 512:(dh + 1) * 512], in1=Y[dh][:, :],
                                                  op=ALU.add), r=[Yb[dh], xtb], w=[xtb])

        load_q(0)
        for b in range(nqb):
            if b + 1 < nqb:
                load_q(b + 1)
            P.dma(xt[:, :, :], xin[b], w=[xtb], sig="xt")
            do_block(b, b > 0)
            if b + 1 < nqb:
                qk_mm(b + 1, 0, 0, cnt[0] % 2, (b + 1) % 2)
            for sq in range(nsubq):
                for dh in range(2):
                    do_proj(b, sq, dh)
            P.dma(xout[b], xt[:, :, :], r=[xtb], sig="xt")
        P.flush()


def phase_even(P, C, x_in, x_out, Wd_, l, j, S):
    nc = P.nc
    W = Wd_
    ntile = S // 128
    with ExitStack() as es:
        def sb(name, shape, dt):
            return es.enter_context(nc.sbuf_tensor(uname(name), shape, dt))

        def ps(name, shape, dt=F32):
            return es.enter_context(nc.psum_tensor(uname(name), shape, dt))
        Win = sb("eWin", [128, 8, 3072], BF16)
        Winb = [P.buf(f"eWin{g}") for g in range(6)]
        Wo = sb("eWo", [128, 8, D], BF16)
        Wob = P.buf("eWo")
        gbc = sb("gbc", [128, D], F32)
        gb = P.buf("gains")
        C.gbcb = gb
        vnorm = sb("vnorm", [128, 512], F32)
        onorm = sb("onorm", [128, 128], F32)
        bs = sb("bs", [128, 8, 1], F32)
        lbr = sb("lbr", [128, 2, 4, 1], F32)
        lb = sb("lb", [128, 4], F32)
        oml = sb("oml", [128, 4], F32)
        wsf = sb("wsf", [128, 8, 128], F32)
        ws16 = sb("ws16", [128, 8, 128], BF16)
        WsT = sb("WsT", [128, 8, 128], BF16)
        trif = sb("trif", [128, 128], F32)
        bmask = sb("bmask", [128, 128], F32)
        rmask = sb("rmask", [128, 512], F32)
        cb = P.buf("econst")
        xt = [sb(f"xt{i}", [128, 1, D], F32) for i in range(4)]
        xtb = [P.buf(f"xt{i}") for i in range(4)]
        hT = [sb(f"hT{i}", [128, 8, 128], BF16) for i in range(2)]
        hTb = [P.buf(f"hT{i}") for i in range(2)]
        S32 = sb("S32", [128, 4, 128], F32)
        S16 = sb("S16", [128, 4, 128], BF16)
        S32b, S16b = P.buf("S32"), P.buf("S16")
        def alloc_tile(i):
            T = Ctx()
            for nm, shape, dt in (("f1", [128, 4, 128], F32), ("kk", [128, 4, 128], F32), ("lf", [128, 4, 128], F32),
                                  ("Bc", [128, 4, 128], F32), ("e1", [128, 4, 128], F32), ("e2", [128, 4, 128], F32),
                                  ("e3", [128, 4, 128], F32), ("eBC", [128, 4, 4, 1], F32), ("qf32", [128, 4, 128], F32),
                                  ("qt", [128, 4, 128], BF16), ("kt_", [128, 4, 128], BF16), ("kh", [128, 4, 128], BF16),
                                  ("khT", [128, 4, 128], BF16), ("v16", [128, 4, 128], BF16), ("gu_", [128, 512], F32),
                                  ("gv", [128, 8, 64], F32), ("sq", [128, 8, 64], F32), ("vn16", [128, 8, 64], BF16),
                                  ("sgl", [128, 512], F32), ("on", [128, 4, 128], F32), ("mT", [128, 8, 128], BF16),
                                  ("sm", [128, 64], F32)):
                setattr(T, nm, sb(f"{nm}_{i}", shape, dt))
                setattr(T, nm + "b", P.buf(f"{nm}_{i}"))
            T.PT = [sb(f"PT{h}_{i}", [128, 128], BF16) for h in range(4)]
            T.PTb = [P.buf(f"PT{h}_{i}") for h in range(4)]
            T.mixed = sb(f"mixed_{i}", [128, D], BF16)
            T.mixb = [P.buf(f"mixA_{i}"), P.buf(f"mixB_{i}")]
            return T
        TB = [alloc_tile(0), alloc_tile(1)]
        NS = [norm_scratch(P, sb, i, 1) for i in range(2)]
        B = [ps(f"B{i}", [128, 512]) for i in range(8)]
        Bb = [P.buf(f"B{i}") for i in range(8)]

        def b3(i, a):
            return B[i][:, :].rearrange("p (a t) -> p a t", a=a)

        def b16(i, a, n):
            return B[i].bitcast(BF16)[:, 0:a * n].rearrange("p (a t) -> p a t", a=a)

        ng = W['norm_gains']
        P.dma(gbc[:, :], ng[l, 1, :].partition_broadcast(128), w=[gb], sig="gbc", par=True)
        P.dma(vnorm[:, :], W['gmlp_v_norm'][j].rearrange("h d -> (h d)").partition_broadcast(128), w=[gb], sig="vnorm", par=True)
        P.dma(onorm[:, :], W['hgrn_out_norm'][j, :].partition_broadcast(128), w=[gb], sig="onorm", par=True)
        P.dma(bs[:, :, :], W['gmlp_b_s'][j].rearrange("h (t o) -> t h o", o=1), w=[gb], sig="bs", slow=True, par=True)
        P.dma(lbr[:, :, :, :], W['hgrn_lb_raw'].rearrange("r (h d o) -> d r h o", h=4, o=1), w=[gb], sig="lbr", slow=True, par=True)
        P.dma(wsf[:, :, :], W['gmlp_w_s'][j].rearrange("h t s -> t h s"), w=[gb], sig="wsf", par=True)
        P.op('pool', lambda e: e.memset(trif[:, :], 1.0), w=[cb])
        P.op('pool', lambda e: e.memset(bmask[:, :], 1.0), w=[cb])
        P.op('pool', lambda e: e.memset(rmask[:, :], 1.0), w=[cb])
        P.op('pool', lambda e: e.memset(rmask[:, :].rearrange("p (c t) -> p c t", t=32)[:, :, 0:1], 0.0), w=[cb])
        P.op('pool', lambda e: e.memset(S32[:, :, :], 0.0), w=[S32b])
        P.op('pool', lambda e: e.memset(S16[:, :, :], 0.0), w=[S16b])
        P.op('pool', lambda e: e.affine_select(out=trif[:, :], in_=trif[:, :], pattern=[[-1, 128]], compare_op=ALU.is_ge, fill=0.0,
                                               base=0, channel_multiplier=1), r=[cb], w=[cb])
        P.op('pool', lambda e: e.affine_select(out=bmask[:, :], in_=bmask[:, :], pattern=[[1, 128]], compare_op=ALU.is_ge, fill=0.0,
                                               base=0, channel_multiplier=-1), r=[cb], w=[cb])
        for cbk in range(1, 4):
            P.op('pool', lambda e, cbk=cbk: e.memset(bmask[0:32 * cbk, 32 * cbk:32 * cbk + 32], 0.0), r=[cb], w=[cb])
        P.op('dve', lambda e: e.tensor_tensor(out=ws16[:, :, :], in0=wsf[:, :, :], in1=trif[:, :].unsqueeze(1).broadcast_to([128, 8, 128]),
                                              op=ALU.mult), r=[gb, cb], w=[cb])
        tws = b16(7, 8, 128)
        for hh in range(8):
            P.op('pe', lambda e, hh=hh: e.transpose(out=tws[:, hh, :], in_=ws16[:, hh, :], identity=C.ident[:, :]), r=[cb, C.constb],
                 w=[Bb[7]])
        P.op('dve', lambda e: e.tensor_copy(out=WsT[:, :, :], in_=tws), r=[Bb[7]], w=[cb])
        if j == 0:
            P.op('pool', lambda e: e.memset(lb[:, :], 0.0), w=[cb])
        else:
            P.op('dve', lambda e: e.tensor_tensor(out=lb[:, :], in0=lbr[:, 1, :, 0], in1=lbr[:, 0, :, 0], op=ALU.subtract), r=[gb], w=[cb])
            P.op('act', lambda e: e.activation(out=lb[:, :], in_=lb[:, :], func=AF.Sigmoid), r=[cb], w=[cb])
            P.op('dve', lambda e: e.tensor_scalar(out=lb[:, :], in0=lb[:, :], scalar1=0.999, scalar2=0.0, op0=ALU.min, op1=ALU.max),
                 r=[cb], w=[cb])
        P.op('dve', lambda e: e.tensor_scalar(out=oml[:, :], in0=lb[:, :], scalar1=-1.0, scalar2=1.0, op0=ALU.mult, op1=ALU.add),
             r=[cb], w=[cb])
        win_r = W['even_w_in'][j].rearrange("(kc p) f -> p kc f", p=128)
        for g in (0, 1, 4, 5, 2, 3):
            P.dma(Win[:, :, g * 512:(g + 1) * 512], win_r[:, :, g * 512:(g + 1) * 512], w=[Winb[g]], sig=f"W{g}", eng='pool')
        wo_r = W['even_w_out'][j].rearrange("(kc p) f -> p kc f", p=128)
        P.dma(Wo[:, :, :], wo_r[:, :, :], w=[Wob], sig="W6", eng='pool')

        xin = x_in.rearrange("(t s p) d -> t p s d", s=1, p=128)
        xout = x_out.rearrange("(t s p) d -> t p s d", s=1, p=128)

        def load_x(t):
            P.dma(xt[t % 4][:, :, :], xin[t], w=[xtb[t % 4]], sig=f"xt{t % 4}")

        def bankset(par):
            R = Ctx()
            R.p = (4 * par, 4 * par + 1, 4 * par + 2)
            R.o = 4 * par + 3
            return R
        BS = [bankset(0), bankset(1)]

        def pre_norm(t):
            p0 = BS[t % 2].p[0]
            tp = B[p0].bitcast(BF16).rearrange("p (k t) -> p k t", k=8)
            emit_norm_T(P, C, xt[t % 4], xtb[t % 4], gbc[:, :], hT[t % 2], hTb[t % 2], tp, Bb[p0], nsub=1, N=NS[t % 2])

        def pre(t):
            T = TB[t % 2]
            R = BS[t % 2]
            p0, p1, p2 = R.p
            h = hT[t % 2]
            hb = hTb[t % 2]

            def proj_tok(bk, c0):
                for k in range(8):
                    P.op('pe', lambda e, k=k: e.matmul(B[bk][:, :], lhsT=h[:, k, :], rhs=Win[:, k, c0:c0 + 512],
                                                       start=(k == 0), stop=(k == 7)), r=[hb, Winb[c0 // 512]], w=[Bb[bk]])

            def proj_feat(bk, c0):
                bv = b3(bk, 4)
                for hh in range(4):
                    for k in range(8):
                        P.op('pe', lambda e, k=k, hh=hh: e.matmul(bv[:, hh, :], lhsT=Win[:, k, c0 + hh * 128:c0 + (hh + 1) * 128],
                                                                  rhs=h[:, k, :], start=(k == 0), stop=(k == 7)),
                             r=[hb, Winb[c0 // 512]], w=[Bb[bk]])
            proj_tok(p1, 0)
            proj_tok(p2, 512)
            P.op('act', lambda e: e.activation(out=T.gu_[:, :], in_=B[p1][:, :], func=AF.Gelu_apprx_tanh), r=[Bb[p1]], w=[T.gu_b])
            P.op('act', lambda e: e.activation(out=T.gv[:, :, :], in_=b3(p2, 8), func=AF.Gelu_apprx_tanh), r=[Bb[p2]], w=[T.gvb])
            proj_tok(p0, 2048)
            proj_tok(p1, 2560)
            P.op('act', lambda e: e.activation(out=T.v16[:, :, :], in_=b3(p0, 4), func=AF.Copy), r=[Bb[p0]], w=[T.v16b])
            P.op('act', lambda e: e.activation(out=T.sgl[:, :], in_=B[p1][:, :], func=AF.Silu), r=[Bb[p1]], w=[T.sglb])
            proj_feat(p2, 1024)
            proj_feat(p0, 1536)
            P.op('act', lambda e: e.activation(out=T.qf32[:, :, :], in_=b3(p2, 4), func=AF.Copy), r=[Bb[p2]], w=[T.qf32b])
            P.op('act', lambda e: e.activation(out=T.f1[:, :, :], in_=b3(p0, 4), func=AF.Sigmoid), r=[Bb[p0]], w=[T.f1b])
            lb_bc = lb[:, :].unsqueeze(2).broadcast_to([128, 4, 128])
            oml_bc = oml[:, :].unsqueeze(2).broadcast_to([128, 4, 128])
            P.op('dve', lambda e: e.tensor_tensor(out=T.f1[:, :, :], in0=T.f1[:, :, :], in1=oml_bc, op=ALU.mult), r=[T.f1b, cb], w=[T.f1b])
            P.op('dve', lambda e: e.tensor_tensor(out=T.f1[:, :, :], in0=T.f1[:, :, :], in1=lb_bc, op=ALU.add), r=[T.f1b, cb], w=[T.f1b])
            P.op('dve', lambda e: e.tensor_scalar(out=T.kk[:, :, :], in0=T.f1[:, :, :], scalar1=-1.0, scalar2=1.0, op0=ALU.mult, op1=ALU.add),
                 r=[T.f1b], w=[T.kkb])
            P.op('dve', lambda e: e.tensor_scalar(out=T.lf[:, :, :], in0=T.f1[:, :, :], scalar1=1e-6, scalar2=None, op0=ALU.max),
                 r=[T.f1b], w=[T.lfb])
            P.op('act', lambda e: e.activation(out=T.lf[:, :, :], in_=T.lf[:, :, :], func=AF.Ln), r=[T.lfb], w=[T.lfb])
            P.op('dve', lambda e: e.tensor_tensor_scan(out=T.Bc[:, :, :].rearrange("p h t -> p (h t)"), data0=rmask[:, :],
                                                       data1=T.lf[:, :, :].rearrange("p h t -> p (h t)"), initial=0.0, op0=ALU.mult,
                                                       op1=ALU.add), r=[T.lfb, cb], w=[T.Bcb])
            P.op('act', lambda e: e.activation(out=T.e1[:, :, :], in_=T.Bc[:, :, :], func=AF.Exp), r=[T.Bcb], w=[T.e1b])
            P.op('dve', lambda e: e.tensor_tensor(out=T.qt[:, :, :], in0=T.qf32[:, :, :], in1=T.e1[:, :, :], op=ALU.mult),
                 r=[T.qf32b, T.e1b], w=[T.qtb])
            P.op('dve', lambda e: e.tensor_scalar(out=T.e2[:, :, :], in0=T.Bc[:, :, :], scalar1=-60.0, scalar2=None, op0=ALU.max),
                 r=[T.Bcb], w=[T.e2b])
            P.op('act', lambda e: e.activation(out=T.e2[:, :, :], in_=T.e2[:, :, :], func=AF.Exp, scale=-1.0), r=[T.e2b], w=[T.e2b])
            P.op('dve', lambda e: e.tensor_tensor(out=T.kt_[:, :, :], in0=T.kk[:, :, :], in1=T.e2[:, :, :], op=ALU.mult),
                 r=[T.kkb, T.e2b], w=[T.kt_b])
            B4 = T.Bc[:, :, :].rearrange("p h (c t) -> p h c t", t=32)
            BCl = B4[:, :, :, 31:32]
            P.op('dve', lambda e: e.tensor_tensor(out=T.e3[:, :, :].rearrange("p h (c t) -> p h c t", t=32),
                                                  in0=BCl.broadcast_to([128, 4, 4, 32]), in1=B4, op=ALU.subtract), r=[T.Bcb], w=[T.e3b])
            P.op('act', lambda e: e.activation(out=T.e3[:, :, :], in_=T.e3[:, :, :], func=AF.Exp), r=[T.e3b], w=[T.e3b])
            P.op('dve', lambda e: e.tensor_tensor(out=T.kh[:, :, :], in0=T.kk[:, :, :], in1=T.e3[:, :, :], op=ALU.mult),
                 r=[T.kkb, T.e3b], w=[T.khb])
            P.op('act', lambda e: e.activation(out=T.eBC[:, :, :, :], in_=BCl, func=AF.Exp), r=[T.Bcb], w=[T.eBCb])
            tk = b16(p1, 4, 128)
            for hh in range(4):
                P.op('pe', lambda e, hh=hh: e.transpose(out=tk[:, hh, :], in_=T.kh[:, hh, :], identity=C.ident[:, :]),
                     r=[T.khb, C.constb], w=[Bb[p1]])
            P.op('act', lambda e: e.activation(out=T.khT[:, :, :], in_=tk, func=AF.Copy), r=[Bb[p1]], w=[T.khTb])
            P.op('dve', lambda e: e.tensor_tensor(out=T.sq[:, :, :], in0=T.gv[:, :, :], in1=T.gv[:, :, :], op=ALU.mult), r=[T.gvb], w=[T.sqb])
            P.op('dve', lambda e: e.tensor_reduce(out=T.sm[:, 0:8], in_=T.sq[:, :, :], axis=AX.X, op=ALU.add), r=[T.sqb], w=[T.smb])
            P.op('dve', lambda e: e.tensor_scalar(out=T.sm[:, 8:16], in0=T.sm[:, 0:8], scalar1=1.0 / 64, scalar2=EPS, op0=ALU.mult,
                                                  op1=ALU.add), r=[T.smb], w=[T.smb])
            P.op('pool', lambda e: e.tensor_tensor(out=T.sm[:, 16:24], in0=T.sm[:, 8:16], in1=C.neghalf[:, 0:8], op=ALU.pow),
                 r=[T.smb, C.constb], w=[T.smb])
            P.op('dve', lambda e: e.tensor_tensor(out=T.sq[:, :, :], in0=T.gv[:, :, :],
                                                  in1=T.sm[:, 16:24].unsqueeze(2).broadcast_to([128, 8, 64]), op=ALU.mult),
                 r=[T.gvb, T.smb], w=[T.sqb])
            P.op('dve', lambda e: e.tensor_tensor(out=T.vn16[:, :, :], in0=T.sq[:, :, :],
                                                  in1=vnorm[:, :].rearrange("p (h d) -> p h d", h=8), op=ALU.mult),
                 r=[T.sqb, gb], w=[T.vn16b])
            Mv = b3(p2, 8)
            for hh in range(8):
                P.op('pe', lambda e, hh=hh: e.matmul(Mv[:, hh, :], lhsT=WsT[:, hh, :], rhs=T.vn16[:, hh, :], start=True, stop=True),
                     r=[cb, T.vn16b], w=[Bb[p2]])
            P.op('dve', lambda e: e.tensor_tensor(out=T.sq[:, :, :], in0=Mv, in1=bs[:, :, :].broadcast_to([128, 8, 64]), op=ALU.add),
                 r=[Bb[p2], gb], w=[T.sqb])
            P.op('dve', lambda e: e.tensor_tensor(out=T.mixed[:, 0:512], in0=T.sq[:, :, :].rearrange("p h d -> p (h d)"), in1=T.gu_[:, :],
                                                  op=ALU.mult), r=[T.sqb, T.gu_b], w=[T.mixb[0]])
            sc = b3(p0, 4)
            for hh in range(4):
                P.op('pe', lambda e, hh=hh: e.matmul(sc[:, hh, :], lhsT=T.kt_[:, hh, :], rhs=T.qt[:, hh, :], start=True, stop=True),
                     r=[T.kt_b, T.qtb], w=[Bb[p0]])
            for hh in range(4):
                P.op('dve', lambda e, hh=hh: e.tensor_tensor(out=T.PT[hh][:, :], in0=sc[:, hh, :], in1=bmask[:, :], op=ALU.mult),
                     r=[Bb[p0], cb], w=[T.PTb[hh]])
            O = b3(R.o, 4)
            for hh in range(4):
                P.op('pe', lambda e, hh=hh: e.matmul(O[:, hh, :], lhsT=T.PT[hh][:, :], rhs=T.v16[:, hh, :], start=(hh == 0), stop=False,
                                                     skip_group_check=True), r=[T.PTb[hh], T.v16b], w=[Bb[R.o]])

        def chain(t):
            T = TB[t % 2]
            R = BS[t % 2]
            p0, p1, p2 = R.p
            O = b3(R.o, 4)
            Ust = b3(p1, 4)
            for c in range(4):
                for hh in range(4):
                    P.op('pe', lambda e, hh=hh, c=c: e.matmul(O[32 * c:32 * c + 32, hh, :], lhsT=T.qt[:, hh, 32 * c:32 * c + 32],
                                                              rhs=S16[:, hh, :], start=False, stop=(c == 3), tile_position=(0, 32 * c),
                                                              skip_group_check=True), r=[T.qtb, S16b], w=[Bb[R.o]])
                for hh in range(4):
                    P.op('pe', lambda e, hh=hh, c=c: e.matmul(Ust[:, hh, :], lhsT=T.khT[32 * c:32 * c + 32, hh, :],
                                                              rhs=T.v16[32 * c:32 * c + 32, hh, :], start=True, stop=True,
                                                              tile_position=(32 * c, 0)), r=[T.khTb, T.v16b], w=[Bb[p1]])
                for hh in range(4):
                    P.op('dve', lambda e, hh=hh, c=c: e.scalar_tensor_tensor(out=S32[:, hh, :], in0=S32[:, hh, :], scalar=T.eBC[:, hh, c, :],
                                                                             in1=Ust[:, hh, :], op0=ALU.mult, op1=ALU.add),
                         r=[S32b, T.eBCb, Bb[p1]], w=[S32b])
                P.op('act', lambda e: e.activation(out=S16[:, :, :], in_=S32[:, :, :], func=AF.Copy), r=[S32b], w=[S16b])

        def post(t):
            T = TB[t % 2]
            R = BS[t % 2]
            p0, p1, p2 = R.p
            x = xt[t % 4]
            xb = xtb[t % 4]
            O = b3(R.o, 4)
            for hh in range(4):
                P.op('act', lambda e, hh=hh: e.activation(out=C.junk[:, 0:128], in_=O[:, hh, :], func=AF.Square,
                                                           accum_out=T.sm[:, 32 + hh:33 + hh]), r=[Bb[R.o]], w=[T.smb, C.junkb])
            P.op('dve', lambda e: e.tensor_scalar(out=T.sm[:, 40:44], in0=T.sm[:, 32:36], scalar1=1.0 / 128, scalar2=EPS, op0=ALU.mult,
                                                  op1=ALU.add), r=[T.smb], w=[T.smb])
            P.op('pool', lambda e: e.tensor_tensor(out=T.sm[:, 44:48], in0=T.sm[:, 40:44], in1=C.neghalf[:, 0:4], op=ALU.pow),
                 r=[T.smb, C.constb], w=[T.smb])
            for hh in range(4):
                P.op('act', lambda e, hh=hh: e.activation(out=T.on[:, hh, :], in_=O[:, hh, :], func=AF.Copy, scale=T.sm[:, 44 + hh:45 + hh]),
                     r=[Bb[R.o], T.smb], w=[T.onb])
            P.op('dve', lambda e: e.tensor_tensor(out=T.on[:, :, :], in0=T.on[:, :, :],
                                                  in1=onorm[:, :].unsqueeze(1).broadcast_to([128, 4, 128]), op=ALU.mult),
                 r=[T.onb, gb], w=[T.onb])
            P.op('dve', lambda e: e.tensor_tensor(out=T.mixed[:, 512:1024], in0=T.on[:, :, :].rearrange("p h d -> p (h d)"), in1=T.sgl[:, :],
                                                  op=ALU.mult), r=[T.onb, T.sglb], w=[T.mixb[1]])
            tm = b16(p2, 8, 128)
            for k in range(8):
                P.op('pe', lambda e, k=k: e.transpose(out=tm[:, k, :], in_=T.mixed[:, k * 128:(k + 1) * 128], identity=C.ident[:, :]),
                     r=[T.mixb[0], T.mixb[1], C.constb], w=[Bb[p2]])
            P.op('act', lambda e: e.activation(out=T.mT[:, :, :], in_=tm, func=AF.Copy), r=[Bb[p2]], w=[T.mTb])
            for dh, bk in ((0, p0), (1, p1)):
                for k in range(8):
                    P.op('pe', lambda e, k=k, dh=dh, bk=bk: e.matmul(B[bk][:, :], lhsT=T.mT[:, k, :], rhs=Wo[:, k, dh * 512:(dh + 1) * 512],
                                                                    start=(k == 0), stop=(k == 7)), r=[T.mTb, Wob], w=[Bb[bk]])
                P.op('dve', lambda e, dh=dh, bk=bk: e.tensor_tensor(out=x[:, 0, dh * 512:(dh + 1) * 512], in0=x[:, 0, dh * 512:(dh + 1) * 512],
                                                                   in1=B[bk][:, :], op=ALU.add), r=[Bb[bk], xb], w=[xb])
            P.dma(xout[t], x[:, :, :], r=[xb], sig=f"xt{t % 4}")

        for t in range(min(4, ntile)):
            load_x(t)
        P.replay([P.capture(lambda t=t: pre_norm(t)) for t in range(min(2, ntile))])
        for t0 in range(0, ntile, 2):
            ts = list(range(t0, min(t0 + 2, ntile)))
            nxt = list(range(t0 + 2, min(t0 + 4, ntile)))
            P.replay([P.capture(lambda t=t: pre(t)) for t in ts])
            chain(ts[0])
            if len(ts) > 1:
                P.replay([P.capture(lambda: post(ts[0])), P.capture(lambda: chain(ts[1]))])
                if nxt:
                    P.replay([P.capture(lambda t=t: pre_norm(t)) for t in nxt])
                post(ts[1])
            else:
                post(ts[0])
            for t in range(t0 + 4, min(t0 + 6, ntile)):
                load_x(t)
        P.flush()


def setup_consts(P, C, es, ident_dram):
    nc = P.nc

    def sb(name, shape, dt):
        return es.enter_context(nc.sbuf_tensor(uname(name), shape, dt))
    C.identf = sb("identf", [128, 128], F32)
    C.ident = sb("ident", [128, 128], BF16)
    C.neghalf = sb("neghalf", [128, 8], F32)
    C.junk = sb("junk", [128, D], BF16)
    C.ss = sb("ss", [128, 8], F32)
    C.vv = sb("vv", [128, 8], F32)
    C.rstd = sb("rstd", [128, 8], F32)
    b0 = P.buf("c0")
    P.dma(C.identf[:, :], ident_dram, w=[b0], sig="identf")
    P.op('dve', lambda e: e.tensor_copy(out=C.ident[:, :], in_=C.identf[:, :]), r=[b0], w=[b0])
    P.op('pool', lambda e: e.memset(C.neghalf[:, :], -0.5), w=[b0])
    P.flush()


def new_phase_bufs(P, C):
    C.constb = P.buf("const")
    C.junkb = P.buf("junk")
    C.ssb = [P.buf(f"ss{s}") for s in range(8)]
    C.vvb = P.buf("vv")
    C.rstdb = P.buf("rstd")
    C.vv2b = [P.buf("vv2_0"), P.buf("vv2_1")]
    C.rstd2b = [P.buf("rstd2_0"), P.buf("rstd2_1")]


def build_nc(S, plan):
    nc = bass.Bass("TRN2", target_bir_lowering=False)

    def din(name, shape, dt=F32):
        return nc.dram_tensor(name, list(shape), dt, kind="ExternalInput").ap()
    x = din("x", [S, D])
    ident = din("ident", [128, 128])
    W = {
        'norm_gains': din("norm_gains", [DEPTH, 5, D]),
        'ffn_w_gate': din("ffn_w_gate", [DEPTH, 2, D, DFF]),
        'ffn_w_up': din("ffn_w_up", [DEPTH, 2, D, DFF]),
        'ffn_w_down': din("ffn_w_down", [DEPTH, 2, DFF, D]),
        'ple_w_gate': din("ple_w_gate", [DEPTH, D, D]),
        'ple_w_proj': din("ple_w_proj", [DEPTH, 256, D]),
        'p': din("p", [DEPTH, S, 256]),
        'positions': din("positions", [S], I32),
        'invf': din("invf", [2, 32]),
        'mla_w_in': din("mla_w_in", [2, D, 704]),
        'mla_q_a_norm': din("mla_q_a_norm", [2, 384]),
        'mla_kv_a_norm': din("mla_kv_a_norm", [2, 256]),
        'mla_w_q_b': din("mla_w_q_b", [2, 384, 1536]),
        'mla_w_kv_b': din("mla_w_kv_b", [2, 256, 2048]),
        'mla_q_norm': din("mla_q_norm", [2, 192]),
        'mla_k_norm': din("mla_k_norm", [2, 192]),
        'mla_w_out': din("mla_w_out", [2, D, D]),
        'even_w_in': din("even_w_in", [2, D, 3072]),
        'gmlp_v_norm': din("gmlp_v_norm", [2, 8, 64]),
        'gmlp_w_s': din("gmlp_w_s", [2, 8, 128, 128]),
        'gmlp_b_s': din("gmlp_b_s", [2, 8, 128]),
        'hgrn_lb_raw': din("hgrn_lb_raw", [2, 512]),
        'hgrn_out_norm': din("hgrn_out_norm", [2, 128]),
        'even_w_out': din("even_w_out", [2, D, D]),
    }
    kd = "ExternalOutput" if DEBUG else "Internal"
    scrQ = (nc.dram_tensor("QTn", [NH, 128, S], BF16, kind=kd).ap(),
            nc.dram_tensor("QTr", [NH, 64, S], BF16, kind=kd).ap(),
            nc.dram_tensor("KTn", [NH, 128, S], BF16, kind=kd).ap(),
            nc.dram_tensor("KTr", [NH, 64, S], BF16, kind=kd).ap(),
            nc.dram_tensor("Vs", [S, NH, 128], BF16, kind=kd).ap())
    y = nc.dram_tensor("y", [S, D], F32, kind="ExternalOutput").ap()
    scr = [nc.dram_tensor(f"xscr{i}", [S, D], F32, kind="Internal").ap() for i in range(3)]
    P = Prog(nc)
    C = Ctx()
    with ExitStack() as es:
        new_phase_bufs(P, C)
        setup_consts(P, C, es, ident)
        cur = x
        si = 0
        for pi, ph in enumerate(plan):
            if pi == len(plan) - 1:
                dst = y
            else:
                dst = scr[si % 3]
                si += 1
            new_phase_bufs(P, C)
            if ph[0] == 'even':
                _, l = ph
                phase_even(P, C, cur, dst, W, l, l // 2, S)
            if ph[0] == 'mla':
                _, l = ph
                phase_mla1(P, C, cur, W, l, l // 2, S, scrQ)
                mid = scr[si % 3]
                si += 1
                new_phase_bufs(P, C)
                phase_mla2(P, C, cur, mid, W, l // 2, 0, S, scrQ)
                new_phase_bufs(P, C)
                phase_mla2(P, C, mid, dst, W, l // 2, 1, S, scrQ)
            if ph[0] == 'ffn':
                _, l, which = ph
                phase_ffn(P, C, cur, dst, W['ffn_w_gate'][l, which], W['ffn_w_up'][l, which], W['ffn_w_down'][l, which],
                          W['norm_gains'][l, 0 if which == 0 else 2, :], S)
            elif ph[0] == 'ple':
                _, l = ph
                phase_ple(P, C, cur, dst, W['p'][l], W['ple_w_gate'][l], W['ple_w_proj'][l], W['norm_gains'][l, 3, :],
                          W['norm_gains'][l, 4, :], S)
            cur = dst
    return nc


WEIGHT_KEYS = ['norm_gains', 'ffn_w_gate', 'ffn_w_up', 'ffn_w_down', 'ple_w_gate', 'ple_w_proj', 'even_w_in', 'gmlp_v_norm',
               'gmlp_w_s', 'gmlp_b_s', 'hgrn_lb_raw', 'hgrn_out_norm', 'even_w_out', 'mla_w_in', 'mla_q_a_norm', 'mla_kv_a_norm',
               'mla_w_q_b', 'mla_w_kv_b', 'mla_q_norm', 'mla_k_norm', 'mla_w_out']
N_CORES = 8
SEQ = 4096
FUSED = True


def layer_plan(l):
    return [('ffn', l, 0), ('even', l) if l % 2 == 0 else ('mla', l), ('ffn', l, 1), ('ple', l)]


def kernel(**inputs):
    x = np.ascontiguousarray(np.asarray(inputs['x'], dtype=np.float32))
    p = np.asarray(inputs['p'], dtype=np.float32)
    pos = np.asarray(inputs['positions']).astype(np.int32)
    Wt = {k: np.ascontiguousarray(np.asarray(inputs[k], dtype=np.float32)) for k in WEIGHT_KEYS}
    ident = np.eye(128, dtype=np.float32)
    invf = (10000.0 ** (-np.arange(0, 64, 2, dtype=np.float32) / 64)).astype(np.float32)
    invf2 = np.ascontiguousarray(np.stack([invf, invf]))
    B = x.shape[0]
    assert B == N_CORES and x.shape[1] == SEQ
    cur = [np.ascontiguousarray(x[b]) for b in range(B)]
    pb = [np.ascontiguousarray(p[:, b]) for b in range(B)]
    posb = [np.ascontiguousarray(pos[b]) for b in range(B)]
    plans = [sum([layer_plan(l) for l in range(DEPTH)], [])] if FUSED else [layer_plan(l) for l in range(DEPTH)]
    for plan in plans:
        nc = build_nc(SEQ, plan)
        in_maps = []
        for b in range(B):
            m = {"x": cur[b], "ident": ident, "invf": invf2, "p": pb[b], "positions": posb[b]}
            m.update(Wt)
            in_maps.append(m)
        res = run_bass_kernel_spmd(nc, in_maps, core_ids=list(range(N_CORES)))
        cur = [np.ascontiguousarray(np.asarray(res.results[b]["y"], dtype=np.float32)) for b in range(B)]
    return np.stack(cur, axis=0)
```
